# Optimizing a Trainium2 kernel written in Bass

```python
import math
import jax
import jax.numpy as jnp
from jax import lax
import numpy as np

D_MODEL = 1024
BATCH = 4
SEQ = 4096
DEPTH = 2

CTX_LEN = 256
GRID_W = 64
NORM_EPS = 1e-6

HY_WIDTH = 1024
HY_SHORT = 3
HY_EMB = 33
HY_BANDS = (HY_EMB - 1) // 2
HY_HIDDEN = 64
HY_TARGET = 1e-2
HY_SHORT_DECAY_PCT = 0.3
HY_LONG_DECAY_PCT = 1.5

ATT_HEADS = 8
ATT_KV_HEADS = 2
HEAD_DIM = 128
ATT_WIDTH = ATT_HEADS * HEAD_DIM
Q_BLOCK = 128
ROPE_THETA = 10000.0

EVEN_MIX = HY_WIDTH + ATT_WIDTH
EVEN_SPLITS = (3 * HY_WIDTH, HY_WIDTH, ATT_WIDTH, ATT_KV_HEADS * HEAD_DIM, ATT_KV_HEADS * HEAD_DIM, ATT_WIDTH)
EVEN_IN = sum(EVEN_SPLITS)

ML_HEADS = 8
ML_QK = 128
ML_V = 256
ML_WIDTH = ML_HEADS * ML_V
ML_CHUNK = 64
ML_SHORT = 3
ODD_SPLITS = (ML_HEADS * ML_QK, ML_HEADS * ML_QK, ML_WIDTH, ML_WIDTH, ML_WIDTH, 4 * ML_HEADS)
ODD_IN = sum(ODD_SPLITS)

N_EVEN = (DEPTH + 1) // 2
N_ODD = DEPTH // 2

kernel_name = 'hybrid_hyena_gqa_mlstm_prefix_dit'


def _split(a, sizes):
    out, start = [], 0
    for s in sizes:
        out.append(a[..., start:start + s])
        start += s
    return out


def rms_norm(x, g):
    xf = x.astype(jnp.float32)
    y = xf * lax.rsqrt(jnp.mean(xf * xf, axis=-1, keepdims=True) + NORM_EPS)
    return (y * g.astype(jnp.float32)).astype(x.dtype)


def dw_conv(u, w, b):
    ch = u.shape[-1]
    k = w.shape[0]
    y = lax.conv_general_dilated(u, w[:, None, :].astype(u.dtype), window_strides=(1,),
                                 padding=[(k // 2, k // 2)], dimension_numbers=('NWC', 'WIO', 'NWC'),
                                 feature_group_count=ch)
    return y + b.astype(u.dtype)


def adaln(cond, w_mod, b_mod):
    m = jax.nn.silu(cond) @ w_mod + b_mod
    return _split(m, (D_MODEL, D_MODEL, D_MODEL))


def axial_rope(x, rows, cols):
    f32 = jnp.float32
    half = x.shape[-1] // 2
    nf = half // 2
    inv = ROPE_THETA ** (-jnp.arange(nf, dtype=f32) / nf)

    def rot(xa, pos):
        ang = pos.astype(f32)[:, None] * inv
        cos = jnp.cos(ang)[None, :, None, :]
        sin = jnp.sin(ang)[None, :, None, :]
        x1 = xa[..., :nf].astype(f32)
        x2 = xa[..., nf:].astype(f32)
        return jnp.concatenate([x1 * cos - x2 * sin, x1 * sin + x2 * cos], axis=-1)

    out = jnp.concatenate([rot(x[..., :half], rows), rot(x[..., half:], cols)], axis=-1)
    return out.astype(x.dtype)


def hyena_filter(n, w1, b1, freq, w2, b2, w3):
    f32 = jnp.float32
    t = jnp.linspace(0.0, 1.0, n, dtype=f32)[:, None]
    bands = jnp.linspace(1e-4, HY_BANDS - 1, HY_BANDS, dtype=f32)
    ang = (2.0 * math.pi / n) * jnp.arange(n, dtype=f32)[:, None] * bands
    z = jnp.concatenate([t, jnp.cos(ang), -jnp.sin(ang)], axis=-1)
    fr = freq.astype(f32)
    hdn = jnp.sin(fr * (z @ w1.astype(f32) + b1.astype(f32)))
    hdn = jnp.sin(fr * (hdn @ w2.astype(f32) + b2.astype(f32)))
    h = (hdn @ w3.astype(f32)).reshape(n, 2, HY_WIDTH)
    max_decay = math.log(HY_TARGET) / HY_SHORT_DECAY_PCT
    min_decay = math.log(HY_TARGET) / HY_LONG_DECAY_PCT
    deltas = jnp.linspace(min_decay, max_decay, HY_WIDTH, dtype=f32)
    h = h * jnp.exp(-t * jnp.abs(deltas))[:, None, :]
    two_sided = jnp.concatenate([h[:, 0], jnp.zeros((1, HY_WIDTH), f32), h[:0:-1, 1]], axis=0)
    return two_sided / jnp.sum(jnp.abs(two_sided), axis=0, keepdims=True)


def bidir_long_conv(u, filt):
    n = u.shape[1]
    uf = jnp.fft.rfft(u.astype(jnp.float32), n=2 * n, axis=1)
    ff = jnp.fft.rfft(filt, axis=0)
    return jnp.fft.irfft(uf * ff[None], n=2 * n, axis=1)[:, :n].astype(u.dtype)


def hyena_mixer(xv, conv_w, conv_b, f_w1, f_b1, f_freq, f_w2, f_b2, f_w3, hy_bias):
    n = xv.shape[1]
    x0, x1, v = _split(dw_conv(xv, conv_w, conv_b), (HY_WIDTH, HY_WIDTH, HY_WIDTH))
    filt = hyena_filter(n, f_w1, f_b1, f_freq, f_w2, f_b2, f_w3)
    g = v * x1
    return x0 * (bidir_long_conv(g, filt) + g * hy_bias.astype(g.dtype))


def block_attention(q, k, v):
    B, n, H, Dh = q.shape
    kvh = k.shape[2]
    grp = H // kvh
    nb = n // Q_BLOCK
    qb = jnp.moveaxis(q.reshape(B, nb, Q_BLOCK, kvh, grp, Dh), 1, 0)
    scale = Dh ** -0.5

    def attend(qblk):
        s = jnp.einsum('bqkgd,btkd->bkgqt', qblk, k, preferred_element_type=jnp.float32) * scale
        p = jax.nn.softmax(s, axis=-1).astype(v.dtype)
        return jnp.einsum('bkgqt,btkd->bqkgd', p, v)

    out = lax.map(attend, qb)
    return jnp.moveaxis(out, 0, 1).reshape(B, n, H * Dh)


def even_mixer(h_lat, h_ctx, rows, cols, need_ctx, w_in, conv_w, conv_b, f_w1, f_b1, f_freq,
               f_w2, f_b2, f_w3, hy_bias, q_norm, k_norm, w_out):
    def project(h):
        B, n = h.shape[:2]
        xv, g_hy, q, k, v, g_att = _split(h @ w_in, EVEN_SPLITS)
        q = rms_norm(q.reshape(B, n, ATT_HEADS, HEAD_DIM), q_norm)
        k = rms_norm(k.reshape(B, n, ATT_KV_HEADS, HEAD_DIM), k_norm)
        v = v.reshape(B, n, ATT_KV_HEADS, HEAD_DIM)
        return xv, g_hy, q, k, v, g_att

    def hyena(xv):
        return hyena_mixer(xv, conv_w, conv_b, f_w1, f_b1, f_freq, f_w2, f_b2, f_w3, hy_bias)

    def combine(hy, att, g_hy, g_att):
        return jnp.concatenate([hy * jax.nn.silu(g_hy), att * jax.nn.silu(g_att)], axis=-1) @ w_out

    xv_l, gh_l, q_l, k_l, v_l, ga_l = project(h_lat)
    xv_c, gh_c, q_c, k_c, v_c, ga_c = project(h_ctx)
    q_l = axial_rope(q_l, rows, cols)
    k_l = axial_rope(k_l, rows, cols)
    att_l = block_attention(q_l, jnp.concatenate([k_l, k_c], axis=1), jnp.concatenate([v_l, v_c], axis=1))
    y_lat = combine(hyena(xv_l), att_l, gh_l, ga_l)
    y_ctx = None
    if need_ctx:
        y_ctx = combine(hyena(xv_c), block_attention(q_c, k_c, v_c), gh_c, ga_c)
    return y_lat, y_ctx


def mlstm_chunkwise(q, k, v, i_pre, f_pre, state):
    f32 = jnp.float32
    B, H, n, _ = q.shape
    nc = n // ML_CHUNK

    def chunks(a):
        a = a.astype(f32)
        return jnp.moveaxis(a.reshape(B, H, nc, ML_CHUNK, *a.shape[3:]), 2, 0)

    lower = jnp.tril(jnp.ones((ML_CHUNK, ML_CHUNK), dtype=bool))

    def step(carry, inp):
        C, nvec, m = carry
        qc, kc, vc, ic, lfc = inp
        b = jnp.cumsum(lfc, axis=-1)
        d = jnp.where(lower, b[..., :, None] - b[..., None, :] + ic[..., None, :], -jnp.inf)
        inter = b + m[..., None]
        m_row = jnp.maximum(inter, jnp.max(d, axis=-1))
        s = jnp.einsum('bhtd,bhsd->bhts', qc, kc) * jnp.exp(d - m_row[..., None])
        w_prev = jnp.exp(inter - m_row)
        num = jnp.einsum('bhts,bhsv->bhtv', s, vc) + w_prev[..., None] * jnp.einsum('bhvd,bhtd->bhtv', C, qc)
        den = jnp.sum(s, axis=-1) + w_prev * jnp.einsum('bhd,bhtd->bht', nvec, qc)
        h = num / jnp.maximum(jnp.abs(den), jnp.exp(-m_row))[..., None]
        b_end = b[..., -1]
        g = b_end[..., None] - b + ic
        m_new = jnp.maximum(b_end + m, jnp.max(g, axis=-1))
        a = jnp.exp(g - m_new[..., None])
        a_prev = jnp.exp(b_end + m - m_new)
        C = a_prev[..., None, None] * C + jnp.einsum('bhsv,bhsd->bhvd', vc * a[..., None], kc)
        nvec = a_prev[..., None] * nvec + jnp.einsum('bhs,bhsd->bhd', a, kc)
        return (C, nvec, m_new), h

    xs = (chunks(q), chunks(k), chunks(v), chunks(i_pre), chunks(jax.nn.log_sigmoid(f_pre.astype(f32))))
    state, hs = lax.scan(step, state, xs)
    return state, jnp.moveaxis(hs, 0, 2).reshape(B, H, n, v.shape[-1])


def odd_mixer(h_lat, h_ctx, need_ctx, w_in, conv_w, conv_b, gate_b, head_norm, w_out):
    f32 = jnp.float32

    def project(h):
        B, n = h.shape[:2]
        q, k, v, o, z, gates = _split(h @ w_in, ODD_SPLITS)
        qk = jax.nn.silu(dw_conv(jnp.concatenate([q, k], axis=-1), conv_w, conv_b))
        q, k = _split(qk, (ML_HEADS * ML_QK, ML_HEADS * ML_QK))
        q = q.reshape(B, n, ML_HEADS, ML_QK).transpose(0, 2, 1, 3)
        k = k.reshape(B, n, ML_HEADS, ML_QK).transpose(0, 2, 1, 3) * (ML_QK ** -0.5)
        v = v.reshape(B, n, ML_HEADS, ML_V).transpose(0, 2, 1, 3)
        gates = (gates + gate_b).astype(f32).reshape(B, n, 4, ML_HEADS).transpose(2, 0, 3, 1)
        return q, k, v, o, z, gates

    def flip(a):
        return jnp.flip(a, axis=2)

    ql, kl, vl, ol, zl, gl = project(h_lat)
    qc, kc, vc, oc, zc, gc = project(h_ctx)
    B = h_lat.shape[0]
    zero = (jnp.zeros((B, ML_HEADS, ML_V, ML_QK), f32), jnp.zeros((B, ML_HEADS, ML_QK), f32),
            jnp.zeros((B, ML_HEADS), f32))
    st_f, hc_f = mlstm_chunkwise(qc, kc, vc, gc[0], gc[1], zero)
    _, hl_f = mlstm_chunkwise(ql, kl, vl, gl[0], gl[1], st_f)
    st_b, hc_b = mlstm_chunkwise(flip(qc), flip(kc), flip(vc), flip(gc[2]), flip(gc[3]), zero)
    _, hl_b = mlstm_chunkwise(flip(ql), flip(kl), flip(vl), flip(gl[2]), flip(gl[3]), st_b)

    def finish(h_f, h_b, o, z):
        Bn, _, n, _ = h_f.shape
        h = (h_f + h_b).transpose(0, 2, 1, 3).reshape(Bn, n, ML_WIDTH).astype(o.dtype) * jax.nn.sigmoid(o)
        h = rms_norm(h.reshape(Bn, n, ML_HEADS, ML_V), head_norm.reshape(ML_HEADS, ML_V)).reshape(Bn, n, ML_WIDTH)
        return (h * jax.nn.silu(z)) @ w_out

    y_lat = finish(hl_f, flip(hl_b), ol, zl)
    y_ctx = finish(hc_f, flip(hc_b), oc, zc) if need_ctx else None
    return y_lat, y_ctx


def setup_inputs(seed: int = 0) -> dict:
    key = jax.random.key(seed)
    keys = jax.random.split(key, 32)
    f32 = jnp.float32

    def nrm(i, shape, scale):
        return jax.random.normal(keys[i], shape, f32) * scale

    D = D_MODEL
    forget_init = jnp.linspace(3.0, 6.0, ML_HEADS, dtype=f32)
    o_gate_b = jnp.concatenate([
        nrm(24, (N_ODD, ML_HEADS), 0.1),
        forget_init + nrm(25, (N_ODD, ML_HEADS), 0.1),
        nrm(26, (N_ODD, ML_HEADS), 0.1),
        forget_init + nrm(27, (N_ODD, ML_HEADS), 0.1)], axis=-1)
    return {
        'x': nrm(0, (BATCH, SEQ, D), 1.0),
        'c': nrm(1, (BATCH, D), 1.0),
        'ctx': nrm(2, (BATCH, CTX_LEN, D), 1.0),
        'c_ctx': nrm(3, (D,), 1.0),
        'w_mod': nrm(4, (DEPTH, D, 3 * D), D ** -0.5),
        'b_mod': nrm(5, (DEPTH, 3 * D), 0.02),
        'g_pre': 1.0 + nrm(6, (DEPTH, D), 0.02),
        'g_post': 1.0 + nrm(7, (DEPTH, D), 0.02),
        'e_w_in': nrm(8, (N_EVEN, D, EVEN_IN), D ** -0.5),
        'e_conv_w': nrm(9, (N_EVEN, HY_SHORT, 3 * HY_WIDTH), HY_SHORT ** -0.5),
        'e_conv_b': nrm(10, (N_EVEN, 3 * HY_WIDTH), 0.02),
        'e_filt_w1': nrm(11, (N_EVEN, HY_EMB, HY_HIDDEN), HY_EMB ** -0.5),
        'e_filt_b1': nrm(12, (N_EVEN, HY_HIDDEN), 0.1),
        'e_filt_freq': 1.0 + nrm(13, (N_EVEN, HY_HIDDEN), 0.02),
        'e_filt_w2': nrm(14, (N_EVEN, HY_HIDDEN, HY_HIDDEN), HY_HIDDEN ** -0.5),
        'e_filt_b2': nrm(15, (N_EVEN, HY_HIDDEN), 0.1),
        'e_filt_w3': nrm(16, (N_EVEN, HY_HIDDEN, 2 * HY_WIDTH), HY_HIDDEN ** -0.5),
        'e_hy_bias': nrm(17, (N_EVEN, HY_WIDTH), 0.1),
        'e_q_norm': 1.0 + nrm(18, (N_EVEN, HEAD_DIM), 0.02),
        'e_k_norm': 1.0 + nrm(19, (N_EVEN, HEAD_DIM), 0.02),
        'e_w_out': nrm(20, (N_EVEN, EVEN_MIX, D), EVEN_MIX ** -0.5),
        'o_w_in': nrm(21, (N_ODD, D, ODD_IN), D ** -0.5),
        'o_conv_w': nrm(22, (N_ODD, ML_SHORT, 2 * ML_HEADS * ML_QK), ML_SHORT ** -0.5),
        'o_conv_b': nrm(23, (N_ODD, 2 * ML_HEADS * ML_QK), 0.02),
        'o_gate_b': o_gate_b,
        'o_head_norm': 1.0 + nrm(28, (N_ODD, ML_WIDTH), 0.02),
        'o_w_out': nrm(29, (N_ODD, ML_WIDTH, D), ML_WIDTH ** -0.5),
    }


def reference(x, c, ctx, c_ctx, w_mod, b_mod, g_pre, g_post, e_w_in, e_conv_w, e_conv_b, e_filt_w1,
              e_filt_b1, e_filt_freq, e_filt_w2, e_filt_b2, e_filt_w3, e_hy_bias, e_q_norm, e_k_norm,
              e_w_out, o_w_in, o_conv_w, o_conv_b, o_gate_b, o_head_norm, o_w_out):
    n_lat = x.shape[1]
    ROWS = n_lat // GRID_W
    rows = jnp.repeat(jnp.arange(ROWS, dtype=jnp.int32), GRID_W)
    cols = jnp.tile(jnp.arange(GRID_W, dtype=jnp.int32), ROWS)
    x_lat, x_ctx = x, ctx
    for layer in range(DEPTH):
        need_ctx = layer < DEPTH - 1
        sh_l, sc_l, ga_l = adaln(c, w_mod[layer], b_mod[layer])
        sh_c, sc_c, ga_c = adaln(c_ctx, w_mod[layer], b_mod[layer])
        h_lat = rms_norm(x_lat, g_pre[layer]) * (1 + sc_l[:, None]) + sh_l[:, None]
        h_ctx = rms_norm(x_ctx, g_pre[layer]) * (1 + sc_c) + sh_c
        j = layer // 2
        if layer % 2 == 0:
            y_lat, y_ctx = even_mixer(h_lat, h_ctx, rows, cols, need_ctx, e_w_in[j], e_conv_w[j], e_conv_b[j],
                                      e_filt_w1[j], e_filt_b1[j], e_filt_freq[j], e_filt_w2[j], e_filt_b2[j],
                                      e_filt_w3[j], e_hy_bias[j], e_q_norm[j], e_k_norm[j], e_w_out[j])
        else:
            y_lat, y_ctx = odd_mixer(h_lat, h_ctx, need_ctx, o_w_in[j], o_conv_w[j], o_conv_b[j], o_gate_b[j],
                                     o_head_norm[j], o_w_out[j])
        x_lat = x_lat + ga_l[:, None] * rms_norm(y_lat, g_post[layer])
        if need_ctx:
            x_ctx = x_ctx + ga_c * rms_norm(y_ctx, g_post[layer])
    return x_lat
```

```python
import math
from contextlib import ExitStack
import numpy as np
import ml_dtypes
import concourse.bass as bass
import concourse.mybir as mybir
from concourse.bass_utils import run_bass_kernel_spmd

F32 = mybir.dt.float32
BF16 = mybir.dt.bfloat16
AF = mybir.ActivationFunctionType
ALU = mybir.AluOpType
AX = mybir.AxisListType
NPBF = ml_dtypes.bfloat16

D = 1024
NLAT = 4096
NCTX = 256
NTOK = NLAT + NCTX
EPS = 1e-6

ENGS = ("pe", "act", "dve", "pool", "sp")
SEM_ROT = 12000
ND = 8
CC_INC = 1
_DT_SIZE = {}


def _dsize(dt):
    if dt not in _DT_SIZE:
        _DT_SIZE[dt] = mybir.dt.size(dt)
    return _DT_SIZE[dt]


def _box(ap):
    t = ap.tensor
    dims = list(ap.ap)
    off = ap.offset
    sp = str(ap.space)
    if sp in ("SB", "PSUM"):
        pstep = 1
        for s in t.shape[1:]:
            pstep *= s
        p0 = off // pstep
        f0 = off % pstep
        pd = dims[0]
        npart = 1 if pd[0] == 0 else pd[1]
        ext = 0
        for st, cnt in dims[1:]:
            ext += (cnt - 1) * abs(st)
        if sp == "PSUM":
            return (t.name, 0, 128, 0, pstep)
        return (t.name, p0, p0 + npart, f0, f0 + ext + 1)
    ext = 0
    for st, cnt in dims:
        ext += (cnt - 1) * abs(st)
    return (t.name, 0, 1, off, off + ext + 1)


class Sched:
    def __init__(self, nc, same_engine_sync=True):
        self.nc = nc
        self.ins = []
        self.track = {}
        self.same = same_engine_sync
        self.last_cp = {e: None for e in ENGS}
        self.last_dm = {e: [] for e in ENGS}
        self.pending = {e: set() for e in ENGS}

    def _deps(self, reads, writes, idx):
        deps = set()
        rb = [_box(a) for a in reads]
        wb = [_box(a) for a in writes]
        for b in rb:
            for ent in self.track.get(b[0], ()):
                e = ent[0]
                if e[1] < b[2] and b[1] < e[2] and e[3] < b[4] and b[3] < e[4]:
                    if ent[1] is not None:
                        deps.add(ent[1])
        for b in wb:
            for ent in self.track.get(b[0], ()):
                e = ent[0]
                if e[1] < b[2] and b[1] < e[2] and e[3] < b[4] and b[3] < e[4]:
                    if ent[1] is not None:
                        deps.add(ent[1])
                    deps.update(ent[2])
        for b in rb:
            lst = self.track.setdefault(b[0], [])
            for ent in lst:
                if ent[0] == b:
                    ent[2].append(idx)
                    break
            else:
                lst.append([b, None, [idx]])
        for b in wb:
            lst = self.track.get(b[0], [])
            keep = []
            for ent in lst:
                e = ent[0]
                if b[1] <= e[1] and e[2] <= b[2] and b[3] <= e[3] and e[4] <= b[4]:
                    continue
                keep.append(ent)
            keep.append([b, idx, []])
            self.track[b[0]] = keep
        deps.discard(idx)
        return deps

    def add(self, eng, fn, r=(), w=(), dma=False):
        idx = len(self.ins)
        deps = self._deps(list(r), list(w), idx)
        if self.pending[eng]:
            deps |= self.pending[eng]
            self.pending[eng] = set()
        self.ins.append(dict(eng=eng, fn=fn, deps=deps, dma=dma, users=set()))
        if dma:
            lo = self.last_dm[eng]
            lo.append(idx)
            if len(lo) > ND:
                lo.pop(0)
        else:
            self.last_cp[eng] = idx
        return idx

    def fence(self):
        allp = set()
        for e in ENGS:
            allp.update(self.last_dm[e])
            if self.last_cp[e] is not None:
                allp.add(self.last_cp[e])
        for e in ENGS:
            self.pending[e] = set(allp) | self.pending[e]
        self.track = {}

    def mm(self, out, lhsT, rhs, start=True, stop=True, **kw):
        r = [lhsT, rhs]
        if not start:
            r.append(out)
        return self.add("pe", lambda e: e.matmul(out, lhsT, rhs, start=start, stop=stop, **kw), r=r, w=[out])

    def transpose(self, out, in_, ident):
        return self.add("pe", lambda e: e.transpose(out, in_, ident), r=[in_, ident], w=[out])

    def act(self, out, in_, func, bias=None, scale=None, accum_out=None):
        r = [in_]
        kw = {}
        if bias is not None:
            kw["bias"] = bias
            if not isinstance(bias, (int, float)):
                r.append(bias)
        if scale is not None:
            kw["scale"] = scale
            if not isinstance(scale, (int, float)):
                r.append(scale)
        w = [out]
        if accum_out is not None:
            kw["accum_out"] = accum_out
            w.append(accum_out)
        return self.add("act", lambda e: e.activation(out, in_, func, **kw), r=r, w=w)

    def tt(self, out, in0, in1, op, eng="dve"):
        return self.add(eng, lambda e: e.tensor_tensor(out, in0, in1, op), r=[in0, in1], w=[out])

    def ts(self, out, in0, s1, s2, op0, op1=None, eng="dve"):
        r = [in0]
        if not isinstance(s1, (int, float)):
            r.append(s1)
        if s2 is not None and not isinstance(s2, (int, float)):
            r.append(s2)
        if op1 is None:
            return self.add(eng, lambda e: e.tensor_scalar(out, in0, s1, None, op0), r=r, w=[out])
        return self.add(eng, lambda e: e.tensor_scalar(out, in0, s1, s2, op0, op1), r=r, w=[out])

    def stt(self, out, in0, scalar, in1, op0, op1):
        r = [in0, in1]
        if not isinstance(scalar, (int, float)):
            r.append(scalar)
        return self.add("dve", lambda e: e.scalar_tensor_tensor(out, in0, scalar, in1, op0, op1), r=r, w=[out])

    def copy(self, out, in_, eng="dve"):
        if eng == "act":
            return self.act(out, in_, AF.Copy)
        return self.add(eng, lambda e: e.tensor_copy(out, in_), r=[in_], w=[out])

    def memset(self, ap, val, eng="dve"):
        return self.add(eng, lambda e: e.memset(ap, val), r=[], w=[ap])

    def recip(self, out, in_):
        return self.add("dve", lambda e: e.reciprocal(out, in_), r=[in_], w=[out])

    def reduce(self, out, in_, op, axis=AX.X):
        return self.add("dve", lambda e: e.tensor_reduce(out, in_, axis, op), r=[in_], w=[out])

    def wrap(self, out, in_, t1, t2):
        self.ts(t1, in_, -math.pi, 2 * math.pi, ALU.is_lt, ALU.mult)
        self.ts(t2, in_, math.pi, -2 * math.pi, ALU.is_gt, ALU.mult)
        self.tt(t1, t1, t2, ALU.add, eng="pool")
        return self.tt(out, in_, t1, ALU.add)

    def allgather(self, out, in_, groups):
        idx = self.add("pool", lambda e: e.collective_compute("AllGather", ALU.bypass, replica_groups=groups,
                                                              ins=[in_], outs=[out]), r=[in_], w=[out], dma=True)
        self.ins[idx]["cc"] = True
        return idx

    def dma(self, out, in_, q="sp", **kw):
        return self.add(q, lambda e: e.dma_start(out, in_, **kw), r=[in_], w=[out], dma=True)

    def emit(self, stack, final_wait=()):
        nc = self.nc
        ins = self.ins
        for i, it in enumerate(ins):
            for d in it["deps"]:
                ins[d]["users"].add(i)

        def new_sem(tag):
            return stack.enter_context(nc.semaphore(tag))

        eng_sem = {e: [new_sem(f"s_{e}_0")] for e in ENGS}
        eng_cnt = {e: 0 for e in ENGS}
        dma_sems = {e: [new_sem(f"d_{e}_{k}") for k in range(ND)] for e in ("sp", "pool", "act")}
        dma_cnt = {e: 0 for e in ENGS}
        for i, it in enumerate(ins):
            e = it["eng"]
            it["sig"] = None
            it["pre"] = []
            if it.get("cc"):
                s = new_sem(f"cc_{i}")
                it["sig"] = (s, CC_INC, CC_INC)
            elif it["dma"]:
                k = dma_cnt[e]
                s = dma_sems[e][k % ND]
                it["sig"] = (s, 16 * (k // ND + 1), 16)
                if k >= ND:
                    it["pre"].append((s, 16 * (k // ND)))
                dma_cnt[e] = k + 1
            else:
                need = False
                for u in it["users"]:
                    ue = ins[u]["eng"]
                    if ue != e or ins[u]["dma"]:
                        need = True
                    elif self.same and e != "pe":
                        need = True
                if need:
                    if eng_cnt[e] >= SEM_ROT:
                        eng_sem[e].append(new_sem(f"s_{e}_{len(eng_sem[e])}"))
                        eng_cnt[e] = 0
                    eng_cnt[e] += 1
                    it["sig"] = (eng_sem[e][-1], eng_cnt[e], 1)
        self.stats = dict(n_ins=len(ins), sig={e: (len(eng_sem[e]) - 1) * SEM_ROT + eng_cnt[e] for e in ENGS},
                          dma=dict(dma_cnt), per_eng={e: sum(1 for it in ins if it["eng"] == e) for e in ENGS})
        waited = {e: {} for e in ENGS}
        streams = {e: [] for e in ENGS}
        for i, it in enumerate(ins):
            e = it["eng"]
            waits = list(it["pre"])
            for d in sorted(it["deps"]):
                pd = ins[d]
                if pd["sig"] is None:
                    continue
                if pd["eng"] == e and not pd["dma"] and (e == "pe" or not self.same):
                    continue
                waits.append((pd["sig"][0], pd["sig"][1]))
            ww = []
            for s, v in waits:
                key = id(s)
                if waited[e].get(key, 0) >= v:
                    continue
                waited[e][key] = v
                ww.append((s, v))
            streams[e].append((ww, it))
        fw = [(ins[d]["sig"][0], ins[d]["sig"][1]) for d in final_wait]
        self.n_ins = len(ins)

        def run(handle, lst, extra=()):
            for ww, it in lst:
                for s, v in ww:
                    handle.wait_ge(s, v)
                inst = it["fn"](handle)
                if it["sig"] is not None:
                    inst.then_inc(it["sig"][0], it["sig"][2])
            for s, v in extra:
                handle.wait_ge(s, v)

        block = stack.enter_context(nc.Block())

        @block.tensor
        def _(e):
            run(e, streams["pe"])

        @block.scalar
        def _(e):
            run(e, streams["act"])

        @block.vector
        def _(e):
            run(e, streams["dve"])

        @block.gpsimd
        def _(e):
            run(e, streams["pool"])

        @block.sync
        def _(e):
            run(e, streams["sp"], fw)


def fft_tables(N1):
    N = 64 * N1
    H = N1 // 2
    n1 = np.arange(N1)
    k1 = np.arange(N1)
    a1 = 2 * np.pi * np.outer(n1, k1) / N1
    CS1 = np.concatenate([np.cos(a1), -np.sin(a1)], 1)
    m = np.arange(128)
    n2m = m // 2
    c2m = m % 2
    aT = 2 * np.pi * np.outer(n2m, k1) / N
    a64 = 2 * np.pi * np.outer(n2m, n2m) / 64
    same = (c2m[:, None] == c2m[None, :])
    BdC = np.cos(a64) * same
    BdS = np.sin(a64) * same
    aI = 2 * np.pi * np.outer(k1, np.arange(64)) / N
    ai = 2 * np.pi * np.outer(k1, np.arange(H)) / N1
    f = lambda a: np.ascontiguousarray(a, dtype=np.float32)
    return dict(CS1=f(CS1), TwFc=f(np.cos(aT)), TwFs=f(np.sin(aT)), BdC=f(BdC), BdS=f(BdS), BdSn=f(-BdS),
                R1=f(np.concatenate([BdC, BdS], 1)), R2=f(np.concatenate([-BdS, BdC], 1)),
                TwIc=f(np.cos(aI)), TwIs=f(np.sin(aI)), Ci=f(np.cos(ai) / N), Sin_=f(-np.sin(ai) / N))


def filter_pos_tables(n):
    N = 2 * n
    N1 = N // 64
    t = np.linspace(0.0, 1.0, n, dtype=np.float32)
    bands = np.linspace(1e-4, 15.0, 16, dtype=np.float32)
    ang = (np.float32(2.0 * math.pi / n) * np.arange(n, dtype=np.float32)[:, None] * bands).astype(np.float32)
    z = np.concatenate([t[:, None], np.cos(ang), -np.sin(ang)], axis=-1).astype(np.float32)
    idx = np.arange(N)
    d = np.where(idx < n, idx, N - idx)
    d[n] = 0
    z2 = z[d]
    BIG = 1.0e4
    negF = np.where(idx < n, -t[d], -BIG).astype(np.float32)
    negB = np.where(idx > n, -t[d], -BIG).astype(np.float32)
    negt = np.stack([negF.reshape(N1, 64), negB.reshape(N1, 64)], axis=1)
    return np.ascontiguousarray(z2.T), np.ascontiguousarray(negt)


def rope_tables():
    nf = 32
    inv = (10000.0 ** (-np.arange(nf, dtype=np.float32) / nf)).astype(np.float32)
    j = np.arange(64, dtype=np.float32)
    dd = np.arange(128)
    ang = j[None, :] * inv[dd % 32][:, None]
    R = np.zeros((128, 128), np.float32)
    for dp in range(128):
        if dp % 64 < 32:
            R[dp, dp + 32] = -1.0
        else:
            R[dp, dp - 32] = 1.0
    return np.cos(ang).astype(np.float32), np.sin(ang).astype(np.float32), np.ascontiguousarray(R.T)


P1_SPEC = None


def p1_host_inputs(inp, b, hf):
    f = lambda a: np.ascontiguousarray(a, dtype=np.float32)
    o = {}
    o["x_all"] = f(np.concatenate([inp["x"][b], inp["ctx"][b]], 0))
    cond = np.stack([inp["c"][b], inp["c_ctx"]], -1)
    o["condT"] = f(cond.reshape(8, 128, 2).transpose(1, 0, 2))
    wm = inp["w_mod"][0][:, :2048]
    o["wmod"] = f(wm.reshape(8, 128, 16, 128).transpose(2, 1, 0, 3))
    o["bmodT"] = f(inp["b_mod"][0][:2048].reshape(16, 128).T)
    o["gpreT"] = f(inp["g_pre"][0].reshape(8, 128).T)
    W = inp["e_w_in"][0]
    cols = []
    cols.append(np.arange(5120 + hf * 128, 5120 + hf * 128 + 128))
    cols.append(np.arange(5376 + hf * 128, 5376 + hf * 128 + 128))
    for h in range(4):
        hd = 4 * hf + h
        cols.append(np.arange(4096 + hd * 128, 4096 + hd * 128 + 128))
        cols.append(np.arange(5632 + hd * 128, 5632 + hd * 128 + 128))
    for ct in range(4):
        c0 = hf * 512 + ct * 128
        cols.append(np.arange(1024 + c0, 1024 + c0 + 128))
        cols.append(np.arange(2048 + c0, 2048 + c0 + 128))
        cols.append(np.arange(c0, c0 + 128))
        cols.append(np.arange(3072 + c0, 3072 + c0 + 128))
    wt = np.stack([W[:, c] for c in cols], 0)
    o["w_in"] = f(wt.reshape(26, 8, 128, 128).transpose(0, 2, 1, 3))
    cw = inp["e_conv_w"][0]
    cb = inp["e_conv_b"][0]
    convw = np.zeros((128, 4, 3, 3), np.float32)
    convb = np.zeros((128, 4, 3), np.float32)
    for ct in range(4):
        c0 = hf * 512 + ct * 128
        for s in range(3):
            convw[:, ct, s, :] = cw[:, s * 1024 + c0: s * 1024 + c0 + 128].T
            convb[:, ct, s] = cb[s * 1024 + c0: s * 1024 + c0 + 128]
    o["convw"] = convw
    o["convb"] = convb
    o["gq"] = f(inp["e_q_norm"][0].reshape(128, 1))
    o["gk"] = f(inp["e_k_norm"][0].reshape(128, 1))
    rc, rs, rT = rope_tables()
    o["ropec"], o["ropes"], o["RmatT"] = rc, rs, rT
    z2l, ntl = filter_pos_tables(NLAT)
    z2c, ntc = filter_pos_tables(NCTX)
    o["z2T_l"], o["negt_l"], o["z2T_c"], o["negt_c"] = z2l, ntl, z2c, ntc
    o["fw1"] = f(inp["e_filt_w1"][0])
    o["fb1"] = f(inp["e_filt_b1"][0].reshape(64, 1))
    o["ffr"] = f(inp["e_filt_freq"][0].reshape(64, 1))
    o["fw2"] = f(inp["e_filt_w2"][0])
    o["fb2"] = f(inp["e_filt_b2"][0].reshape(64, 1))
    w3 = inp["e_filt_w3"][0].reshape(64, 2, 1024)
    o["fw3"] = f(w3[:, :, hf * 512:(hf + 1) * 512])
    min_decay = math.log(1e-2) / 1.5
    max_decay = math.log(1e-2) / 0.3
    deltas = np.abs(np.linspace(min_decay, max_decay, 1024, dtype=np.float32))
    o["adel"] = f(np.broadcast_to(deltas[hf * 512:(hf + 1) * 512][None, :], (128, 512)))
    o["hyb"] = f(inp["e_hy_bias"][0][hf * 512:(hf + 1) * 512].reshape(1, 512))
    for N1, tag in ((128, "l"), (8, "c")):
        for k, v in fft_tables(N1).items():
            o[f"ft_{tag}_{k}"] = v
    o["identF"] = np.eye(128, dtype=np.float32)
    return o


def _declare_inputs(nc, sample):
    aps = {}
    for k, v in sample.items():
        dt = F32 if v.dtype == np.float32 else BF16
        aps[k] = nc.dram_tensor(k, list(v.shape), dt, kind="ExternalInput").ap()
    return aps


class Chunked:
    def __init__(self, chunks, rows, ntok):
        self.chunks = chunks
        self.rows = rows
        self.ntok = ntok

    @staticmethod
    def make(nc, name, rows, ntok, dt, csize, **kw):
        ch = []
        for k, t0 in enumerate(range(0, ntok, csize)):
            n = min(csize, ntok - t0)
            ch.append((t0, n, nc.dram_tensor(f"{name}_{k}", [rows, n], dt, **kw).ap()))
        return Chunked(ch, rows, ntok)

    @staticmethod
    def wrap(ap):
        return Chunked([(0, ap.shape[1], ap)], ap.shape[0], ap.shape[1])

    def cols(self, r0, r1, t0, n):
        for (c0, cn, ap) in self.chunks:
            if c0 <= t0 and t0 + n <= c0 + cn:
                return ap[r0:r1, t0 - c0:t0 - c0 + n]
        raise AssertionError(("straddles chunks", t0, n))

    def pieces(self, r0, r1):
        for (c0, cn, ap) in self.chunks:
            yield c0, cn, ap[r0:r1, :]


class Ctx:
    def __init__(self, nc, stack):
        self.nc = nc
        self.stack = stack
        self.n = 0

    def sb(self, name, shape, dt, stack=None):
        self.n += 1
        return (stack or self.stack).enter_context(self.nc.sbuf_tensor(f"{name}_{self.n}", list(shape), dt))

    def ps(self, name, shape, dt, stack=None):
        self.n += 1
        return (stack or self.stack).enter_context(self.nc.psum_tensor(f"{name}_{self.n}", list(shape), dt))


def load_fft_tabs(S, C, A, tag, N1):
    H = N1 // 2
    T = {}
    def ld(name, shape, dt, q):
        t = C.sb(f"ft{tag}{name}", shape, dt)
        S.dma(t[:], A[f"ft_{tag}_{name}"], q=q)
        return t
    T["CS1"] = ld("CS1", [N1, 2 * N1], BF16, "pool")
    T["TwFc"] = ld("TwFc", [128, N1], F32, "sp")
    T["TwFs"] = ld("TwFs", [128, N1], F32, "sp")
    T["BdC"] = ld("BdC", [128, 128], BF16, "pool")
    T["BdS"] = ld("BdS", [128, 128], BF16, "pool")
    T["BdSn"] = ld("BdSn", [128, 128], BF16, "pool")
    T["R1"] = ld("R1", [128, 256], BF16, "pool")
    T["R2"] = ld("R2", [128, 256], BF16, "pool")
    T["TwIc"] = ld("TwIc", [N1, 64], F32, "sp")
    T["TwIs"] = ld("TwIs", [N1, 64], F32, "sp")
    T["Ci"] = ld("Ci", [N1, H], BF16, "pool")
    T["Sin_"] = ld("Sin_", [N1, H], BF16, "pool")
    return T


def fft_conv_core(S, C, PS, T, N1, X, Xf, Ball_re, Ball_im, tmp, evac):
    H = N1 // 2
    W2 = 2 * N1
    m1, m2, Ar, Ai, Br_, Bi_, Fh, Yr, Yi = tmp
    for gp in range(32):
        p0 = 2 * gp
        pg1, pf1, pg3, pf3, pI = PS[0], PS[1], PS[2], PS[3], PS[4]
        for j in range(2):
            S.mm(pg1[:, j * W2:(j + 1) * W2], X[0:H, p0 + j, :], T["CS1"][0:H, :])
            S.mm(pf1[:, j * W2:(j + 1) * W2], Xf[0:N1, p0 + j, :], T["CS1"][0:N1, :])
        tc = T["TwFc"][:, :].unsqueeze(1).broadcast_to([128, 4, N1])
        tsn = T["TwFs"][:, :].unsqueeze(1).broadcast_to([128, 4, N1])
        m1a = m1[:, 0:4 * N1].rearrange("p (a n) -> p a n", n=N1)
        m2a = m2[:, 0:4 * N1].rearrange("p (a n) -> p a n", n=N1)
        m1v = m1[:, 0:4 * N1].rearrange("p (a r n) -> p a r n", r=2, n=N1)
        m2v = m2[:, 0:4 * N1].rearrange("p (a r n) -> p a r n", r=2, n=N1)
        for (pp, ar, ai) in ((pg1, Ar, Ai), (pf1, Br_, Bi_)):
            v = pp[:, 0:2 * W2].rearrange("p (a n) -> p a n", n=N1)
            S.tt(m1a, v, tc, ALU.mult)
            S.tt(m2a, v, tsn, ALU.mult)
            arv = ar[:, 0:W2].rearrange("p (a n) -> p a n", n=N1)
            aiv = ai[:, 0:W2].rearrange("p (a n) -> p a n", n=N1)
            S.tt(arv, m1v[:, :, 0, :], m2v[:, :, 1, :], ALU.add, eng="pool")
            S.tt(aiv, m1v[:, :, 1, :], m2v[:, :, 0, :], ALU.subtract, eng="pool")
        for (pp, ar, ai) in ((pg3, Ar, Ai), (pf3, Br_, Bi_)):
            S.mm(pp[:, 0:W2], T["BdC"][:], ar[:, 0:W2], start=True, stop=False)
            S.mm(pp[:, 0:W2], T["BdS"][:], ai[:, 0:W2], start=False, stop=True)
            S.mm(pp[:, W2:2 * W2], T["BdC"][:], ai[:, 0:W2], start=True, stop=False)
            S.mm(pp[:, W2:2 * W2], T["BdSn"][:], ar[:, 0:W2], start=False, stop=True)
        S.act(Fh[:, 0:2 * W2], pf3[:, 0:2 * W2], AF.Copy)
        gv = pg3[:, 0:2 * W2].rearrange("p (r n) -> p r n", r=2)
        fr = Fh[:, 0:W2].unsqueeze(1).broadcast_to([128, 2, W2])
        fi = Fh[:, W2:2 * W2].unsqueeze(1).broadcast_to([128, 2, W2])
        q1 = m1[:, 0:2 * W2].rearrange("p (r n) -> p r n", r=2)
        q2 = m2[:, 0:2 * W2].rearrange("p (r n) -> p r n", r=2)
        S.tt(q1, gv, fr, ALU.mult)
        S.tt(q2, gv, fi, ALU.mult)
        S.tt(Yr[:, 0:W2], q1[:, 0, :], q2[:, 1, :], ALU.subtract, eng="pool")
        S.tt(Yi[:, 0:W2], q2[:, 0, :], q1[:, 1, :], ALU.add, eng="pool")
        for j in range(2):
            S.mm(pI[0:N1, j * 256:(j + 1) * 256], Yr[:, j * N1:(j + 1) * N1], T["R1"][:], start=True, stop=False)
            S.mm(pI[0:N1, j * 256:(j + 1) * 256], Yi[:, j * N1:(j + 1) * N1], T["R2"][:], start=False, stop=True)
        iv = pI[0:N1, :].rearrange("p (a n c) -> p a n c", a=4, c=2)
        tic = T["TwIc"][:, :].unsqueeze(1).unsqueeze(3).broadcast_to([N1, 4, 64, 2])
        tis = T["TwIs"][:, :].unsqueeze(1).unsqueeze(3).broadcast_to([N1, 4, 64, 2])
        n1v = m1[0:N1, :].rearrange("p (a n c) -> p a n c", a=4, c=2)
        n2v = m2[0:N1, :].rearrange("p (a n c) -> p a n c", a=4, c=2)
        S.tt(n1v, iv, tic, ALU.mult)
        S.tt(n2v, iv, tis, ALU.mult)
        n1p = m1[0:N1, :].rearrange("p (j r n c) -> p j r n c", j=2, r=2, c=2)
        n2p = m2[0:N1, :].rearrange("p (j r n c) -> p j r n c", j=2, r=2, c=2)
        ore = Ball_re[0:N1, :, 2 * p0:2 * p0 + 4].rearrange("p n (j c) -> p j n c", c=2)
        oim = Ball_im[0:N1, :, 2 * p0:2 * p0 + 4].rearrange("p n (j c) -> p j n c", c=2)
        S.tt(ore, n1p[:, :, 0, :, :], n2p[:, :, 1, :, :], ALU.subtract, eng="pool")
        S.tt(oim, n2p[:, :, 0, :, :], n1p[:, :, 1, :, :], ALU.add, eng="pool")
    ng = min(64, 512 // H)
    k = 0
    for n2_0 in range(0, 64, ng):
        po = PS[5 + (k % 2)]
        k += 1
        for j in range(ng):
            n2 = n2_0 + j
            S.mm(po[:, j * H:(j + 1) * H], Ball_re[0:N1, n2, :], T["Ci"][0:N1, :], start=True, stop=False)
            S.mm(po[:, j * H:(j + 1) * H], Ball_im[0:N1, n2, :], T["Sin_"][0:N1, :], start=False, stop=True)
        evac(po, n2_0, ng)


def build_p1(nc, sample, n_ct=4, do_att=True, do_hy=True, dbg=None, stop=None, shared=None):
    if shared is None:
        A = _declare_inputs(nc, sample)
        mix = Chunked.wrap(nc.dram_tensor("mix", [1024, NTOK], BF16, kind="ExternalOutput").ap())
    else:
        A = shared["A"]
        mix = shared["mix0"]
    xf_l = nc.dram_tensor("xf_l", [4, 128, 8192], BF16, kind="Internal").ap()
    xf_c = nc.dram_tensor("xf_c", [4, 8, 8192], BF16, kind="Internal").ap()
    outs = []
    with ExitStack() as st:
        if shared is None:
            S = Sched(nc)
            C = Ctx(nc, st)
            PS = [C.ps(f"pb{i}", [128, 512], F32) for i in range(7)]
            pTb = C.ps("pTb", [128, 1024], BF16)
        else:
            S, C, PS, pTb = shared["S"], shared["C"], shared["PS"], shared["pTb"]
            C.stack = st
        identF = C.sb("identF", [128, 128], F32)
        identB = C.sb("identB", [128, 128], BF16)
        onesF = C.sb("onesF", [128, 128], F32)
        onesB = C.sb("onesB", [128, 128], BF16)
        S.dma(identF[:], A["identF"])
        S.dma(identB[:], A["identF"], q="pool")
        S.memset(onesF[:], 1.0)
        S.memset(onesB[:], 1.0, eng="pool")
        rnT = C.sb("rnT", [128, 4, 2], F32)
        TL = load_fft_tabs(S, C, A, "l", 128)
        TC = load_fft_tabs(S, C, A, "c", 8)
        m1 = C.sb("m1", [128, 512], F32)
        m2 = C.sb("m2", [128, 512], F32)
        Ar = C.sb("Ar", [128, 256], BF16)
        Ai = C.sb("Ai", [128, 256], BF16)
        Br_ = C.sb("Br", [128, 256], BF16)
        Bi_ = C.sb("Bi", [128, 256], BF16)
        Fh = C.sb("Fh", [128, 512], F32)
        Yr = C.sb("Yr", [128, 256], BF16)
        Yi = C.sb("Yi", [128, 256], BF16)
        ftmp = (m1, m2, Ar, Ai, Br_, Bi_, Fh, Yr, Yi)
        if stop == 'const':
            S.emit(st, final_wait=[])
            return nc

        if do_hy:
            with ExitStack() as s0:
                fw1 = C.sb("fw1", [33, 64], F32, s0); S.dma(fw1[:], A["fw1"])
                fw2 = C.sb("fw2", [64, 64], F32, s0); S.dma(fw2[:], A["fw2"])
                fb1 = C.sb("fb1", [64, 1], F32, s0); S.dma(fb1[:], A["fb1"])
                fb2 = C.sb("fb2", [64, 1], F32, s0); S.dma(fb2[:], A["fb2"])
                ffr = C.sb("ffr", [64, 1], F32, s0); S.dma(ffr[:], A["ffr"])
                fw3 = C.sb("fw3", [64, 2, 512], BF16, s0); S.dma(fw3[:], A["fw3"], q="pool")
                adel = C.sb("adel", [128, 512], F32, s0); S.dma(adel[:], A["adel"])
                hyb = C.sb("hyb", [1, 512], F32, s0); S.dma(hyb[:], A["hyb"])
                frb1 = C.sb("frb1", [64, 1], F32, s0)
                frb2 = C.sb("frb2", [64, 1], F32, s0)
                S.tt(frb1[:], fb1[:], ffr[:], ALU.mult)
                S.tt(frb2[:], fb2[:], ffr[:], ALU.mult)
                hdn2 = C.sb("hdn2", [64, 8192], BF16, s0)
                z2 = C.sb("z2", [33, 512], F32, s0)
                u1 = C.sb("u1", [64, 512], F32, s0)
                h1 = C.sb("h1", [64, 512], F32, s0)
                wt1 = C.sb("wt1", [64, 512], F32, s0)
                wt2 = C.sb("wt2", [64, 512], F32, s0)
                negt = C.sb("negt", [128, 2, 64], F32, s0)
                wbuf = C.sb("wbuf", [128, 2, 512], F32, s0)
                fgrp = C.sb("fgrp", [128, 512], F32, s0)
                fgr2 = C.sb("fgr2", [128, 512], F32, s0)
                red = C.sb("red", [128, 128], F32, s0)
                acc = C.sb("acc", [128, 128], F32, s0)
                nrow = C.sb("nrow", [1, 128], F32, s0)
                ncol = C.sb("ncol", [128, 1], F32, s0)
                Xfo = C.sb("Xfo", [128, 64, 128], BF16, s0)
                for (tag, N1, z2T, ntab, xfd, li) in (("l", 128, A["z2T_l"], A["negt_l"], xf_l, 0),
                                                      ("c", 8, A["z2T_c"], A["negt_c"], xf_c, 1)):
                    N = 64 * N1
                    S.dma(negt[0:N1, :, :], ntab)
                    for blk in range(N // 512):
                        sl = slice(blk * 512, (blk + 1) * 512)
                        S.dma(z2[:], z2T[:, sl])
                        S.mm(PS[5][0:64, :], fw1[:], z2[:])
                        S.ts(u1[:], PS[5][0:64, :], ffr[:, 0:1], frb1[:, 0:1], ALU.mult, ALU.add)
                        S.wrap(u1[:], u1[:], wt1[:], wt2[:])
                        S.wrap(u1[:], u1[:], wt1[:], wt2[:])
                        S.act(h1[:], u1[:], AF.Sin)
                        S.mm(PS[6][0:64, :], fw2[:], h1[:])
                        S.ts(u1[:], PS[6][0:64, :], ffr[:, 0:1], frb2[:, 0:1], ALU.mult, ALU.add)
                        S.wrap(u1[:], u1[:], wt1[:], wt2[:])
                        S.wrap(u1[:], u1[:], wt1[:], wt2[:])
                        S.act(hdn2[:, sl], u1[:], AF.Sin)
                    for ct in range(n_ct):
                        S.memset(acc[0:N1, :], 0.0)
                        for g4 in range(16):
                            for j in range(4):
                                n2 = g4 * 4 + j
                                for dr in range(2):
                                    S.mm(PS[5 + dr][0:N1, j * 128:(j + 1) * 128], hdn2[:, n2:N:64],
                                         fw3[:, dr, ct * 128:(ct + 1) * 128])
                                    S.act(wbuf[0:N1, dr, j * 128:(j + 1) * 128], adel[0:N1, ct * 128:(ct + 1) * 128],
                                          AF.Exp, scale=negt[0:N1, dr, n2:n2 + 1])
                            S.tt(fgrp[0:N1, :], PS[5][0:N1, :], wbuf[0:N1, 0, :], ALU.mult)
                            S.tt(fgr2[0:N1, :], PS[6][0:N1, :], wbuf[0:N1, 1, :], ALU.mult)
                            S.tt(fgrp[0:N1, :], fgrp[0:N1, :], fgr2[0:N1, :], ALU.add, eng="pool")
                            S.act(fgr2[0:N1, :], fgrp[0:N1, :], AF.Abs)
                            S.reduce(red[0:N1, :], fgr2[0:N1, :].rearrange("p (j c) -> p c j", j=4), ALU.add)
                            S.tt(acc[0:N1, :], acc[0:N1, :], red[0:N1, :], ALU.add, eng="pool")
                            ov = Xfo[0:N1, :, :].rearrange("p a (n c) -> p n a c", c=2)[:, g4 * 4:g4 * 4 + 4, :, :]
                            iv = fgrp[0:N1, :].rearrange("p (j a c) -> p j a c", j=4, c=2)
                            S.copy(ov, iv, eng="act")
                        S.mm(PS[5][0:128, 0:1], acc[0:N1, :], onesF[0:N1, 0:1])
                        S.mm(PS[6][0:1, 0:128], onesF[0:N1, 0:1], acc[0:N1, :])
                        S.copy(ncol[:], PS[5][0:128, 0:1])
                        S.recip(rnT[:, ct, li:li + 1], ncol[:])
                        S.tt(nrow[:], PS[6][0:1, 0:128], hyb[0:1, ct * 128:(ct + 1) * 128], ALU.mult)
                        tap = Xfo[0:1, :, 0:2]
                        S.tt(tap, tap, nrow[0:1, :].rearrange("p (a c) -> p a c", c=2), ALU.add)
                        S.dma(xfd[ct, :, :], Xfo[0:N1, :, :].rearrange("p a n -> p (a n)"))
            S.fence()

        hT = C.sb("hT", [128, 8, NTOK], BF16)
        with ExitStack() as s1:
            condT = C.sb("condT", [128, 8, 2], F32, s1); S.dma(condT[:], A["condT"])
            scond = C.sb("scond", [128, 8, 2], F32, s1)
            S.act(scond[:], condT[:], AF.Silu)
            bmodT = C.sb("bmodT", [128, 16], F32, s1); S.dma(bmodT[:], A["bmodT"])
            gpreT = C.sb("gpreT", [128, 8], F32, s1); S.dma(gpreT[:], A["gpreT"])
            modT = C.sb("modT", [128, 16, 2], F32, s1)
            wm = [C.sb(f"wm{i}", [128, 8, 128], F32, s1) for i in range(2)]
            for fb in range(16):
                w = wm[fb % 2]
                S.dma(w[:], A["wmod"][fb], q="sp" if fb % 2 == 0 else "act")
                for kt in range(8):
                    S.mm(PS[fb % 2][:, 0:2], w[:, kt, :], scond[:, kt, :], start=(kt == 0), stop=(kt == 7))
                S.ts(modT[:, fb, :], PS[fb % 2][:, 0:2], bmodT[:, fb:fb + 1], None, ALU.add)
            if stop == 'mod':
                S.emit(st, final_wait=[])
                return nc
            Amod = C.sb("Amod", [128, 8, 2], F32, s1)
            S.ts(Amod[:], modT[:, 8:16, :], 1.0, None, ALU.add)
            S.tt(Amod[:], Amod[:], gpreT[:, :].unsqueeze(2).broadcast_to([128, 8, 2]), ALU.mult)
            xts = [C.sb(f"xt{i}", [128, 1024], F32, s1) for i in range(3)]
            junk = C.sb("junk", [128, 1024], BF16, s1)
            ssq = C.sb("ssq", [128, 34], F32, s1)
            rstd = C.sb("rstd", [128, 34], F32, s1)
            import os
            for i in range(int(os.environ.get('K_NT', '34'))):
                xt = xts[i % 3]
                S.dma(xt[:], A["x_all"][i * 128:(i + 1) * 128, :], q="sp" if i % 2 == 0 else "act")
                import os
                lvl = int(os.environ.get("K_DEBUG", "9"))
                if lvl < 1:
                    continue
                S.act(junk[:], xt[:], AF.Square, accum_out=ssq[:, i:i + 1])
                if lvl < 2:
                    continue
                S.act(rstd[:, i:i + 1], ssq[:, i:i + 1], AF.Sqrt, bias=EPS, scale=1.0 / D)
                if lvl < 3:
                    continue
                S.recip(rstd[:, i:i + 1], rstd[:, i:i + 1])
                S.ts(xt[:], xt[:], rstd[:, i:i + 1], None, ALU.mult)
                if lvl < 4:
                    continue
                jj = 0 if i < 32 else 1
                for half in range(2):
                    pt = PS[2 + half]
                    for k4 in range(4):
                        kt = half * 4 + k4
                        S.transpose(pt[:, k4 * 128:(k4 + 1) * 128], xt[:, kt * 128:(kt + 1) * 128], identF[:])
                    if lvl < 5:
                        continue
                    for k4 in range(4):
                        kt = half * 4 + k4
                        S.ts(hT[:, kt, i * 128:(i + 1) * 128], pt[:, k4 * 128:(k4 + 1) * 128],
                             Amod[:, kt, jj:jj + 1], modT[:, kt, jj:jj + 1], ALU.mult, ALU.add)
        S.fence()
        if dbg == "hT":
            hTo = nc.dram_tensor("hTo", [128, 8, NTOK], BF16, kind="ExternalOutput").ap()
            outs.append(S.dma(hTo, hT[:]))

        wbufs = [C.sb(f"wcol{i}", [128, 8, 128], BF16) for i in range(3)]
        wstate = {"n": 0}

        def load_w(tile_idx):
            w = wbufs[wstate["n"] % 3]
            wstate["n"] += 1
            S.dma(w[:], A["w_in"][tile_idx], q="pool")
            return w

        blocks = [(tb * 512, 512) for tb in range(8)] + [(4096, 256)]

        def proj_fm(w, blk, pt):
            t0, n = blk
            for kt in range(8):
                S.mm(pt[:, 0:n], w[:, kt, :], hT[:, kt, t0:t0 + n], start=(kt == 0), stop=(kt == 7))

        if do_att:
            with ExitStack() as s2:
                gq = C.sb("gq", [128, 1], F32, s2); S.dma(gq[:], A["gq"])
                gk = C.sb("gk", [128, 1], F32, s2); S.dma(gk[:], A["gk"])
                S.ts(gq[:], gq[:], 128.0 ** -0.5, None, ALU.mult)
                ropec = C.sb("ropec", [128, 64], F32, s2); S.dma(ropec[:], A["ropec"])
                ropes = C.sb("ropes", [128, 64], F32, s2); S.dma(ropes[:], A["ropes"])
                RmT = C.sb("RmT", [128, 128], F32, s2); S.dma(RmT[:], A["RmatT"])
                KT = C.sb("KT", [128, NTOK], BF16, s2)
                V = C.sb("V", [128, 34, 128], BF16, s2)
                sq = C.sb("sq", [128, 512], BF16, s2)
                rinv = C.sb("rinv", [128, 512], F32, s2)
                kn = C.sb("kn", [128, 512], F32, s2)
                t1 = C.sb("t1", [128, 512], F32, s2)
                t2 = C.sb("t2", [128, 512], F32, s2)

                def norm_rope(pt, blk, g, out_ap):
                    t0, n = blk
                    S.act(sq[:, 0:n], pt[:, 0:n], AF.Square)
                    S.mm(PS[2][:, 0:n], onesB[:], sq[:, 0:n])
                    S.act(rinv[:, 0:n], PS[2][:, 0:n], AF.Sqrt, bias=EPS, scale=1.0 / 128)
                    S.recip(rinv[:, 0:n], rinv[:, 0:n])
                    if t0 >= NLAT:
                        S.stt(out_ap, pt[:, 0:n], g[:, 0:1], rinv[:, 0:n], ALU.mult, ALU.mult)
                        return
                    S.stt(kn[:, 0:n], pt[:, 0:n], g[:, 0:1], rinv[:, 0:n], ALU.mult, ALU.mult)
                    S.mm(PS[2][:, 0:n], RmT[:], kn[:, 0:n])
                    r0 = t0 // 64
                    for (p0, p1) in ((0, 64), (64, 128)):
                        if p0 == 0:
                            cv = ropec[p0:p1, r0:r0 + 8].unsqueeze(2).broadcast_to([64, 8, 64])
                            sv = ropes[p0:p1, r0:r0 + 8].unsqueeze(2).broadcast_to([64, 8, 64])
                        else:
                            cv = ropec[p0:p1, :].unsqueeze(1).broadcast_to([64, 8, 64])
                            sv = ropes[p0:p1, :].unsqueeze(1).broadcast_to([64, 8, 64])
                        v3 = lambda a: a[p0:p1, 0:512].rearrange("p (r c) -> p r c", c=64)
                        S.tt(v3(t1), v3(kn), cv, ALU.mult, eng="pool")
                        S.tt(v3(t2), v3(PS[2]), sv, ALU.mult)
                    S.tt(out_ap, t1[:, 0:n], t2[:, 0:n], ALU.add)

                wk = load_w(0)
                for bi, blk in enumerate(blocks):
                    proj_fm(wk, blk, PS[bi % 2])
                    norm_rope(PS[bi % 2], blk, gk, KT[:, blk[0]:blk[0] + blk[1]])
                wv = load_w(1)
                for i4 in range(0, 34, 4):
                    nt = min(4, 34 - i4)
                    pt = PS[(i4 // 4) % 2]
                    for j in range(nt):
                        i = i4 + j
                        for kt in range(8):
                            S.mm(pt[:, j * 128:(j + 1) * 128], hT[:, kt, i * 128:(i + 1) * 128], wv[:, kt, :],
                                 start=(kt == 0), stop=(kt == 7))
                    S.act(V[:, i4:i4 + nt, :].rearrange("p a d -> p (a d)"), pt[:, 0:nt * 128], AF.Copy)
                qTb = [C.sb(f"qTb{i}", [128, 512], BF16, s2) for i in range(2)]
                gab = [C.sb(f"gab{i}", [128, 512], F32, s2) for i in range(2)]
                Pb = [C.sb(f"Pb{i}", [128, 512], BF16, s2) for i in range(3)]
                rden = C.sb("rden", [1, 512], F32, s2)
                o1 = C.sb("o1", [128, 512], F32, s2)
                mo = [C.sb(f"mo{i}", [128, 512], BF16, s2) for i in range(2)]
                work = [(h, bi) for h in range(4) for bi in range(len(blocks))]
                wq = {}

                def prep(idx):
                    h, bi = work[idx]
                    blk = blocks[bi]
                    if bi == 0:
                        wq["q"] = load_w(2 + 2 * h)
                        wq["g"] = load_w(3 + 2 * h)
                    proj_fm(wq["q"], blk, PS[0])
                    norm_rope(PS[0], blk, gq, qTb[idx % 2][:, 0:blk[1]])
                    proj_fm(wq["g"], blk, PS[1])
                    S.act(gab[idx % 2][:, 0:blk[1]], PS[1][:, 0:blk[1]], AF.Silu)

                def attend(idx):
                    h, bi = work[idx]
                    t0, n = blocks[bi]
                    ktiles = list(range(34)) if t0 < NLAT else [32, 33]
                    pO, pD = PS[5], PS[6]
                    for ki, kt_i in enumerate(ktiles):
                        pS = PS[3 + ki % 2]
                        P = Pb[ki % 3]
                        S.mm(pS[:, 0:n], KT[:, kt_i * 128:(kt_i + 1) * 128], qTb[idx % 2][:, 0:n])
                        S.act(P[:, 0:n], pS[:, 0:n], AF.Exp)
                        S.mm(pO[:, 0:n], V[:, kt_i, :], P[:, 0:n], start=(ki == 0), stop=(ki == len(ktiles) - 1))
                        S.mm(pD[0:1, 0:n], onesB[:, 0:1], P[:, 0:n], start=(ki == 0), stop=(ki == len(ktiles) - 1))
                    S.recip(rden[0:1, 0:n], pD[0:1, 0:n])
                    S.mm(PS[2][:, 0:n], onesF[0:1, :], rden[0:1, 0:n])
                    S.tt(o1[:, 0:n], pO[:, 0:n], gab[idx % 2][:, 0:n], ALU.mult)
                    m = mo[idx % 2]
                    S.tt(m[:, 0:n], o1[:, 0:n], PS[2][:, 0:n], ALU.mult)
                    outs.append(S.dma(mix.cols(512 + h * 128, 512 + (h + 1) * 128, t0, n), m[:, 0:n], q="sp"))

                prep(0)
                for idx in range(len(work)):
                    if idx + 1 < len(work):
                        prep(idx + 1)
                    attend(idx)
            S.fence()

        if do_hy:
            with ExitStack() as s3:
                convw = C.sb("convw", [128, 4, 3, 3], F32, s3); S.dma(convw[:], A["convw"])
                convb = C.sb("convb", [128, 4, 3], F32, s3); S.dma(convb[:], A["convb"])
                SEG = NLAT + 2 + NCTX + 2
                ust = C.sb("ust", [128, SEG], BF16, s3)
                S.memset(ust[:], 0.0)
                ctmp = C.sb("ctmp", [128, 2048], F32, s3)
                x1c = C.sb("x1c", [128, NTOK], BF16, s3)
                gT = C.sb("gT", [128, NTOK], BF16, s3)
                xg = C.sb("xg", [128, NTOK], BF16, s3)
                Xl = C.sb("Xl", [64, 64, 128], BF16, s3)
                Xfl = C.sb("Xfl", [128, 64, 128], BF16, s3)
                Bre = C.sb("Bre", [128, 64, 128], BF16, s3)
                Bim = C.sb("Bim", [128, 64, 128], BF16, s3)
                segs = [(0, 1, 2048), (2048, 2049, 2048), (4096, NLAT + 3, 256)]

                def conv_stream(ct, s, out_fn):
                    for (t0, u0, n) in segs:
                        S.act(ctmp[:, 0:n], ust[:, u0:u0 + n], AF.Identity, bias=convb[:, ct, s:s + 1],
                              scale=convw[:, ct, s, 1:2])
                        S.stt(ctmp[:, 0:n], ust[:, u0 - 1:u0 - 1 + n], convw[:, ct, s, 0:1], ctmp[:, 0:n],
                              ALU.mult, ALU.add)
                        out_fn(t0, n, ust[:, u0 + 1:u0 + 1 + n], convw[:, ct, s, 2:3], ctmp[:, 0:n])

                def ust_cols(blk):
                    t0, n = blk
                    return slice(t0 + 1, t0 + 1 + n) if t0 < NLAT else slice(NLAT + 3, NLAT + 3 + n)

                for ct in range(n_ct):
                    wbase = 10 + 4 * ct
                    S.dma(Xfl[:, :, :].rearrange("p a n -> p (a n)"), xf_l[ct, :, :], q="act")
                    w = load_w(wbase + 0)
                    for bi, blk in enumerate(blocks):
                        proj_fm(w, blk, PS[bi % 2])
                        S.act(ust[:, ust_cols(blk)], PS[bi % 2][:, 0:blk[1]], AF.Copy)
                    conv_stream(ct, 1, lambda t0, n, a, sc, b: S.stt(x1c[:, t0:t0 + n], a, sc, b, ALU.mult, ALU.add))
                    w = load_w(wbase + 1)
                    for bi, blk in enumerate(blocks):
                        proj_fm(w, blk, PS[bi % 2])
                        S.act(ust[:, ust_cols(blk)], PS[bi % 2][:, 0:blk[1]], AF.Copy)

                    def gfn(t0, n, a, sc, b):
                        S.stt(b, a, sc, b, ALU.mult, ALU.add)
                        S.tt(gT[:, t0:t0 + n], b, x1c[:, t0:t0 + n], ALU.mult, eng="pool")
                    conv_stream(ct, 2, gfn)
                    w = load_w(wbase + 3)
                    for bi, blk in enumerate(blocks):
                        proj_fm(w, blk, PS[bi % 2])
                        S.act(xg[:, blk[0]:blk[0] + blk[1]], PS[bi % 2][:, 0:blk[1]], AF.Silu)
                    w = load_w(wbase + 2)
                    for bi, blk in enumerate(blocks):
                        proj_fm(w, blk, PS[bi % 2])
                        S.act(ust[:, ust_cols(blk)], PS[bi % 2][:, 0:blk[1]], AF.Copy)

                    def xfn(t0, n, a, sc, b):
                        S.stt(b, a, sc, b, ALU.mult, ALU.add)
                        S.tt(xg[:, t0:t0 + n], b, xg[:, t0:t0 + n], ALU.mult, eng="pool")
                    conv_stream(ct, 0, xfn)
                    for n2_0 in range(0, 64, 8):
                        for j in range(8):
                            S.transpose(pTb[0:64, j * 128:(j + 1) * 128], gT[:, n2_0 + j:NLAT:64], identB[:])
                        ov = Xl[0:64, :, :].rearrange("p a (n c) -> p n a c", c=2)[:, n2_0:n2_0 + 8, :, :]
                        iv = pTb[0:64, :].rearrange("p (n a c) -> p n a c", n=8, c=2)
                        S.copy(ov, iv, eng="act")

                    def mk_evac(tok0, H, li):
                        def evac(po, n2_0, ng):
                            ov = gT[:, tok0:tok0 + 64 * H].rearrange("p (a n) -> p n a", n=64)[:, n2_0:n2_0 + ng, :]
                            xv = xg[:, tok0:tok0 + 64 * H].rearrange("p (a n) -> p n a", n=64)[:, n2_0:n2_0 + ng, :]
                            iv = po[:, 0:ng * H].rearrange("p (n a) -> p n a", a=H)
                            S.stt(ov, iv, rnT[:, ct, li:li + 1], xv, ALU.mult, ALU.mult)
                        return evac
                    fft_conv_core(S, C, PS, TL, 128, Xl, Xfl, Bre, Bim, ftmp, mk_evac(0, 64, 0))
                    S.dma(Xfl[0:8, :, :].rearrange("p a n -> p (a n)"), xf_c[ct, :, :], q="act")
                    for n2_0 in range(0, 64, 8):
                        for j in range(8):
                            S.transpose(pTb[0:4, j * 128:(j + 1) * 128], gT[:, NLAT + n2_0 + j:NTOK:64], identB[:])
                        ov = Xl[0:4, :, :].rearrange("p a (n c) -> p n a c", c=2)[:, n2_0:n2_0 + 8, :, :]
                        iv = pTb[0:4, :].rearrange("p (n a c) -> p n a c", n=8, c=2)
                        S.copy(ov, iv, eng="act")
                    fft_conv_core(S, C, PS, TC, 8, Xl, Xfl, Bre, Bim, ftmp, mk_evac(NLAT, 4, 1))
                    for (c0, cn, cap) in mix.pieces(ct * 128, (ct + 1) * 128):
                        outs.append(S.dma(cap, gT[:, c0:c0 + cn], q="sp"))
        if shared is None:
            S.emit(st, final_wait=outs)
    return nc


def gate_rows(S, C, PS, A, pre, scond, onesF, stk):
    rows = C.sb("grow", [2, 1024], F32, stk)
    bro = C.sb("gbro", [2, 1024], F32, stk); S.dma(bro[:], A[pre + "bgate2"])
    gpo = C.sb("ggpo", [2, 1024], F32, stk); S.dma(gpo[:], A[pre + "gpost2"])
    sel2 = C.sb("sel2", [2, 2, 128], F32, stk); S.dma(sel2[:], A["sel2"])
    wg = [C.sb(f"wgate{i}", [128, 1024], F32, stk) for i in range(2)]
    for kt in range(8):
        w = wg[kt % 2]
        S.dma(w[:], A[pre + "wgate"][kt], q="sp" if kt % 2 == 0 else "act")
        for nb in range(2):
            S.mm(PS[nb][0:2, :], scond[:, kt, 0:2], w[:, nb * 512:(nb + 1) * 512], start=(kt == 0), stop=(kt == 7))
    for nb in range(2):
        S.tt(rows[:, nb * 512:(nb + 1) * 512], PS[nb][0:2, :], bro[:, nb * 512:(nb + 1) * 512], ALU.add)
    S.tt(rows[:], rows[:], gpo[:], ALU.mult)
    gg = []
    for j in range(2):
        g = C.sb(f"gg{j}", [128, 1024], F32, stk)
        for nb in range(2):
            S.mm(PS[2 + nb][:, :], sel2[0:2, j, :], rows[0:2, nb * 512:(nb + 1) * 512])
            S.copy(g[:, nb * 512:(nb + 1) * 512], PS[2 + nb][:, :], eng="act")
        gg.append(g)
    return gg


def outproj_residual(S, C, PS, A, stk, mixg, wout_ap, x_in, tiles, gg, x_out, identF, post=None):
    wout = C.sb("wout", [128, 16, 1024], BF16, stk)
    for kt in range(16):
        S.dma(wout[:, kt, :], wout_ap[kt], q="pool")
    mts = [C.sb(f"mt{i}", [128, 16, 512], BF16, stk) for i in range(2)]
    xts = [C.sb(f"xo{i}", [128, 1024], F32, stk) for i in range(3)]
    junk = C.sb("junk2", [128, 512], BF16, stk)
    ss2 = C.sb("ss2", [128, 2], F32, stk)
    rs = C.sb("rs", [128, 1], F32, stk)
    tmp = C.sb("tmpo", [128, 1024], F32, stk)
    outs = []
    if not isinstance(mixg, Chunked):
        mixg = Chunked.wrap(mixg)
    cur = {"t0": None, "mt": None, "n": 0}
    for idx, (i, tok0, jj) in enumerate(tiles):
        g4 = tok0 // 512
        if cur["t0"] != g4:
            mt = mts[cur["n"] % 2]
            cur["n"] += 1
            ncol = min(512, mixg.ntok - g4 * 512)
            S.dma(mt[:, :, 0:ncol], mixg.cols(0, mixg.rows, g4 * 512, ncol).rearrange("(kt p) t -> p kt t", p=128), q="sp")
            cur["t0"], cur["mt"] = g4, mt
        mt = cur["mt"]
        c0 = tok0 - g4 * 512
        xt = xts[idx % 3]
        S.dma(xt[:], x_in[tok0:tok0 + 128, :], q="act")
        for nb in range(2):
            for kt in range(16):
                S.mm(PS[nb][:, :], mt[:, kt, c0:c0 + 128], wout[:, kt, nb * 512:(nb + 1) * 512],
                     start=(kt == 0), stop=(kt == 15))
            S.act(junk[:], PS[nb][:, :], AF.Square, accum_out=ss2[:, nb:nb + 1])
        S.tt(rs[:], ss2[:, 0:1], ss2[:, 1:2], ALU.add)
        S.act(rs[:], rs[:], AF.Sqrt, bias=EPS, scale=1.0 / D)
        S.recip(rs[:], rs[:])
        for nb in range(2):
            sl = slice(nb * 512, (nb + 1) * 512)
            S.stt(tmp[:, sl], PS[nb][:, :], rs[:, 0:1], gg[jj][:, sl], ALU.mult, ALU.mult)
        S.tt(xt[:], xt[:], tmp[:], ALU.add, eng="pool")
        if x_out is not None:
            outs.append(S.dma(x_out[i * 128:(i + 1) * 128, :], xt[:], q="sp"))
        if post is not None:
            post(i, xt, jj)
    return outs


def adaln_shift_scale(S, C, PS, A, pre, scond, stk):
    bmodT = C.sb("bmodT", [128, 16], F32, stk); S.dma(bmodT[:], A[pre + "bmodT"])
    gpreT = C.sb("gpreT", [128, 8], F32, stk); S.dma(gpreT[:], A[pre + "gpreT"])
    modT = C.sb("modT", [128, 16, 2], F32, stk)
    wm = [C.sb(f"wm{i}", [128, 8, 128], F32, stk) for i in range(2)]
    for fb in range(16):
        w = wm[fb % 2]
        S.dma(w[:], A[pre + "wmod"][fb], q="sp" if fb % 2 == 0 else "act")
        for kt in range(8):
            S.mm(PS[4 + fb % 2][:, 0:2], w[:, kt, :], scond[:, kt, :], start=(kt == 0), stop=(kt == 7))
        S.ts(modT[:, fb, :], PS[4 + fb % 2][:, 0:2], bmodT[:, fb:fb + 1], None, ALU.add)
    Amod = C.sb("Amod", [128, 8, 2], F32, stk)
    S.ts(Amod[:], modT[:, 8:16, :], 1.0, None, ALU.add)
    S.tt(Amod[:], Amod[:], gpreT[:, :].unsqueeze(2).broadcast_to([128, 8, 2]), ALU.mult)
    return Amod, modT


def p2_host_inputs(inp, b, hf, mix0g):
    f = lambda a: np.ascontiguousarray(a, dtype=np.float32)
    o = {}
    o["mix0g"] = np.ascontiguousarray(mix0g)
    o["x_all"] = f(np.concatenate([inp["x"][b], inp["ctx"][b]], 0))
    cond = np.stack([inp["c"][b], inp["c_ctx"]], -1)
    o["condT"] = f(cond.reshape(8, 128, 2).transpose(1, 0, 2))
    o["l0_wgate"] = f(inp["w_mod"][0][:, 2048:3072].reshape(8, 128, 1024))
    o["l0_bgate2"] = f(np.broadcast_to(inp["b_mod"][0][2048:3072][None], (2, 1024)))
    o["l0_gpost2"] = f(np.broadcast_to(inp["g_post"][0][None], (2, 1024)))
    sel2 = np.zeros((2, 2, 128), np.float32); sel2[0, 0] = 1; sel2[1, 1] = 1
    o["sel2"] = sel2
    wo = inp["e_w_out"][0]
    order = np.concatenate([np.arange(r * 512, r * 512 + 512) if part == 0 else np.arange(1024 + r * 512, 1024 + r * 512 + 512)
                            for r in range(2) for part in range(2)])
    o["wout0"] = f(wo[order].reshape(16, 128, 1024))
    wm = inp["w_mod"][1][:, :2048]
    o["l1_wmod"] = f(wm.reshape(8, 128, 16, 128).transpose(2, 1, 0, 3))
    o["l1_bmodT"] = f(inp["b_mod"][1][:2048].reshape(16, 128).T)
    o["l1_gpreT"] = f(inp["g_pre"][1].reshape(8, 128).T)
    W = inp["o_w_in"][0]
    heads = [4 * hf + h for h in range(4)]
    def tl(cols):
        return W[:, cols].reshape(8, 128, len(cols)).transpose(1, 0, 2)
    o["wq"] = f(np.stack([tl(np.arange(hd * 128, hd * 128 + 128)) for hd in heads]))
    o["wk"] = f(np.stack([tl(np.arange(1024 + hd * 128, 1024 + hd * 128 + 128)) for hd in heads]))
    o["wv"] = f(np.stack([tl(np.arange(2048 + hd * 256, 2048 + hd * 256 + 256)) for hd in heads]))
    o["woz"] = f(np.stack([tl(np.concatenate([np.arange(4096 + hd * 256, 4096 + hd * 256 + 256),
                                              np.arange(6144 + hd * 256, 6144 + hd * 256 + 256)])) for hd in heads]))
    gcols = np.array([8192 + g * 8 + hd for g in range(4) for hd in heads])
    o["wg"] = f(tl(gcols))
    cw = inp["o_conv_w"][0]; cb = inp["o_conv_b"][0]
    convw = np.zeros((128, 4, 2, 3), np.float32); convb = np.zeros((128, 4, 2), np.float32)
    for h, hd in enumerate(heads):
        for s in range(2):
            c0 = s * 1024 + hd * 128
            convw[:, h, s, :] = cw[:, c0:c0 + 128].T
            convb[:, h, s] = cb[c0:c0 + 128]
    o["convw1"] = convw; o["convb1"] = convb
    gb = inp["o_gate_b"][0].reshape(4, 8)[:, heads]
    o["gateb"] = f(gb.T)
    hn = inp["o_head_norm"][0].reshape(8, 256)[heads]
    o["hn"] = f(np.broadcast_to(hn[None], (128, 4, 256)))
    t = np.arange(128)
    same = (t[:, None] // 64) == (t[None, :] // 64)
    o["maskF"] = f((same & (t[:, None] <= t[None, :])))
    o["maskB"] = f((same & (t[:, None] >= t[None, :])))
    sel4 = np.zeros((4, 4, 128), np.float32)
    for h in range(4):
        sel4[h, h] = 1
    o["sel4"] = sel4
    o["identF"] = np.eye(128, dtype=np.float32)
    return o


def build_p2(nc, sample, n_heads=4, stop=None, shared=None):
    if shared is None:
        A = _declare_inputs(nc, sample)
        x1o = nc.dram_tensor("x1o", [NTOK, D], F32, kind="ExternalOutput").ap()
        mix1 = Chunked.wrap(nc.dram_tensor("mix1", [1024, NLAT], BF16, kind="ExternalOutput").ap())
    else:
        A = shared["A"]
        x1o = shared["x1o"]
        mix1 = shared["mix1"]
    outs = []
    NCH = NTOK // 64
    with ExitStack() as st:
        if shared is None:
            S = Sched(nc)
            C = Ctx(nc, st)
            PS = [C.ps(f"pb{i}", [128, 512], F32) for i in range(7)]
            pTb = C.ps("pTb", [128, 1024], BF16)
        else:
            S, C, PS, pTb = shared["S"], shared["C"], shared["PS"], shared["pTb"]
            C.stack = st
        identF = C.sb("identF", [128, 128], F32); S.dma(identF[:], A["identF"])
        identB = C.sb("identB", [128, 128], BF16); S.dma(identB[:], A["identF"], q="pool")
        onesF = C.sb("onesF", [128, 128], F32); S.memset(onesF[:], 1.0)
        condT = C.sb("condT", [128, 8, 2], F32); S.dma(condT[:], A["condT"])
        scond = C.sb("scond", [128, 8, 2], F32)
        S.act(scond[:], condT[:], AF.Silu)
        h1T = C.sb("h1T", [128, 8, NTOK], BF16)
        tokq = C.sb("tokq", [128, 34, 24], F32)
        ebB = C.sb("ebB", [128, 2, 4, NCH], F32)
        with ExitStack() as sa:
            Amod, modT = adaln_shift_scale(S, C, PS, A, "l1_", scond, sa)
            gg = gate_rows(S, C, PS, A, "l0_", scond, onesF, sa)
            ssq = C.sb("ssq", [128, 1], F32, sa)
            rstd = C.sb("rstd", [128, 1], F32, sa)
            junk = C.sb("junk", [128, 1024], BF16, sa)
            xh = C.sb("xh", [128, 1024], F32, sa)

            def post(i, xt, jj):
                S.act(junk[:], xt[:], AF.Square, accum_out=ssq[:, 0:1])
                S.act(rstd[:], ssq[:], AF.Sqrt, bias=EPS, scale=1.0 / D)
                S.recip(rstd[:], rstd[:])
                S.ts(xh[:], xt[:], rstd[:, 0:1], None, ALU.mult)
                for half in range(2):
                    pt = PS[2 + half]
                    for k4 in range(4):
                        kt = half * 4 + k4
                        S.transpose(pt[:, k4 * 128:(k4 + 1) * 128], xh[:, kt * 128:(kt + 1) * 128], identF[:])
                    for k4 in range(4):
                        kt = half * 4 + k4
                        S.ts(h1T[:, kt, i * 128:(i + 1) * 128], pt[:, k4 * 128:(k4 + 1) * 128],
                             Amod[:, kt, jj:jj + 1], modT[:, kt, jj:jj + 1], ALU.mult, ALU.add)
            tiles = [(i, i * 128, 0 if i < 32 else 1) for i in range(34)]
            outs += outproj_residual(S, C, PS, A, sa, A["mix0g"], A["wout0"], A["x_all"], tiles, gg, x1o, identF, post)
        S.fence()
        if stop == "A":
            dbg = nc.dram_tensor("h1To", [128, 8, NTOK], BF16, kind="ExternalOutput").ap()
            outs.append(S.dma(dbg, h1T[:]))
            S.emit(st, final_wait=outs)
            return nc
        blocks = [(tb * 512, 512) for tb in range(8)] + [(4096, 256)]
        with ExitStack() as sb_:
            wg = C.sb("wg", [128, 8, 16], BF16, sb_); S.dma(wg[:], A["wg"], q="pool")
            gateb = C.sb("gateb", [4, 4], F32, sb_); S.dma(gateb[:], A["gateb"])
            sel4 = C.sb("sel4", [4, 4, 128], F32, sb_); S.dma(sel4[:], A["sel4"])
            gi = C.sb("gi", [4, NTOK], F32, sb_)
            gf = C.sb("gf", [4, NTOK], F32, sb_)
            cumB = C.sb("cumB", [4, NTOK], F32, sb_)
            e1p = C.sb("e1p", [4, NTOK], F32, sb_)
            ebr = C.sb("ebr", [4, NCH], F32, sb_)
            v3 = lambda t_: t_[:, :].rearrange("p (c j) -> p c j", j=64)
            for d in range(2):
                for gq_, dstg in ((2 * d, gi), (2 * d + 1, gf)):
                    for bi, (t0, n) in enumerate(blocks):
                        pt = PS[bi % 2]
                        for kt in range(8):
                            S.mm(pt[0:4, 0:n], wg[:, kt, gq_ * 4:(gq_ + 1) * 4], h1T[:, kt, t0:t0 + n], start=(kt == 0), stop=(kt == 7))
                        S.ts(dstg[:, t0:t0 + n], pt[0:4, 0:n], gateb[:, gq_:gq_ + 1], None, ALU.add)
                S.act(gf[:], gf[:], AF.Exp, scale=-1.0)
                S.act(gf[:], gf[:], AF.Ln, bias=1.0)
                src, dst = gf, cumB
                for k in range(6):
                    sh = 1 << k
                    s3, d3 = v3(src), v3(dst)
                    if d == 0:
                        S.tt(d3[:, :, sh:64], s3[:, :, sh:64], s3[:, :, 0:64 - sh], ALU.add)
                        S.copy(d3[:, :, 0:sh], s3[:, :, 0:sh], eng="pool")
                    else:
                        S.tt(d3[:, :, 0:64 - sh], s3[:, :, 0:64 - sh], s3[:, :, sh:64], ALU.add)
                        S.copy(d3[:, :, 64 - sh:64], s3[:, :, 64 - sh:64], eng="pool")
                    src, dst = dst, src
                cum, thr = src, dst
                e1 = gi
                endpos = 63 if d == 0 else 0
                S.act(ebr[:], v3(cum)[:, :, endpos], AF.Exp, scale=-1.0)
                S.tt(e1[:], gi[:], cum[:], ALU.add)
                S.act(e1[:], e1[:], AF.Exp)
                S.ts(e1[:], e1[:], 128.0 ** -0.5, None, ALU.mult)
                S.act(thr[:], cum[:], AF.Exp)
                S.tt(v3(e1p), v3(e1), ebr[:, :].unsqueeze(2).broadcast_to([4, NCH, 64]), ALU.mult)
                for h in range(4):
                    S.mm(PS[2][:, 0:NCH], sel4[0:4, h, :], ebr[0:4, :])
                    S.copy(ebB[:, d, h, :], PS[2][:, 0:NCH], eng="act")
                for i in range(34):
                    pt = PS[3 + i % 2]
                    for qi, src_t in enumerate((e1, e1p, thr)):
                        S.transpose(pt[:, qi * 4:(qi + 1) * 4], src_t[0:4, i * 128:(i + 1) * 128], identF[0:4, 0:4])
                    S.copy(tokq[:, i, d * 12:(d + 1) * 12], pt[:, 0:12], eng="act")
        S.fence()
        if stop == "B":
            dbg = nc.dram_tensor("tokqo", [128, 34, 24], F32, kind="ExternalOutput").ap()
            outs.append(S.dma(dbg, tokq[:]))
            dbg2 = nc.dram_tensor("ebBo", [128, 2, 4, NCH], F32, kind="ExternalOutput").ap()
            outs.append(S.dma(dbg2, ebB[:]))
            S.emit(st, final_wait=outs)
            return nc
        with ExitStack() as sc:
            convw = C.sb("convw1", [128, 4, 2, 3], F32, sc); S.dma(convw[:], A["convw1"])
            convb = C.sb("convb1", [128, 4, 2], F32, sc); S.dma(convb[:], A["convb1"])
            hn = C.sb("hn", [128, 256], F32, sc)
            maskF = C.sb("maskF", [128, 128], F32, sc); S.dma(maskF[:], A["maskF"])
            maskB = C.sb("maskB", [128, 128], F32, sc); S.dma(maskB[:], A["maskB"])
            masks = (maskF, maskB)
            wbuf = C.sb("wbuf", [128, 8, 512], BF16, sc)
            wq = wbuf[:, :, 0:128]
            wk = wbuf[:, :, 128:256]
            wv = wbuf[:, :, 256:512]
            woz = wbuf
            SEG = NLAT + 2 + NCTX + 2
            ust = C.sb("ust", [128, SEG], BF16, sc); S.memset(ust[:], 0.0)
            ctmp = C.sb("ctmp", [128, 1024], F32, sc)
            qT = C.sb("qT", [128, NTOK], BF16, sc)
            kT = C.sb("kT", [128, NTOK], BF16, sc)
            ktD = [C.sb(f"ktD{d}", [128, 34, 128], BF16, sc) for d in range(2)]
            vaug = C.sb("vaug", [128, 34, 257], BF16, sc)
            S.memset(vaug[:, :, 256:257], 1.0)
            hsum = C.sb("hsum", [128, 32, 256], F32, sc)
            Cst = [C.sb(f"Cst{d}", [128, 257], F32, sc) for d in range(2)]
            Cbf = [C.sb(f"Cbf{d}", [128, 257], BF16, sc) for d in range(2)]
            Sm = [C.sb(f"Sm{d}", [128, 128], BF16, sc) for d in range(2)]
            dm = [C.sb(f"dm{d}", [128, 1], F32, sc) for d in range(2)]
            so = C.sb("so", [128, 256], F32, sc)
            sz = C.sb("sz", [128, 256], F32, sc)
            hh = C.sb("hh", [128, 256], F32, sc)
            junk3 = C.sb("junk3", [128, 256], BF16, sc)
            ssq3 = C.sb("ssq3", [128, 1], F32, sc)
            mixv = C.sb("mixv", [128, 256], BF16, sc)
            mob = [C.sb(f"mob{i}", [128, 2, 512], BF16, sc) for i in range(2)]
            segs = [(k * 1024, k * 1024 + 1, 1024) for k in range(4)] + [(4096, NLAT + 3, 256)]

            def ust_cols(blk):
                t0, n = blk
                return slice(t0 + 1, t0 + 1 + n) if t0 < NLAT else slice(NLAT + 3, NLAT + 3 + n)

            for h in range(n_heads):
                S.dma(wq, A["wq"][h], q="pool")
                S.dma(wk, A["wk"][h], q="pool")
                S.dma(wv, A["wv"][h], q="pool")
                S.dma(hn[:], A["hn"][:, h, :])
                S.memset(hsum[:], 0.0, eng="pool")
                for (w, s, dstT) in ((wq, 0, qT), (wk, 1, kT)):
                    for bi, blk in enumerate(blocks):
                        t0, n = blk
                        pt = PS[bi % 2]
                        for kt in range(8):
                            S.mm(pt[:, 0:n], w[:, kt, :], h1T[:, kt, t0:t0 + n], start=(kt == 0), stop=(kt == 7))
                        S.act(ust[:, ust_cols(blk)], pt[:, 0:n], AF.Copy)
                    for (t0, u0, n) in segs:
                        S.act(ctmp[:, 0:n], ust[:, u0:u0 + n], AF.Identity, bias=convb[:, h, s:s + 1], scale=convw[:, h, s, 1:2])
                        S.stt(ctmp[:, 0:n], ust[:, u0 - 1:u0 - 1 + n], convw[:, h, s, 0:1], ctmp[:, 0:n], ALU.mult, ALU.add)
                        S.stt(ctmp[:, 0:n], ust[:, u0 + 1:u0 + 1 + n], convw[:, h, s, 2:3], ctmp[:, 0:n], ALU.mult, ALU.add)
                        S.act(dstT[:, t0:t0 + n], ctmp[:, 0:n], AF.Silu)
                for i4 in range(0, 34, 8):
                    nt = min(8, 34 - i4)
                    for j in range(nt):
                        i = i4 + j
                        S.transpose(pTb[:, j * 128:(j + 1) * 128], kT[:, i * 128:(i + 1) * 128], identB[:])
                    for j in range(nt):
                        i = i4 + j
                        for d in range(2):
                            S.act(ktD[d][:, i, :], pTb[:, j * 128:(j + 1) * 128], AF.Identity, scale=tokq[:, i, d * 12 + 4 + h:d * 12 + 5 + h])
                for i in range(34):
                    pt = PS[i % 2]
                    for kt in range(8):
                        S.mm(pt[:, 0:256], h1T[:, kt, i * 128:(i + 1) * 128], wbuf[:, kt, 256:512], start=(kt == 0), stop=(kt == 7))
                    S.act(vaug[:, i, 0:256], pt[:, 0:256], AF.Copy)
                for d in range(2):
                    S.memset(Cst[d][:], 0.0)
                    S.memset(Cbf[d][:], 0.0, eng="pool")
                orderF = [32, 33] + list(range(32))
                orderB = [33, 32] + list(range(31, -1, -1))
                for step in range(34):
                    for d, tile_i in ((0, orderF[step]), (1, orderB[step])):
                        i = tile_i
                        is_lat = i < 32
                        pS, pN, pU = PS[3 * d], PS[3 * d + 1], PS[3 * d + 2]
                        tk = slice(i * 128, (i + 1) * 128)
                        chunks = (0, 1) if d == 0 else (1, 0)
                        if is_lat:
                            S.mm(pS[:, 0:128], kT[:, tk], qT[:, tk])
                            S.stt(Sm[d][:], pS[:, 0:128], tokq[:, i, d * 12 + h:d * 12 + h + 1], masks[d][:], ALU.mult, ALU.mult)
                            S.mm(pN[:, 0:257], Sm[d][:], vaug[:, i, :], start=True, stop=False)
                        for ci, c in enumerate(chunks):
                            rows = slice(c * 64, (c + 1) * 64)
                            ch = i * 2 + c
                            if is_lat:
                                S.mm(pN[rows, 0:257], qT[:, i * 128 + c * 64:i * 128 + (c + 1) * 64], Cbf[d][:, :],
                                     start=False, stop=True)
                            S.mm(pU[:, 0:257], ktD[d][rows, i, :], vaug[rows, i, :])
                            S.stt(Cst[d][:], Cst[d][:], ebB[:, d, h, ch:ch + 1], pU[:, 0:257], ALU.mult, ALU.add)
                            S.act(Cbf[d][:], Cst[d][:], AF.Copy)
                        if is_lat:
                            S.act(dm[d][:], pN[:, 256:257], AF.Abs)
                            S.ts(dm[d][:], dm[d][:], tokq[:, i, d * 12 + 8 + h:d * 12 + 9 + h], None, ALU.max)
                            S.recip(dm[d][:], dm[d][:])
                            S.stt(hsum[:, i, :], pN[:, 0:256], dm[d][:, 0:1], hsum[:, i, :], ALU.mult, ALU.add)
                S.dma(woz[:], A["woz"][h], q="pool")
                for i in range(32):
                    pt = PS[6]
                    for kt in range(8):
                        S.mm(pt[:, :], h1T[:, kt, i * 128:(i + 1) * 128], woz[:, kt, :], start=(kt == 0), stop=(kt == 7))
                    S.act(so[:], pt[:, 0:256], AF.Sigmoid)
                    S.act(sz[:], pt[:, 256:512], AF.Silu)
                    S.tt(hh[:], hsum[:, i, :], so[:], ALU.mult)
                    S.act(junk3[:], hh[:], AF.Square, accum_out=ssq3[:, 0:1])
                    S.act(ssq3[:], ssq3[:], AF.Sqrt, bias=EPS, scale=1.0 / 256)
                    S.recip(ssq3[:], ssq3[:])
                    S.tt(sz[:], sz[:], hn[:], ALU.mult, eng="pool")
                    S.stt(mixv[:], hh[:], ssq3[:, 0:1], sz[:], ALU.mult, ALU.mult)
                    mo_ = mob[(i // 4) % 2]
                    for j in range(2):
                        S.transpose(pTb[:, j * 128:(j + 1) * 128], mixv[:, j * 128:(j + 1) * 128], identB[:])
                    S.copy(mo_[:, :, (i % 4) * 128:(i % 4 + 1) * 128],
                           pTb[:, 0:256].rearrange("p (j t) -> p j t", j=2), eng="act")
                    if i % 4 == 3:
                        t0 = (i // 4) * 512
                        dst = mix1.cols(h * 256, (h + 1) * 256, t0, 512).rearrange("(j p) t -> p j t", p=128)
                        outs.append(S.dma(dst, mo_[:, :, :], q="sp"))
        if shared is None:
            S.emit(st, final_wait=outs)
    return nc


def p3_host_inputs(inp, b, hf, mix1g, x1):
    f = lambda a: np.ascontiguousarray(a, dtype=np.float32)
    o = {}
    o["mix1g"] = np.ascontiguousarray(mix1g[:, hf * 2048:(hf + 1) * 2048])
    o["x1loc"] = f(x1[hf * 2048:(hf + 1) * 2048])
    cond = np.stack([inp["c"][b], inp["c_ctx"]], -1)
    o["condT"] = f(cond.reshape(8, 128, 2).transpose(1, 0, 2))
    o["l1_wgate"] = f(inp["w_mod"][1][:, 2048:3072].reshape(8, 128, 1024))
    o["l1_bgate2"] = f(np.broadcast_to(inp["b_mod"][1][2048:3072][None], (2, 1024)))
    o["l1_gpost2"] = f(np.broadcast_to(inp["g_post"][1][None], (2, 1024)))
    sel2 = np.zeros((2, 2, 128), np.float32); sel2[0, 0] = 1; sel2[1, 1] = 1
    o["sel2"] = sel2
    o["wout1"] = f(inp["o_w_out"][0].reshape(16, 128, 1024))
    o["identF"] = np.eye(128, dtype=np.float32)
    return o


def build_p3(nc, sample, shared=None):
    if shared is None:
        A = _declare_inputs(nc, sample)
        yo = nc.dram_tensor("yo", [2048, D], F32, kind="ExternalOutput").ap()
        ntile = 16
    else:
        A = shared["A"]
        yo = shared["yo"]
        ntile = 32
    with ExitStack() as st:
        if shared is None:
            S = Sched(nc)
            C = Ctx(nc, st)
            PS = [C.ps(f"pb{i}", [128, 512], F32) for i in range(7)]
        else:
            S, C, PS = shared["S"], shared["C"], shared["PS"]
            C.stack = st
        identF = C.sb("identF", [128, 128], F32); S.dma(identF[:], A["identF"])
        onesF = C.sb("onesF", [128, 128], F32); S.memset(onesF[:], 1.0)
        condT = C.sb("condT", [128, 8, 2], F32); S.dma(condT[:], A["condT"])
        scond = C.sb("scond", [128, 8, 2], F32)
        S.act(scond[:], condT[:], AF.Silu)
        gg = gate_rows(S, C, PS, A, "l1_", scond, onesF, st)
        tiles = [(i, i * 128, 0) for i in range(ntile)]
        outs = outproj_residual(S, C, PS, A, st, A["mix1g"], A["wout1"], A["x1loc"], tiles, gg, yo, identF, None)
        if shared is None:
            S.emit(st, final_wait=outs)
        else:
            shared["outs"] += outs
    return nc


CORES = [(b, hf) for b in range(4) for hf in range(2)]


def _launch(build, maps, **kw):
    nc = bass.Bass("TRN2", target_bir_lowering=False)
    build(nc, maps[0], **kw)
    res = run_bass_kernel_spmd(nc, maps, core_ids=list(range(len(maps))))
    return res.results


GROUPS = [[0, 1], [2, 3], [4, 5], [6, 7]]


def fused_host_inputs(inp, b, hf):
    o = {}
    dummy_mix0 = np.zeros((2048, NTOK), NPBF)
    p2 = p2_host_inputs(inp, b, hf, dummy_mix0)
    p3 = p3_host_inputs(inp, b, hf, np.zeros((2048, NLAT), NPBF), np.zeros((NLAT, D), np.float32))
    for d in (p3, p2, p1_host_inputs(inp, b, hf)):
        o.update(d)
    for k in ("mix0g", "mix1g", "x1loc"):
        o.pop(k)
    return o


def build_fused(nc, sample):
    A = _declare_inputs(nc, sample)
    yo = nc.dram_tensor("yo", [NLAT, D], F32, kind="ExternalOutput").ap()
    CS = 1024
    mix0 = Chunked.make(nc, "mix0", 1024, NTOK, BF16, CS, kind="Internal")
    mix0g = Chunked.make(nc, "mix0g", 2048, NTOK, BF16, CS, kind="Internal", addr_space="Local")
    x1o = nc.dram_tensor("x1o", [NTOK, D], F32, kind="Internal").ap()
    mix1 = Chunked.make(nc, "mix1", 1024, NLAT, BF16, CS, kind="Internal")
    mix1g = Chunked.make(nc, "mix1g", 2048, NLAT, BF16, CS, kind="Internal", addr_space="Local")
    A["mix0g"] = mix0g
    A["mix1g"] = mix1g
    A["x1loc"] = x1o[0:NLAT, :]
    with ExitStack() as st:
        S = Sched(nc)
        C = Ctx(nc, st)
        PS = [C.ps(f"pb{i}", [128, 512], F32) for i in range(7)]
        pTb = C.ps("pTb", [128, 1024], BF16)
        sh = dict(S=S, C=C, PS=PS, pTb=pTb, A=A, mix0=mix0, x1o=x1o, mix1=mix1, yo=yo, outs=[])
        import os
        nocc = os.environ.get("K_FUSE_NOCC") == "1"
        light = os.environ.get("K_P1_LIGHT") == "1"

        def exchange(gch, lch):
            for (_, _, go), (_, _, gi_) in zip(gch.chunks, lch.chunks):
                if nocc:
                    S.dma(go[0:1024, :], gi_, q="sp")
                    S.dma(go[1024:2048, :], gi_, q="act")
                else:
                    S.allgather(go, gi_, GROUPS)
        if light:
            build_p1(nc, sample, shared=sh, n_ct=1, do_att=False)
        else:
            build_p1(nc, sample, shared=sh)
        S.fence()
        exchange(mix0g, mix0)
        build_p2(nc, sample, shared=sh, n_heads=(1 if light else 4))
        S.fence()
        exchange(mix1g, mix1)
        build_p3(nc, sample, shared=sh)
        C.stack = st
        S.emit(st, final_wait=sh["outs"])
        print("fused stats", S.stats, flush=True)
    return nc


def kernel(**inputs):
    inp = {k: np.asarray(v) for k, v in inputs.items()}
    res = _launch(build_fused, [fused_host_inputs(inp, b, hf) for (b, hf) in CORES])
    out = np.zeros((4, NLAT, D), np.float32)
    for ci, (b, hf) in enumerate(CORES):
        out[b, hf * 2048:(hf + 1) * 2048] = np.asarray(res[ci]["yo"])[hf * 2048:(hf + 1) * 2048]
    return out
```

```python
import math
from contextlib import ExitStack
import numpy as np
import ml_dtypes
import concourse.bass as bass
import concourse.mybir as mybir
from concourse.bass_utils import run_bass_kernel_spmd

F32 = mybir.dt.float32
BF16 = mybir.dt.bfloat16
AF = mybir.ActivationFunctionType
ALU = mybir.AluOpType
AX = mybir.AxisListType
NPBF = ml_dtypes.bfloat16

D = 1024
NLAT = 4096
NCTX = 256
NTOK = NLAT + NCTX
EPS = 1e-6

ENGS = ("pe", "act", "dve", "pool", "sp")
SEM_ROT = 12000
ND = 8
CC_INC = 1
_DT_SIZE = {}


def _dsize(dt):
    if dt not in _DT_SIZE:
        _DT_SIZE[dt] = mybir.dt.size(dt)
    return _DT_SIZE[dt]


def _box(ap):
    t = ap.tensor
    dims = list(ap.ap)
    off = ap.offset
    sp = str(ap.space)
    if sp in ("SB", "PSUM"):
        pstep = 1
        for s in t.shape[1:]:
            pstep *= s
        p0 = off // pstep
        f0 = off % pstep
        pd = dims[0]
        npart = 1 if pd[0] == 0 else pd[1]
        ext = 0
        for st, cnt in dims[1:]:
            ext += (cnt - 1) * abs(st)
        if sp == "PSUM":
            return (t.name, 0, 128, 0, pstep)
        return (t.name, p0, p0 + npart, f0, f0 + ext + 1)
    ext = 0
    for st, cnt in dims:
        ext += (cnt - 1) * abs(st)
    return (t.name, 0, 1, off, off + ext + 1)


class Sched:
    def __init__(self, nc, same_engine_sync=True):
        self.nc = nc
        self.ins = []
        self.track = {}
        self.same = same_engine_sync
        self.last_cp = {e: None for e in ENGS}
        self.last_dm = {e: [] for e in ENGS}
        self.pending = {e: set() for e in ENGS}

    def _deps(self, reads, writes, idx):
        deps = set()
        rb = [_box(a) for a in reads]
        wb = [_box(a) for a in writes]
        for b in rb:
            for ent in self.track.get(b[0], ()):
                e = ent[0]
                if e[1] < b[2] and b[1] < e[2] and e[3] < b[4] and b[3] < e[4]:
                    if ent[1] is not None:
                        deps.add(ent[1])
        for b in wb:
            for ent in self.track.get(b[0], ()):
                e = ent[0]
                if e[1] < b[2] and b[1] < e[2] and e[3] < b[4] and b[3] < e[4]:
                    if ent[1] is not None:
                        deps.add(ent[1])
                    deps.update(ent[2])
        for b in rb:
            lst = self.track.setdefault(b[0], [])
            for ent in lst:
                if ent[0] == b:
                    ent[2].append(idx)
                    break
            else:
                lst.append([b, None, [idx]])
        for b in wb:
            lst = self.track.get(b[0], [])
            keep = []
            for ent in lst:
                e = ent[0]
                if b[1] <= e[1] and e[2] <= b[2] and b[3] <= e[3] and e[4] <= b[4]:
                    continue
                keep.append(ent)
            keep.append([b, idx, []])
            self.track[b[0]] = keep
        deps.discard(idx)
        return deps

    def add(self, eng, fn, r=(), w=(), dma=False):
        idx = len(self.ins)
        deps = self._deps(list(r), list(w), idx)
        if self.pending[eng]:
            deps |= self.pending[eng]
            self.pending[eng] = set()
        self.ins.append(dict(eng=eng, fn=fn, deps=deps, dma=dma, users=set()))
        if dma:
            lo = self.last_dm[eng]
            lo.append(idx)
            if len(lo) > ND:
                lo.pop(0)
        else:
            self.last_cp[eng] = idx
        return idx

    def fence(self):
        allp = set()
        for e in ENGS:
            allp.update(self.last_dm[e])
            if self.last_cp[e] is not None:
                allp.add(self.last_cp[e])
        for e in ENGS:
            self.pending[e] = set(allp) | self.pending[e]
        self.track = {}

    def mm(self, out, lhsT, rhs, start=True, stop=True, **kw):
        r = [lhsT, rhs]
        if not start:
            r.append(out)
        return self.add("pe", lambda e: e.matmul(out, lhsT, rhs, start=start, stop=stop, **kw), r=r, w=[out])

    def transpose(self, out, in_, ident):
        return self.add("pe", lambda e: e.transpose(out, in_, ident), r=[in_, ident], w=[out])

    def act(self, out, in_, func, bias=None, scale=None, accum_out=None):
        r = [in_]
        kw = {}
        if bias is not None:
            kw["bias"] = bias
            if not isinstance(bias, (int, float)):
                r.append(bias)
        if scale is not None:
            kw["scale"] = scale
            if not isinstance(scale, (int, float)):
                r.append(scale)
        w = [out]
        if accum_out is not None:
            kw["accum_out"] = accum_out
            w.append(accum_out)
        return self.add("act", lambda e: e.activation(out, in_, func, **kw), r=r, w=w)

    def tt(self, out, in0, in1, op, eng="dve"):
        return self.add(eng, lambda e: e.tensor_tensor(out, in0, in1, op), r=[in0, in1], w=[out])

    def ts(self, out, in0, s1, s2, op0, op1=None, eng="dve"):
        r = [in0]
        if not isinstance(s1, (int, float)):
            r.append(s1)
        if s2 is not None and not isinstance(s2, (int, float)):
            r.append(s2)
        if op1 is None:
            return self.add(eng, lambda e: e.tensor_scalar(out, in0, s1, None, op0), r=r, w=[out])
        return self.add(eng, lambda e: e.tensor_scalar(out, in0, s1, s2, op0, op1), r=r, w=[out])

    def stt(self, out, in0, scalar, in1, op0, op1):
        r = [in0, in1]
        if not isinstance(scalar, (int, float)):
            r.append(scalar)
        return self.add("dve", lambda e: e.scalar_tensor_tensor(out, in0, scalar, in1, op0, op1), r=r, w=[out])

    def copy(self, out, in_, eng="dve"):
        if eng == "act":
            return self.act(out, in_, AF.Copy)
        return self.add(eng, lambda e: e.tensor_copy(out, in_), r=[in_], w=[out])

    def memset(self, ap, val, eng="dve"):
        return self.add(eng, lambda e: e.memset(ap, val), r=[], w=[ap])

    def recip(self, out, in_):
        return self.add("dve", lambda e: e.reciprocal(out, in_), r=[in_], w=[out])

    def reduce(self, out, in_, op, axis=AX.X):
        return self.add("dve", lambda e: e.tensor_reduce(out, in_, axis, op), r=[in_], w=[out])

    def wrap(self, out, in_, t1, t2):
        self.ts(t1, in_, -math.pi, 2 * math.pi, ALU.is_lt, ALU.mult)
        self.ts(t2, in_, math.pi, -2 * math.pi, ALU.is_gt, ALU.mult)
        self.tt(t1, t1, t2, ALU.add, eng="pool")
        return self.tt(out, in_, t1, ALU.add)

    def allgather(self, out, in_, groups):
        idx = self.add("pool", lambda e: e.collective_compute("AllGather", ALU.bypass, replica_groups=groups,
                                                              ins=[in_], outs=[out]), r=[in_], w=[out], dma=True)
        self.ins[idx]["cc"] = True
        return idx

    def dma(self, out, in_, q="sp", **kw):
        return self.add(q, lambda e: e.dma_start(out, in_, **kw), r=[in_], w=[out], dma=True)

    def emit(self, stack, final_wait=()):
        nc = self.nc
        ins = self.ins
        for i, it in enumerate(ins):
            for d in it["deps"]:
                ins[d]["users"].add(i)

        def new_sem(tag):
            return stack.enter_context(nc.semaphore(tag))

        eng_sem = {e: [new_sem(f"s_{e}_0")] for e in ENGS}
        eng_cnt = {e: 0 for e in ENGS}
        dma_sems = {e: [new_sem(f"d_{e}_{k}") for k in range(ND)] for e in ("sp", "pool", "act")}
        dma_cnt = {e: 0 for e in ENGS}
        for i, it in enumerate(ins):
            e = it["eng"]
            it["sig"] = None
            it["pre"] = []
            if it.get("cc"):
                s = new_sem(f"cc_{i}")
                it["sig"] = (s, CC_INC, CC_INC)
            elif it["dma"]:
                k = dma_cnt[e]
                s = dma_sems[e][k % ND]
                it["sig"] = (s, 16 * (k // ND + 1), 16)
                if k >= ND:
                    it["pre"].append((s, 16 * (k // ND)))
                dma_cnt[e] = k + 1
            else:
                need = False
                for u in it["users"]:
                    ue = ins[u]["eng"]
                    if ue != e or ins[u]["dma"]:
                        need = True
                    elif self.same and e != "pe":
                        need = True
                if need:
                    if eng_cnt[e] >= SEM_ROT:
                        eng_sem[e].append(new_sem(f"s_{e}_{len(eng_sem[e])}"))
                        eng_cnt[e] = 0
                    eng_cnt[e] += 1
                    it["sig"] = (eng_sem[e][-1], eng_cnt[e], 1)
        self.stats = dict(n_ins=len(ins), sig={e: (len(eng_sem[e]) - 1) * SEM_ROT + eng_cnt[e] for e in ENGS},
                          dma=dict(dma_cnt), per_eng={e: sum(1 for it in ins if it["eng"] == e) for e in ENGS})
        waited = {e: {} for e in ENGS}
        streams = {e: [] for e in ENGS}
        for i, it in enumerate(ins):
            e = it["eng"]
            waits = list(it["pre"])
            for d in sorted(it["deps"]):
                pd = ins[d]
                if pd["sig"] is None:
                    continue
                if pd["eng"] == e and not pd["dma"] and (e == "pe" or not self.same):
                    continue
                waits.append((pd["sig"][0], pd["sig"][1]))
            ww = []
            for s, v in waits:
                key = id(s)
                if waited[e].get(key, 0) >= v:
                    continue
                waited[e][key] = v
                ww.append((s, v))
            streams[e].append((ww, it))
        fw = [(ins[d]["sig"][0], ins[d]["sig"][1]) for d in final_wait]
        self.n_ins = len(ins)

        def run(handle, lst, extra=()):
            for ww, it in lst:
                for s, v in ww:
                    handle.wait_ge(s, v)
                inst = it["fn"](handle)
                if it["sig"] is not None:
                    inst.then_inc(it["sig"][0], it["sig"][2])
            for s, v in extra:
                handle.wait_ge(s, v)

        block = stack.enter_context(nc.Block())

        @block.tensor
        def _(e):
            run(e, streams["pe"])

        @block.scalar
        def _(e):
            run(e, streams["act"])

        @block.vector
        def _(e):
            run(e, streams["dve"])

        @block.gpsimd
        def _(e):
            run(e, streams["pool"])

        @block.sync
        def _(e):
            run(e, streams["sp"], fw)


def fft_tables(N1):
    N = 64 * N1
    H = N1 // 2
    n1 = np.arange(N1)
    k1 = np.arange(N1)
    a1 = 2 * np.pi * np.outer(n1, k1) / N1
    CS1 = np.concatenate([np.cos(a1), -np.sin(a1)], 1)
    m = np.arange(128)
    n2m = m // 2
    c2m = m % 2
    aT = 2 * np.pi * np.outer(n2m, k1) / N
    a64 = 2 * np.pi * np.outer(n2m, n2m) / 64
    same = (c2m[:, None] == c2m[None, :])
    BdC = np.cos(a64) * same
    BdS = np.sin(a64) * same
    aI = 2 * np.pi * np.outer(k1, np.arange(64)) / N
    ai = 2 * np.pi * np.outer(k1, np.arange(H)) / N1
    f = lambda a: np.ascontiguousarray(a, dtype=np.float32)
    return dict(CS1=f(CS1), TwFc=f(np.cos(aT)), TwFs=f(np.sin(aT)), BdC=f(BdC), BdS=f(BdS), BdSn=f(-BdS),
                R1=f(np.concatenate([BdC, BdS], 1)), R2=f(np.concatenate([-BdS, BdC], 1)),
                TwIc=f(np.cos(aI)), TwIs=f(np.sin(aI)), Ci=f(np.cos(ai) / N), Sin_=f(-np.sin(ai) / N))


def filter_pos_tables(n):
    N = 2 * n
    N1 = N // 64
    t = np.linspace(0.0, 1.0, n, dtype=np.float32)
    bands = np.linspace(1e-4, 15.0, 16, dtype=np.float32)
    ang = (np.float32(2.0 * math.pi / n) * np.arange(n, dtype=np.float32)[:, None] * bands).astype(np.float32)
    z = np.concatenate([t[:, None], np.cos(ang), -np.sin(ang)], axis=-1).astype(np.float32)
    idx = np.arange(N)
    d = np.where(idx < n, idx, N - idx)
    d[n] = 0
    z2 = z[d]
    BIG = 1.0e4
    negF = np.where(idx < n, -t[d], -BIG).astype(np.float32)
    negB = np.where(idx > n, -t[d], -BIG).astype(np.float32)
    negt = np.stack([negF.reshape(N1, 64), negB.reshape(N1, 64)], axis=1)
    return np.ascontiguousarray(z2.T), np.ascontiguousarray(negt)


def rope_tables():
    nf = 32
    inv = (10000.0 ** (-np.arange(nf, dtype=np.float32) / nf)).astype(np.float32)
    j = np.arange(64, dtype=np.float32)
    dd = np.arange(128)
    ang = j[None, :] * inv[dd % 32][:, None]
    R = np.zeros((128, 128), np.float32)
    for dp in range(128):
        if dp % 64 < 32:
            R[dp, dp + 32] = -1.0
        else:
            R[dp, dp - 32] = 1.0
    return np.cos(ang).astype(np.float32), np.sin(ang).astype(np.float32), np.ascontiguousarray(R.T)


P1_SPEC = None


def p1_host_inputs(inp, b, hf):
    f = lambda a: np.ascontiguousarray(a, dtype=np.float32)
    o = {}
    o["x_all"] = f(np.concatenate([inp["x"][b], inp["ctx"][b]], 0))
    cond = np.stack([inp["c"][b], inp["c_ctx"]], -1)
    o["condT"] = f(cond.reshape(8, 128, 2).transpose(1, 0, 2))
    wm = inp["w_mod"][0][:, :2048]
    o["wmod"] = f(wm.reshape(8, 128, 16, 128).transpose(2, 1, 0, 3))
    o["bmodT"] = f(inp["b_mod"][0][:2048].reshape(16, 128).T)
    o["gpreT"] = f(inp["g_pre"][0].reshape(8, 128).T)
    W = inp["e_w_in"][0]
    cols = []
    cols.append(np.arange(5120 + hf * 128, 5120 + hf * 128 + 128))
    cols.append(np.arange(5376 + hf * 128, 5376 + hf * 128 + 128))
    for h in range(4):
        hd = 4 * hf + h
        cols.append(np.arange(4096 + hd * 128, 4096 + hd * 128 + 128))
        cols.append(np.arange(5632 + hd * 128, 5632 + hd * 128 + 128))
    for ct in range(4):
        c0 = hf * 512 + ct * 128
        cols.append(np.arange(1024 + c0, 1024 + c0 + 128))
        cols.append(np.arange(2048 + c0, 2048 + c0 + 128))
        cols.append(np.arange(c0, c0 + 128))
        cols.append(np.arange(3072 + c0, 3072 + c0 + 128))
    wt = np.stack([W[:, c] for c in cols], 0)
    o["w_in"] = f(wt.reshape(26, 8, 128, 128).transpose(0, 2, 1, 3))
    cw = inp["e_conv_w"][0]
    cb = inp["e_conv_b"][0]
    convw = np.zeros((128, 4, 3, 3), np.float32)
    convb = np.zeros((128, 4, 3), np.float32)
    for ct in range(4):
        c0 = hf * 512 + ct * 128
        for s in range(3):
            convw[:, ct, s, :] = cw[:, s * 1024 + c0: s * 1024 + c0 + 128].T
            convb[:, ct, s] = cb[s * 1024 + c0: s * 1024 + c0 + 128]
    o["convw"] = convw
    o["convb"] = convb
    o["gq"] = f(inp["e_q_norm"][0].reshape(128, 1))
    o["gk"] = f(inp["e_k_norm"][0].reshape(128, 1))
    rc, rs, rT = rope_tables()
    o["ropec"], o["ropes"], o["RmatT"] = rc, rs, rT
    z2l, ntl = filter_pos_tables(NLAT)
    z2c, ntc = filter_pos_tables(NCTX)
    o["z2T_l"], o["negt_l"], o["z2T_c"], o["negt_c"] = z2l, ntl, z2c, ntc
    o["fw1"] = f(inp["e_filt_w1"][0])
    o["fb1"] = f(inp["e_filt_b1"][0].reshape(64, 1))
    o["ffr"] = f(inp["e_filt_freq"][0].reshape(64, 1))
    o["fw2"] = f(inp["e_filt_w2"][0])
    o["fb2"] = f(inp["e_filt_b2"][0].reshape(64, 1))
    w3 = inp["e_filt_w3"][0].reshape(64, 2, 1024)
    o["fw3"] = f(w3[:, :, hf * 512:(hf + 1) * 512])
    min_decay = math.log(1e-2) / 1.5
    max_decay = math.log(1e-2) / 0.3
    deltas = np.abs(np.linspace(min_decay, max_decay, 1024, dtype=np.float32))
    o["adel"] = f(np.broadcast_to(deltas[hf * 512:(hf + 1) * 512][None, :], (128, 512)))
    o["hyb"] = f(inp["e_hy_bias"][0][hf * 512:(hf + 1) * 512].reshape(1, 512))
    for N1, tag in ((128, "l"), (8, "c")):
        for k, v in fft_tables(N1).items():
            o[f"ft_{tag}_{k}"] = v
    o["identF"] = np.eye(128, dtype=np.float32)
    return o


def _declare_inputs(nc, sample):
    aps = {}
    for k, v in sample.items():
        dt = F32 if v.dtype == np.float32 else BF16
        aps[k] = nc.dram_tensor(k, list(v.shape), dt, kind="ExternalInput").ap()
    return aps


class Chunked:
    def __init__(self, chunks, rows, ntok):
        self.chunks = chunks
        self.rows = rows
        self.ntok = ntok

    @staticmethod
    def make(nc, name, rows, ntok, dt, csize, **kw):
        ch = []
        for k, t0 in enumerate(range(0, ntok, csize)):
            n = min(csize, ntok - t0)
            ch.append((t0, n, nc.dram_tensor(f"{name}_{k}", [rows, n], dt, **kw).ap()))
        return Chunked(ch, rows, ntok)

    @staticmethod
    def wrap(ap):
        return Chunked([(0, ap.shape[1], ap)], ap.shape[0], ap.shape[1])

    def cols(self, r0, r1, t0, n):
        for (c0, cn, ap) in self.chunks:
            if c0 <= t0 and t0 + n <= c0 + cn:
                return ap[r0:r1, t0 - c0:t0 - c0 + n]
        raise AssertionError(("straddles chunks", t0, n))

    def pieces(self, r0, r1):
        for (c0, cn, ap) in self.chunks:
            yield c0, cn, ap[r0:r1, :]


class Ctx:
    def __init__(self, nc, stack):
        self.nc = nc
        self.stack = stack
        self.n = 0

    def sb(self, name, shape, dt, stack=None):
        self.n += 1
        return (stack or self.stack).enter_context(self.nc.sbuf_tensor(f"{name}_{self.n}", list(shape), dt))

    def ps(self, name, shape, dt, stack=None):
        self.n += 1
        return (stack or self.stack).enter_context(self.nc.psum_tensor(f"{name}_{self.n}", list(shape), dt))


def load_fft_tabs(S, C, A, tag, N1):
    H = N1 // 2
    T = {}
    def ld(name, shape, dt, q):
        t = C.sb(f"ft{tag}{name}", shape, dt)
        S.dma(t[:], A[f"ft_{tag}_{name}"], q=q)
        return t
    T["CS1"] = ld("CS1", [N1, 2 * N1], BF16, "pool")
    T["TwFc"] = ld("TwFc", [128, N1], F32, "sp")
    T["TwFs"] = ld("TwFs", [128, N1], F32, "sp")
    T["BdC"] = ld("BdC", [128, 128], BF16, "pool")
    T["BdS"] = ld("BdS", [128, 128], BF16, "pool")
    T["BdSn"] = ld("BdSn", [128, 128], BF16, "pool")
    T["R1"] = ld("R1", [128, 256], BF16, "pool")
    T["R2"] = ld("R2", [128, 256], BF16, "pool")
    T["TwIc"] = ld("TwIc", [N1, 64], F32, "sp")
    T["TwIs"] = ld("TwIs", [N1, 64], F32, "sp")
    T["Ci"] = ld("Ci", [N1, H], BF16, "pool")
    T["Sin_"] = ld("Sin_", [N1, H], BF16, "pool")
    return T


def fft_conv_core(S, C, PS, T, N1, X, Xf, Ball_re, Ball_im, tmps, evac):
    H = N1 // 2
    W2 = 2 * N1
    Gf = 512 // W2
    NG = 64 // Gf
    GN = Gf * N1
    tc = T["TwFc"][:, :].unsqueeze(1).broadcast_to([128, 2 * Gf, N1])
    tsn = T["TwFs"][:, :].unsqueeze(1).broadcast_to([128, 2 * Gf, N1])
    tic = T["TwIc"][:, :].unsqueeze(1).unsqueeze(3).broadcast_to([N1, 4, 64, 2])
    tis = T["TwIs"][:, :].unsqueeze(1).unsqueeze(3).broadcast_to([N1, 4, 64, 2])

    def stageA(g):
        m1, m2, Ar, Ai, Br_, Bi_, Fh, Yr, Yi = tmps[g % 2]
        pg1, pf1 = PS[2 * (g % 2)], PS[2 * (g % 2) + 1]
        p0 = g * Gf
        for j in range(Gf):
            S.mm(pg1[:, j * W2:(j + 1) * W2], X[0:H, p0 + j, :], T["CS1"][0:H, :])
        for j in range(Gf):
            S.mm(pf1[:, j * W2:(j + 1) * W2], Xf[0:N1, p0 + j, :], T["CS1"][0:N1, :])
        m1a = m1[:, :].rearrange("p (a n) -> p a n", n=N1)
        m2a = m2[:, :].rearrange("p (a n) -> p a n", n=N1)
        m1v = m1[:, :].rearrange("p (a r n) -> p a r n", r=2, n=N1)
        m2v = m2[:, :].rearrange("p (a r n) -> p a r n", r=2, n=N1)
        for (pp, ar, ai) in ((pg1, Ar, Ai), (pf1, Br_, Bi_)):
            v = pp[:, :].rearrange("p (a n) -> p a n", n=N1)
            S.tt(m1a, v, tc, ALU.mult)
            S.tt(m2a, v, tsn, ALU.mult)
            arv = ar[:, :].rearrange("p (a n) -> p a n", n=N1)
            aiv = ai[:, :].rearrange("p (a n) -> p a n", n=N1)
            S.tt(arv, m1v[:, :, 0, :], m2v[:, :, 1, :], ALU.add, eng="pool")
            S.tt(aiv, m1v[:, :, 1, :], m2v[:, :, 0, :], ALU.subtract, eng="pool")

    def stageB(g):
        m1, m2, Ar, Ai, Br_, Bi_, Fh, Yr, Yi = tmps[g % 2]
        pg3, pf3 = PS[4], PS[5]
        for (pp, ar, ai) in ((pg3, Ar, Ai), (pf3, Br_, Bi_)):
            S.mm(pp[:, 0:GN], T["BdC"][:], ar[:, :], start=True, stop=False)
            S.mm(pp[:, 0:GN], T["BdS"][:], ai[:, :], start=False, stop=True)
            S.mm(pp[:, GN:2 * GN], T["BdC"][:], ai[:, :], start=True, stop=False)
            S.mm(pp[:, GN:2 * GN], T["BdSn"][:], ar[:, :], start=False, stop=True)
        S.act(Fh[:, :], pf3[:, :], AF.Copy)
        gv = pg3[:, :].rearrange("p (r n) -> p r n", r=2)
        fr = Fh[:, 0:GN].unsqueeze(1).broadcast_to([128, 2, GN])
        fi = Fh[:, GN:2 * GN].unsqueeze(1).broadcast_to([128, 2, GN])
        q1 = m1[:, :].rearrange("p (r n) -> p r n", r=2)
        q2 = m2[:, :].rearrange("p (r n) -> p r n", r=2)
        S.tt(q1, gv, fr, ALU.mult)
        S.tt(q2, gv, fi, ALU.mult)
        S.tt(Yr[:, :], q1[:, 0, :], q2[:, 1, :], ALU.subtract, eng="pool")
        S.tt(Yi[:, :], q2[:, 0, :], q1[:, 1, :], ALU.add, eng="pool")

    def stageC(g):
        m1, m2, Ar, Ai, Br_, Bi_, Fh, Yr, Yi = tmps[g % 2]
        pI = PS[6]
        for sub in range(Gf // 2):
            pr0 = g * Gf + 2 * sub
            for jj in range(2):
                j = 2 * sub + jj
                S.mm(pI[0:N1, jj * 256:(jj + 1) * 256], Yr[:, j * N1:(j + 1) * N1], T["R1"][:], start=True, stop=False)
                S.mm(pI[0:N1, jj * 256:(jj + 1) * 256], Yi[:, j * N1:(j + 1) * N1], T["R2"][:], start=False, stop=True)
            iv = pI[0:N1, :].rearrange("p (a n c) -> p a n c", a=4, c=2)
            n1v = m1[0:N1, :].rearrange("p (a n c) -> p a n c", a=4, c=2)
            n2v = m2[0:N1, :].rearrange("p (a n c) -> p a n c", a=4, c=2)
            S.tt(n1v, iv, tic, ALU.mult)
            S.tt(n2v, iv, tis, ALU.mult)
            n1p = m1[0:N1, :].rearrange("p (j r n c) -> p j r n c", j=2, r=2, c=2)
            n2p = m2[0:N1, :].rearrange("p (j r n c) -> p j r n c", j=2, r=2, c=2)
            ore = Ball_re[0:N1, :, 2 * pr0:2 * pr0 + 4].rearrange("p n (j c) -> p j n c", c=2)
            oim = Ball_im[0:N1, :, 2 * pr0:2 * pr0 + 4].rearrange("p n (j c) -> p j n c", c=2)
            S.tt(ore, n1p[:, :, 0, :, :], n2p[:, :, 1, :, :], ALU.subtract, eng="pool")
            S.tt(oim, n2p[:, :, 0, :, :], n1p[:, :, 1, :, :], ALU.add, eng="pool")

    stageA(0)
    for g in range(NG):
        if g + 1 < NG:
            stageA(g + 1)
        stageB(g)
        stageC(g)
    ng = min(64, 512 // H)
    k = 0
    for n2_0 in range(0, 64, ng):
        po = PS[k % 2]
        k += 1
        for j in range(ng):
            n2 = n2_0 + j
            S.mm(po[:, j * H:(j + 1) * H], Ball_re[0:N1, n2, :], T["Ci"][0:N1, :], start=True, stop=False)
            S.mm(po[:, j * H:(j + 1) * H], Ball_im[0:N1, n2, :], T["Sin_"][0:N1, :], start=False, stop=True)
        evac(po, n2_0, ng)


def build_p1(nc, sample, n_ct=4, do_att=True, do_hy=True, dbg=None, stop=None, shared=None):
    if shared is None:
        A = _declare_inputs(nc, sample)
        mix = Chunked.wrap(nc.dram_tensor("mix", [1024, NTOK], BF16, kind="ExternalOutput").ap())
    else:
        A = shared["A"]
        mix = shared["mix0"]
    xf_l = nc.dram_tensor("xf_l", [4, 128, 8192], BF16, kind="Internal").ap()
    xf_c = nc.dram_tensor("xf_c", [4, 8, 8192], BF16, kind="Internal").ap()
    outs = []
    with ExitStack() as st:
        if shared is None:
            S = Sched(nc)
            C = Ctx(nc, st)
            PS = [C.ps(f"pb{i}", [128, 512], F32) for i in range(7)]
            pTb = C.ps("pTb", [128, 1024], BF16)
        else:
            S, C, PS, pTb = shared["S"], shared["C"], shared["PS"], shared["pTb"]
            C.stack = st
        identF = C.sb("identF", [128, 128], F32)
        identB = C.sb("identB", [128, 128], BF16)
        onesF = C.sb("onesF", [128, 128], F32)
        onesB = C.sb("onesB", [128, 128], BF16)
        S.dma(identF[:], A["identF"])
        S.dma(identB[:], A["identF"], q="pool")
        S.memset(onesF[:], 1.0)
        S.memset(onesB[:], 1.0, eng="pool")
        rnT = C.sb("rnT", [128, 4, 2], F32)
        TL = load_fft_tabs(S, C, A, "l", 128)
        TC = load_fft_tabs(S, C, A, "c", 8)
        ftmp = []
        for par in range(2):
            ftmp.append((C.sb("m1", [128, 512], F32), C.sb("m2", [128, 512], F32),
                         C.sb("Ar", [128, 256], BF16), C.sb("Ai", [128, 256], BF16),
                         C.sb("Br", [128, 256], BF16), C.sb("Bi", [128, 256], BF16),
                         C.sb("Fh", [128, 512], F32),
                         C.sb("Yr", [128, 256], BF16), C.sb("Yi", [128, 256], BF16)))
        if stop == 'const':
            S.emit(st, final_wait=[])
            return nc

        if do_hy:
            with ExitStack() as s0:
                fw1 = C.sb("fw1", [33, 64], F32, s0); S.dma(fw1[:], A["fw1"])
                fw2 = C.sb("fw2", [64, 64], F32, s0); S.dma(fw2[:], A["fw2"])
                fb1 = C.sb("fb1", [64, 1], F32, s0); S.dma(fb1[:], A["fb1"])
                fb2 = C.sb("fb2", [64, 1], F32, s0); S.dma(fb2[:], A["fb2"])
                ffr = C.sb("ffr", [64, 1], F32, s0); S.dma(ffr[:], A["ffr"])
                fw3 = C.sb("fw3", [64, 2, 512], BF16, s0); S.dma(fw3[:], A["fw3"], q="pool")
                adel = C.sb("adel", [128, 512], F32, s0); S.dma(adel[:], A["adel"])
                hyb = C.sb("hyb", [1, 512], F32, s0); S.dma(hyb[:], A["hyb"])
                frb1 = C.sb("frb1", [64, 1], F32, s0)
                frb2 = C.sb("frb2", [64, 1], F32, s0)
                S.tt(frb1[:], fb1[:], ffr[:], ALU.mult)
                S.tt(frb2[:], fb2[:], ffr[:], ALU.mult)
                hdn2 = C.sb("hdn2", [64, 8192], BF16, s0)
                z2 = C.sb("z2", [33, 512], F32, s0)
                u1 = C.sb("u1", [64, 512], F32, s0)
                h1 = C.sb("h1", [64, 512], F32, s0)
                wt1 = C.sb("wt1", [64, 512], F32, s0)
                wt2 = C.sb("wt2", [64, 512], F32, s0)
                negt = C.sb("negt", [128, 2, 64], F32, s0)
                wbuf = C.sb("wbuf", [128, 2, 512], F32, s0)
                fgrp = C.sb("fgrp", [128, 512], F32, s0)
                fgr2 = C.sb("fgr2", [128, 512], F32, s0)
                red = C.sb("red", [128, 128], F32, s0)
                acc = C.sb("acc", [128, 128], F32, s0)
                nrow = C.sb("nrow", [1, 128], F32, s0)
                ncol = C.sb("ncol", [128, 1], F32, s0)
                Xfo = C.sb("Xfo", [128, 64, 128], BF16, s0)
                for (tag, N1, z2T, ntab, xfd, li) in (("l", 128, A["z2T_l"], A["negt_l"], xf_l, 0),
                                                      ("c", 8, A["z2T_c"], A["negt_c"], xf_c, 1)):
                    N = 64 * N1
                    S.dma(negt[0:N1, :, :], ntab)
                    for blk in range(N // 512):
                        sl = slice(blk * 512, (blk + 1) * 512)
                        S.dma(z2[:], z2T[:, sl])
                        S.mm(PS[5][0:64, :], fw1[:], z2[:])
                        S.ts(u1[:], PS[5][0:64, :], ffr[:, 0:1], frb1[:, 0:1], ALU.mult, ALU.add)
                        S.wrap(u1[:], u1[:], wt1[:], wt2[:])
                        S.wrap(u1[:], u1[:], wt1[:], wt2[:])
                        S.act(h1[:], u1[:], AF.Sin)
                        S.mm(PS[6][0:64, :], fw2[:], h1[:])
                        S.ts(u1[:], PS[6][0:64, :], ffr[:, 0:1], frb2[:, 0:1], ALU.mult, ALU.add)
                        S.wrap(u1[:], u1[:], wt1[:], wt2[:])
                        S.wrap(u1[:], u1[:], wt1[:], wt2[:])
                        S.act(hdn2[:, sl], u1[:], AF.Sin)
                    for ct in range(n_ct):
                        S.memset(acc[0:N1, :], 0.0)
                        for g4 in range(16):
                            for j in range(4):
                                n2 = g4 * 4 + j
                                for dr in range(2):
                                    S.mm(PS[5 + dr][0:N1, j * 128:(j + 1) * 128], hdn2[:, n2:N:64],
                                         fw3[:, dr, ct * 128:(ct + 1) * 128])
                                    S.act(wbuf[0:N1, dr, j * 128:(j + 1) * 128], adel[0:N1, ct * 128:(ct + 1) * 128],
                                          AF.Exp, scale=negt[0:N1, dr, n2:n2 + 1])
                            S.tt(fgrp[0:N1, :], PS[5][0:N1, :], wbuf[0:N1, 0, :], ALU.mult)
                            S.tt(fgr2[0:N1, :], PS[6][0:N1, :], wbuf[0:N1, 1, :], ALU.mult)
                            S.tt(fgrp[0:N1, :], fgrp[0:N1, :], fgr2[0:N1, :], ALU.add, eng="pool")
                            S.act(fgr2[0:N1, :], fgrp[0:N1, :], AF.Abs)
                            S.reduce(red[0:N1, :], fgr2[0:N1, :].rearrange("p (j c) -> p c j", j=4), ALU.add)
                            S.tt(acc[0:N1, :], acc[0:N1, :], red[0:N1, :], ALU.add, eng="pool")
                            ov = Xfo[0:N1, :, :].rearrange("p a (n c) -> p n a c", c=2)[:, g4 * 4:g4 * 4 + 4, :, :]
                            iv = fgrp[0:N1, :].rearrange("p (j a c) -> p j a c", j=4, c=2)
                            S.copy(ov, iv, eng="act")
                        S.mm(PS[5][0:128, 0:1], acc[0:N1, :], onesF[0:N1, 0:1])
                        S.mm(PS[6][0:1, 0:128], onesF[0:N1, 0:1], acc[0:N1, :])
                        S.copy(ncol[:], PS[5][0:128, 0:1])
                        S.recip(rnT[:, ct, li:li + 1], ncol[:])
                        S.tt(nrow[:], PS[6][0:1, 0:128], hyb[0:1, ct * 128:(ct + 1) * 128], ALU.mult)
                        tap = Xfo[0:1, :, 0:2]
                        S.tt(tap, tap, nrow[0:1, :].rearrange("p (a c) -> p a c", c=2), ALU.add)
                        S.dma(xfd[ct, :, :], Xfo[0:N1, :, :].rearrange("p a n -> p (a n)"))
            S.fence()

        hT = C.sb("hT", [128, 8, NTOK], BF16)
        with ExitStack() as s1:
            condT = C.sb("condT", [128, 8, 2], F32, s1); S.dma(condT[:], A["condT"])
            scond = C.sb("scond", [128, 8, 2], F32, s1)
            S.act(scond[:], condT[:], AF.Silu)
            bmodT = C.sb("bmodT", [128, 16], F32, s1); S.dma(bmodT[:], A["bmodT"])
            gpreT = C.sb("gpreT", [128, 8], F32, s1); S.dma(gpreT[:], A["gpreT"])
            modT = C.sb("modT", [128, 16, 2], F32, s1)
            wm = [C.sb(f"wm{i}", [128, 8, 128], F32, s1) for i in range(2)]
            for fb in range(16):
                w = wm[fb % 2]
                S.dma(w[:], A["wmod"][fb], q="sp" if fb % 2 == 0 else "act")
                for kt in range(8):
                    S.mm(PS[fb % 2][:, 0:2], w[:, kt, :], scond[:, kt, :], start=(kt == 0), stop=(kt == 7))
                S.ts(modT[:, fb, :], PS[fb % 2][:, 0:2], bmodT[:, fb:fb + 1], None, ALU.add)
            if stop == 'mod':
                S.emit(st, final_wait=[])
                return nc
            Amod = C.sb("Amod", [128, 8, 2], F32, s1)
            S.ts(Amod[:], modT[:, 8:16, :], 1.0, None, ALU.add)
            S.tt(Amod[:], Amod[:], gpreT[:, :].unsqueeze(2).broadcast_to([128, 8, 2]), ALU.mult)
            xts = [C.sb(f"xt{i}", [128, 1024], F32, s1) for i in range(3)]
            junk = C.sb("junk", [128, 1024], BF16, s1)
            ssq = C.sb("ssq", [128, 34], F32, s1)
            rstd = C.sb("rstd", [128, 34], F32, s1)
            import os
            for i in range(int(os.environ.get('K_NT', '34'))):
                xt = xts[i % 3]
                S.dma(xt[:], A["x_all"][i * 128:(i + 1) * 128, :], q="sp" if i % 2 == 0 else "act")
                import os
                lvl = int(os.environ.get("K_DEBUG", "9"))
                if lvl < 1:
                    continue
                S.act(junk[:], xt[:], AF.Square, accum_out=ssq[:, i:i + 1])
                if lvl < 2:
                    continue
                S.act(rstd[:, i:i + 1], ssq[:, i:i + 1], AF.Sqrt, bias=EPS, scale=1.0 / D)
                if lvl < 3:
                    continue
                S.recip(rstd[:, i:i + 1], rstd[:, i:i + 1])
                S.ts(xt[:], xt[:], rstd[:, i:i + 1], None, ALU.mult)
                if lvl < 4:
                    continue
                jj = 0 if i < 32 else 1
                for half in range(2):
                    pt = PS[2 + half]
                    for k4 in range(4):
                        kt = half * 4 + k4
                        S.transpose(pt[:, k4 * 128:(k4 + 1) * 128], xt[:, kt * 128:(kt + 1) * 128], identF[:])
                    if lvl < 5:
                        continue
                    for k4 in range(4):
                        kt = half * 4 + k4
                        S.ts(hT[:, kt, i * 128:(i + 1) * 128], pt[:, k4 * 128:(k4 + 1) * 128],
                             Amod[:, kt, jj:jj + 1], modT[:, kt, jj:jj + 1], ALU.mult, ALU.add)
        S.fence()
        if dbg == "hT":
            hTo = nc.dram_tensor("hTo", [128, 8, NTOK], BF16, kind="ExternalOutput").ap()
            outs.append(S.dma(hTo, hT[:]))

        wbufs = [C.sb(f"wcol{i}", [128, 8, 128], BF16) for i in range(2)]
        wstate = {"n": 0}

        def load_w(tile_idx):
            w = wbufs[wstate["n"] % 2]
            wstate["n"] += 1
            S.dma(w[:], A["w_in"][tile_idx], q="pool")
            return w

        blocks = [(tb * 512, 512) for tb in range(8)] + [(4096, 256)]

        def proj_fm(w, blk, pt):
            t0, n = blk
            for kt in range(8):
                S.mm(pt[:, 0:n], w[:, kt, :], hT[:, kt, t0:t0 + n], start=(kt == 0), stop=(kt == 7))

        if do_att:
            with ExitStack() as s2:
                gq = C.sb("gq", [128, 1], F32, s2); S.dma(gq[:], A["gq"])
                gk = C.sb("gk", [128, 1], F32, s2); S.dma(gk[:], A["gk"])
                S.ts(gq[:], gq[:], 128.0 ** -0.5, None, ALU.mult)
                ropec = C.sb("ropec", [128, 64], F32, s2); S.dma(ropec[:], A["ropec"])
                ropes = C.sb("ropes", [128, 64], F32, s2); S.dma(ropes[:], A["ropes"])
                RmT = C.sb("RmT", [128, 128], F32, s2); S.dma(RmT[:], A["RmatT"])
                KT = C.sb("KT", [128, NTOK], BF16, s2)
                V = C.sb("V", [128, 34, 128], BF16, s2)
                sq = C.sb("sq", [128, 512], BF16, s2)
                rinv = C.sb("rinv", [128, 512], F32, s2)
                kn = C.sb("kn", [128, 512], F32, s2)
                t1 = C.sb("t1", [128, 512], F32, s2)
                t2 = C.sb("t2", [128, 512], F32, s2)

                def norm_rope(pt, blk, g, out_ap):
                    t0, n = blk
                    S.act(sq[:, 0:n], pt[:, 0:n], AF.Square)
                    S.mm(PS[2][:, 0:n], onesB[:], sq[:, 0:n])
                    S.act(rinv[:, 0:n], PS[2][:, 0:n], AF.Sqrt, bias=EPS, scale=1.0 / 128)
                    S.recip(rinv[:, 0:n], rinv[:, 0:n])
                    if t0 >= NLAT:
                        S.stt(out_ap, pt[:, 0:n], g[:, 0:1], rinv[:, 0:n], ALU.mult, ALU.mult)
                        return
                    S.stt(kn[:, 0:n], pt[:, 0:n], g[:, 0:1], rinv[:, 0:n], ALU.mult, ALU.mult)
                    S.mm(PS[2][:, 0:n], RmT[:], kn[:, 0:n])
                    r0 = t0 // 64
                    for (p0, p1) in ((0, 64), (64, 128)):
                        if p0 == 0:
                            cv = ropec[p0:p1, r0:r0 + 8].unsqueeze(2).broadcast_to([64, 8, 64])
                            sv = ropes[p0:p1, r0:r0 + 8].unsqueeze(2).broadcast_to([64, 8, 64])
                        else:
                            cv = ropec[p0:p1, :].unsqueeze(1).broadcast_to([64, 8, 64])
                            sv = ropes[p0:p1, :].unsqueeze(1).broadcast_to([64, 8, 64])
                        v3 = lambda a: a[p0:p1, 0:512].rearrange("p (r c) -> p r c", c=64)
                        S.tt(v3(t1), v3(kn), cv, ALU.mult, eng="pool")
                        S.tt(v3(t2), v3(PS[2]), sv, ALU.mult)
                    S.tt(out_ap, t1[:, 0:n], t2[:, 0:n], ALU.add)

                wk = load_w(0)
                for bi, blk in enumerate(blocks):
                    proj_fm(wk, blk, PS[bi % 2])
                    norm_rope(PS[bi % 2], blk, gk, KT[:, blk[0]:blk[0] + blk[1]])
                wv = load_w(1)
                for i4 in range(0, 34, 4):
                    nt = min(4, 34 - i4)
                    pt = PS[(i4 // 4) % 2]
                    for j in range(nt):
                        i = i4 + j
                        for kt in range(8):
                            S.mm(pt[:, j * 128:(j + 1) * 128], hT[:, kt, i * 128:(i + 1) * 128], wv[:, kt, :],
                                 start=(kt == 0), stop=(kt == 7))
                    S.act(V[:, i4:i4 + nt, :].rearrange("p a d -> p (a d)"), pt[:, 0:nt * 128], AF.Copy)
                qTb = [C.sb(f"qTb{i}", [128, 512], BF16, s2) for i in range(2)]
                gab = [C.sb(f"gab{i}", [128, 512], F32, s2) for i in range(2)]
                Pb = [C.sb(f"Pb{i}", [128, 512], BF16, s2) for i in range(3)]
                rden = C.sb("rden", [1, 512], F32, s2)
                o1 = C.sb("o1", [128, 512], F32, s2)
                mo = [C.sb(f"mo{i}", [128, 512], BF16, s2) for i in range(2)]
                work = [(h, bi) for h in range(4) for bi in range(len(blocks))]
                wq = {}

                def prep(idx):
                    h, bi = work[idx]
                    blk = blocks[bi]
                    if bi == 0:
                        wq["q"] = load_w(2 + 2 * h)
                        wq["g"] = load_w(3 + 2 * h)
                    proj_fm(wq["q"], blk, PS[0])
                    norm_rope(PS[0], blk, gq, qTb[idx % 2][:, 0:blk[1]])
                    proj_fm(wq["g"], blk, PS[1])
                    S.act(gab[idx % 2][:, 0:blk[1]], PS[1][:, 0:blk[1]], AF.Silu)

                def attend(idx):
                    h, bi = work[idx]
                    t0, n = blocks[bi]
                    ktiles = list(range(34)) if t0 < NLAT else [32, 33]
                    pO, pD = PS[5], PS[6]
                    nk = len(ktiles)

                    def smm(ki):
                        kt_i = ktiles[ki]
                        S.mm(PS[3 + ki % 2][:, 0:n], KT[:, kt_i * 128:(kt_i + 1) * 128], qTb[idx % 2][:, 0:n])
                    smm(0)
                    if nk > 1:
                        smm(1)
                    for ki, kt_i in enumerate(ktiles):
                        pS = PS[3 + ki % 2]
                        P = Pb[ki % 3]
                        S.act(P[:, 0:n], pS[:, 0:n], AF.Exp)
                        S.mm(pO[:, 0:n], V[:, kt_i, :], P[:, 0:n], start=(ki == 0), stop=(ki == nk - 1))
                        S.mm(pD[0:1, 0:n], onesB[:, 0:1], P[:, 0:n], start=(ki == 0), stop=(ki == nk - 1))
                        if ki + 2 < nk:
                            smm(ki + 2)
                    S.recip(rden[0:1, 0:n], pD[0:1, 0:n])
                    S.mm(PS[2][:, 0:n], onesF[0:1, :], rden[0:1, 0:n])
                    S.tt(o1[:, 0:n], pO[:, 0:n], gab[idx % 2][:, 0:n], ALU.mult)
                    m = mo[idx % 2]
                    S.tt(m[:, 0:n], o1[:, 0:n], PS[2][:, 0:n], ALU.mult)
                    outs.append(S.dma(mix.cols(512 + h * 128, 512 + (h + 1) * 128, t0, n), m[:, 0:n], q="sp"))

                prep(0)
                for idx in range(len(work)):
                    if idx + 1 < len(work):
                        prep(idx + 1)
                    attend(idx)
            S.fence()

        if do_hy:
            with ExitStack() as s3:
                convw = C.sb("convw", [128, 4, 3, 3], F32, s3); S.dma(convw[:], A["convw"])
                convb = C.sb("convb", [128, 4, 3], F32, s3); S.dma(convb[:], A["convb"])
                SEG = NLAT + 2 + NCTX + 2
                ust = C.sb("ust", [128, SEG], BF16, s3)
                S.memset(ust[:], 0.0)
                ctmp = C.sb("ctmp", [128, 1024], F32, s3)
                x1c = C.sb("x1c", [128, NTOK], BF16, s3)
                gT = C.sb("gT", [128, NTOK], BF16, s3)
                xg = C.sb("xg", [128, NTOK], BF16, s3)
                Xl = C.sb("Xl", [64, 64, 128], BF16, s3)
                Xfl = C.sb("Xfl", [128, 64, 128], BF16, s3)
                Bre = C.sb("Bre", [128, 64, 128], BF16, s3)
                Bim = C.sb("Bim", [128, 64, 128], BF16, s3)
                segs = [(k * 1024, k * 1024 + 1, 1024) for k in range(4)] + [(4096, NLAT + 3, 256)]

                def conv_stream(ct, s, out_fn):
                    for (t0, u0, n) in segs:
                        S.act(ctmp[:, 0:n], ust[:, u0:u0 + n], AF.Identity, bias=convb[:, ct, s:s + 1],
                              scale=convw[:, ct, s, 1:2])
                        S.stt(ctmp[:, 0:n], ust[:, u0 - 1:u0 - 1 + n], convw[:, ct, s, 0:1], ctmp[:, 0:n],
                              ALU.mult, ALU.add)
                        out_fn(t0, n, ust[:, u0 + 1:u0 + 1 + n], convw[:, ct, s, 2:3], ctmp[:, 0:n])

                def ust_cols(blk):
                    t0, n = blk
                    return slice(t0 + 1, t0 + 1 + n) if t0 < NLAT else slice(NLAT + 3, NLAT + 3 + n)

                for ct in range(n_ct):
                    wbase = 10 + 4 * ct
                    S.dma(Xfl[:, :, :].rearrange("p a n -> p (a n)"), xf_l[ct, :, :], q="act")
                    w = load_w(wbase + 0)
                    for bi, blk in enumerate(blocks):
                        proj_fm(w, blk, PS[bi % 2])
                        S.act(ust[:, ust_cols(blk)], PS[bi % 2][:, 0:blk[1]], AF.Copy)
                    conv_stream(ct, 1, lambda t0, n, a, sc, b: S.stt(x1c[:, t0:t0 + n], a, sc, b, ALU.mult, ALU.add))
                    w = load_w(wbase + 1)
                    for bi, blk in enumerate(blocks):
                        proj_fm(w, blk, PS[bi % 2])
                        S.act(ust[:, ust_cols(blk)], PS[bi % 2][:, 0:blk[1]], AF.Copy)

                    def gfn(t0, n, a, sc, b):
                        S.stt(b, a, sc, b, ALU.mult, ALU.add)
                        S.tt(gT[:, t0:t0 + n], b, x1c[:, t0:t0 + n], ALU.mult, eng="pool")
                    conv_stream(ct, 2, gfn)
                    w = load_w(wbase + 3)
                    for bi, blk in enumerate(blocks):
                        proj_fm(w, blk, PS[bi % 2])
                        S.act(xg[:, blk[0]:blk[0] + blk[1]], PS[bi % 2][:, 0:blk[1]], AF.Silu)
                    w = load_w(wbase + 2)
                    for bi, blk in enumerate(blocks):
                        proj_fm(w, blk, PS[bi % 2])
                        S.act(ust[:, ust_cols(blk)], PS[bi % 2][:, 0:blk[1]], AF.Copy)

                    def xfn(t0, n, a, sc, b):
                        S.stt(b, a, sc, b, ALU.mult, ALU.add)
                        S.tt(xg[:, t0:t0 + n], b, xg[:, t0:t0 + n], ALU.mult, eng="pool")
                    conv_stream(ct, 0, xfn)
                    for n2_0 in range(0, 64, 8):
                        for j in range(8):
                            S.transpose(pTb[0:64, j * 128:(j + 1) * 128], gT[:, n2_0 + j:NLAT:64], identB[:])
                        ov = Xl[0:64, :, :].rearrange("p a (n c) -> p n a c", c=2)[:, n2_0:n2_0 + 8, :, :]
                        iv = pTb[0:64, :].rearrange("p (n a c) -> p n a c", n=8, c=2)
                        S.copy(ov, iv, eng="act")

                    def mk_evac(tok0, H, li):
                        def evac(po, n2_0, ng):
                            ov = gT[:, tok0:tok0 + 64 * H].rearrange("p (a n) -> p n a", n=64)[:, n2_0:n2_0 + ng, :]
                            xv = xg[:, tok0:tok0 + 64 * H].rearrange("p (a n) -> p n a", n=64)[:, n2_0:n2_0 + ng, :]
                            iv = po[:, 0:ng * H].rearrange("p (n a) -> p n a", a=H)
                            S.stt(ov, iv, rnT[:, ct, li:li + 1], xv, ALU.mult, ALU.mult)
                        return evac
                    fft_conv_core(S, C, PS, TL, 128, Xl, Xfl, Bre, Bim, ftmp, mk_evac(0, 64, 0))
                    S.dma(Xfl[0:8, :, :].rearrange("p a n -> p (a n)"), xf_c[ct, :, :], q="act")
                    for n2_0 in range(0, 64, 8):
                        for j in range(8):
                            S.transpose(pTb[0:4, j * 128:(j + 1) * 128], gT[:, NLAT + n2_0 + j:NTOK:64], identB[:])
                        ov = Xl[0:4, :, :].rearrange("p a (n c) -> p n a c", c=2)[:, n2_0:n2_0 + 8, :, :]
                        iv = pTb[0:4, :].rearrange("p (n a c) -> p n a c", n=8, c=2)
                        S.copy(ov, iv, eng="act")
                    fft_conv_core(S, C, PS, TC, 8, Xl, Xfl, Bre, Bim, ftmp, mk_evac(NLAT, 4, 1))
                    for (c0, cn, cap) in mix.pieces(ct * 128, (ct + 1) * 128):
                        outs.append(S.dma(cap, gT[:, c0:c0 + cn], q="sp"))
        if shared is None:
            S.emit(st, final_wait=outs)
    return nc


def gate_rows(S, C, PS, A, pre, scond, onesF, stk):
    rows = C.sb("grow", [2, 1024], F32, stk)
    bro = C.sb("gbro", [2, 1024], F32, stk); S.dma(bro[:], A[pre + "bgate2"])
    gpo = C.sb("ggpo", [2, 1024], F32, stk); S.dma(gpo[:], A[pre + "gpost2"])
    sel2 = C.sb("sel2", [2, 2, 128], F32, stk); S.dma(sel2[:], A["sel2"])
    wg = [C.sb(f"wgate{i}", [128, 1024], F32, stk) for i in range(2)]
    for kt in range(8):
        w = wg[kt % 2]
        S.dma(w[:], A[pre + "wgate"][kt], q="sp" if kt % 2 == 0 else "act")
        for nb in range(2):
            S.mm(PS[nb][0:2, :], scond[:, kt, 0:2], w[:, nb * 512:(nb + 1) * 512], start=(kt == 0), stop=(kt == 7))
    for nb in range(2):
        S.tt(rows[:, nb * 512:(nb + 1) * 512], PS[nb][0:2, :], bro[:, nb * 512:(nb + 1) * 512], ALU.add)
    S.tt(rows[:], rows[:], gpo[:], ALU.mult)
    gg = []
    for j in range(2):
        g = C.sb(f"gg{j}", [128, 1024], F32, stk)
        for nb in range(2):
            S.mm(PS[2 + nb][:, :], sel2[0:2, j, :], rows[0:2, nb * 512:(nb + 1) * 512])
            S.copy(g[:, nb * 512:(nb + 1) * 512], PS[2 + nb][:, :], eng="act")
        gg.append(g)
    return gg


def outproj_residual(S, C, PS, A, stk, mixg, wout_ap, x_in, tiles, gg, x_out, identF, post=None):
    wout = C.sb("wout", [128, 16, 1024], BF16, stk)
    for kt in range(16):
        S.dma(wout[:, kt, :], wout_ap[kt], q="pool")
    mts = [C.sb(f"mt{i}", [128, 16, 512], BF16, stk) for i in range(2)]
    xts = [C.sb(f"xo{i}", [128, 1024], F32, stk) for i in range(3)]
    junk = C.sb("junk2", [128, 512], BF16, stk)
    ss2 = C.sb("ss2", [128, 2], F32, stk)
    rs = C.sb("rs", [128, 1], F32, stk)
    tmp = C.sb("tmpo", [128, 1024], F32, stk)
    outs = []
    if not isinstance(mixg, Chunked):
        mixg = Chunked.wrap(mixg)
    cur = {"t0": None, "mt": None, "n": 0}
    for idx, (i, tok0, jj) in enumerate(tiles):
        g4 = tok0 // 512
        if cur["t0"] != g4:
            mt = mts[cur["n"] % 2]
            cur["n"] += 1
            ncol = min(512, mixg.ntok - g4 * 512)
            S.dma(mt[:, :, 0:ncol], mixg.cols(0, mixg.rows, g4 * 512, ncol).rearrange("(kt p) t -> p kt t", p=128), q="sp")
            cur["t0"], cur["mt"] = g4, mt
        mt = cur["mt"]
        c0 = tok0 - g4 * 512
        xt = xts[idx % 3]
        S.dma(xt[:], x_in[tok0:tok0 + 128, :], q="act")
        for nb in range(2):
            for kt in range(16):
                S.mm(PS[nb][:, :], mt[:, kt, c0:c0 + 128], wout[:, kt, nb * 512:(nb + 1) * 512],
                     start=(kt == 0), stop=(kt == 15))
            S.act(junk[:], PS[nb][:, :], AF.Square, accum_out=ss2[:, nb:nb + 1])
        S.tt(rs[:], ss2[:, 0:1], ss2[:, 1:2], ALU.add)
        S.act(rs[:], rs[:], AF.Sqrt, bias=EPS, scale=1.0 / D)
        S.recip(rs[:], rs[:])
        for nb in range(2):
            sl = slice(nb * 512, (nb + 1) * 512)
            S.stt(tmp[:, sl], PS[nb][:, :], rs[:, 0:1], gg[jj][:, sl], ALU.mult, ALU.mult)
        S.tt(xt[:], xt[:], tmp[:], ALU.add, eng="pool")
        if x_out is not None:
            outs.append(S.dma(x_out[i * 128:(i + 1) * 128, :], xt[:], q="sp"))
        if post is not None:
            post(i, xt, jj)
    return outs


def adaln_shift_scale(S, C, PS, A, pre, scond, stk):
    bmodT = C.sb("bmodT", [128, 16], F32, stk); S.dma(bmodT[:], A[pre + "bmodT"])
    gpreT = C.sb("gpreT", [128, 8], F32, stk); S.dma(gpreT[:], A[pre + "gpreT"])
    modT = C.sb("modT", [128, 16, 2], F32, stk)
    wm = [C.sb(f"wm{i}", [128, 8, 128], F32, stk) for i in range(2)]
    for fb in range(16):
        w = wm[fb % 2]
        S.dma(w[:], A[pre + "wmod"][fb], q="sp" if fb % 2 == 0 else "act")
        for kt in range(8):
            S.mm(PS[4 + fb % 2][:, 0:2], w[:, kt, :], scond[:, kt, :], start=(kt == 0), stop=(kt == 7))
        S.ts(modT[:, fb, :], PS[4 + fb % 2][:, 0:2], bmodT[:, fb:fb + 1], None, ALU.add)
    Amod = C.sb("Amod", [128, 8, 2], F32, stk)
    S.ts(Amod[:], modT[:, 8:16, :], 1.0, None, ALU.add)
    S.tt(Amod[:], Amod[:], gpreT[:, :].unsqueeze(2).broadcast_to([128, 8, 2]), ALU.mult)
    return Amod, modT


def p2_host_inputs(inp, b, hf, mix0g):
    f = lambda a: np.ascontiguousarray(a, dtype=np.float32)
    o = {}
    o["mix0g"] = np.ascontiguousarray(mix0g)
    o["x_all"] = f(np.concatenate([inp["x"][b], inp["ctx"][b]], 0))
    cond = np.stack([inp["c"][b], inp["c_ctx"]], -1)
    o["condT"] = f(cond.reshape(8, 128, 2).transpose(1, 0, 2))
    o["l0_wgate"] = f(inp["w_mod"][0][:, 2048:3072].reshape(8, 128, 1024))
    o["l0_bgate2"] = f(np.broadcast_to(inp["b_mod"][0][2048:3072][None], (2, 1024)))
    o["l0_gpost2"] = f(np.broadcast_to(inp["g_post"][0][None], (2, 1024)))
    sel2 = np.zeros((2, 2, 128), np.float32); sel2[0, 0] = 1; sel2[1, 1] = 1
    o["sel2"] = sel2
    wo = inp["e_w_out"][0]
    order = np.concatenate([np.arange(r * 512, r * 512 + 512) if part == 0 else np.arange(1024 + r * 512, 1024 + r * 512 + 512)
                            for r in range(2) for part in range(2)])
    o["wout0"] = f(wo[order].reshape(16, 128, 1024))
    wm = inp["w_mod"][1][:, :2048]
    o["l1_wmod"] = f(wm.reshape(8, 128, 16, 128).transpose(2, 1, 0, 3))
    o["l1_bmodT"] = f(inp["b_mod"][1][:2048].reshape(16, 128).T)
    o["l1_gpreT"] = f(inp["g_pre"][1].reshape(8, 128).T)
    W = inp["o_w_in"][0]
    heads = [4 * hf + h for h in range(4)]
    def tl(cols):
        return W[:, cols].reshape(8, 128, len(cols)).transpose(1, 0, 2)
    o["wq"] = f(np.stack([tl(np.arange(hd * 128, hd * 128 + 128)) for hd in heads]))
    o["wk"] = f(np.stack([tl(np.arange(1024 + hd * 128, 1024 + hd * 128 + 128)) for hd in heads]))
    o["wv"] = f(np.stack([tl(np.arange(2048 + hd * 256, 2048 + hd * 256 + 256)) for hd in heads]))
    o["woz"] = f(np.stack([tl(np.concatenate([np.arange(4096 + hd * 256, 4096 + hd * 256 + 256),
                                              np.arange(6144 + hd * 256, 6144 + hd * 256 + 256)])) for hd in heads]))
    gcols = np.array([8192 + g * 8 + hd for g in range(4) for hd in heads])
    o["wg"] = f(tl(gcols))
    cw = inp["o_conv_w"][0]; cb = inp["o_conv_b"][0]
    convw = np.zeros((128, 4, 2, 3), np.float32); convb = np.zeros((128, 4, 2), np.float32)
    for h, hd in enumerate(heads):
        for s in range(2):
            c0 = s * 1024 + hd * 128
            convw[:, h, s, :] = cw[:, c0:c0 + 128].T
            convb[:, h, s] = cb[c0:c0 + 128]
    o["convw1"] = convw; o["convb1"] = convb
    gb = inp["o_gate_b"][0].reshape(4, 8)[:, heads]
    o["gateb"] = f(gb.T)
    hn = inp["o_head_norm"][0].reshape(8, 256)[heads]
    o["hn"] = f(np.broadcast_to(hn[None], (128, 4, 256)))
    t = np.arange(128)
    same = (t[:, None] // 64) == (t[None, :] // 64)
    o["maskF"] = f((same & (t[:, None] <= t[None, :])))
    o["maskB"] = f((same & (t[:, None] >= t[None, :])))
    sel4 = np.zeros((4, 4, 128), np.float32)
    for h in range(4):
        sel4[h, h] = 1
    o["sel4"] = sel4
    o["identF"] = np.eye(128, dtype=np.float32)
    return o


def build_p2(nc, sample, n_heads=4, stop=None, shared=None):
    if shared is None:
        A = _declare_inputs(nc, sample)
        x1o = nc.dram_tensor("x1o", [NTOK, D], F32, kind="ExternalOutput").ap()
        mix1 = Chunked.wrap(nc.dram_tensor("mix1", [1024, NLAT], BF16, kind="ExternalOutput").ap())
    else:
        A = shared["A"]
        x1o = shared["x1o"]
        mix1 = shared["mix1"]
    outs = []
    NCH = NTOK // 64
    with ExitStack() as st:
        if shared is None:
            S = Sched(nc)
            C = Ctx(nc, st)
            PS = [C.ps(f"pb{i}", [128, 512], F32) for i in range(7)]
            pTb = C.ps("pTb", [128, 1024], BF16)
        else:
            S, C, PS, pTb = shared["S"], shared["C"], shared["PS"], shared["pTb"]
            C.stack = st
        identF = C.sb("identF", [128, 128], F32); S.dma(identF[:], A["identF"])
        identB = C.sb("identB", [128, 128], BF16); S.dma(identB[:], A["identF"], q="pool")
        onesF = C.sb("onesF", [128, 128], F32); S.memset(onesF[:], 1.0)
        condT = C.sb("condT", [128, 8, 2], F32); S.dma(condT[:], A["condT"])
        scond = C.sb("scond", [128, 8, 2], F32)
        S.act(scond[:], condT[:], AF.Silu)
        h1T = C.sb("h1T", [128, 8, NTOK], BF16)
        tokq = C.sb("tokq", [128, 34, 24], F32)
        ebB = C.sb("ebB", [128, 2, 4, NCH], F32)
        with ExitStack() as sa:
            Amod, modT = adaln_shift_scale(S, C, PS, A, "l1_", scond, sa)
            gg = gate_rows(S, C, PS, A, "l0_", scond, onesF, sa)
            ssq = C.sb("ssq", [128, 1], F32, sa)
            rstd = C.sb("rstd", [128, 1], F32, sa)
            junk = C.sb("junk", [128, 1024], BF16, sa)
            xh = C.sb("xh", [128, 1024], F32, sa)

            def post(i, xt, jj):
                S.act(junk[:], xt[:], AF.Square, accum_out=ssq[:, 0:1])
                S.act(rstd[:], ssq[:], AF.Sqrt, bias=EPS, scale=1.0 / D)
                S.recip(rstd[:], rstd[:])
                S.ts(xh[:], xt[:], rstd[:, 0:1], None, ALU.mult)
                for half in range(2):
                    pt = PS[2 + half]
                    for k4 in range(4):
                        kt = half * 4 + k4
                        S.transpose(pt[:, k4 * 128:(k4 + 1) * 128], xh[:, kt * 128:(kt + 1) * 128], identF[:])
                    for k4 in range(4):
                        kt = half * 4 + k4
                        S.ts(h1T[:, kt, i * 128:(i + 1) * 128], pt[:, k4 * 128:(k4 + 1) * 128],
                             Amod[:, kt, jj:jj + 1], modT[:, kt, jj:jj + 1], ALU.mult, ALU.add)
            tiles = [(i, i * 128, 0 if i < 32 else 1) for i in range(34)]
            outs += outproj_residual(S, C, PS, A, sa, A["mix0g"], A["wout0"], A["x_all"], tiles, gg, x1o, identF, post)
        S.fence()
        if stop == "A":
            dbg = nc.dram_tensor("h1To", [128, 8, NTOK], BF16, kind="ExternalOutput").ap()
            outs.append(S.dma(dbg, h1T[:]))
            S.emit(st, final_wait=outs)
            return nc
        blocks = [(tb * 512, 512) for tb in range(8)] + [(4096, 256)]
        with ExitStack() as sb_:
            wg = C.sb("wg", [128, 8, 16], BF16, sb_); S.dma(wg[:], A["wg"], q="pool")
            gateb = C.sb("gateb", [4, 4], F32, sb_); S.dma(gateb[:], A["gateb"])
            sel4 = C.sb("sel4", [4, 4, 128], F32, sb_); S.dma(sel4[:], A["sel4"])
            gi = C.sb("gi", [4, NTOK], F32, sb_)
            gf = C.sb("gf", [4, NTOK], F32, sb_)
            cumB = C.sb("cumB", [4, NTOK], F32, sb_)
            e1p = C.sb("e1p", [4, NTOK], F32, sb_)
            ebr = C.sb("ebr", [4, NCH], F32, sb_)
            v3 = lambda t_: t_[:, :].rearrange("p (c j) -> p c j", j=64)
            for d in range(2):
                for gq_, dstg in ((2 * d, gi), (2 * d + 1, gf)):
                    for bi, (t0, n) in enumerate(blocks):
                        pt = PS[bi % 2]
                        for kt in range(8):
                            S.mm(pt[0:4, 0:n], wg[:, kt, gq_ * 4:(gq_ + 1) * 4], h1T[:, kt, t0:t0 + n], start=(kt == 0), stop=(kt == 7))
                        S.ts(dstg[:, t0:t0 + n], pt[0:4, 0:n], gateb[:, gq_:gq_ + 1], None, ALU.add)
                S.act(gf[:], gf[:], AF.Exp, scale=-1.0)
                S.act(gf[:], gf[:], AF.Ln, bias=1.0)
                src, dst = gf, cumB
                for k in range(6):
                    sh = 1 << k
                    s3, d3 = v3(src), v3(dst)
                    if d == 0:
                        S.tt(d3[:, :, sh:64], s3[:, :, sh:64], s3[:, :, 0:64 - sh], ALU.add)
                        S.copy(d3[:, :, 0:sh], s3[:, :, 0:sh], eng="pool")
                    else:
                        S.tt(d3[:, :, 0:64 - sh], s3[:, :, 0:64 - sh], s3[:, :, sh:64], ALU.add)
                        S.copy(d3[:, :, 64 - sh:64], s3[:, :, 64 - sh:64], eng="pool")
                    src, dst = dst, src
                cum, thr = src, dst
                e1 = gi
                endpos = 63 if d == 0 else 0
                S.act(ebr[:], v3(cum)[:, :, endpos], AF.Exp, scale=-1.0)
                S.tt(e1[:], gi[:], cum[:], ALU.add)
                S.act(e1[:], e1[:], AF.Exp)
                S.ts(e1[:], e1[:], 128.0 ** -0.5, None, ALU.mult)
                S.act(thr[:], cum[:], AF.Exp)
                S.tt(v3(e1p), v3(e1), ebr[:, :].unsqueeze(2).broadcast_to([4, NCH, 64]), ALU.mult)
                for h in range(4):
                    S.mm(PS[2][:, 0:NCH], sel4[0:4, h, :], ebr[0:4, :])
                    S.copy(ebB[:, d, h, :], PS[2][:, 0:NCH], eng="act")
                for i in range(34):
                    pt = PS[3 + i % 2]
                    for qi, src_t in enumerate((e1, e1p, thr)):
                        S.transpose(pt[:, qi * 4:(qi + 1) * 4], src_t[0:4, i * 128:(i + 1) * 128], identF[0:4, 0:4])
                    S.copy(tokq[:, i, d * 12:(d + 1) * 12], pt[:, 0:12], eng="act")
        S.fence()
        if stop == "B":
            dbg = nc.dram_tensor("tokqo", [128, 34, 24], F32, kind="ExternalOutput").ap()
            outs.append(S.dma(dbg, tokq[:]))
            dbg2 = nc.dram_tensor("ebBo", [128, 2, 4, NCH], F32, kind="ExternalOutput").ap()
            outs.append(S.dma(dbg2, ebB[:]))
            S.emit(st, final_wait=outs)
            return nc
        with ExitStack() as sc:
            convw = C.sb("convw1", [128, 4, 2, 3], F32, sc); S.dma(convw[:], A["convw1"])
            convb = C.sb("convb1", [128, 4, 2], F32, sc); S.dma(convb[:], A["convb1"])
            hn = C.sb("hn", [128, 256], F32, sc)
            maskF = C.sb("maskF", [128, 128], F32, sc); S.dma(maskF[:], A["maskF"])
            maskB = C.sb("maskB", [128, 128], F32, sc); S.dma(maskB[:], A["maskB"])
            masks = (maskF, maskB)
            wbuf = C.sb("wbuf", [128, 8, 512], BF16, sc)
            wq = wbuf[:, :, 0:128]
            wk = wbuf[:, :, 128:256]
            wv = wbuf[:, :, 256:512]
            woz = wbuf
            SEG = NLAT + 2 + NCTX + 2
            ust = C.sb("ust", [128, SEG], BF16, sc); S.memset(ust[:], 0.0)
            ctmp = C.sb("ctmp", [128, 1024], F32, sc)
            qT = C.sb("qT", [128, NTOK], BF16, sc)
            kT = C.sb("kT", [128, NTOK], BF16, sc)
            ktD = [C.sb(f"ktD{d}", [128, 34, 128], BF16, sc) for d in range(2)]
            vaug = C.sb("vaug", [128, 34, 257], BF16, sc)
            S.memset(vaug[:, :, 256:257], 1.0)
            hsum = C.sb("hsum", [128, 32, 256], F32, sc)
            Cst = [C.sb(f"Cst{d}", [128, 257], F32, sc) for d in range(2)]
            Cbf = [C.sb(f"Cbf{d}", [128, 257], BF16, sc) for d in range(2)]
            Sm = [C.sb(f"Sm{d}", [128, 128], BF16, sc) for d in range(2)]
            dm = [C.sb(f"dm{d}", [128, 1], F32, sc) for d in range(2)]
            so = C.sb("so", [128, 256], F32, sc)
            sz = C.sb("sz", [128, 256], F32, sc)
            hh = C.sb("hh", [128, 256], F32, sc)
            junk3 = C.sb("junk3", [128, 256], BF16, sc)
            ssq3 = C.sb("ssq3", [128, 1], F32, sc)
            mixv = C.sb("mixv", [128, 256], BF16, sc)
            mob = [C.sb(f"mob{i}", [128, 2, 512], BF16, sc) for i in range(2)]
            segs = [(k * 1024, k * 1024 + 1, 1024) for k in range(4)] + [(4096, NLAT + 3, 256)]

            def ust_cols(blk):
                t0, n = blk
                return slice(t0 + 1, t0 + 1 + n) if t0 < NLAT else slice(NLAT + 3, NLAT + 3 + n)

            for h in range(n_heads):
                S.dma(wq, A["wq"][h], q="pool")
                S.dma(wk, A["wk"][h], q="pool")
                S.dma(wv, A["wv"][h], q="pool")
                S.dma(hn[:], A["hn"][:, h, :])
                S.memset(hsum[:], 0.0, eng="pool")
                for (w, s, dstT) in ((wq, 0, qT), (wk, 1, kT)):
                    for bi, blk in enumerate(blocks):
                        t0, n = blk
                        pt = PS[bi % 2]
                        for kt in range(8):
                            S.mm(pt[:, 0:n], w[:, kt, :], h1T[:, kt, t0:t0 + n], start=(kt == 0), stop=(kt == 7))
                        S.act(ust[:, ust_cols(blk)], pt[:, 0:n], AF.Copy)
                    for (t0, u0, n) in segs:
                        S.act(ctmp[:, 0:n], ust[:, u0:u0 + n], AF.Identity, bias=convb[:, h, s:s + 1], scale=convw[:, h, s, 1:2])
                        S.stt(ctmp[:, 0:n], ust[:, u0 - 1:u0 - 1 + n], convw[:, h, s, 0:1], ctmp[:, 0:n], ALU.mult, ALU.add)
                        S.stt(ctmp[:, 0:n], ust[:, u0 + 1:u0 + 1 + n], convw[:, h, s, 2:3], ctmp[:, 0:n], ALU.mult, ALU.add)
                        S.act(dstT[:, t0:t0 + n], ctmp[:, 0:n], AF.Silu)
                for i4 in range(0, 34, 8):
                    nt = min(8, 34 - i4)
                    for j in range(nt):
                        i = i4 + j
                        S.transpose(pTb[:, j * 128:(j + 1) * 128], kT[:, i * 128:(i + 1) * 128], identB[:])
                    for j in range(nt):
                        i = i4 + j
                        for d in range(2):
                            S.act(ktD[d][:, i, :], pTb[:, j * 128:(j + 1) * 128], AF.Identity, scale=tokq[:, i, d * 12 + 4 + h:d * 12 + 5 + h])
                for i in range(34):
                    pt = PS[i % 2]
                    for kt in range(8):
                        S.mm(pt[:, 0:256], h1T[:, kt, i * 128:(i + 1) * 128], wbuf[:, kt, 256:512], start=(kt == 0), stop=(kt == 7))
                    S.act(vaug[:, i, 0:256], pt[:, 0:256], AF.Copy)
                for d in range(2):
                    S.memset(Cst[d][:], 0.0)
                    S.memset(Cbf[d][:], 0.0, eng="pool")
                orderF = [32, 33] + list(range(32))
                orderB = [33, 32] + list(range(31, -1, -1))
                for step in range(34):
                    for d, tile_i in ((0, orderF[step]), (1, orderB[step])):
                        i = tile_i
                        is_lat = i < 32
                        pS, pN, pU = PS[3 * d], PS[3 * d + 1], PS[3 * d + 2]
                        tk = slice(i * 128, (i + 1) * 128)
                        chunks = (0, 1) if d == 0 else (1, 0)
                        if is_lat:
                            S.mm(pS[:, 0:128], kT[:, tk], qT[:, tk])
                            S.stt(Sm[d][:], pS[:, 0:128], tokq[:, i, d * 12 + h:d * 12 + h + 1], masks[d][:], ALU.mult, ALU.mult)
                            S.mm(pN[:, 0:257], Sm[d][:], vaug[:, i, :], start=True, stop=False)
                        for ci, c in enumerate(chunks):
                            rows = slice(c * 64, (c + 1) * 64)
                            ch = i * 2 + c
                            if is_lat:
                                S.mm(pN[rows, 0:257], qT[:, i * 128 + c * 64:i * 128 + (c + 1) * 64], Cbf[d][:, :],
                                     start=False, stop=True)
                            S.mm(pU[:, 0:257], ktD[d][rows, i, :], vaug[rows, i, :])
                            S.stt(Cst[d][:], Cst[d][:], ebB[:, d, h, ch:ch + 1], pU[:, 0:257], ALU.mult, ALU.add)
                            S.act(Cbf[d][:], Cst[d][:], AF.Copy)
                        if is_lat:
                            S.act(dm[d][:], pN[:, 256:257], AF.Abs)
                            S.ts(dm[d][:], dm[d][:], tokq[:, i, d * 12 + 8 + h:d * 12 + 9 + h], None, ALU.max)
                            S.recip(dm[d][:], dm[d][:])
                            S.stt(hsum[:, i, :], pN[:, 0:256], dm[d][:, 0:1], hsum[:, i, :], ALU.mult, ALU.add)
                S.dma(woz[:], A["woz"][h], q="pool")
                for i in range(32):
                    pt = PS[6]
                    for kt in range(8):
                        S.mm(pt[:, :], h1T[:, kt, i * 128:(i + 1) * 128], woz[:, kt, :], start=(kt == 0), stop=(kt == 7))
                    S.act(so[:], pt[:, 0:256], AF.Sigmoid)
                    S.act(sz[:], pt[:, 256:512], AF.Silu)
                    S.tt(hh[:], hsum[:, i, :], so[:], ALU.mult)
                    S.act(junk3[:], hh[:], AF.Square, accum_out=ssq3[:, 0:1])
                    S.act(ssq3[:], ssq3[:], AF.Sqrt, bias=EPS, scale=1.0 / 256)
                    S.recip(ssq3[:], ssq3[:])
                    S.tt(sz[:], sz[:], hn[:], ALU.mult, eng="pool")
                    S.stt(mixv[:], hh[:], ssq3[:, 0:1], sz[:], ALU.mult, ALU.mult)
                    mo_ = mob[(i // 4) % 2]
                    for j in range(2):
                        S.transpose(pTb[:, j * 128:(j + 1) * 128], mixv[:, j * 128:(j + 1) * 128], identB[:])
                    S.copy(mo_[:, :, (i % 4) * 128:(i % 4 + 1) * 128],
                           pTb[:, 0:256].rearrange("p (j t) -> p j t", j=2), eng="act")
                    if i % 4 == 3:
                        t0 = (i // 4) * 512
                        dst = mix1.cols(h * 256, (h + 1) * 256, t0, 512).rearrange("(j p) t -> p j t", p=128)
                        outs.append(S.dma(dst, mo_[:, :, :], q="sp"))
        if shared is None:
            S.emit(st, final_wait=outs)
    return nc


def p3_host_inputs(inp, b, hf, mix1g, x1):
    f = lambda a: np.ascontiguousarray(a, dtype=np.float32)
    o = {}
    o["mix1g"] = np.ascontiguousarray(mix1g[:, hf * 2048:(hf + 1) * 2048])
    o["x1loc"] = f(x1[hf * 2048:(hf + 1) * 2048])
    cond = np.stack([inp["c"][b], inp["c_ctx"]], -1)
    o["condT"] = f(cond.reshape(8, 128, 2).transpose(1, 0, 2))
    o["l1_wgate"] = f(inp["w_mod"][1][:, 2048:3072].reshape(8, 128, 1024))
    o["l1_bgate2"] = f(np.broadcast_to(inp["b_mod"][1][2048:3072][None], (2, 1024)))
    o["l1_gpost2"] = f(np.broadcast_to(inp["g_post"][1][None], (2, 1024)))
    sel2 = np.zeros((2, 2, 128), np.float32); sel2[0, 0] = 1; sel2[1, 1] = 1
    o["sel2"] = sel2
    o["wout1"] = f(inp["o_w_out"][0].reshape(16, 128, 1024))
    o["identF"] = np.eye(128, dtype=np.float32)
    return o


def build_p3(nc, sample, shared=None):
    if shared is None:
        A = _declare_inputs(nc, sample)
        yo = nc.dram_tensor("yo", [2048, D], F32, kind="ExternalOutput").ap()
        ntile = 16
    else:
        A = shared["A"]
        yo = shared["yo"]
        ntile = 32
    with ExitStack() as st:
        if shared is None:
            S = Sched(nc)
            C = Ctx(nc, st)
            PS = [C.ps(f"pb{i}", [128, 512], F32) for i in range(7)]
        else:
            S, C, PS = shared["S"], shared["C"], shared["PS"]
            C.stack = st
        identF = C.sb("identF", [128, 128], F32); S.dma(identF[:], A["identF"])
        onesF = C.sb("onesF", [128, 128], F32); S.memset(onesF[:], 1.0)
        condT = C.sb("condT", [128, 8, 2], F32); S.dma(condT[:], A["condT"])
        scond = C.sb("scond", [128, 8, 2], F32)
        S.act(scond[:], condT[:], AF.Silu)
        gg = gate_rows(S, C, PS, A, "l1_", scond, onesF, st)
        tiles = [(i, i * 128, 0) for i in range(ntile)]
        outs = outproj_residual(S, C, PS, A, st, A["mix1g"], A["wout1"], A["x1loc"], tiles, gg, yo, identF, None)
        if shared is None:
            S.emit(st, final_wait=outs)
        else:
            shared["outs"] += outs
    return nc


CORES = [(b, hf) for b in range(4) for hf in range(2)]


def _launch(build, maps, **kw):
    nc = bass.Bass("TRN2", target_bir_lowering=False)
    build(nc, maps[0], **kw)
    res = run_bass_kernel_spmd(nc, maps, core_ids=list(range(len(maps))))
    return res.results


GROUPS = [[0, 1], [2, 3], [4, 5], [6, 7]]


def fused_host_inputs(inp, b, hf):
    o = {}
    dummy_mix0 = np.zeros((2048, NTOK), NPBF)
    p2 = p2_host_inputs(inp, b, hf, dummy_mix0)
    p3 = p3_host_inputs(inp, b, hf, np.zeros((2048, NLAT), NPBF), np.zeros((NLAT, D), np.float32))
    for d in (p3, p2, p1_host_inputs(inp, b, hf)):
        o.update(d)
    for k in ("mix0g", "mix1g", "x1loc"):
        o.pop(k)
    return o


def build_fused(nc, sample):
    A = _declare_inputs(nc, sample)
    yo = nc.dram_tensor("yo", [NLAT, D], F32, kind="ExternalOutput").ap()
    CS = 1024
    mix0 = Chunked.make(nc, "mix0", 1024, NTOK, BF16, CS, kind="Internal")
    mix0g = Chunked.make(nc, "mix0g", 2048, NTOK, BF16, CS, kind="Internal", addr_space="Local")
    x1o = nc.dram_tensor("x1o", [NTOK, D], F32, kind="Internal").ap()
    mix1 = Chunked.make(nc, "mix1", 1024, NLAT, BF16, CS, kind="Internal")
    mix1g = Chunked.make(nc, "mix1g", 2048, NLAT, BF16, CS, kind="Internal", addr_space="Local")
    A["mix0g"] = mix0g
    A["mix1g"] = mix1g
    A["x1loc"] = x1o[0:NLAT, :]
    with ExitStack() as st:
        S = Sched(nc)
        C = Ctx(nc, st)
        PS = [C.ps(f"pb{i}", [128, 512], F32) for i in range(7)]
        pTb = C.ps("pTb", [128, 1024], BF16)
        sh = dict(S=S, C=C, PS=PS, pTb=pTb, A=A, mix0=mix0, x1o=x1o, mix1=mix1, yo=yo, outs=[])
        import os
        nocc = os.environ.get("K_FUSE_NOCC") == "1"
        light = os.environ.get("K_P1_LIGHT") == "1"

        def exchange(gch, lch):
            for (_, _, go), (_, _, gi_) in zip(gch.chunks, lch.chunks):
                if nocc:
                    S.dma(go[0:1024, :], gi_, q="sp")
                    S.dma(go[1024:2048, :], gi_, q="act")
                else:
                    S.allgather(go, gi_, GROUPS)
        if light:
            build_p1(nc, sample, shared=sh, n_ct=1, do_att=False)
        else:
            build_p1(nc, sample, shared=sh)
        S.fence()
        exchange(mix0g, mix0)
        build_p2(nc, sample, shared=sh, n_heads=(1 if light else 4))
        S.fence()
        exchange(mix1g, mix1)
        build_p3(nc, sample, shared=sh)
        C.stack = st
        S.emit(st, final_wait=sh["outs"])
        print("fused stats", S.stats, flush=True)
    return nc


def kernel(**inputs):
    inp = {k: np.asarray(v) for k, v in inputs.items()}
    res = _launch(build_fused, [fused_host_inputs(inp, b, hf) for (b, hf) in CORES])
    out = np.zeros((4, NLAT, D), np.float32)
    for ci, (b, hf) in enumerate(CORES):
        out[b, hf * 2048:(hf + 1) * 2048] = np.asarray(res[ci]["yo"])[hf * 2048:(hf + 1) * 2048]
    return out
```

```python
import math
from contextlib import ExitStack
import numpy as np
import ml_dtypes
import concourse.bass as bass
import concourse.mybir as mybir
from concourse.bass_utils import run_bass_kernel_spmd

F32 = mybir.dt.float32
BF16 = mybir.dt.bfloat16
AF = mybir.ActivationFunctionType
ALU = mybir.AluOpType
AX = mybir.AxisListType
NPBF = ml_dtypes.bfloat16

D = 1024
NLAT = 4096
NCTX = 256
NTOK = NLAT + NCTX
EPS = 1e-6

ENGS = ("pe", "act", "dve", "pool", "sp")
SEM_ROT = 12000
ND = 8
CC_INC = 1
_DT_SIZE = {}


def _dsize(dt):
    if dt not in _DT_SIZE:
        _DT_SIZE[dt] = mybir.dt.size(dt)
    return _DT_SIZE[dt]


def _box(ap):
    t = ap.tensor
    dims = list(ap.ap)
    off = ap.offset
    sp = str(ap.space)
    if sp in ("SB", "PSUM"):
        pstep = 1
        for s in t.shape[1:]:
            pstep *= s
        p0 = off // pstep
        f0 = off % pstep
        pd = dims[0]
        npart = 1 if pd[0] == 0 else pd[1]
        ext = 0
        for st, cnt in dims[1:]:
            ext += (cnt - 1) * abs(st)
        if sp == "PSUM":
            return (t.name, 0, 128, 0, pstep)
        return (t.name, p0, p0 + npart, f0, f0 + ext + 1)
    ext = 0
    for st, cnt in dims:
        ext += (cnt - 1) * abs(st)
    return (t.name, 0, 1, off, off + ext + 1)


class Sched:
    def __init__(self, nc, same_engine_sync=None):
        import os
        if same_engine_sync is None:
            same_engine_sync = os.environ.get('K_SAME', '1') == '1'
        self.nc = nc
        self.ins = []
        self.track = {}
        self.same = same_engine_sync
        self.last_cp = {e: None for e in ENGS}
        self.last_dm = {e: [] for e in ENGS}
        self.pending = {e: set() for e in ENGS}

    def _deps(self, reads, writes, idx):
        deps = set()
        rb = [_box(a) for a in reads]
        wb = [_box(a) for a in writes]
        for b in rb:
            for ent in self.track.get(b[0], ()):
                e = ent[0]
                if e[1] < b[2] and b[1] < e[2] and e[3] < b[4] and b[3] < e[4]:
                    if ent[1] is not None:
                        deps.add(ent[1])
        for b in wb:
            for ent in self.track.get(b[0], ()):
                e = ent[0]
                if e[1] < b[2] and b[1] < e[2] and e[3] < b[4] and b[3] < e[4]:
                    if ent[1] is not None:
                        deps.add(ent[1])
                    deps.update(ent[2])
        for b in rb:
            lst = self.track.setdefault(b[0], [])
            for ent in lst:
                if ent[0] == b:
                    ent[2].append(idx)
                    break
            else:
                lst.append([b, None, [idx]])
        for b in wb:
            lst = self.track.get(b[0], [])
            keep = []
            for ent in lst:
                e = ent[0]
                if b[1] <= e[1] and e[2] <= b[2] and b[3] <= e[3] and e[4] <= b[4]:
                    continue
                keep.append(ent)
            keep.append([b, idx, []])
            self.track[b[0]] = keep
        deps.discard(idx)
        return deps

    def add(self, eng, fn, r=(), w=(), dma=False):
        idx = len(self.ins)
        deps = self._deps(list(r), list(w), idx)
        if self.pending[eng]:
            deps |= self.pending[eng]
            self.pending[eng] = set()
        self.ins.append(dict(eng=eng, fn=fn, deps=deps, dma=dma, users=set()))
        if dma:
            lo = self.last_dm[eng]
            lo.append(idx)
            if len(lo) > ND:
                lo.pop(0)
        else:
            self.last_cp[eng] = idx
        return idx

    def fence(self):
        allp = set()
        for e in ENGS:
            allp.update(self.last_dm[e])
            if self.last_cp[e] is not None:
                allp.add(self.last_cp[e])
        for e in ENGS:
            self.pending[e] = set(allp) | self.pending[e]
        self.track = {}

    def mm(self, out, lhsT, rhs, start=True, stop=True, **kw):
        r = [lhsT, rhs]
        if not start:
            r.append(out)
        return self.add("pe", lambda e: e.matmul(out, lhsT, rhs, start=start, stop=stop, **kw), r=r, w=[out])

    def transpose(self, out, in_, ident):
        return self.add("pe", lambda e: e.transpose(out, in_, ident), r=[in_, ident], w=[out])

    def act(self, out, in_, func, bias=None, scale=None, accum_out=None):
        r = [in_]
        kw = {}
        if bias is not None:
            kw["bias"] = bias
            if not isinstance(bias, (int, float)):
                r.append(bias)
        if scale is not None:
            kw["scale"] = scale
            if not isinstance(scale, (int, float)):
                r.append(scale)
        w = [out]
        if accum_out is not None:
            kw["accum_out"] = accum_out
            w.append(accum_out)
        return self.add("act", lambda e: e.activation(out, in_, func, **kw), r=r, w=w)

    def tt(self, out, in0, in1, op, eng="dve"):
        return self.add(eng, lambda e: e.tensor_tensor(out, in0, in1, op), r=[in0, in1], w=[out])

    def ts(self, out, in0, s1, s2, op0, op1=None, eng="dve"):
        r = [in0]
        if not isinstance(s1, (int, float)):
            r.append(s1)
        if s2 is not None and not isinstance(s2, (int, float)):
            r.append(s2)
        if op1 is None:
            return self.add(eng, lambda e: e.tensor_scalar(out, in0, s1, None, op0), r=r, w=[out])
        return self.add(eng, lambda e: e.tensor_scalar(out, in0, s1, s2, op0, op1), r=r, w=[out])

    def stt(self, out, in0, scalar, in1, op0, op1):
        r = [in0, in1]
        if not isinstance(scalar, (int, float)):
            r.append(scalar)
        return self.add("dve", lambda e: e.scalar_tensor_tensor(out, in0, scalar, in1, op0, op1), r=r, w=[out])

    def copy(self, out, in_, eng="dve"):
        if eng == "act":
            return self.act(out, in_, AF.Copy)
        return self.add(eng, lambda e: e.tensor_copy(out, in_), r=[in_], w=[out])

    def memset(self, ap, val, eng="dve"):
        return self.add(eng, lambda e: e.memset(ap, val), r=[], w=[ap])

    def recip(self, out, in_):
        return self.add("dve", lambda e: e.reciprocal(out, in_), r=[in_], w=[out])

    def reduce(self, out, in_, op, axis=AX.X):
        return self.add("dve", lambda e: e.tensor_reduce(out, in_, axis, op), r=[in_], w=[out])

    def wrap(self, out, in_, t1, t2):
        self.ts(t1, in_, -math.pi, 2 * math.pi, ALU.is_lt, ALU.mult)
        self.ts(t2, in_, math.pi, -2 * math.pi, ALU.is_gt, ALU.mult)
        self.tt(t1, t1, t2, ALU.add, eng="pool")
        return self.tt(out, in_, t1, ALU.add)

    def allgather(self, out, in_, groups):
        idx = self.add("pool", lambda e: e.collective_compute("AllGather", ALU.bypass, replica_groups=groups,
                                                              ins=[in_], outs=[out]), r=[in_], w=[out], dma=True)
        self.ins[idx]["cc"] = True
        return idx

    def dma(self, out, in_, q="sp", **kw):
        return self.add(q, lambda e: e.dma_start(out, in_, **kw), r=[in_], w=[out], dma=True)

    def emit(self, stack, final_wait=()):
        nc = self.nc
        ins = self.ins
        for i, it in enumerate(ins):
            for d in it["deps"]:
                ins[d]["users"].add(i)

        def new_sem(tag):
            return stack.enter_context(nc.semaphore(tag))

        eng_sem = {e: [new_sem(f"s_{e}_0")] for e in ENGS}
        eng_cnt = {e: 0 for e in ENGS}
        dma_sems = {e: [new_sem(f"d_{e}_{k}") for k in range(ND)] for e in ("sp", "pool", "act")}
        dma_cnt = {e: 0 for e in ENGS}
        for i, it in enumerate(ins):
            e = it["eng"]
            it["sig"] = None
            it["pre"] = []
            if it.get("cc"):
                s = new_sem(f"cc_{i}")
                it["sig"] = (s, CC_INC, CC_INC)
            elif it["dma"]:
                k = dma_cnt[e]
                s = dma_sems[e][k % ND]
                it["sig"] = (s, 16 * (k // ND + 1), 16)
                if k >= ND:
                    it["pre"].append((s, 16 * (k // ND)))
                dma_cnt[e] = k + 1
            else:
                need = False
                for u in it["users"]:
                    ue = ins[u]["eng"]
                    if ue != e or ins[u]["dma"]:
                        need = True
                    elif self.same and e != "pe":
                        need = True
                if need:
                    if eng_cnt[e] >= SEM_ROT:
                        eng_sem[e].append(new_sem(f"s_{e}_{len(eng_sem[e])}"))
                        eng_cnt[e] = 0
                    eng_cnt[e] += 1
                    it["sig"] = (eng_sem[e][-1], eng_cnt[e], 1)
        self.stats = dict(n_ins=len(ins), sig={e: (len(eng_sem[e]) - 1) * SEM_ROT + eng_cnt[e] for e in ENGS},
                          dma=dict(dma_cnt), per_eng={e: sum(1 for it in ins if it["eng"] == e) for e in ENGS})
        waited = {e: {} for e in ENGS}
        streams = {e: [] for e in ENGS}
        for i, it in enumerate(ins):
            e = it["eng"]
            waits = list(it["pre"])
            for d in sorted(it["deps"]):
                pd = ins[d]
                if pd["sig"] is None:
                    continue
                if pd["eng"] == e and not pd["dma"] and (e == "pe" or not self.same):
                    continue
                waits.append((pd["sig"][0], pd["sig"][1]))
            ww = []
            for s, v in waits:
                key = id(s)
                if waited[e].get(key, 0) >= v:
                    continue
                waited[e][key] = v
                ww.append((s, v))
            streams[e].append((ww, it))
        fw = [(ins[d]["sig"][0], ins[d]["sig"][1]) for d in final_wait]
        self.n_ins = len(ins)

        def run(handle, lst, extra=()):
            for ww, it in lst:
                for s, v in ww:
                    handle.wait_ge(s, v)
                inst = it["fn"](handle)
                if it["sig"] is not None:
                    inst.then_inc(it["sig"][0], it["sig"][2])
            for s, v in extra:
                handle.wait_ge(s, v)

        block = stack.enter_context(nc.Block())

        @block.tensor
        def _(e):
            run(e, streams["pe"])

        @block.scalar
        def _(e):
            run(e, streams["act"])

        @block.vector
        def _(e):
            run(e, streams["dve"])

        @block.gpsimd
        def _(e):
            run(e, streams["pool"])

        @block.sync
        def _(e):
            run(e, streams["sp"], fw)


def fft_tables(N1):
    N = 64 * N1
    H = N1 // 2
    n1 = np.arange(N1)
    k1 = np.arange(N1)
    a1 = 2 * np.pi * np.outer(n1, k1) / N1
    CS1 = np.concatenate([np.cos(a1), -np.sin(a1)], 1)
    m = np.arange(128)
    n2m = m // 2
    c2m = m % 2
    aT = 2 * np.pi * np.outer(n2m, k1) / N
    a64 = 2 * np.pi * np.outer(n2m, n2m) / 64
    same = (c2m[:, None] == c2m[None, :])
    BdC = np.cos(a64) * same
    BdS = np.sin(a64) * same
    aI = 2 * np.pi * np.outer(k1, np.arange(64)) / N
    ai = 2 * np.pi * np.outer(k1, np.arange(H)) / N1
    f = lambda a: np.ascontiguousarray(a, dtype=np.float32)
    return dict(CS1=f(CS1), TwFc=f(np.cos(aT)), TwFs=f(np.sin(aT)), BdC=f(BdC), BdS=f(BdS), BdSn=f(-BdS),
                R1=f(np.concatenate([BdC, BdS], 1)), R2=f(np.concatenate([-BdS, BdC], 1)),
                TwIc=f(np.cos(aI)), TwIs=f(np.sin(aI)), Ci=f(np.cos(ai) / N), Sin_=f(-np.sin(ai) / N))


def filter_pos_tables(n):
    N = 2 * n
    N1 = N // 64
    t = np.linspace(0.0, 1.0, n, dtype=np.float32)
    bands = np.linspace(1e-4, 15.0, 16, dtype=np.float32)
    ang = (np.float32(2.0 * math.pi / n) * np.arange(n, dtype=np.float32)[:, None] * bands).astype(np.float32)
    z = np.concatenate([t[:, None], np.cos(ang), -np.sin(ang)], axis=-1).astype(np.float32)
    idx = np.arange(N)
    d = np.where(idx < n, idx, N - idx)
    d[n] = 0
    z2 = z[d]
    BIG = 1.0e4
    negF = np.where(idx < n, -t[d], -BIG).astype(np.float32)
    negB = np.where(idx > n, -t[d], -BIG).astype(np.float32)
    negt = np.stack([negF.reshape(N1, 64), negB.reshape(N1, 64)], axis=1)
    return np.ascontiguousarray(z2.T), np.ascontiguousarray(negt)


def rope_tables():
    nf = 32
    inv = (10000.0 ** (-np.arange(nf, dtype=np.float32) / nf)).astype(np.float32)
    j = np.arange(64, dtype=np.float32)
    dd = np.arange(128)
    ang = j[None, :] * inv[dd % 32][:, None]
    R = np.zeros((128, 128), np.float32)
    for dp in range(128):
        if dp % 64 < 32:
            R[dp, dp + 32] = -1.0
        else:
            R[dp, dp - 32] = 1.0
    return np.cos(ang).astype(np.float32), np.sin(ang).astype(np.float32), np.ascontiguousarray(R.T)


P1_SPEC = None


def p1_host_inputs(inp, b, hf):
    f = lambda a: np.ascontiguousarray(a, dtype=np.float32)
    o = {}
    o["x_all"] = f(np.concatenate([inp["x"][b], inp["ctx"][b]], 0))
    cond = np.stack([inp["c"][b], inp["c_ctx"]], -1)
    o["condT"] = f(cond.reshape(8, 128, 2).transpose(1, 0, 2))
    wm = inp["w_mod"][0][:, :2048]
    o["wmod"] = f(wm.reshape(8, 128, 16, 128).transpose(2, 1, 0, 3))
    o["bmodT"] = f(inp["b_mod"][0][:2048].reshape(16, 128).T)
    o["gpreT"] = f(inp["g_pre"][0].reshape(8, 128).T)
    W = inp["e_w_in"][0]
    cols = []
    cols.append(np.arange(5120 + hf * 128, 5120 + hf * 128 + 128))
    cols.append(np.arange(5376 + hf * 128, 5376 + hf * 128 + 128))
    for h in range(4):
        hd = 4 * hf + h
        cols.append(np.arange(4096 + hd * 128, 4096 + hd * 128 + 128))
        cols.append(np.arange(5632 + hd * 128, 5632 + hd * 128 + 128))
    for ct in range(4):
        c0 = hf * 512 + ct * 128
        cols.append(np.arange(1024 + c0, 1024 + c0 + 128))
        cols.append(np.arange(2048 + c0, 2048 + c0 + 128))
        cols.append(np.arange(c0, c0 + 128))
        cols.append(np.arange(3072 + c0, 3072 + c0 + 128))
    wt = np.stack([W[:, c] for c in cols], 0)
    o["w_in"] = f(wt.reshape(26, 8, 128, 128).transpose(0, 2, 1, 3))
    cw = inp["e_conv_w"][0]
    cb = inp["e_conv_b"][0]
    convw = np.zeros((128, 4, 3, 3), np.float32)
    convb = np.zeros((128, 4, 3), np.float32)
    for ct in range(4):
        c0 = hf * 512 + ct * 128
        for s in range(3):
            convw[:, ct, s, :] = cw[:, s * 1024 + c0: s * 1024 + c0 + 128].T
            convb[:, ct, s] = cb[s * 1024 + c0: s * 1024 + c0 + 128]
    o["convw"] = convw
    o["convb"] = convb
    o["gq"] = f(inp["e_q_norm"][0].reshape(128, 1))
    o["gk"] = f(inp["e_k_norm"][0].reshape(128, 1))
    rc, rs, rT = rope_tables()
    o["ropec"], o["ropes"], o["RmatT"] = rc, rs, rT
    z2l, ntl = filter_pos_tables(NLAT)
    z2c, ntc = filter_pos_tables(NCTX)
    o["z2T_l"], o["negt_l"], o["z2T_c"], o["negt_c"] = z2l, ntl, z2c, ntc
    o["fw1"] = f(inp["e_filt_w1"][0])
    o["fb1"] = f(inp["e_filt_b1"][0].reshape(64, 1))
    o["ffr"] = f(inp["e_filt_freq"][0].reshape(64, 1))
    o["fw2"] = f(inp["e_filt_w2"][0])
    o["fb2"] = f(inp["e_filt_b2"][0].reshape(64, 1))
    w3 = inp["e_filt_w3"][0].reshape(64, 2, 1024)
    o["fw3"] = f(w3[:, :, hf * 512:(hf + 1) * 512])
    min_decay = math.log(1e-2) / 1.5
    max_decay = math.log(1e-2) / 0.3
    deltas = np.abs(np.linspace(min_decay, max_decay, 1024, dtype=np.float32))
    o["adel"] = f(np.broadcast_to(deltas[hf * 512:(hf + 1) * 512][None, :], (128, 512)))
    o["hyb"] = f(inp["e_hy_bias"][0][hf * 512:(hf + 1) * 512].reshape(1, 512))
    for N1, tag in ((128, "l"), (8, "c")):
        for k, v in fft_tables(N1).items():
            o[f"ft_{tag}_{k}"] = v
    o["identF"] = np.eye(128, dtype=np.float32)
    return o


def _declare_inputs(nc, sample):
    aps = {}
    for k, v in sample.items():
        dt = F32 if v.dtype == np.float32 else BF16
        aps[k] = nc.dram_tensor(k, list(v.shape), dt, kind="ExternalInput").ap()
    return aps


class Chunked:
    def __init__(self, chunks, rows, ntok):
        self.chunks = chunks
        self.rows = rows
        self.ntok = ntok

    @staticmethod
    def make(nc, name, rows, ntok, dt, csize, **kw):
        ch = []
        for k, t0 in enumerate(range(0, ntok, csize)):
            n = min(csize, ntok - t0)
            ch.append((t0, n, nc.dram_tensor(f"{name}_{k}", [rows, n], dt, **kw).ap()))
        return Chunked(ch, rows, ntok)

    @staticmethod
    def wrap(ap):
        return Chunked([(0, ap.shape[1], ap)], ap.shape[0], ap.shape[1])

    def cols(self, r0, r1, t0, n):
        for (c0, cn, ap) in self.chunks:
            if c0 <= t0 and t0 + n <= c0 + cn:
                return ap[r0:r1, t0 - c0:t0 - c0 + n]
        raise AssertionError(("straddles chunks", t0, n))

    def pieces(self, r0, r1):
        for (c0, cn, ap) in self.chunks:
            yield c0, cn, ap[r0:r1, :]


class Ctx:
    def __init__(self, nc, stack):
        self.nc = nc
        self.stack = stack
        self.n = 0

    def sb(self, name, shape, dt, stack=None):
        self.n += 1
        return (stack or self.stack).enter_context(self.nc.sbuf_tensor(f"{name}_{self.n}", list(shape), dt))

    def ps(self, name, shape, dt, stack=None):
        self.n += 1
        return (stack or self.stack).enter_context(self.nc.psum_tensor(f"{name}_{self.n}", list(shape), dt))


def load_fft_tabs(S, C, A, tag, N1):
    H = N1 // 2
    T = {}
    def ld(name, shape, dt, q):
        t = C.sb(f"ft{tag}{name}", shape, dt)
        S.dma(t[:], A[f"ft_{tag}_{name}"], q=q)
        return t
    T["CS1"] = ld("CS1", [N1, 2 * N1], BF16, "pool")
    T["TwFc"] = ld("TwFc", [128, N1], F32, "sp")
    T["TwFs"] = ld("TwFs", [128, N1], F32, "sp")
    T["BdC"] = ld("BdC", [128, 128], BF16, "pool")
    T["BdS"] = ld("BdS", [128, 128], BF16, "pool")
    T["BdSn"] = ld("BdSn", [128, 128], BF16, "pool")
    T["R1"] = ld("R1", [128, 256], BF16, "pool")
    T["R2"] = ld("R2", [128, 256], BF16, "pool")
    T["TwIc"] = ld("TwIc", [N1, 64], F32, "sp")
    T["TwIs"] = ld("TwIs", [N1, 64], F32, "sp")
    T["Ci"] = ld("Ci", [N1, H], BF16, "pool")
    T["Sin_"] = ld("Sin_", [N1, H], BF16, "pool")
    return T


def fft_conv_core(S, C, PS, T, N1, X, Xf, Ball_re, Ball_im, tmps, evac):
    H = N1 // 2
    W2 = 2 * N1
    Gf = 512 // W2
    NG = 64 // Gf
    GN = Gf * N1
    tc = T["TwFc"][:, :].unsqueeze(1).broadcast_to([128, 2 * Gf, N1])
    tsn = T["TwFs"][:, :].unsqueeze(1).broadcast_to([128, 2 * Gf, N1])
    tic = T["TwIc"][:, :].unsqueeze(1).unsqueeze(3).broadcast_to([N1, 4, 64, 2])
    tis = T["TwIs"][:, :].unsqueeze(1).unsqueeze(3).broadcast_to([N1, 4, 64, 2])

    def stageA(g):
        m1, m2, Ar, Ai, Br_, Bi_, Fh, Yr, Yi = tmps[g % 2]
        pg1, pf1 = PS[2 * (g % 2)], PS[2 * (g % 2) + 1]
        p0 = g * Gf
        for j in range(Gf):
            S.mm(pg1[:, j * W2:(j + 1) * W2], X[0:H, p0 + j, :], T["CS1"][0:H, :])
        for j in range(Gf):
            S.mm(pf1[:, j * W2:(j + 1) * W2], Xf[0:N1, p0 + j, :], T["CS1"][0:N1, :])
        m1a = m1[:, :].rearrange("p (a n) -> p a n", n=N1)
        m2a = m2[:, :].rearrange("p (a n) -> p a n", n=N1)
        m1v = m1[:, :].rearrange("p (a r n) -> p a r n", r=2, n=N1)
        m2v = m2[:, :].rearrange("p (a r n) -> p a r n", r=2, n=N1)
        for (pp, ar, ai) in ((pg1, Ar, Ai), (pf1, Br_, Bi_)):
            v = pp[:, :].rearrange("p (a n) -> p a n", n=N1)
            S.tt(m1a, v, tc, ALU.mult)
            S.tt(m2a, v, tsn, ALU.mult)
            arv = ar[:, :].rearrange("p (a n) -> p a n", n=N1)
            aiv = ai[:, :].rearrange("p (a n) -> p a n", n=N1)
            S.tt(arv, m1v[:, :, 0, :], m2v[:, :, 1, :], ALU.add, eng="pool")
            S.tt(aiv, m1v[:, :, 1, :], m2v[:, :, 0, :], ALU.subtract, eng=("dve" if pp is pf1 else "pool"))

    def stageB(g):
        m1, m2, Ar, Ai, Br_, Bi_, Fh, Yr, Yi = tmps[g % 2]
        pg3, pf3 = PS[4], PS[5]
        for (pp, ar, ai) in ((pg3, Ar, Ai), (pf3, Br_, Bi_)):
            S.mm(pp[:, 0:GN], T["BdC"][:], ar[:, :], start=True, stop=False)
            S.mm(pp[:, 0:GN], T["BdS"][:], ai[:, :], start=False, stop=True)
            S.mm(pp[:, GN:2 * GN], T["BdC"][:], ai[:, :], start=True, stop=False)
            S.mm(pp[:, GN:2 * GN], T["BdSn"][:], ar[:, :], start=False, stop=True)
        S.act(Fh[:, :], pf3[:, :], AF.Copy)
        gv = pg3[:, :].rearrange("p (r n) -> p r n", r=2)
        fr = Fh[:, 0:GN].unsqueeze(1).broadcast_to([128, 2, GN])
        fi = Fh[:, GN:2 * GN].unsqueeze(1).broadcast_to([128, 2, GN])
        q1 = m1[:, :].rearrange("p (r n) -> p r n", r=2)
        q2 = m2[:, :].rearrange("p (r n) -> p r n", r=2)
        S.tt(q1, gv, fr, ALU.mult)
        S.tt(q2, gv, fi, ALU.mult)
        S.tt(Yr[:, :], q1[:, 0, :], q2[:, 1, :], ALU.subtract, eng="pool")
        S.tt(Yi[:, :], q2[:, 0, :], q1[:, 1, :], ALU.add, eng="dve")

    def stageC(g):
        m1, m2, Ar, Ai, Br_, Bi_, Fh, Yr, Yi = tmps[g % 2]
        pI = PS[6]
        for sub in range(Gf // 2):
            pr0 = g * Gf + 2 * sub
            for jj in range(2):
                j = 2 * sub + jj
                S.mm(pI[0:N1, jj * 256:(jj + 1) * 256], Yr[:, j * N1:(j + 1) * N1], T["R1"][:], start=True, stop=False)
                S.mm(pI[0:N1, jj * 256:(jj + 1) * 256], Yi[:, j * N1:(j + 1) * N1], T["R2"][:], start=False, stop=True)
            iv = pI[0:N1, :].rearrange("p (a n c) -> p a n c", a=4, c=2)
            n1v = m1[0:N1, :].rearrange("p (a n c) -> p a n c", a=4, c=2)
            n2v = m2[0:N1, :].rearrange("p (a n c) -> p a n c", a=4, c=2)
            S.tt(n1v, iv, tic, ALU.mult)
            S.tt(n2v, iv, tis, ALU.mult)
            n1p = m1[0:N1, :].rearrange("p (j r n c) -> p j r n c", j=2, r=2, c=2)
            n2p = m2[0:N1, :].rearrange("p (j r n c) -> p j r n c", j=2, r=2, c=2)
            ore = Ball_re[0:N1, :, 2 * pr0:2 * pr0 + 4].rearrange("p n (j c) -> p j n c", c=2)
            oim = Ball_im[0:N1, :, 2 * pr0:2 * pr0 + 4].rearrange("p n (j c) -> p j n c", c=2)
            S.tt(ore, n1p[:, :, 0, :, :], n2p[:, :, 1, :, :], ALU.subtract, eng="pool")
            S.tt(oim, n2p[:, :, 0, :, :], n1p[:, :, 1, :, :], ALU.add, eng="pool")

    stageA(0)
    for g in range(NG):
        if g + 1 < NG:
            stageA(g + 1)
        stageB(g)
        stageC(g)
    ng = min(64, 512 // H)
    k = 0
    for n2_0 in range(0, 64, ng):
        po = PS[k % 2]
        k += 1
        for j in range(ng):
            n2 = n2_0 + j
            S.mm(po[:, j * H:(j + 1) * H], Ball_re[0:N1, n2, :], T["Ci"][0:N1, :], start=True, stop=False)
            S.mm(po[:, j * H:(j + 1) * H], Ball_im[0:N1, n2, :], T["Sin_"][0:N1, :], start=False, stop=True)
        evac(po, n2_0, ng)


def build_p1(nc, sample, n_ct=4, do_att=True, do_hy=True, dbg=None, stop=None, shared=None):
    if shared is None:
        A = _declare_inputs(nc, sample)
        mix = Chunked.wrap(nc.dram_tensor("mix", [1024, NTOK], BF16, kind="ExternalOutput").ap())
    else:
        A = shared["A"]
        mix = shared["mix0"]
    xf_l = nc.dram_tensor("xf_l", [4, 128, 8192], BF16, kind="Internal").ap()
    xf_c = nc.dram_tensor("xf_c", [4, 8, 8192], BF16, kind="Internal").ap()
    outs = []
    with ExitStack() as st:
        if shared is None:
            S = Sched(nc)
            C = Ctx(nc, st)
            PS = [C.ps(f"pb{i}", [128, 512], F32) for i in range(7)]
            pTb = C.ps("pTb", [128, 1024], BF16)
        else:
            S, C, PS, pTb = shared["S"], shared["C"], shared["PS"], shared["pTb"]
            C.stack = st
        identF = C.sb("identF", [128, 128], F32)
        identB = C.sb("identB", [128, 128], BF16)
        onesF = C.sb("onesF", [128, 128], F32)
        onesB = C.sb("onesB", [128, 128], BF16)
        S.dma(identF[:], A["identF"])
        S.dma(identB[:], A["identF"], q="pool")
        S.memset(onesF[:], 1.0)
        S.memset(onesB[:], 1.0, eng="pool")
        rnT = C.sb("rnT", [128, 4, 2], F32)
        TL = load_fft_tabs(S, C, A, "l", 128)
        TC = load_fft_tabs(S, C, A, "c", 8)
        ftmp = []
        for par in range(2):
            ftmp.append((C.sb("m1", [128, 512], F32), C.sb("m2", [128, 512], F32),
                         C.sb("Ar", [128, 256], BF16), C.sb("Ai", [128, 256], BF16),
                         C.sb("Br", [128, 256], BF16), C.sb("Bi", [128, 256], BF16),
                         C.sb("Fh", [128, 512], F32),
                         C.sb("Yr", [128, 256], BF16), C.sb("Yi", [128, 256], BF16)))
        if stop == 'const':
            S.emit(st, final_wait=[])
            return nc

        import os
        SKIP = os.environ.get("K_SKIP", "")
        if do_hy and "F" not in SKIP:
            with ExitStack() as s0:
                fw1 = C.sb("fw1", [33, 64], F32, s0); S.dma(fw1[:], A["fw1"])
                fw2 = C.sb("fw2", [64, 64], F32, s0); S.dma(fw2[:], A["fw2"])
                fb1 = C.sb("fb1", [64, 1], F32, s0); S.dma(fb1[:], A["fb1"])
                fb2 = C.sb("fb2", [64, 1], F32, s0); S.dma(fb2[:], A["fb2"])
                ffr = C.sb("ffr", [64, 1], F32, s0); S.dma(ffr[:], A["ffr"])
                fw3 = C.sb("fw3", [64, 2, 512], BF16, s0); S.dma(fw3[:], A["fw3"], q="pool")
                adel = C.sb("adel", [128, 512], F32, s0); S.dma(adel[:], A["adel"])
                hyb = C.sb("hyb", [1, 512], F32, s0); S.dma(hyb[:], A["hyb"])
                frb1 = C.sb("frb1", [64, 1], F32, s0)
                frb2 = C.sb("frb2", [64, 1], F32, s0)
                S.tt(frb1[:], fb1[:], ffr[:], ALU.mult)
                S.tt(frb2[:], fb2[:], ffr[:], ALU.mult)
                hdn2 = C.sb("hdn2", [64, 8192], BF16, s0)
                z2 = C.sb("z2", [33, 512], F32, s0)
                u1 = C.sb("u1", [64, 512], F32, s0)
                h1 = C.sb("h1", [64, 512], F32, s0)
                wt1 = C.sb("wt1", [64, 512], F32, s0)
                wt2 = C.sb("wt2", [64, 512], F32, s0)
                negt = C.sb("negt", [128, 2, 64], F32, s0)
                wbuf_s = [C.sb("wbuf", [128, 2, 512], F32, s0) for _ in range(2)]
                fgrp_s = [C.sb("fgrp", [128, 512], F32, s0) for _ in range(2)]
                fgr2_s = [C.sb("fgr2", [128, 512], F32, s0) for _ in range(2)]
                acc = C.sb("acc", [128, 512], F32, s0)
                nrow = C.sb("nrow", [1, 512], F32, s0)
                ncol = C.sb("ncol", [128, 1], F32, s0)
                Xfo = C.sb("Xfo", [128, 4, 64, 128], BF16, s0)
                for (tag, N1, z2T, ntab, xfd, li) in (("l", 128, A["z2T_l"], A["negt_l"], xf_l, 0),
                                                      ("c", 8, A["z2T_c"], A["negt_c"], xf_c, 1)):
                    N = 64 * N1
                    S.dma(negt[0:N1, :, :], ntab)
                    for blk in range(N // 512):
                        sl = slice(blk * 512, (blk + 1) * 512)
                        S.dma(z2[:], z2T[:, sl])
                        S.mm(PS[5][0:64, :], fw1[:], z2[:])
                        S.ts(u1[:], PS[5][0:64, :], ffr[:, 0:1], frb1[:, 0:1], ALU.mult, ALU.add)
                        S.wrap(u1[:], u1[:], wt1[:], wt2[:])
                        S.wrap(u1[:], u1[:], wt1[:], wt2[:])
                        S.act(h1[:], u1[:], AF.Sin)
                        S.mm(PS[6][0:64, :], fw2[:], h1[:])
                        S.ts(u1[:], PS[6][0:64, :], ffr[:, 0:1], frb2[:, 0:1], ALU.mult, ALU.add)
                        S.wrap(u1[:], u1[:], wt1[:], wt2[:])
                        S.wrap(u1[:], u1[:], wt1[:], wt2[:])
                        S.act(hdn2[:, sl], u1[:], AF.Sin)
                    S.memset(acc[0:N1, :], 0.0)
                    for n2 in range(64):
                        wbuf, fgrp, fgr2 = wbuf_s[n2 % 2], fgrp_s[n2 % 2], fgr2_s[n2 % 2]
                        pF = (PS[1 + 2 * (n2 % 2)], PS[2 + 2 * (n2 % 2)])
                        for dr in range(2):
                            S.mm(pF[dr][0:N1, :], hdn2[:, n2:N:64], fw3[:, dr, :])
                            S.act(wbuf[0:N1, dr, :], adel[0:N1, :], AF.Exp, scale=negt[0:N1, dr, n2:n2 + 1])
                        S.tt(fgrp[0:N1, :], pF[0][0:N1, :], wbuf[0:N1, 0, :], ALU.mult)
                        S.tt(fgr2[0:N1, :], pF[1][0:N1, :], wbuf[0:N1, 1, :], ALU.mult)
                        S.tt(fgrp[0:N1, :], fgrp[0:N1, :], fgr2[0:N1, :], ALU.add, eng="pool")
                        S.act(fgr2[0:N1, :], fgrp[0:N1, :], AF.Abs)
                        S.tt(acc[0:N1, :], acc[0:N1, :], fgr2[0:N1, :], ALU.add, eng="pool")
                        ov = Xfo[0:N1, :, :, 2 * n2:2 * n2 + 2]
                        iv = fgrp[0:N1, :].rearrange("p (t a c) -> p t a c", t=4, c=2)
                        S.copy(ov, iv, eng="act")
                    S.mm(PS[6][0:1, 0:512], onesF[0:N1, 0:1], acc[0:N1, :])
                    S.tt(nrow[:], PS[6][0:1, 0:512], hyb[0:1, :], ALU.mult)
                    for ct in range(n_ct):
                        S.mm(PS[5][0:128, 0:1], acc[0:N1, ct * 128:(ct + 1) * 128], onesF[0:N1, 0:1])
                        S.copy(ncol[:], PS[5][0:128, 0:1])
                        S.recip(rnT[:, ct, li:li + 1], ncol[:])
                        tap = Xfo[0:1, ct, :, 0:2]
                        S.tt(tap, tap, nrow[0:1, ct * 128:(ct + 1) * 128].rearrange("p (a c) -> p a c", c=2), ALU.add)
                        S.dma(xfd[ct, :, :], Xfo[0:N1, ct, :, :].rearrange("p a n -> p (a n)"))
            S.fence()

        hT = C.sb("hT", [128, 8, NTOK], BF16)
        with ExitStack() as s1:
            condT = C.sb("condT", [128, 8, 2], F32, s1); S.dma(condT[:], A["condT"])
            scond = C.sb("scond", [128, 8, 2], F32, s1)
            S.act(scond[:], condT[:], AF.Silu)
            bmodT = C.sb("bmodT", [128, 16], F32, s1); S.dma(bmodT[:], A["bmodT"])
            gpreT = C.sb("gpreT", [128, 8], F32, s1); S.dma(gpreT[:], A["gpreT"])
            modT = C.sb("modT", [128, 16, 2], F32, s1)
            wm = [C.sb(f"wm{i}", [128, 8, 128], F32, s1) for i in range(2)]
            for fb in range(16):
                w = wm[fb % 2]
                S.dma(w[:], A["wmod"][fb], q="sp" if fb % 2 == 0 else "act")
                for kt in range(8):
                    S.mm(PS[fb % 2][:, 0:2], w[:, kt, :], scond[:, kt, :], start=(kt == 0), stop=(kt == 7))
                S.ts(modT[:, fb, :], PS[fb % 2][:, 0:2], bmodT[:, fb:fb + 1], None, ALU.add)
            if stop == 'mod':
                S.emit(st, final_wait=[])
                return nc
            Amod = C.sb("Amod", [128, 8, 2], F32, s1)
            S.ts(Amod[:], modT[:, 8:16, :], 1.0, None, ALU.add)
            S.tt(Amod[:], Amod[:], gpreT[:, :].unsqueeze(2).broadcast_to([128, 8, 2]), ALU.mult)
            xts = [C.sb(f"xt{i}", [128, 1024], F32, s1) for i in range(3)]
            junk = C.sb("junk", [128, 1024], BF16, s1)
            ssq = C.sb("ssq", [128, 34], F32, s1)
            rstd = C.sb("rstd", [128, 34], F32, s1)
            import os
            for i in range(int(os.environ.get('K_NT', '34'))):
                xt = xts[i % 3]
                S.dma(xt[:], A["x_all"][i * 128:(i + 1) * 128, :], q="sp" if i % 2 == 0 else "act")
                import os
                lvl = int(os.environ.get("K_DEBUG", "9"))
                if lvl < 1:
                    continue
                S.act(junk[:], xt[:], AF.Square, accum_out=ssq[:, i:i + 1])
                if lvl < 2:
                    continue
                S.act(rstd[:, i:i + 1], ssq[:, i:i + 1], AF.Sqrt, bias=EPS, scale=1.0 / D)
                if lvl < 3:
                    continue
                S.recip(rstd[:, i:i + 1], rstd[:, i:i + 1])
                S.ts(xt[:], xt[:], rstd[:, i:i + 1], None, ALU.mult)
                if lvl < 4:
                    continue
                jj = 0 if i < 32 else 1
                for half in range(2):
                    pt = PS[2 + half]
                    for k4 in range(4):
                        kt = half * 4 + k4
                        S.transpose(pt[:, k4 * 128:(k4 + 1) * 128], xt[:, kt * 128:(kt + 1) * 128], identF[:])
                    if lvl < 5:
                        continue
                    for k4 in range(4):
                        kt = half * 4 + k4
                        S.ts(hT[:, kt, i * 128:(i + 1) * 128], pt[:, k4 * 128:(k4 + 1) * 128],
                             Amod[:, kt, jj:jj + 1], modT[:, kt, jj:jj + 1], ALU.mult, ALU.add)
        S.fence()
        if dbg == "hT":
            hTo = nc.dram_tensor("hTo", [128, 8, NTOK], BF16, kind="ExternalOutput").ap()
            outs.append(S.dma(hTo, hT[:]))

        wbufs = [C.sb(f"wcol{i}", [128, 8, 128], BF16) for i in range(2)]
        wstate = {"n": 0}

        def load_w(tile_idx):
            w = wbufs[wstate["n"] % 2]
            wstate["n"] += 1
            S.dma(w[:], A["w_in"][tile_idx], q="pool")
            return w

        blocks = [(tb * 512, 512) for tb in range(8)] + [(4096, 256)]

        def proj_fm(w, blk, pt):
            t0, n = blk
            for kt in range(8):
                S.mm(pt[:, 0:n], w[:, kt, :], hT[:, kt, t0:t0 + n], start=(kt == 0), stop=(kt == 7))

        if do_att:
            with ExitStack() as s2:
                gq = C.sb("gq", [128, 1], F32, s2); S.dma(gq[:], A["gq"])
                gk = C.sb("gk", [128, 1], F32, s2); S.dma(gk[:], A["gk"])
                S.ts(gq[:], gq[:], 128.0 ** -0.5, None, ALU.mult)
                ropec = C.sb("ropec", [128, 64], F32, s2); S.dma(ropec[:], A["ropec"])
                ropes = C.sb("ropes", [128, 64], F32, s2); S.dma(ropes[:], A["ropes"])
                RmT = C.sb("RmT", [128, 128], F32, s2); S.dma(RmT[:], A["RmatT"])
                KT = C.sb("KT", [128, NTOK], BF16, s2)
                V = C.sb("V", [128, 34, 128], BF16, s2)
                sq = C.sb("sq", [128, 512], BF16, s2)
                rinv = C.sb("rinv", [128, 512], F32, s2)
                kn = C.sb("kn", [128, 512], F32, s2)
                t1 = C.sb("t1", [128, 512], F32, s2)
                t2 = C.sb("t2", [128, 512], F32, s2)

                def norm_rope(pt, blk, g, out_ap):
                    t0, n = blk
                    S.act(sq[:, 0:n], pt[:, 0:n], AF.Square)
                    S.mm(PS[2][:, 0:n], onesB[:], sq[:, 0:n])
                    S.act(rinv[:, 0:n], PS[2][:, 0:n], AF.Sqrt, bias=EPS, scale=1.0 / 128)
                    S.recip(rinv[:, 0:n], rinv[:, 0:n])
                    if t0 >= NLAT:
                        S.stt(out_ap, pt[:, 0:n], g[:, 0:1], rinv[:, 0:n], ALU.mult, ALU.mult)
                        return
                    S.stt(kn[:, 0:n], pt[:, 0:n], g[:, 0:1], rinv[:, 0:n], ALU.mult, ALU.mult)
                    S.mm(PS[2][:, 0:n], RmT[:], kn[:, 0:n])
                    r0 = t0 // 64
                    for (p0, p1) in ((0, 64), (64, 128)):
                        if p0 == 0:
                            cv = ropec[p0:p1, r0:r0 + 8].unsqueeze(2).broadcast_to([64, 8, 64])
                            sv = ropes[p0:p1, r0:r0 + 8].unsqueeze(2).broadcast_to([64, 8, 64])
                        else:
                            cv = ropec[p0:p1, :].unsqueeze(1).broadcast_to([64, 8, 64])
                            sv = ropes[p0:p1, :].unsqueeze(1).broadcast_to([64, 8, 64])
                        v3 = lambda a: a[p0:p1, 0:512].rearrange("p (r c) -> p r c", c=64)
                        S.tt(v3(t1), v3(kn), cv, ALU.mult, eng="pool")
                        S.tt(v3(t2), v3(PS[2]), sv, ALU.mult)
                    S.tt(out_ap, t1[:, 0:n], t2[:, 0:n], ALU.add)

                wk = load_w(0)
                for bi, blk in enumerate(blocks):
                    proj_fm(wk, blk, PS[bi % 2])
                    norm_rope(PS[bi % 2], blk, gk, KT[:, blk[0]:blk[0] + blk[1]])
                wv = load_w(1)
                for i4 in range(0, 34, 4):
                    nt = min(4, 34 - i4)
                    pt = PS[(i4 // 4) % 2]
                    for j in range(nt):
                        i = i4 + j
                        for kt in range(8):
                            S.mm(pt[:, j * 128:(j + 1) * 128], hT[:, kt, i * 128:(i + 1) * 128], wv[:, kt, :],
                                 start=(kt == 0), stop=(kt == 7))
                    S.act(V[:, i4:i4 + nt, :].rearrange("p a d -> p (a d)"), pt[:, 0:nt * 128], AF.Copy)
                qTb = [C.sb(f"qTb{i}", [128, 512], BF16, s2) for i in range(2)]
                gab = [C.sb(f"gab{i}", [128, 512], F32, s2) for i in range(2)]
                Pb = [C.sb(f"Pb{i}", [128, 512], BF16, s2) for i in range(3)]
                rden = C.sb("rden", [1, 512], F32, s2)
                o1 = C.sb("o1", [128, 512], F32, s2)
                mo = [C.sb(f"mo{i}", [128, 512], BF16, s2) for i in range(2)]
                work = [(h, bi) for h in range(4) for bi in range(len(blocks))]
                wq = {}

                def prep(idx):
                    h, bi = work[idx]
                    blk = blocks[bi]
                    if bi == 0:
                        wq["q"] = load_w(2 + 2 * h)
                        wq["g"] = load_w(3 + 2 * h)
                    proj_fm(wq["q"], blk, PS[0])
                    norm_rope(PS[0], blk, gq, qTb[idx % 2][:, 0:blk[1]])
                    proj_fm(wq["g"], blk, PS[1])
                    S.act(gab[idx % 2][:, 0:blk[1]], PS[1][:, 0:blk[1]], AF.Silu)

                def attend(idx):
                    h, bi = work[idx]
                    t0, n = blocks[bi]
                    ktiles = list(range(34)) if t0 < NLAT else [32, 33]
                    pO, pD = PS[5], PS[6]
                    nk = len(ktiles)

                    def smm(ki):
                        kt_i = ktiles[ki]
                        S.mm(PS[3 + ki % 2][:, 0:n], KT[:, kt_i * 128:(kt_i + 1) * 128], qTb[idx % 2][:, 0:n])
                    smm(0)
                    if nk > 1:
                        smm(1)
                    for ki, kt_i in enumerate(ktiles):
                        pS = PS[3 + ki % 2]
                        P = Pb[ki % 3]
                        S.act(P[:, 0:n], pS[:, 0:n], AF.Exp)
                        S.mm(pO[:, 0:n], V[:, kt_i, :], P[:, 0:n], start=(ki == 0), stop=(ki == nk - 1))
                        S.mm(pD[0:1, 0:n], onesB[:, 0:1], P[:, 0:n], start=(ki == 0), stop=(ki == nk - 1))
                        if ki + 2 < nk:
                            smm(ki + 2)
                    S.recip(rden[0:1, 0:n], pD[0:1, 0:n])
                    S.mm(PS[2][:, 0:n], onesF[0:1, :], rden[0:1, 0:n])
                    S.tt(o1[:, 0:n], pO[:, 0:n], gab[idx % 2][:, 0:n], ALU.mult)
                    m = mo[idx % 2]
                    S.tt(m[:, 0:n], o1[:, 0:n], PS[2][:, 0:n], ALU.mult)
                    outs.append(S.dma(mix.cols(512 + h * 128, 512 + (h + 1) * 128, t0, n), m[:, 0:n], q="sp"))

                prep(0)
                for idx in range(len(work)):
                    if idx + 1 < len(work):
                        prep(idx + 1)
                    attend(idx)
            S.fence()

        if do_hy:
            with ExitStack() as s3:
                convw = C.sb("convw", [128, 4, 3, 3], F32, s3); S.dma(convw[:], A["convw"])
                convb = C.sb("convb", [128, 4, 3], F32, s3); S.dma(convb[:], A["convb"])
                SEG = NLAT + 2 + NCTX + 2
                ust = C.sb("ust", [128, SEG], BF16, s3)
                S.memset(ust[:], 0.0)
                ctmp = C.sb("ctmp", [128, 1024], F32, s3)
                x1c = C.sb("x1c", [128, NTOK], BF16, s3)
                gT = C.sb("gT", [128, NTOK], BF16, s3)
                xg = C.sb("xg", [128, NTOK], BF16, s3)
                Xl = C.sb("Xl", [64, 64, 128], BF16, s3)
                Xfl = C.sb("Xfl", [128, 64, 128], BF16, s3)
                Bre = C.sb("Bre", [128, 64, 128], BF16, s3)
                Bim = C.sb("Bim", [128, 64, 128], BF16, s3)
                segs = [(k * 1024, k * 1024 + 1, 1024) for k in range(4)] + [(4096, NLAT + 3, 256)]

                def conv_stream(ct, s, out_fn):
                    for (t0, u0, n) in segs:
                        S.act(ctmp[:, 0:n], ust[:, u0:u0 + n], AF.Identity, bias=convb[:, ct, s:s + 1],
                              scale=convw[:, ct, s, 1:2])
                        S.stt(ctmp[:, 0:n], ust[:, u0 - 1:u0 - 1 + n], convw[:, ct, s, 0:1], ctmp[:, 0:n],
                              ALU.mult, ALU.add)
                        out_fn(t0, n, ust[:, u0 + 1:u0 + 1 + n], convw[:, ct, s, 2:3], ctmp[:, 0:n])

                def ust_cols(blk):
                    t0, n = blk
                    return slice(t0 + 1, t0 + 1 + n) if t0 < NLAT else slice(NLAT + 3, NLAT + 3 + n)

                for ct in range(n_ct):
                    wbase = 10 + 4 * ct
                    S.dma(Xfl[:, :, :].rearrange("p a n -> p (a n)"), xf_l[ct, :, :], q="act")
                    w = load_w(wbase + 0)
                    for bi, blk in enumerate(blocks):
                        proj_fm(w, blk, PS[bi % 2])
                        S.act(ust[:, ust_cols(blk)], PS[bi % 2][:, 0:blk[1]], AF.Copy)
                    conv_stream(ct, 1, lambda t0, n, a, sc, b: S.stt(x1c[:, t0:t0 + n], a, sc, b, ALU.mult, ALU.add))
                    w = load_w(wbase + 1)
                    for bi, blk in enumerate(blocks):
                        proj_fm(w, blk, PS[bi % 2])
                        S.act(ust[:, ust_cols(blk)], PS[bi % 2][:, 0:blk[1]], AF.Copy)

                    def gfn(t0, n, a, sc, b):
                        S.stt(b, a, sc, b, ALU.mult, ALU.add)
                        S.tt(gT[:, t0:t0 + n], b, x1c[:, t0:t0 + n], ALU.mult, eng="pool")
                    conv_stream(ct, 2, gfn)
                    w = load_w(wbase + 3)
                    for bi, blk in enumerate(blocks):
                        proj_fm(w, blk, PS[bi % 2])
                        S.act(xg[:, blk[0]:blk[0] + blk[1]], PS[bi % 2][:, 0:blk[1]], AF.Silu)
                    w = load_w(wbase + 2)
                    for bi, blk in enumerate(blocks):
                        proj_fm(w, blk, PS[bi % 2])
                        S.act(ust[:, ust_cols(blk)], PS[bi % 2][:, 0:blk[1]], AF.Copy)

                    def xfn(t0, n, a, sc, b):
                        S.stt(b, a, sc, b, ALU.mult, ALU.add)
                        S.tt(xg[:, t0:t0 + n], b, xg[:, t0:t0 + n], ALU.mult, eng="pool")
                    conv_stream(ct, 0, xfn)
                    for n2_0 in range(0, 64, 8):
                        for j in range(8):
                            S.transpose(pTb[0:64, j * 128:(j + 1) * 128], gT[:, n2_0 + j:NLAT:64], identB[:])
                        ov = Xl[0:64, :, :].rearrange("p a (n c) -> p n a c", c=2)[:, n2_0:n2_0 + 8, :, :]
                        iv = pTb[0:64, :].rearrange("p (n a c) -> p n a c", n=8, c=2)
                        S.copy(ov, iv, eng="act")

                    def mk_evac(tok0, H, li):
                        def evac(po, n2_0, ng):
                            ov = gT[:, tok0:tok0 + 64 * H].rearrange("p (a n) -> p n a", n=64)[:, n2_0:n2_0 + ng, :]
                            xv = xg[:, tok0:tok0 + 64 * H].rearrange("p (a n) -> p n a", n=64)[:, n2_0:n2_0 + ng, :]
                            iv = po[:, 0:ng * H].rearrange("p (n a) -> p n a", a=H)
                            S.stt(ov, iv, rnT[:, ct, li:li + 1], xv, ALU.mult, ALU.mult)
                        return evac
                    if "L" not in SKIP:
                        fft_conv_core(S, C, PS, TL, 128, Xl, Xfl, Bre, Bim, ftmp, mk_evac(0, 64, 0))
                    S.dma(Xfl[0:8, :, :].rearrange("p a n -> p (a n)"), xf_c[ct, :, :], q="act")
                    for n2_0 in range(0, 64, 8):
                        for j in range(8):
                            S.transpose(pTb[0:4, j * 128:(j + 1) * 128], gT[:, NLAT + n2_0 + j:NTOK:64], identB[:])
                        ov = Xl[0:4, :, :].rearrange("p a (n c) -> p n a c", c=2)[:, n2_0:n2_0 + 8, :, :]
                        iv = pTb[0:4, :].rearrange("p (n a c) -> p n a c", n=8, c=2)
                        S.copy(ov, iv, eng="act")
                    if "C" not in SKIP:
                        fft_conv_core(S, C, PS, TC, 8, Xl, Xfl, Bre, Bim, ftmp, mk_evac(NLAT, 4, 1))
                    for (c0, cn, cap) in mix.pieces(ct * 128, (ct + 1) * 128):
                        outs.append(S.dma(cap, gT[:, c0:c0 + cn], q="sp"))
        if shared is None:
            S.emit(st, final_wait=outs)
    return nc


def gate_rows(S, C, PS, A, pre, scond, onesF, stk):
    rows = C.sb("grow", [2, 1024], F32, stk)
    bro = C.sb("gbro", [2, 1024], F32, stk); S.dma(bro[:], A[pre + "bgate2"])
    gpo = C.sb("ggpo", [2, 1024], F32, stk); S.dma(gpo[:], A[pre + "gpost2"])
    sel2 = C.sb("sel2", [2, 2, 128], F32, stk); S.dma(sel2[:], A["sel2"])
    wg = [C.sb(f"wgate{i}", [128, 1024], F32, stk) for i in range(2)]
    for kt in range(8):
        w = wg[kt % 2]
        S.dma(w[:], A[pre + "wgate"][kt], q="sp" if kt % 2 == 0 else "act")
        for nb in range(2):
            S.mm(PS[nb][0:2, :], scond[:, kt, 0:2], w[:, nb * 512:(nb + 1) * 512], start=(kt == 0), stop=(kt == 7))
    for nb in range(2):
        S.tt(rows[:, nb * 512:(nb + 1) * 512], PS[nb][0:2, :], bro[:, nb * 512:(nb + 1) * 512], ALU.add)
    S.tt(rows[:], rows[:], gpo[:], ALU.mult)
    gg = []
    for j in range(2):
        g = C.sb(f"gg{j}", [128, 1024], F32, stk)
        for nb in range(2):
            S.mm(PS[2 + nb][:, :], sel2[0:2, j, :], rows[0:2, nb * 512:(nb + 1) * 512])
            S.copy(g[:, nb * 512:(nb + 1) * 512], PS[2 + nb][:, :], eng="act")
        gg.append(g)
    return gg


def outproj_residual(S, C, PS, A, stk, mixg, wout_ap, x_in, tiles, gg, x_out, identF, post=None):
    wout = C.sb("wout", [128, 16, 1024], BF16, stk)
    for kt in range(16):
        S.dma(wout[:, kt, :], wout_ap[kt], q="pool")
    mts = [C.sb(f"mt{i}", [128, 16, 512], BF16, stk) for i in range(2)]
    xts = [C.sb(f"xo{i}", [128, 1024], F32, stk) for i in range(3)]
    junk = C.sb("junk2", [128, 512], BF16, stk)
    ss2 = C.sb("ss2", [128, 2], F32, stk)
    rs = C.sb("rs", [128, 1], F32, stk)
    tmp = C.sb("tmpo", [128, 1024], F32, stk)
    outs = []
    if not isinstance(mixg, Chunked):
        mixg = Chunked.wrap(mixg)
    cur = {"t0": None, "mt": None, "n": 0}
    for idx, (i, tok0, jj) in enumerate(tiles):
        g4 = tok0 // 512
        if cur["t0"] != g4:
            mt = mts[cur["n"] % 2]
            cur["n"] += 1
            ncol = min(512, mixg.ntok - g4 * 512)
            S.dma(mt[:, :, 0:ncol], mixg.cols(0, mixg.rows, g4 * 512, ncol).rearrange("(kt p) t -> p kt t", p=128), q="sp")
            cur["t0"], cur["mt"] = g4, mt
        mt = cur["mt"]
        c0 = tok0 - g4 * 512
        xt = xts[idx % 3]
        S.dma(xt[:], x_in[tok0:tok0 + 128, :], q="act")
        for nb in range(2):
            for kt in range(16):
                S.mm(PS[nb][:, :], mt[:, kt, c0:c0 + 128], wout[:, kt, nb * 512:(nb + 1) * 512],
                     start=(kt == 0), stop=(kt == 15))
            S.act(junk[:], PS[nb][:, :], AF.Square, accum_out=ss2[:, nb:nb + 1])
        S.tt(rs[:], ss2[:, 0:1], ss2[:, 1:2], ALU.add)
        S.act(rs[:], rs[:], AF.Sqrt, bias=EPS, scale=1.0 / D)
        S.recip(rs[:], rs[:])
        for nb in range(2):
            sl = slice(nb * 512, (nb + 1) * 512)
            S.stt(tmp[:, sl], PS[nb][:, :], rs[:, 0:1], gg[jj][:, sl], ALU.mult, ALU.mult)
        S.tt(xt[:], xt[:], tmp[:], ALU.add, eng="pool")
        if x_out is not None:
            outs.append(S.dma(x_out[i * 128:(i + 1) * 128, :], xt[:], q="sp"))
        if post is not None:
            post(i, xt, jj)
    return outs


def adaln_shift_scale(S, C, PS, A, pre, scond, stk):
    bmodT = C.sb("bmodT", [128, 16], F32, stk); S.dma(bmodT[:], A[pre + "bmodT"])
    gpreT = C.sb("gpreT", [128, 8], F32, stk); S.dma(gpreT[:], A[pre + "gpreT"])
    modT = C.sb("modT", [128, 16, 2], F32, stk)
    wm = [C.sb(f"wm{i}", [128, 8, 128], F32, stk) for i in range(2)]
    for fb in range(16):
        w = wm[fb % 2]
        S.dma(w[:], A[pre + "wmod"][fb], q="sp" if fb % 2 == 0 else "act")
        for kt in range(8):
            S.mm(PS[4 + fb % 2][:, 0:2], w[:, kt, :], scond[:, kt, :], start=(kt == 0), stop=(kt == 7))
        S.ts(modT[:, fb, :], PS[4 + fb % 2][:, 0:2], bmodT[:, fb:fb + 1], None, ALU.add)
    Amod = C.sb("Amod", [128, 8, 2], F32, stk)
    S.ts(Amod[:], modT[:, 8:16, :], 1.0, None, ALU.add)
    S.tt(Amod[:], Amod[:], gpreT[:, :].unsqueeze(2).broadcast_to([128, 8, 2]), ALU.mult)
    return Amod, modT


def p2_host_inputs(inp, b, hf, mix0g):
    f = lambda a: np.ascontiguousarray(a, dtype=np.float32)
    o = {}
    o["mix0g"] = np.ascontiguousarray(mix0g)
    o["x_all"] = f(np.concatenate([inp["x"][b], inp["ctx"][b]], 0))
    cond = np.stack([inp["c"][b], inp["c_ctx"]], -1)
    o["condT"] = f(cond.reshape(8, 128, 2).transpose(1, 0, 2))
    o["l0_wgate"] = f(inp["w_mod"][0][:, 2048:3072].reshape(8, 128, 1024))
    o["l0_bgate2"] = f(np.broadcast_to(inp["b_mod"][0][2048:3072][None], (2, 1024)))
    o["l0_gpost2"] = f(np.broadcast_to(inp["g_post"][0][None], (2, 1024)))
    sel2 = np.zeros((2, 2, 128), np.float32); sel2[0, 0] = 1; sel2[1, 1] = 1
    o["sel2"] = sel2
    wo = inp["e_w_out"][0]
    order = np.concatenate([np.arange(r * 512, r * 512 + 512) if part == 0 else np.arange(1024 + r * 512, 1024 + r * 512 + 512)
                            for r in range(2) for part in range(2)])
    o["wout0"] = f(wo[order].reshape(16, 128, 1024))
    wm = inp["w_mod"][1][:, :2048]
    o["l1_wmod"] = f(wm.reshape(8, 128, 16, 128).transpose(2, 1, 0, 3))
    o["l1_bmodT"] = f(inp["b_mod"][1][:2048].reshape(16, 128).T)
    o["l1_gpreT"] = f(inp["g_pre"][1].reshape(8, 128).T)
    W = inp["o_w_in"][0]
    heads = [4 * hf + h for h in range(4)]
    def tl(cols):
        return W[:, cols].reshape(8, 128, len(cols)).transpose(1, 0, 2)
    o["wq"] = f(np.stack([tl(np.arange(hd * 128, hd * 128 + 128)) for hd in heads]))
    o["wk"] = f(np.stack([tl(np.arange(1024 + hd * 128, 1024 + hd * 128 + 128)) for hd in heads]))
    o["wv"] = f(np.stack([tl(np.arange(2048 + hd * 256, 2048 + hd * 256 + 256)) for hd in heads]))
    o["woz"] = f(np.stack([tl(np.concatenate([np.arange(4096 + hd * 256, 4096 + hd * 256 + 256),
                                              np.arange(6144 + hd * 256, 6144 + hd * 256 + 256)])) for hd in heads]))
    gcols = np.array([8192 + g * 8 + hd for g in range(4) for hd in heads])
    o["wg"] = f(tl(gcols))
    cw = inp["o_conv_w"][0]; cb = inp["o_conv_b"][0]
    convw = np.zeros((128, 4, 2, 3), np.float32); convb = np.zeros((128, 4, 2), np.float32)
    for h, hd in enumerate(heads):
        for s in range(2):
            c0 = s * 1024 + hd * 128
            convw[:, h, s, :] = cw[:, c0:c0 + 128].T
            convb[:, h, s] = cb[c0:c0 + 128]
    o["convw1"] = convw; o["convb1"] = convb
    gb = inp["o_gate_b"][0].reshape(4, 8)[:, heads]
    o["gateb"] = f(gb.T)
    hn = inp["o_head_norm"][0].reshape(8, 256)[heads]
    o["hn"] = f(np.broadcast_to(hn[None], (128, 4, 256)))
    t = np.arange(128)
    same = (t[:, None] // 64) == (t[None, :] // 64)
    o["maskF"] = f((same & (t[:, None] <= t[None, :])))
    o["maskB"] = f((same & (t[:, None] >= t[None, :])))
    sel4 = np.zeros((4, 4, 128), np.float32)
    for h in range(4):
        sel4[h, h] = 1
    o["sel4"] = sel4
    o["identF"] = np.eye(128, dtype=np.float32)
    return o


def build_p2(nc, sample, n_heads=4, stop=None, shared=None):
    if shared is None:
        A = _declare_inputs(nc, sample)
        x1o = nc.dram_tensor("x1o", [NTOK, D], F32, kind="ExternalOutput").ap()
        mix1 = Chunked.wrap(nc.dram_tensor("mix1", [1024, NLAT], BF16, kind="ExternalOutput").ap())
    else:
        A = shared["A"]
        x1o = shared["x1o"]
        mix1 = shared["mix1"]
    outs = []
    NCH = NTOK // 64
    with ExitStack() as st:
        if shared is None:
            S = Sched(nc)
            C = Ctx(nc, st)
            PS = [C.ps(f"pb{i}", [128, 512], F32) for i in range(7)]
            pTb = C.ps("pTb", [128, 1024], BF16)
        else:
            S, C, PS, pTb = shared["S"], shared["C"], shared["PS"], shared["pTb"]
            C.stack = st
        identF = C.sb("identF", [128, 128], F32); S.dma(identF[:], A["identF"])
        identB = C.sb("identB", [128, 128], BF16); S.dma(identB[:], A["identF"], q="pool")
        onesF = C.sb("onesF", [128, 128], F32); S.memset(onesF[:], 1.0)
        condT = C.sb("condT", [128, 8, 2], F32); S.dma(condT[:], A["condT"])
        scond = C.sb("scond", [128, 8, 2], F32)
        S.act(scond[:], condT[:], AF.Silu)
        h1T = C.sb("h1T", [128, 8, NTOK], BF16)
        tokq = C.sb("tokq", [128, 34, 24], F32)
        ebB = C.sb("ebB", [128, 2, 4, NCH], F32)
        with ExitStack() as sa:
            Amod, modT = adaln_shift_scale(S, C, PS, A, "l1_", scond, sa)
            gg = gate_rows(S, C, PS, A, "l0_", scond, onesF, sa)
            ssq = C.sb("ssq", [128, 1], F32, sa)
            rstd = C.sb("rstd", [128, 1], F32, sa)
            junk = C.sb("junk", [128, 1024], BF16, sa)
            xh = C.sb("xh", [128, 1024], F32, sa)

            def post(i, xt, jj):
                S.act(junk[:], xt[:], AF.Square, accum_out=ssq[:, 0:1])
                S.act(rstd[:], ssq[:], AF.Sqrt, bias=EPS, scale=1.0 / D)
                S.recip(rstd[:], rstd[:])
                S.ts(xh[:], xt[:], rstd[:, 0:1], None, ALU.mult)
                for half in range(2):
                    pt = PS[2 + half]
                    for k4 in range(4):
                        kt = half * 4 + k4
                        S.transpose(pt[:, k4 * 128:(k4 + 1) * 128], xh[:, kt * 128:(kt + 1) * 128], identF[:])
                    for k4 in range(4):
                        kt = half * 4 + k4
                        S.ts(h1T[:, kt, i * 128:(i + 1) * 128], pt[:, k4 * 128:(k4 + 1) * 128],
                             Amod[:, kt, jj:jj + 1], modT[:, kt, jj:jj + 1], ALU.mult, ALU.add)
            tiles = [(i, i * 128, 0 if i < 32 else 1) for i in range(34)]
            outs += outproj_residual(S, C, PS, A, sa, A["mix0g"], A["wout0"], A["x_all"], tiles, gg, x1o, identF, post)
        S.fence()
        if stop == "A":
            dbg = nc.dram_tensor("h1To", [128, 8, NTOK], BF16, kind="ExternalOutput").ap()
            outs.append(S.dma(dbg, h1T[:]))
            S.emit(st, final_wait=outs)
            return nc
        blocks = [(tb * 512, 512) for tb in range(8)] + [(4096, 256)]
        with ExitStack() as sb_:
            wg = C.sb("wg", [128, 8, 16], BF16, sb_); S.dma(wg[:], A["wg"], q="pool")
            gateb = C.sb("gateb", [4, 4], F32, sb_); S.dma(gateb[:], A["gateb"])
            sel4 = C.sb("sel4", [4, 4, 128], F32, sb_); S.dma(sel4[:], A["sel4"])
            gi = C.sb("gi", [4, NTOK], F32, sb_)
            gf = C.sb("gf", [4, NTOK], F32, sb_)
            cumB = C.sb("cumB", [4, NTOK], F32, sb_)
            e1p = C.sb("e1p", [4, NTOK], F32, sb_)
            ebr = C.sb("ebr", [4, NCH], F32, sb_)
            v3 = lambda t_: t_[:, :].rearrange("p (c j) -> p c j", j=64)
            for d in range(2):
                for gq_, dstg in ((2 * d, gi), (2 * d + 1, gf)):
                    for bi, (t0, n) in enumerate(blocks):
                        pt = PS[bi % 2]
                        for kt in range(8):
                            S.mm(pt[0:4, 0:n], wg[:, kt, gq_ * 4:(gq_ + 1) * 4], h1T[:, kt, t0:t0 + n], start=(kt == 0), stop=(kt == 7))
                        S.ts(dstg[:, t0:t0 + n], pt[0:4, 0:n], gateb[:, gq_:gq_ + 1], None, ALU.add)
                S.act(gf[:], gf[:], AF.Exp, scale=-1.0)
                S.act(gf[:], gf[:], AF.Ln, bias=1.0)
                src, dst = gf, cumB
                for k in range(6):
                    sh = 1 << k
                    s3, d3 = v3(src), v3(dst)
                    if d == 0:
                        S.tt(d3[:, :, sh:64], s3[:, :, sh:64], s3[:, :, 0:64 - sh], ALU.add)
                        S.copy(d3[:, :, 0:sh], s3[:, :, 0:sh], eng="pool")
                    else:
                        S.tt(d3[:, :, 0:64 - sh], s3[:, :, 0:64 - sh], s3[:, :, sh:64], ALU.add)
                        S.copy(d3[:, :, 64 - sh:64], s3[:, :, 64 - sh:64], eng="pool")
                    src, dst = dst, src
                cum, thr = src, dst
                e1 = gi
                endpos = 63 if d == 0 else 0
                S.act(ebr[:], v3(cum)[:, :, endpos], AF.Exp, scale=-1.0)
                S.tt(e1[:], gi[:], cum[:], ALU.add)
                S.act(e1[:], e1[:], AF.Exp)
                S.ts(e1[:], e1[:], 128.0 ** -0.5, None, ALU.mult)
                S.act(thr[:], cum[:], AF.Exp)
                S.tt(v3(e1p), v3(e1), ebr[:, :].unsqueeze(2).broadcast_to([4, NCH, 64]), ALU.mult)
                for h in range(4):
                    S.mm(PS[2][:, 0:NCH], sel4[0:4, h, :], ebr[0:4, :])
                    S.copy(ebB[:, d, h, :], PS[2][:, 0:NCH], eng="act")
                for i in range(34):
                    pt = PS[3 + i % 2]
                    for qi, src_t in enumerate((e1, e1p, thr)):
                        S.transpose(pt[:, qi * 4:(qi + 1) * 4], src_t[0:4, i * 128:(i + 1) * 128], identF[0:4, 0:4])
                    S.copy(tokq[:, i, d * 12:(d + 1) * 12], pt[:, 0:12], eng="act")
        S.fence()
        if stop == "B":
            dbg = nc.dram_tensor("tokqo", [128, 34, 24], F32, kind="ExternalOutput").ap()
            outs.append(S.dma(dbg, tokq[:]))
            dbg2 = nc.dram_tensor("ebBo", [128, 2, 4, NCH], F32, kind="ExternalOutput").ap()
            outs.append(S.dma(dbg2, ebB[:]))
            S.emit(st, final_wait=outs)
            return nc
        with ExitStack() as sc:
            convw = C.sb("convw1", [128, 4, 2, 3], F32, sc); S.dma(convw[:], A["convw1"])
            convb = C.sb("convb1", [128, 4, 2], F32, sc); S.dma(convb[:], A["convb1"])
            hn = C.sb("hn", [128, 256], F32, sc)
            maskF = C.sb("maskF", [128, 128], F32, sc); S.dma(maskF[:], A["maskF"])
            maskB = C.sb("maskB", [128, 128], F32, sc); S.dma(maskB[:], A["maskB"])
            masks = (maskF, maskB)
            wbuf = C.sb("wbuf", [128, 8, 512], BF16, sc)
            wq = wbuf[:, :, 0:128]
            wk = wbuf[:, :, 128:256]
            wv = wbuf[:, :, 256:512]
            woz = wbuf
            SEG = NLAT + 2 + NCTX + 2
            ust = C.sb("ust", [128, SEG], BF16, sc); S.memset(ust[:], 0.0)
            ctmp = C.sb("ctmp", [128, 1024], F32, sc)
            qT = C.sb("qT", [128, NTOK], BF16, sc)
            kT = C.sb("kT", [128, NTOK], BF16, sc)
            ktD = [C.sb(f"ktD{d}", [128, 34, 128], BF16, sc) for d in range(2)]
            vaug = C.sb("vaug", [128, 34, 257], BF16, sc)
            S.memset(vaug[:, :, 256:257], 1.0)
            hsum = C.sb("hsum", [128, 32, 256], F32, sc)
            Cst = [C.sb(f"Cst{d}", [128, 257], F32, sc) for d in range(2)]
            Cbf = [C.sb(f"Cbf{d}", [128, 257], BF16, sc) for d in range(2)]
            Sm = [C.sb(f"Sm{d}", [128, 128], BF16, sc) for d in range(2)]
            dm = [C.sb(f"dm{d}", [128, 1], F32, sc) for d in range(2)]
            fin = [(C.sb("so", [128, 256], F32, sc), C.sb("sz", [128, 256], F32, sc), C.sb("hh", [128, 256], F32, sc),
                    C.sb("junk3", [128, 256], BF16, sc), C.sb("ssq3", [128, 1], F32, sc), C.sb("mixv", [128, 256], BF16, sc))
                   for _ in range(2)]
            mob = [C.sb(f"mob{i}", [128, 2, 512], BF16, sc) for i in range(2)]
            segs = [(k * 1024, k * 1024 + 1, 1024) for k in range(4)] + [(4096, NLAT + 3, 256)]

            def ust_cols(blk):
                t0, n = blk
                return slice(t0 + 1, t0 + 1 + n) if t0 < NLAT else slice(NLAT + 3, NLAT + 3 + n)

            for h in range(n_heads):
                S.dma(wq, A["wq"][h], q="pool")
                S.dma(wk, A["wk"][h], q="pool")
                S.dma(wv, A["wv"][h], q="pool")
                S.dma(hn[:], A["hn"][:, h, :])
                S.memset(hsum[:], 0.0, eng="pool")
                for (w, s, dstT) in ((wq, 0, qT), (wk, 1, kT)):
                    for bi, blk in enumerate(blocks):
                        t0, n = blk
                        pt = PS[bi % 2]
                        for kt in range(8):
                            S.mm(pt[:, 0:n], w[:, kt, :], h1T[:, kt, t0:t0 + n], start=(kt == 0), stop=(kt == 7))
                        S.act(ust[:, ust_cols(blk)], pt[:, 0:n], AF.Copy)
                    for (t0, u0, n) in segs:
                        S.act(ctmp[:, 0:n], ust[:, u0:u0 + n], AF.Identity, bias=convb[:, h, s:s + 1], scale=convw[:, h, s, 1:2])
                        S.stt(ctmp[:, 0:n], ust[:, u0 - 1:u0 - 1 + n], convw[:, h, s, 0:1], ctmp[:, 0:n], ALU.mult, ALU.add)
                        S.stt(ctmp[:, 0:n], ust[:, u0 + 1:u0 + 1 + n], convw[:, h, s, 2:3], ctmp[:, 0:n], ALU.mult, ALU.add)
                        S.act(dstT[:, t0:t0 + n], ctmp[:, 0:n], AF.Silu)
                for i4 in range(0, 34, 8):
                    nt = min(8, 34 - i4)
                    for j in range(nt):
                        i = i4 + j
                        S.transpose(pTb[:, j * 128:(j + 1) * 128], kT[:, i * 128:(i + 1) * 128], identB[:])
                    for j in range(nt):
                        i = i4 + j
                        for d in range(2):
                            S.act(ktD[d][:, i, :], pTb[:, j * 128:(j + 1) * 128], AF.Identity, scale=tokq[:, i, d * 12 + 4 + h:d * 12 + 5 + h])
                for i in range(34):
                    pt = PS[i % 2]
                    for kt in range(8):
                        S.mm(pt[:, 0:256], h1T[:, kt, i * 128:(i + 1) * 128], wbuf[:, kt, 256:512], start=(kt == 0), stop=(kt == 7))
                    S.act(vaug[:, i, 0:256], pt[:, 0:256], AF.Copy)
                for d in range(2):
                    S.memset(Cst[d][:], 0.0)
                    S.memset(Cbf[d][:], 0.0, eng="pool")
                orderF = [32, 33] + list(range(32))
                orderB = [33, 32] + list(range(31, -1, -1))
                for step in range(34):
                    for d, tile_i in ((0, orderF[step]), (1, orderB[step])):
                        i = tile_i
                        is_lat = i < 32
                        pS, pN, pU = PS[3 * d], PS[3 * d + 1], PS[3 * d + 2]
                        tk = slice(i * 128, (i + 1) * 128)
                        chunks = (0, 1) if d == 0 else (1, 0)
                        if is_lat:
                            S.mm(pS[:, 0:128], kT[:, tk], qT[:, tk])
                            S.stt(Sm[d][:], pS[:, 0:128], tokq[:, i, d * 12 + h:d * 12 + h + 1], masks[d][:], ALU.mult, ALU.mult)
                            S.mm(pN[:, 0:257], Sm[d][:], vaug[:, i, :], start=True, stop=False)
                        for ci, c in enumerate(chunks):
                            rows = slice(c * 64, (c + 1) * 64)
                            ch = i * 2 + c
                            if is_lat:
                                S.mm(pN[rows, 0:257], qT[:, i * 128 + c * 64:i * 128 + (c + 1) * 64], Cbf[d][:, :],
                                     start=False, stop=True)
                            S.mm(pU[:, 0:257], ktD[d][rows, i, :], vaug[rows, i, :])
                            S.stt(Cst[d][:], Cst[d][:], ebB[:, d, h, ch:ch + 1], pU[:, 0:257], ALU.mult, ALU.add)
                            S.act(Cbf[d][:], Cst[d][:], AF.Copy)
                        if is_lat:
                            S.act(dm[d][:], pN[:, 256:257], AF.Abs)
                            S.ts(dm[d][:], dm[d][:], tokq[:, i, d * 12 + 8 + h:d * 12 + 9 + h], None, ALU.max)
                            S.recip(dm[d][:], dm[d][:])
                            S.stt(hsum[:, i, :], pN[:, 0:256], dm[d][:, 0:1], hsum[:, i, :], ALU.mult, ALU.add)
                S.dma(woz[:], A["woz"][h], q="pool")
                for i in range(32):
                    pt = PS[i % 6]
                    so, sz, hh, junk3, ssq3, mixv = fin[i % 2]
                    for kt in range(8):
                        S.mm(pt[:, :], h1T[:, kt, i * 128:(i + 1) * 128], woz[:, kt, :], start=(kt == 0), stop=(kt == 7))
                    S.act(so[:], pt[:, 0:256], AF.Sigmoid)
                    S.act(sz[:], pt[:, 256:512], AF.Silu)
                    S.tt(hh[:], hsum[:, i, :], so[:], ALU.mult)
                    S.act(junk3[:], hh[:], AF.Square, accum_out=ssq3[:, 0:1])
                    S.act(ssq3[:], ssq3[:], AF.Sqrt, bias=EPS, scale=1.0 / 256)
                    S.recip(ssq3[:], ssq3[:])
                    S.tt(sz[:], sz[:], hn[:], ALU.mult, eng="pool")
                    S.stt(mixv[:], hh[:], ssq3[:, 0:1], sz[:], ALU.mult, ALU.mult)
                    mo_ = mob[(i // 4) % 2]
                    for j in range(2):
                        S.transpose(pTb[:, j * 128:(j + 1) * 128], mixv[:, j * 128:(j + 1) * 128], identB[:])
                    S.copy(mo_[:, :, (i % 4) * 128:(i % 4 + 1) * 128],
                           pTb[:, 0:256].rearrange("p (j t) -> p j t", j=2), eng="act")
                    if i % 4 == 3:
                        t0 = (i // 4) * 512
                        dst = mix1.cols(h * 256, (h + 1) * 256, t0, 512).rearrange("(j p) t -> p j t", p=128)
                        outs.append(S.dma(dst, mo_[:, :, :], q="sp"))
        if shared is None:
            S.emit(st, final_wait=outs)
    return nc


def p3_host_inputs(inp, b, hf, mix1g, x1):
    f = lambda a: np.ascontiguousarray(a, dtype=np.float32)
    o = {}
    o["mix1g"] = np.ascontiguousarray(mix1g[:, hf * 2048:(hf + 1) * 2048])
    o["x1loc"] = f(x1[hf * 2048:(hf + 1) * 2048])
    cond = np.stack([inp["c"][b], inp["c_ctx"]], -1)
    o["condT"] = f(cond.reshape(8, 128, 2).transpose(1, 0, 2))
    o["l1_wgate"] = f(inp["w_mod"][1][:, 2048:3072].reshape(8, 128, 1024))
    o["l1_bgate2"] = f(np.broadcast_to(inp["b_mod"][1][2048:3072][None], (2, 1024)))
    o["l1_gpost2"] = f(np.broadcast_to(inp["g_post"][1][None], (2, 1024)))
    sel2 = np.zeros((2, 2, 128), np.float32); sel2[0, 0] = 1; sel2[1, 1] = 1
    o["sel2"] = sel2
    o["wout1"] = f(inp["o_w_out"][0].reshape(16, 128, 1024))
    o["identF"] = np.eye(128, dtype=np.float32)
    return o


def build_p3(nc, sample, shared=None):
    if shared is None:
        A = _declare_inputs(nc, sample)
        yo = nc.dram_tensor("yo", [2048, D], F32, kind="ExternalOutput").ap()
        ntile = 16
    else:
        A = shared["A"]
        yo = shared["yo"]
        ntile = 32
    with ExitStack() as st:
        if shared is None:
            S = Sched(nc)
            C = Ctx(nc, st)
            PS = [C.ps(f"pb{i}", [128, 512], F32) for i in range(7)]
        else:
            S, C, PS = shared["S"], shared["C"], shared["PS"]
            C.stack = st
        identF = C.sb("identF", [128, 128], F32); S.dma(identF[:], A["identF"])
        onesF = C.sb("onesF", [128, 128], F32); S.memset(onesF[:], 1.0)
        condT = C.sb("condT", [128, 8, 2], F32); S.dma(condT[:], A["condT"])
        scond = C.sb("scond", [128, 8, 2], F32)
        S.act(scond[:], condT[:], AF.Silu)
        gg = gate_rows(S, C, PS, A, "l1_", scond, onesF, st)
        tiles = [(i, i * 128, 0) for i in range(ntile)]
        outs = outproj_residual(S, C, PS, A, st, A["mix1g"], A["wout1"], A["x1loc"], tiles, gg, yo, identF, None)
        if shared is None:
            S.emit(st, final_wait=outs)
        else:
            shared["outs"] += outs
    return nc


CORES = [(b, hf) for b in range(4) for hf in range(2)]


def _launch(build, maps, **kw):
    nc = bass.Bass("TRN2", target_bir_lowering=False)
    build(nc, maps[0], **kw)
    res = run_bass_kernel_spmd(nc, maps, core_ids=list(range(len(maps))))
    return res.results


GROUPS = [[0, 1], [2, 3], [4, 5], [6, 7]]


def fused_host_inputs(inp, b, hf):
    o = {}
    dummy_mix0 = np.zeros((2048, NTOK), NPBF)
    p2 = p2_host_inputs(inp, b, hf, dummy_mix0)
    p3 = p3_host_inputs(inp, b, hf, np.zeros((2048, NLAT), NPBF), np.zeros((NLAT, D), np.float32))
    for d in (p3, p2, p1_host_inputs(inp, b, hf)):
        o.update(d)
    for k in ("mix0g", "mix1g", "x1loc"):
        o.pop(k)
    return o


def build_fused(nc, sample):
    A = _declare_inputs(nc, sample)
    yo = nc.dram_tensor("yo", [NLAT, D], F32, kind="ExternalOutput").ap()
    CS = 1024
    mix0 = Chunked.make(nc, "mix0", 1024, NTOK, BF16, CS, kind="Internal")
    mix0g = Chunked.make(nc, "mix0g", 2048, NTOK, BF16, CS, kind="Internal", addr_space="Local")
    x1o = nc.dram_tensor("x1o", [NTOK, D], F32, kind="Internal").ap()
    mix1 = Chunked.make(nc, "mix1", 1024, NLAT, BF16, CS, kind="Internal")
    mix1g = Chunked.make(nc, "mix1g", 2048, NLAT, BF16, CS, kind="Internal", addr_space="Local")
    A["mix0g"] = mix0g
    A["mix1g"] = mix1g
    A["x1loc"] = x1o[0:NLAT, :]
    with ExitStack() as st:
        S = Sched(nc)
        C = Ctx(nc, st)
        PS = [C.ps(f"pb{i}", [128, 512], F32) for i in range(7)]
        pTb = C.ps("pTb", [128, 1024], BF16)
        sh = dict(S=S, C=C, PS=PS, pTb=pTb, A=A, mix0=mix0, x1o=x1o, mix1=mix1, yo=yo, outs=[])
        import os
        nocc = os.environ.get("K_FUSE_NOCC") == "1"
        light = os.environ.get("K_P1_LIGHT") == "1"

        def exchange(gch, lch):
            for (_, _, go), (_, _, gi_) in zip(gch.chunks, lch.chunks):
                if nocc:
                    S.dma(go[0:1024, :], gi_, q="sp")
                    S.dma(go[1024:2048, :], gi_, q="act")
                else:
                    S.allgather(go, gi_, GROUPS)
        if light:
            build_p1(nc, sample, shared=sh, n_ct=1, do_att=False)
        else:
            build_p1(nc, sample, shared=sh)
        S.fence()
        exchange(mix0g, mix0)
        build_p2(nc, sample, shared=sh, n_heads=(1 if light else 4))
        S.fence()
        exchange(mix1g, mix1)
        build_p3(nc, sample, shared=sh)
        C.stack = st
        S.emit(st, final_wait=sh["outs"])
        print("fused stats", S.stats, flush=True)
    return nc


def kernel(**inputs):
    inp = {k: np.asarray(v) for k, v in inputs.items()}
    res = _launch(build_fused, [fused_host_inputs(inp, b, hf) for (b, hf) in CORES])
    out = np.zeros((4, NLAT, D), np.float32)
    for ci, (b, hf) in enumerate(CORES):
        out[b, hf * 2048:(hf + 1) * 2048] = np.asarray(res[ci]["yo"])[hf * 2048:(hf + 1) * 2048]
    return out
```

```python
import math
from contextlib import ExitStack
import numpy as np
import ml_dtypes
import concourse.bass as bass
import concourse.mybir as mybir
from concourse.bass_utils import run_bass_kernel_spmd

F32 = mybir.dt.float32
BF16 = mybir.dt.bfloat16
AF = mybir.ActivationFunctionType
ALU = mybir.AluOpType
AX = mybir.AxisListType
NPBF = ml_dtypes.bfloat16

D = 1024
NLAT = 4096
NCTX = 256
NTOK = NLAT + NCTX
EPS = 1e-6

ENGS = ("pe", "act", "dve", "pool", "sp")
SEM_ROT = 12000
ND = 8
CC_INC = 1
_DT_SIZE = {}


def _dsize(dt):
    if dt not in _DT_SIZE:
        _DT_SIZE[dt] = mybir.dt.size(dt)
    return _DT_SIZE[dt]


def _box(ap):
    t = ap.tensor
    dims = list(ap.ap)
    off = ap.offset
    sp = str(ap.space)
    if sp in ("SB", "PSUM"):
        pstep = 1
        for s in t.shape[1:]:
            pstep *= s
        p0 = off // pstep
        f0 = off % pstep
        pd = dims[0]
        npart = 1 if pd[0] == 0 else pd[1]
        ext = 0
        for st, cnt in dims[1:]:
            ext += (cnt - 1) * abs(st)
        if sp == "PSUM":
            return (t.name, 0, 128, 0, pstep)
        return (t.name, p0, p0 + npart, f0, f0 + ext + 1)
    ext = 0
    for st, cnt in dims:
        ext += (cnt - 1) * abs(st)
    return (t.name, 0, 1, off, off + ext + 1)


class Sched:
    def __init__(self, nc, same_engine_sync=None):
        import os
        if same_engine_sync is None:
            same_engine_sync = os.environ.get('K_SAME', '1') == '1'
        self.nc = nc
        self.ins = []
        self.track = {}
        self.same = same_engine_sync
        self.last_cp = {e: None for e in ENGS}
        self.last_dm = {e: [] for e in ENGS}
        self.pending = {e: set() for e in ENGS}

    def _deps(self, reads, writes, idx):
        deps = set()
        rb = [_box(a) for a in reads]
        wb = [_box(a) for a in writes]
        for b in rb:
            for ent in self.track.get(b[0], ()):
                e = ent[0]
                if e[1] < b[2] and b[1] < e[2] and e[3] < b[4] and b[3] < e[4]:
                    if ent[1] is not None:
                        deps.add(ent[1])
        for b in wb:
            for ent in self.track.get(b[0], ()):
                e = ent[0]
                if e[1] < b[2] and b[1] < e[2] and e[3] < b[4] and b[3] < e[4]:
                    if ent[1] is not None:
                        deps.add(ent[1])
                    deps.update(ent[2])
        for b in rb:
            lst = self.track.setdefault(b[0], [])
            for ent in lst:
                if ent[0] == b:
                    ent[2].append(idx)
                    break
            else:
                lst.append([b, None, [idx]])
        for b in wb:
            lst = self.track.get(b[0], [])
            keep = []
            for ent in lst:
                e = ent[0]
                if b[1] <= e[1] and e[2] <= b[2] and b[3] <= e[3] and e[4] <= b[4]:
                    continue
                keep.append(ent)
            keep.append([b, idx, []])
            self.track[b[0]] = keep
        deps.discard(idx)
        return deps

    def add(self, eng, fn, r=(), w=(), dma=False):
        idx = len(self.ins)
        deps = self._deps(list(r), list(w), idx)
        if self.pending[eng]:
            deps |= self.pending[eng]
            self.pending[eng] = set()
        self.ins.append(dict(eng=eng, fn=fn, deps=deps, dma=dma, users=set()))
        if dma:
            lo = self.last_dm[eng]
            lo.append(idx)
            if len(lo) > ND:
                lo.pop(0)
        else:
            self.last_cp[eng] = idx
        return idx

    def fence(self):
        allp = set()
        for e in ENGS:
            allp.update(self.last_dm[e])
            if self.last_cp[e] is not None:
                allp.add(self.last_cp[e])
        for e in ENGS:
            self.pending[e] = set(allp) | self.pending[e]
        self.track = {}

    def mm(self, out, lhsT, rhs, start=True, stop=True, **kw):
        r = [lhsT, rhs]
        if not start:
            r.append(out)
        return self.add("pe", lambda e: e.matmul(out, lhsT, rhs, start=start, stop=stop, **kw), r=r, w=[out])

    def transpose(self, out, in_, ident):
        return self.add("pe", lambda e: e.transpose(out, in_, ident), r=[in_, ident], w=[out])

    def act(self, out, in_, func, bias=None, scale=None, accum_out=None):
        r = [in_]
        kw = {}
        if bias is not None:
            kw["bias"] = bias
            if not isinstance(bias, (int, float)):
                r.append(bias)
        if scale is not None:
            kw["scale"] = scale
            if not isinstance(scale, (int, float)):
                r.append(scale)
        w = [out]
        if accum_out is not None:
            kw["accum_out"] = accum_out
            w.append(accum_out)
        return self.add("act", lambda e: e.activation(out, in_, func, **kw), r=r, w=w)

    def tt(self, out, in0, in1, op, eng="dve"):
        return self.add(eng, lambda e: e.tensor_tensor(out, in0, in1, op), r=[in0, in1], w=[out])

    def ts(self, out, in0, s1, s2, op0, op1=None, eng="dve"):
        r = [in0]
        if not isinstance(s1, (int, float)):
            r.append(s1)
        if s2 is not None and not isinstance(s2, (int, float)):
            r.append(s2)
        if op1 is None:
            return self.add(eng, lambda e: e.tensor_scalar(out, in0, s1, None, op0), r=r, w=[out])
        return self.add(eng, lambda e: e.tensor_scalar(out, in0, s1, s2, op0, op1), r=r, w=[out])

    def stt(self, out, in0, scalar, in1, op0, op1):
        r = [in0, in1]
        if not isinstance(scalar, (int, float)):
            r.append(scalar)
        return self.add("dve", lambda e: e.scalar_tensor_tensor(out, in0, scalar, in1, op0, op1), r=r, w=[out])

    def copy(self, out, in_, eng="dve"):
        if eng == "act":
            return self.act(out, in_, AF.Copy)
        return self.add(eng, lambda e: e.tensor_copy(out, in_), r=[in_], w=[out])

    def memset(self, ap, val, eng="dve"):
        return self.add(eng, lambda e: e.memset(ap, val), r=[], w=[ap])

    def recip(self, out, in_):
        return self.add("dve", lambda e: e.reciprocal(out, in_), r=[in_], w=[out])

    def reduce(self, out, in_, op, axis=AX.X):
        return self.add("dve", lambda e: e.tensor_reduce(out, in_, axis, op), r=[in_], w=[out])

    def wrap(self, out, in_, t1, t2):
        self.ts(t1, in_, -math.pi, 2 * math.pi, ALU.is_lt, ALU.mult)
        self.ts(t2, in_, math.pi, -2 * math.pi, ALU.is_gt, ALU.mult)
        self.tt(t1, t1, t2, ALU.add, eng="pool")
        return self.tt(out, in_, t1, ALU.add)

    def allgather(self, out, in_, groups):
        idx = self.add("pool", lambda e: e.collective_compute("AllGather", ALU.bypass, replica_groups=groups,
                                                              ins=[in_], outs=[out]), r=[in_], w=[out], dma=True)
        self.ins[idx]["cc"] = True
        return idx

    def dma(self, out, in_, q="sp", **kw):
        return self.add(q, lambda e: e.dma_start(out, in_, **kw), r=[in_], w=[out], dma=True)

    def emit(self, stack, final_wait=()):
        nc = self.nc
        ins = self.ins
        for i, it in enumerate(ins):
            for d in it["deps"]:
                ins[d]["users"].add(i)

        def new_sem(tag):
            return stack.enter_context(nc.semaphore(tag))

        eng_sem = {e: [new_sem(f"s_{e}_0")] for e in ENGS}
        eng_cnt = {e: 0 for e in ENGS}
        dma_sems = {e: [new_sem(f"d_{e}_{k}") for k in range(ND)] for e in ("sp", "pool", "act")}
        dma_cnt = {e: 0 for e in ENGS}
        for i, it in enumerate(ins):
            e = it["eng"]
            it["sig"] = None
            it["pre"] = []
            if it.get("cc"):
                s = new_sem(f"cc_{i}")
                it["sig"] = (s, CC_INC, CC_INC)
            elif it["dma"]:
                k = dma_cnt[e]
                s = dma_sems[e][k % ND]
                it["sig"] = (s, 16 * (k // ND + 1), 16)
                if k >= ND:
                    it["pre"].append((s, 16 * (k // ND)))
                dma_cnt[e] = k + 1
            else:
                need = False
                for u in it["users"]:
                    ue = ins[u]["eng"]
                    if ue != e or ins[u]["dma"]:
                        need = True
                    elif self.same and e != "pe":
                        need = True
                if need:
                    if eng_cnt[e] >= SEM_ROT:
                        eng_sem[e].append(new_sem(f"s_{e}_{len(eng_sem[e])}"))
                        eng_cnt[e] = 0
                    eng_cnt[e] += 1
                    it["sig"] = (eng_sem[e][-1], eng_cnt[e], 1)
        self.stats = dict(n_ins=len(ins), sig={e: (len(eng_sem[e]) - 1) * SEM_ROT + eng_cnt[e] for e in ENGS},
                          dma=dict(dma_cnt), per_eng={e: sum(1 for it in ins if it["eng"] == e) for e in ENGS})
        waited = {e: {} for e in ENGS}
        streams = {e: [] for e in ENGS}
        for i, it in enumerate(ins):
            e = it["eng"]
            waits = list(it["pre"])
            for d in sorted(it["deps"]):
                pd = ins[d]
                if pd["sig"] is None:
                    continue
                if pd["eng"] == e and not pd["dma"] and (e == "pe" or not self.same):
                    continue
                waits.append((pd["sig"][0], pd["sig"][1]))
            ww = []
            for s, v in waits:
                key = id(s)
                if waited[e].get(key, 0) >= v:
                    continue
                waited[e][key] = v
                ww.append((s, v))
            streams[e].append((ww, it))
        fw = [(ins[d]["sig"][0], ins[d]["sig"][1]) for d in final_wait]
        self.n_ins = len(ins)

        def run(handle, lst, extra=()):
            for ww, it in lst:
                for s, v in ww:
                    handle.wait_ge(s, v)
                inst = it["fn"](handle)
                if it["sig"] is not None:
                    inst.then_inc(it["sig"][0], it["sig"][2])
            for s, v in extra:
                handle.wait_ge(s, v)

        block = stack.enter_context(nc.Block())

        @block.tensor
        def _(e):
            run(e, streams["pe"])

        @block.scalar
        def _(e):
            run(e, streams["act"])

        @block.vector
        def _(e):
            run(e, streams["dve"])

        @block.gpsimd
        def _(e):
            run(e, streams["pool"])

        @block.sync
        def _(e):
            run(e, streams["sp"], fw)


def fft_tables(N1):
    N = 64 * N1
    H = N1 // 2
    n1 = np.arange(N1)
    k1 = np.arange(N1)
    a1 = 2 * np.pi * np.outer(n1, k1) / N1
    CS1 = np.concatenate([np.cos(a1), -np.sin(a1)], 1)
    m = np.arange(128)
    n2m = m // 2
    c2m = m % 2
    aT = 2 * np.pi * np.outer(n2m, k1) / N
    a64 = 2 * np.pi * np.outer(n2m, n2m) / 64
    same = (c2m[:, None] == c2m[None, :])
    BdC = np.cos(a64) * same
    BdS = np.sin(a64) * same
    aI = 2 * np.pi * np.outer(k1, np.arange(64)) / N
    ai = 2 * np.pi * np.outer(k1, np.arange(H)) / N1
    f = lambda a: np.ascontiguousarray(a, dtype=np.float32)
    return dict(CS1=f(CS1), TwFc=f(np.cos(aT)), TwFs=f(np.sin(aT)), BdC=f(BdC), BdS=f(BdS), BdSn=f(-BdS),
                R1=f(np.concatenate([BdC, BdS], 1)), R2=f(np.concatenate([-BdS, BdC], 1)),
                TwIc=f(np.cos(aI)), TwIs=f(np.sin(aI)), Ci=f(np.cos(ai) / N), Sin_=f(-np.sin(ai) / N))


def filter_pos_tables(n):
    N = 2 * n
    N1 = N // 64
    t = np.linspace(0.0, 1.0, n, dtype=np.float32)
    bands = np.linspace(1e-4, 15.0, 16, dtype=np.float32)
    ang = (np.float32(2.0 * math.pi / n) * np.arange(n, dtype=np.float32)[:, None] * bands).astype(np.float32)
    z = np.concatenate([t[:, None], np.cos(ang), -np.sin(ang)], axis=-1).astype(np.float32)
    idx = np.arange(N)
    d = np.where(idx < n, idx, N - idx)
    d[n] = 0
    z2 = z[d]
    BIG = 1.0e4
    negF = np.where(idx < n, -t[d], -BIG).astype(np.float32)
    negB = np.where(idx > n, -t[d], -BIG).astype(np.float32)
    negt = np.stack([negF.reshape(N1, 64), negB.reshape(N1, 64)], axis=1)
    return np.ascontiguousarray(z2.T), np.ascontiguousarray(negt)


def rope_tables():
    nf = 32
    inv = (10000.0 ** (-np.arange(nf, dtype=np.float32) / nf)).astype(np.float32)
    j = np.arange(64, dtype=np.float32)
    dd = np.arange(128)
    ang = j[None, :] * inv[dd % 32][:, None]
    R = np.zeros((128, 128), np.float32)
    for dp in range(128):
        if dp % 64 < 32:
            R[dp, dp + 32] = -1.0
        else:
            R[dp, dp - 32] = 1.0
    return np.cos(ang).astype(np.float32), np.sin(ang).astype(np.float32), np.ascontiguousarray(R.T)


P1_SPEC = None


def p1_host_inputs(inp, b, hf):
    f = lambda a: np.ascontiguousarray(a, dtype=np.float32)
    o = {}
    o["x_all"] = f(np.concatenate([inp["x"][b], inp["ctx"][b]], 0))
    cond = np.stack([inp["c"][b], inp["c_ctx"]], -1)
    o["condT"] = f(cond.reshape(8, 128, 2).transpose(1, 0, 2))
    wm = inp["w_mod"][0][:, :2048]
    o["wmod"] = f(wm.reshape(8, 128, 16, 128).transpose(2, 1, 0, 3))
    o["bmodT"] = f(inp["b_mod"][0][:2048].reshape(16, 128).T)
    o["gpreT"] = f(inp["g_pre"][0].reshape(8, 128).T)
    W = inp["e_w_in"][0]
    cols = []
    cols.append(np.arange(5120 + hf * 128, 5120 + hf * 128 + 128))
    cols.append(np.arange(5376 + hf * 128, 5376 + hf * 128 + 128))
    for h in range(4):
        hd = 4 * hf + h
        cols.append(np.arange(4096 + hd * 128, 4096 + hd * 128 + 128))
        cols.append(np.arange(5632 + hd * 128, 5632 + hd * 128 + 128))
    for ct in range(4):
        c0 = hf * 512 + ct * 128
        cols.append(np.arange(1024 + c0, 1024 + c0 + 128))
        cols.append(np.arange(2048 + c0, 2048 + c0 + 128))
        cols.append(np.arange(c0, c0 + 128))
        cols.append(np.arange(3072 + c0, 3072 + c0 + 128))
    wt = np.stack([W[:, c] for c in cols], 0)
    o["w_in"] = f(wt.reshape(26, 8, 128, 128).transpose(0, 2, 1, 3))
    cw = inp["e_conv_w"][0]
    cb = inp["e_conv_b"][0]
    convw = np.zeros((128, 4, 3, 3), np.float32)
    convb = np.zeros((128, 4, 3), np.float32)
    for ct in range(4):
        c0 = hf * 512 + ct * 128
        for s in range(3):
            convw[:, ct, s, :] = cw[:, s * 1024 + c0: s * 1024 + c0 + 128].T
            convb[:, ct, s] = cb[s * 1024 + c0: s * 1024 + c0 + 128]
    o["convw"] = convw
    o["convb"] = convb
    o["gq"] = f(inp["e_q_norm"][0].reshape(128, 1))
    o["gk"] = f(inp["e_k_norm"][0].reshape(128, 1))
    rc, rs, rT = rope_tables()
    o["ropec"], o["ropes"], o["RmatT"] = rc, rs, rT
    z2l, ntl = filter_pos_tables(NLAT)
    z2c, ntc = filter_pos_tables(NCTX)
    o["z2T_l"], o["negt_l"], o["z2T_c"], o["negt_c"] = z2l, ntl, z2c, ntc
    o["fw1"] = f(inp["e_filt_w1"][0])
    o["fb1"] = f(inp["e_filt_b1"][0].reshape(64, 1))
    o["ffr"] = f(inp["e_filt_freq"][0].reshape(64, 1))
    o["fw2"] = f(inp["e_filt_w2"][0])
    o["fb2"] = f(inp["e_filt_b2"][0].reshape(64, 1))
    w3 = inp["e_filt_w3"][0].reshape(64, 2, 1024)
    o["fw3"] = f(w3[:, :, hf * 512:(hf + 1) * 512])
    min_decay = math.log(1e-2) / 1.5
    max_decay = math.log(1e-2) / 0.3
    deltas = np.abs(np.linspace(min_decay, max_decay, 1024, dtype=np.float32))
    o["adel"] = f(np.broadcast_to(deltas[hf * 512:(hf + 1) * 512][None, :], (128, 512)))
    o["hyb"] = f(inp["e_hy_bias"][0][hf * 512:(hf + 1) * 512].reshape(1, 512))
    for N1, tag in ((128, "l"), (8, "c")):
        for k, v in fft_tables(N1).items():
            o[f"ft_{tag}_{k}"] = v
    o["identF"] = np.eye(128, dtype=np.float32)
    return o


def _declare_inputs(nc, sample):
    aps = {}
    for k, v in sample.items():
        dt = F32 if v.dtype == np.float32 else BF16
        aps[k] = nc.dram_tensor(k, list(v.shape), dt, kind="ExternalInput").ap()
    return aps


class Chunked:
    def __init__(self, chunks, rows, ntok):
        self.chunks = chunks
        self.rows = rows
        self.ntok = ntok

    @staticmethod
    def make(nc, name, rows, ntok, dt, csize, **kw):
        ch = []
        for k, t0 in enumerate(range(0, ntok, csize)):
            n = min(csize, ntok - t0)
            ch.append((t0, n, nc.dram_tensor(f"{name}_{k}", [rows, n], dt, **kw).ap()))
        return Chunked(ch, rows, ntok)

    @staticmethod
    def wrap(ap):
        return Chunked([(0, ap.shape[1], ap)], ap.shape[0], ap.shape[1])

    def cols(self, r0, r1, t0, n):
        for (c0, cn, ap) in self.chunks:
            if c0 <= t0 and t0 + n <= c0 + cn:
                return ap[r0:r1, t0 - c0:t0 - c0 + n]
        raise AssertionError(("straddles chunks", t0, n))

    def pieces(self, r0, r1):
        for (c0, cn, ap) in self.chunks:
            yield c0, cn, ap[r0:r1, :]


class Ctx:
    def __init__(self, nc, stack):
        self.nc = nc
        self.stack = stack
        self.n = 0

    def sb(self, name, shape, dt, stack=None):
        self.n += 1
        return (stack or self.stack).enter_context(self.nc.sbuf_tensor(f"{name}_{self.n}", list(shape), dt))

    def ps(self, name, shape, dt, stack=None):
        self.n += 1
        return (stack or self.stack).enter_context(self.nc.psum_tensor(f"{name}_{self.n}", list(shape), dt))


def load_fft_tabs(S, C, A, tag, N1):
    H = N1 // 2
    T = {}
    def ld(name, shape, dt, q):
        t = C.sb(f"ft{tag}{name}", shape, dt)
        S.dma(t[:], A[f"ft_{tag}_{name}"], q=q)
        return t
    T["CS1"] = ld("CS1", [N1, 2 * N1], BF16, "pool")
    T["TwFc"] = ld("TwFc", [128, N1], F32, "sp")
    T["TwFs"] = ld("TwFs", [128, N1], F32, "sp")
    T["BdC"] = ld("BdC", [128, 128], BF16, "pool")
    T["BdS"] = ld("BdS", [128, 128], BF16, "pool")
    T["BdSn"] = ld("BdSn", [128, 128], BF16, "pool")
    T["R1"] = ld("R1", [128, 256], BF16, "pool")
    T["R2"] = ld("R2", [128, 256], BF16, "pool")
    T["TwIc"] = ld("TwIc", [N1, 64], F32, "sp")
    T["TwIs"] = ld("TwIs", [N1, 64], F32, "sp")
    T["Ci"] = ld("Ci", [N1, H], BF16, "pool")
    T["Sin_"] = ld("Sin_", [N1, H], BF16, "pool")
    return T


def fft_conv_core(S, C, PS, T, N1, X, Xf, Ball_re, Ball_im, tmps, evac):
    H = N1 // 2
    W2 = 2 * N1
    Gf = 512 // W2
    NG = 64 // Gf
    GN = Gf * N1
    tc = T["TwFc"][:, :].unsqueeze(1).broadcast_to([128, 2 * Gf, N1])
    tsn = T["TwFs"][:, :].unsqueeze(1).broadcast_to([128, 2 * Gf, N1])
    tic = T["TwIc"][:, :].unsqueeze(1).unsqueeze(3).broadcast_to([N1, 4, 64, 2])
    tis = T["TwIs"][:, :].unsqueeze(1).unsqueeze(3).broadcast_to([N1, 4, 64, 2])

    def stageA(g):
        m1, m2, Ar, Ai, Br_, Bi_, Fh, Yr, Yi = tmps[g % 2]
        pg1, pf1 = PS[2 * (g % 2)], PS[2 * (g % 2) + 1]
        p0 = g * Gf
        for j in range(Gf):
            S.mm(pg1[:, j * W2:(j + 1) * W2], X[0:H, p0 + j, :], T["CS1"][0:H, :])
        for j in range(Gf):
            S.mm(pf1[:, j * W2:(j + 1) * W2], Xf[0:N1, p0 + j, :], T["CS1"][0:N1, :])
        m1a = m1[:, :].rearrange("p (a n) -> p a n", n=N1)
        m2a = m2[:, :].rearrange("p (a n) -> p a n", n=N1)
        m1v = m1[:, :].rearrange("p (a r n) -> p a r n", r=2, n=N1)
        m2v = m2[:, :].rearrange("p (a r n) -> p a r n", r=2, n=N1)
        for (pp, ar, ai) in ((pg1, Ar, Ai), (pf1, Br_, Bi_)):
            v = pp[:, :].rearrange("p (a n) -> p a n", n=N1)
            S.tt(m1a, v, tc, ALU.mult)
            S.tt(m2a, v, tsn, ALU.mult)
            arv = ar[:, :].rearrange("p (a n) -> p a n", n=N1)
            aiv = ai[:, :].rearrange("p (a n) -> p a n", n=N1)
            S.tt(arv, m1v[:, :, 0, :], m2v[:, :, 1, :], ALU.add, eng="pool")
            S.tt(aiv, m1v[:, :, 1, :], m2v[:, :, 0, :], ALU.subtract, eng=("dve" if pp is pf1 else "pool"))

    def stageB(g):
        m1, m2, Ar, Ai, Br_, Bi_, Fh, Yr, Yi = tmps[g % 2]
        pg3, pf3 = PS[4], PS[5]
        for (pp, ar, ai) in ((pg3, Ar, Ai), (pf3, Br_, Bi_)):
            S.mm(pp[:, 0:GN], T["BdC"][:], ar[:, :], start=True, stop=False)
            S.mm(pp[:, 0:GN], T["BdS"][:], ai[:, :], start=False, stop=True)
            S.mm(pp[:, GN:2 * GN], T["BdC"][:], ai[:, :], start=True, stop=False)
            S.mm(pp[:, GN:2 * GN], T["BdSn"][:], ar[:, :], start=False, stop=True)
        S.act(Fh[:, :], pf3[:, :], AF.Copy)
        gv = pg3[:, :].rearrange("p (r n) -> p r n", r=2)
        fr = Fh[:, 0:GN].unsqueeze(1).broadcast_to([128, 2, GN])
        fi = Fh[:, GN:2 * GN].unsqueeze(1).broadcast_to([128, 2, GN])
        q1 = m1[:, :].rearrange("p (r n) -> p r n", r=2)
        q2 = m2[:, :].rearrange("p (r n) -> p r n", r=2)
        S.tt(q1, gv, fr, ALU.mult)
        S.tt(q2, gv, fi, ALU.mult)
        S.tt(Yr[:, :], q1[:, 0, :], q2[:, 1, :], ALU.subtract, eng="pool")
        S.tt(Yi[:, :], q2[:, 0, :], q1[:, 1, :], ALU.add, eng="dve")

    def stageC(g):
        m1, m2, Ar, Ai, Br_, Bi_, Fh, Yr, Yi = tmps[g % 2]
        pI = PS[6]
        for sub in range(Gf // 2):
            pr0 = g * Gf + 2 * sub
            for jj in range(2):
                j = 2 * sub + jj
                S.mm(pI[0:N1, jj * 256:(jj + 1) * 256], Yr[:, j * N1:(j + 1) * N1], T["R1"][:], start=True, stop=False)
                S.mm(pI[0:N1, jj * 256:(jj + 1) * 256], Yi[:, j * N1:(j + 1) * N1], T["R2"][:], start=False, stop=True)
            iv = pI[0:N1, :].rearrange("p (a n c) -> p a n c", a=4, c=2)
            n1v = m1[0:N1, :].rearrange("p (a n c) -> p a n c", a=4, c=2)
            n2v = m2[0:N1, :].rearrange("p (a n c) -> p a n c", a=4, c=2)
            S.tt(n1v, iv, tic, ALU.mult)
            S.tt(n2v, iv, tis, ALU.mult)
            n1p = m1[0:N1, :].rearrange("p (j r n c) -> p j r n c", j=2, r=2, c=2)
            n2p = m2[0:N1, :].rearrange("p (j r n c) -> p j r n c", j=2, r=2, c=2)
            ore = Ball_re[0:N1, :, 2 * pr0:2 * pr0 + 4].rearrange("p n (j c) -> p j n c", c=2)
            oim = Ball_im[0:N1, :, 2 * pr0:2 * pr0 + 4].rearrange("p n (j c) -> p j n c", c=2)
            S.tt(ore, n1p[:, :, 0, :, :], n2p[:, :, 1, :, :], ALU.subtract, eng="pool")
            S.tt(oim, n2p[:, :, 0, :, :], n1p[:, :, 1, :, :], ALU.add, eng="pool")

    stageA(0)
    for g in range(NG):
        if g + 1 < NG:
            stageA(g + 1)
        stageB(g)
        stageC(g)
    ng = min(64, 512 // H)
    k = 0
    for n2_0 in range(0, 64, ng):
        po = PS[k % 2]
        k += 1
        for j in range(ng):
            n2 = n2_0 + j
            S.mm(po[:, j * H:(j + 1) * H], Ball_re[0:N1, n2, :], T["Ci"][0:N1, :], start=True, stop=False)
            S.mm(po[:, j * H:(j + 1) * H], Ball_im[0:N1, n2, :], T["Sin_"][0:N1, :], start=False, stop=True)
        evac(po, n2_0, ng)


def build_p1(nc, sample, n_ct=4, do_att=True, do_hy=True, dbg=None, stop=None, shared=None):
    if shared is None:
        A = _declare_inputs(nc, sample)
        mix = Chunked.wrap(nc.dram_tensor("mix", [1024, NTOK], BF16, kind="ExternalOutput").ap())
    else:
        A = shared["A"]
        mix = shared["mix0"]
    xf_l = nc.dram_tensor("xf_l", [4, 128, 8192], BF16, kind="Internal").ap()
    xf_c = nc.dram_tensor("xf_c", [4, 8, 8192], BF16, kind="Internal").ap()
    outs = []
    with ExitStack() as st:
        if shared is None:
            S = Sched(nc)
            C = Ctx(nc, st)
            PS = [C.ps(f"pb{i}", [128, 512], F32) for i in range(7)]
            pTb = C.ps("pTb", [128, 1024], BF16)
        else:
            S, C, PS, pTb = shared["S"], shared["C"], shared["PS"], shared["pTb"]
            C.stack = st
        identF = C.sb("identF", [128, 128], F32)
        identB = C.sb("identB", [128, 128], BF16)
        onesF = C.sb("onesF", [128, 128], F32)
        onesB = C.sb("onesB", [128, 128], BF16)
        S.dma(identF[:], A["identF"])
        S.dma(identB[:], A["identF"], q="pool")
        S.memset(onesF[:], 1.0)
        S.memset(onesB[:], 1.0, eng="pool")
        rnT = C.sb("rnT", [128, 4, 2], F32)
        TL = load_fft_tabs(S, C, A, "l", 128)
        TC = load_fft_tabs(S, C, A, "c", 8)
        ftmp = []
        for par in range(2):
            ftmp.append((C.sb("m1", [128, 512], F32), C.sb("m2", [128, 512], F32),
                         C.sb("Ar", [128, 256], BF16), C.sb("Ai", [128, 256], BF16),
                         C.sb("Br", [128, 256], BF16), C.sb("Bi", [128, 256], BF16),
                         C.sb("Fh", [128, 512], F32),
                         C.sb("Yr", [128, 256], BF16), C.sb("Yi", [128, 256], BF16)))
        if stop == 'const':
            S.emit(st, final_wait=[])
            return nc

        import os
        SKIP = os.environ.get("K_SKIP", "")
        if do_hy and "F" not in SKIP:
            with ExitStack() as s0:
                fw1 = C.sb("fw1", [33, 64], F32, s0); S.dma(fw1[:], A["fw1"])
                fw2 = C.sb("fw2", [64, 64], F32, s0); S.dma(fw2[:], A["fw2"])
                fb1 = C.sb("fb1", [64, 1], F32, s0); S.dma(fb1[:], A["fb1"])
                fb2 = C.sb("fb2", [64, 1], F32, s0); S.dma(fb2[:], A["fb2"])
                ffr = C.sb("ffr", [64, 1], F32, s0); S.dma(ffr[:], A["ffr"])
                fw3 = C.sb("fw3", [64, 2, 512], BF16, s0); S.dma(fw3[:], A["fw3"], q="pool")
                adel = C.sb("adel", [128, 512], F32, s0); S.dma(adel[:], A["adel"])
                hyb = C.sb("hyb", [1, 512], F32, s0); S.dma(hyb[:], A["hyb"])
                frb1 = C.sb("frb1", [64, 1], F32, s0)
                frb2 = C.sb("frb2", [64, 1], F32, s0)
                S.tt(frb1[:], fb1[:], ffr[:], ALU.mult)
                S.tt(frb2[:], fb2[:], ffr[:], ALU.mult)
                hdn2 = C.sb("hdn2", [64, 8192], BF16, s0)
                z2 = C.sb("z2", [33, 512], F32, s0)
                u1 = C.sb("u1", [64, 512], F32, s0)
                h1 = C.sb("h1", [64, 512], F32, s0)
                wt1 = C.sb("wt1", [64, 512], F32, s0)
                wt2 = C.sb("wt2", [64, 512], F32, s0)
                negt = C.sb("negt", [128, 2, 64], F32, s0)
                wbuf_s = [C.sb("wbuf", [128, 2, 512], F32, s0) for _ in range(2)]
                fgrp_s = [C.sb("fgrp", [128, 512], F32, s0) for _ in range(2)]
                fgr2_s = [C.sb("fgr2", [128, 512], F32, s0) for _ in range(2)]
                acc = C.sb("acc", [128, 512], F32, s0)
                nrow = C.sb("nrow", [1, 512], F32, s0)
                ncol = C.sb("ncol", [128, 1], F32, s0)
                Xfo = C.sb("Xfo", [128, 4, 64, 128], BF16, s0)
                for (tag, N1, z2T, ntab, xfd, li) in (("l", 128, A["z2T_l"], A["negt_l"], xf_l, 0),
                                                      ("c", 8, A["z2T_c"], A["negt_c"], xf_c, 1)):
                    N = 64 * N1
                    S.dma(negt[0:N1, :, :], ntab)
                    for blk in range(N // 512):
                        sl = slice(blk * 512, (blk + 1) * 512)
                        S.dma(z2[:], z2T[:, sl])
                        S.mm(PS[5][0:64, :], fw1[:], z2[:])
                        S.ts(u1[:], PS[5][0:64, :], ffr[:, 0:1], frb1[:, 0:1], ALU.mult, ALU.add)
                        S.wrap(u1[:], u1[:], wt1[:], wt2[:])
                        S.wrap(u1[:], u1[:], wt1[:], wt2[:])
                        S.act(h1[:], u1[:], AF.Sin)
                        S.mm(PS[6][0:64, :], fw2[:], h1[:])
                        S.ts(u1[:], PS[6][0:64, :], ffr[:, 0:1], frb2[:, 0:1], ALU.mult, ALU.add)
                        S.wrap(u1[:], u1[:], wt1[:], wt2[:])
                        S.wrap(u1[:], u1[:], wt1[:], wt2[:])
                        S.act(hdn2[:, sl], u1[:], AF.Sin)
                    S.memset(acc[0:N1, :], 0.0)
                    for n2 in range(64):
                        wbuf, fgrp, fgr2 = wbuf_s[n2 % 2], fgrp_s[n2 % 2], fgr2_s[n2 % 2]
                        pF = (PS[1 + 2 * (n2 % 2)], PS[2 + 2 * (n2 % 2)])
                        for dr in range(2):
                            S.mm(pF[dr][0:N1, :], hdn2[:, n2:N:64], fw3[:, dr, :])
                            S.act(wbuf[0:N1, dr, :], adel[0:N1, :], AF.Exp, scale=negt[0:N1, dr, n2:n2 + 1])
                        S.tt(fgrp[0:N1, :], pF[0][0:N1, :], wbuf[0:N1, 0, :], ALU.mult)
                        S.tt(fgr2[0:N1, :], pF[1][0:N1, :], wbuf[0:N1, 1, :], ALU.mult)
                        S.tt(fgrp[0:N1, :], fgrp[0:N1, :], fgr2[0:N1, :], ALU.add, eng="pool")
                        S.act(fgr2[0:N1, :], fgrp[0:N1, :], AF.Abs)
                        S.tt(acc[0:N1, :], acc[0:N1, :], fgr2[0:N1, :], ALU.add, eng="pool")
                        ov = Xfo[0:N1, :, :, 2 * n2:2 * n2 + 2]
                        iv = fgrp[0:N1, :].rearrange("p (t a c) -> p t a c", t=4, c=2)
                        S.copy(ov, iv, eng="act")
                    S.mm(PS[6][0:1, 0:512], onesF[0:N1, 0:1], acc[0:N1, :])
                    S.tt(nrow[:], PS[6][0:1, 0:512], hyb[0:1, :], ALU.mult)
                    for ct in range(n_ct):
                        S.mm(PS[5][0:128, 0:1], acc[0:N1, ct * 128:(ct + 1) * 128], onesF[0:N1, 0:1])
                        S.copy(ncol[:], PS[5][0:128, 0:1])
                        S.recip(rnT[:, ct, li:li + 1], ncol[:])
                        tap = Xfo[0:1, ct, :, 0:2]
                        S.tt(tap, tap, nrow[0:1, ct * 128:(ct + 1) * 128].rearrange("p (a c) -> p a c", c=2), ALU.add)
                        S.dma(xfd[ct, :, :], Xfo[0:N1, ct, :, :].rearrange("p a n -> p (a n)"))
            S.fence()

        hT = C.sb("hT", [128, 8, NTOK], BF16)
        with ExitStack() as s1:
            condT = C.sb("condT", [128, 8, 2], F32, s1); S.dma(condT[:], A["condT"])
            scond = C.sb("scond", [128, 8, 2], F32, s1)
            S.act(scond[:], condT[:], AF.Silu)
            bmodT = C.sb("bmodT", [128, 16], F32, s1); S.dma(bmodT[:], A["bmodT"])
            gpreT = C.sb("gpreT", [128, 8], F32, s1); S.dma(gpreT[:], A["gpreT"])
            modT = C.sb("modT", [128, 16, 2], F32, s1)
            wm = [C.sb(f"wm{i}", [128, 8, 128], F32, s1) for i in range(2)]
            for fb in range(16):
                w = wm[fb % 2]
                S.dma(w[:], A["wmod"][fb], q="sp" if fb % 2 == 0 else "act")
                for kt in range(8):
                    S.mm(PS[fb % 2][:, 0:2], w[:, kt, :], scond[:, kt, :], start=(kt == 0), stop=(kt == 7))
                S.ts(modT[:, fb, :], PS[fb % 2][:, 0:2], bmodT[:, fb:fb + 1], None, ALU.add)
            if stop == 'mod':
                S.emit(st, final_wait=[])
                return nc
            Amod = C.sb("Amod", [128, 8, 2], F32, s1)
            S.ts(Amod[:], modT[:, 8:16, :], 1.0, None, ALU.add)
            S.tt(Amod[:], Amod[:], gpreT[:, :].unsqueeze(2).broadcast_to([128, 8, 2]), ALU.mult)
            xts = [C.sb(f"xt{i}", [128, 1024], F32, s1) for i in range(3)]
            junk = C.sb("junk", [128, 1024], BF16, s1)
            ssq = C.sb("ssq", [128, 34], F32, s1)
            rstd = C.sb("rstd", [128, 34], F32, s1)
            import os
            for i in range(int(os.environ.get('K_NT', '34'))):
                xt = xts[i % 3]
                S.dma(xt[:], A["x_all"][i * 128:(i + 1) * 128, :], q="sp" if i % 2 == 0 else "act")
                import os
                lvl = int(os.environ.get("K_DEBUG", "9"))
                if lvl < 1:
                    continue
                S.act(junk[:], xt[:], AF.Square, accum_out=ssq[:, i:i + 1])
                if lvl < 2:
                    continue
                S.act(rstd[:, i:i + 1], ssq[:, i:i + 1], AF.Sqrt, bias=EPS, scale=1.0 / D)
                if lvl < 3:
                    continue
                S.recip(rstd[:, i:i + 1], rstd[:, i:i + 1])
                S.ts(xt[:], xt[:], rstd[:, i:i + 1], None, ALU.mult)
                if lvl < 4:
                    continue
                jj = 0 if i < 32 else 1
                for half in range(2):
                    pt = PS[2 + half]
                    for k4 in range(4):
                        kt = half * 4 + k4
                        S.transpose(pt[:, k4 * 128:(k4 + 1) * 128], xt[:, kt * 128:(kt + 1) * 128], identF[:])
                    if lvl < 5:
                        continue
                    for k4 in range(4):
                        kt = half * 4 + k4
                        S.ts(hT[:, kt, i * 128:(i + 1) * 128], pt[:, k4 * 128:(k4 + 1) * 128],
                             Amod[:, kt, jj:jj + 1], modT[:, kt, jj:jj + 1], ALU.mult, ALU.add)
        S.fence()
        if dbg == "hT":
            hTo = nc.dram_tensor("hTo", [128, 8, NTOK], BF16, kind="ExternalOutput").ap()
            outs.append(S.dma(hTo, hT[:]))

        wbufs = [C.sb(f"wcol{i}", [128, 8, 128], BF16) for i in range(2)]
        wstate = {"n": 0}

        def load_w(tile_idx):
            w = wbufs[wstate["n"] % 2]
            wstate["n"] += 1
            S.dma(w[:], A["w_in"][tile_idx], q="pool")
            return w

        blocks = [(tb * 512, 512) for tb in range(8)] + [(4096, 256)]

        def proj_fm(w, blk, pt):
            t0, n = blk
            for kt in range(8):
                S.mm(pt[:, 0:n], w[:, kt, :], hT[:, kt, t0:t0 + n], start=(kt == 0), stop=(kt == 7))

        if do_att:
            with ExitStack() as s2:
                gq = C.sb("gq", [128, 1], F32, s2); S.dma(gq[:], A["gq"])
                gk = C.sb("gk", [128, 1], F32, s2); S.dma(gk[:], A["gk"])
                S.ts(gq[:], gq[:], 128.0 ** -0.5, None, ALU.mult)
                ropec = C.sb("ropec", [128, 64], F32, s2); S.dma(ropec[:], A["ropec"])
                ropes = C.sb("ropes", [128, 64], F32, s2); S.dma(ropes[:], A["ropes"])
                RmT = C.sb("RmT", [128, 128], F32, s2); S.dma(RmT[:], A["RmatT"])
                KT = C.sb("KT", [128, NTOK], BF16, s2)
                V = C.sb("V", [128, 34, 128], BF16, s2)
                sq = C.sb("sq", [128, 512], BF16, s2)
                rinv = C.sb("rinv", [128, 512], F32, s2)
                kn = C.sb("kn", [128, 512], F32, s2)
                t1 = C.sb("t1", [128, 512], F32, s2)
                t2 = C.sb("t2", [128, 512], F32, s2)

                def norm_rope(pt, blk, g, out_ap):
                    t0, n = blk
                    S.act(sq[:, 0:n], pt[:, 0:n], AF.Square)
                    S.mm(PS[2][:, 0:n], onesB[:], sq[:, 0:n])
                    S.act(rinv[:, 0:n], PS[2][:, 0:n], AF.Sqrt, bias=EPS, scale=1.0 / 128)
                    S.recip(rinv[:, 0:n], rinv[:, 0:n])
                    if t0 >= NLAT:
                        S.stt(out_ap, pt[:, 0:n], g[:, 0:1], rinv[:, 0:n], ALU.mult, ALU.mult)
                        return
                    S.stt(kn[:, 0:n], pt[:, 0:n], g[:, 0:1], rinv[:, 0:n], ALU.mult, ALU.mult)
                    S.mm(PS[2][:, 0:n], RmT[:], kn[:, 0:n])
                    r0 = t0 // 64
                    for (p0, p1) in ((0, 64), (64, 128)):
                        if p0 == 0:
                            cv = ropec[p0:p1, r0:r0 + 8].unsqueeze(2).broadcast_to([64, 8, 64])
                            sv = ropes[p0:p1, r0:r0 + 8].unsqueeze(2).broadcast_to([64, 8, 64])
                        else:
                            cv = ropec[p0:p1, :].unsqueeze(1).broadcast_to([64, 8, 64])
                            sv = ropes[p0:p1, :].unsqueeze(1).broadcast_to([64, 8, 64])
                        v3 = lambda a: a[p0:p1, 0:512].rearrange("p (r c) -> p r c", c=64)
                        S.tt(v3(t1), v3(kn), cv, ALU.mult, eng="pool")
                        S.tt(v3(t2), v3(PS[2]), sv, ALU.mult)
                    S.tt(out_ap, t1[:, 0:n], t2[:, 0:n], ALU.add)

                wk = load_w(0)
                for bi, blk in enumerate(blocks):
                    proj_fm(wk, blk, PS[bi % 2])
                    norm_rope(PS[bi % 2], blk, gk, KT[:, blk[0]:blk[0] + blk[1]])
                wv = load_w(1)
                for i4 in range(0, 34, 4):
                    nt = min(4, 34 - i4)
                    pt = PS[(i4 // 4) % 2]
                    for j in range(nt):
                        i = i4 + j
                        for kt in range(8):
                            S.mm(pt[:, j * 128:(j + 1) * 128], hT[:, kt, i * 128:(i + 1) * 128], wv[:, kt, :],
                                 start=(kt == 0), stop=(kt == 7))
                    S.act(V[:, i4:i4 + nt, :].rearrange("p a d -> p (a d)"), pt[:, 0:nt * 128], AF.Copy)
                qTb = [C.sb(f"qTb{i}", [128, 512], BF16, s2) for i in range(2)]
                gab = [C.sb(f"gab{i}", [128, 512], F32, s2) for i in range(2)]
                Pb = [C.sb(f"Pb{i}", [128, 512], BF16, s2) for i in range(3)]
                rden = C.sb("rden", [1, 512], F32, s2)
                o1 = C.sb("o1", [128, 512], F32, s2)
                mo = [C.sb(f"mo{i}", [128, 512], BF16, s2) for i in range(2)]
                work = [(h, bi) for h in range(4) for bi in range(len(blocks))]
                wq = {}

                def prep(idx):
                    h, bi = work[idx]
                    blk = blocks[bi]
                    if bi == 0:
                        wq["q"] = load_w(2 + 2 * h)
                        wq["g"] = load_w(3 + 2 * h)
                    proj_fm(wq["q"], blk, PS[0])
                    norm_rope(PS[0], blk, gq, qTb[idx % 2][:, 0:blk[1]])
                    proj_fm(wq["g"], blk, PS[1])
                    S.act(gab[idx % 2][:, 0:blk[1]], PS[1][:, 0:blk[1]], AF.Silu)

                def attend(idx):
                    h, bi = work[idx]
                    t0, n = blocks[bi]
                    ktiles = list(range(34)) if t0 < NLAT else [32, 33]
                    pO, pD = PS[5], PS[6]
                    nk = len(ktiles)

                    def smm(ki):
                        kt_i = ktiles[ki]
                        S.mm(PS[3 + ki % 2][:, 0:n], KT[:, kt_i * 128:(kt_i + 1) * 128], qTb[idx % 2][:, 0:n])
                    smm(0)
                    if nk > 1:
                        smm(1)
                    for ki, kt_i in enumerate(ktiles):
                        pS = PS[3 + ki % 2]
                        P = Pb[ki % 3]
                        S.act(P[:, 0:n], pS[:, 0:n], AF.Exp)
                        S.mm(pO[:, 0:n], V[:, kt_i, :], P[:, 0:n], start=(ki == 0), stop=(ki == nk - 1))
                        S.mm(pD[0:1, 0:n], onesB[:, 0:1], P[:, 0:n], start=(ki == 0), stop=(ki == nk - 1))
                        if ki + 2 < nk:
                            smm(ki + 2)
                    S.recip(rden[0:1, 0:n], pD[0:1, 0:n])
                    S.mm(PS[2][:, 0:n], onesF[0:1, :], rden[0:1, 0:n])
                    S.tt(o1[:, 0:n], pO[:, 0:n], gab[idx % 2][:, 0:n], ALU.mult)
                    m = mo[idx % 2]
                    S.tt(m[:, 0:n], o1[:, 0:n], PS[2][:, 0:n], ALU.mult)
                    outs.append(S.dma(mix.cols(512 + h * 128, 512 + (h + 1) * 128, t0, n), m[:, 0:n], q="sp"))

                prep(0)
                for idx in range(len(work)):
                    if idx + 1 < len(work):
                        prep(idx + 1)
                    attend(idx)
            S.fence()

        if do_hy:
            with ExitStack() as s3:
                convw = C.sb("convw", [128, 4, 3, 3], F32, s3); S.dma(convw[:], A["convw"])
                convb = C.sb("convb", [128, 4, 3], F32, s3); S.dma(convb[:], A["convb"])
                SEG = NLAT + 2 + NCTX + 2
                ust = C.sb("ust", [128, SEG], BF16, s3)
                S.memset(ust[:], 0.0)
                ctmp = C.sb("ctmp", [128, 1024], F32, s3)
                x1c = C.sb("x1c", [128, NTOK], BF16, s3)
                gT = C.sb("gT", [128, NTOK], BF16, s3)
                xg = C.sb("xg", [128, NTOK], BF16, s3)
                Xl = C.sb("Xl", [64, 64, 128], BF16, s3)
                Xfl = C.sb("Xfl", [128, 64, 128], BF16, s3)
                Bre = C.sb("Bre", [128, 64, 128], BF16, s3)
                Bim = C.sb("Bim", [128, 64, 128], BF16, s3)
                segs = [(k * 1024, k * 1024 + 1, 1024) for k in range(4)] + [(4096, NLAT + 3, 256)]

                def conv_stream(ct, s, out_fn):
                    for (t0, u0, n) in segs:
                        S.act(ctmp[:, 0:n], ust[:, u0:u0 + n], AF.Identity, bias=convb[:, ct, s:s + 1],
                              scale=convw[:, ct, s, 1:2])
                        S.stt(ctmp[:, 0:n], ust[:, u0 - 1:u0 - 1 + n], convw[:, ct, s, 0:1], ctmp[:, 0:n],
                              ALU.mult, ALU.add)
                        out_fn(t0, n, ust[:, u0 + 1:u0 + 1 + n], convw[:, ct, s, 2:3], ctmp[:, 0:n])

                def ust_cols(blk):
                    t0, n = blk
                    return slice(t0 + 1, t0 + 1 + n) if t0 < NLAT else slice(NLAT + 3, NLAT + 3 + n)

                for ct in range(n_ct):
                    wbase = 10 + 4 * ct
                    S.dma(Xfl[:, :, :].rearrange("p a n -> p (a n)"), xf_l[ct, :, :], q="act")
                    w = load_w(wbase + 0)
                    for bi, blk in enumerate(blocks):
                        proj_fm(w, blk, PS[bi % 2])
                        S.act(ust[:, ust_cols(blk)], PS[bi % 2][:, 0:blk[1]], AF.Copy)
                    conv_stream(ct, 1, lambda t0, n, a, sc, b: S.stt(x1c[:, t0:t0 + n], a, sc, b, ALU.mult, ALU.add))
                    w = load_w(wbase + 1)
                    for bi, blk in enumerate(blocks):
                        proj_fm(w, blk, PS[bi % 2])
                        S.act(ust[:, ust_cols(blk)], PS[bi % 2][:, 0:blk[1]], AF.Copy)

                    def gfn(t0, n, a, sc, b):
                        S.stt(b, a, sc, b, ALU.mult, ALU.add)
                        S.tt(gT[:, t0:t0 + n], b, x1c[:, t0:t0 + n], ALU.mult, eng="pool")
                    conv_stream(ct, 2, gfn)
                    w = load_w(wbase + 3)
                    for bi, blk in enumerate(blocks):
                        proj_fm(w, blk, PS[bi % 2])
                        S.act(xg[:, blk[0]:blk[0] + blk[1]], PS[bi % 2][:, 0:blk[1]], AF.Silu)
                    w = load_w(wbase + 2)
                    for bi, blk in enumerate(blocks):
                        proj_fm(w, blk, PS[bi % 2])
                        S.act(ust[:, ust_cols(blk)], PS[bi % 2][:, 0:blk[1]], AF.Copy)

                    def xfn(t0, n, a, sc, b):
                        S.stt(b, a, sc, b, ALU.mult, ALU.add)
                        S.tt(xg[:, t0:t0 + n], b, xg[:, t0:t0 + n], ALU.mult, eng="pool")
                    conv_stream(ct, 0, xfn)
                    for n2_0 in range(0, 64, 8):
                        for j in range(8):
                            S.transpose(pTb[0:64, j * 128:(j + 1) * 128], gT[:, n2_0 + j:NLAT:64], identB[:])
                        ov = Xl[0:64, :, :].rearrange("p a (n c) -> p n a c", c=2)[:, n2_0:n2_0 + 8, :, :]
                        iv = pTb[0:64, :].rearrange("p (n a c) -> p n a c", n=8, c=2)
                        S.copy(ov, iv, eng="act")

                    def mk_evac(tok0, H, li):
                        def evac(po, n2_0, ng):
                            ov = gT[:, tok0:tok0 + 64 * H].rearrange("p (a n) -> p n a", n=64)[:, n2_0:n2_0 + ng, :]
                            xv = xg[:, tok0:tok0 + 64 * H].rearrange("p (a n) -> p n a", n=64)[:, n2_0:n2_0 + ng, :]
                            iv = po[:, 0:ng * H].rearrange("p (n a) -> p n a", a=H)
                            S.stt(ov, iv, rnT[:, ct, li:li + 1], xv, ALU.mult, ALU.mult)
                        return evac
                    if "L" not in SKIP:
                        fft_conv_core(S, C, PS, TL, 128, Xl, Xfl, Bre, Bim, ftmp, mk_evac(0, 64, 0))
                    S.dma(Xfl[0:8, :, :].rearrange("p a n -> p (a n)"), xf_c[ct, :, :], q="act")
                    for n2_0 in range(0, 64, 8):
                        for j in range(8):
                            S.transpose(pTb[0:4, j * 128:(j + 1) * 128], gT[:, NLAT + n2_0 + j:NTOK:64], identB[:])
                        ov = Xl[0:4, :, :].rearrange("p a (n c) -> p n a c", c=2)[:, n2_0:n2_0 + 8, :, :]
                        iv = pTb[0:4, :].rearrange("p (n a c) -> p n a c", n=8, c=2)
                        S.copy(ov, iv, eng="act")
                    if "C" not in SKIP:
                        fft_conv_core(S, C, PS, TC, 8, Xl, Xfl, Bre, Bim, ftmp, mk_evac(NLAT, 4, 1))
                    for (c0, cn, cap) in mix.pieces(ct * 128, (ct + 1) * 128):
                        outs.append(S.dma(cap, gT[:, c0:c0 + cn], q="sp"))
        if shared is None:
            S.emit(st, final_wait=outs)
    return nc


def gate_rows(S, C, PS, A, pre, scond, onesF, stk):
    rows = C.sb("grow", [2, 1024], F32, stk)
    bro = C.sb("gbro", [2, 1024], F32, stk); S.dma(bro[:], A[pre + "bgate2"])
    gpo = C.sb("ggpo", [2, 1024], F32, stk); S.dma(gpo[:], A[pre + "gpost2"])
    sel2 = C.sb("sel2", [2, 2, 128], F32, stk); S.dma(sel2[:], A["sel2"])
    wg = [C.sb(f"wgate{i}", [128, 1024], F32, stk) for i in range(2)]
    for kt in range(8):
        w = wg[kt % 2]
        S.dma(w[:], A[pre + "wgate"][kt], q="sp" if kt % 2 == 0 else "act")
        for nb in range(2):
            S.mm(PS[nb][0:2, :], scond[:, kt, 0:2], w[:, nb * 512:(nb + 1) * 512], start=(kt == 0), stop=(kt == 7))
    for nb in range(2):
        S.tt(rows[:, nb * 512:(nb + 1) * 512], PS[nb][0:2, :], bro[:, nb * 512:(nb + 1) * 512], ALU.add)
    S.tt(rows[:], rows[:], gpo[:], ALU.mult)
    gg = []
    for j in range(2):
        g = C.sb(f"gg{j}", [128, 1024], F32, stk)
        for nb in range(2):
            S.mm(PS[2 + nb][:, :], sel2[0:2, j, :], rows[0:2, nb * 512:(nb + 1) * 512])
            S.copy(g[:, nb * 512:(nb + 1) * 512], PS[2 + nb][:, :], eng="act")
        gg.append(g)
    return gg


def outproj_residual(S, C, PS, A, stk, mixg, wout_ap, x_in, tiles, gg, x_out, identF, post=None):
    wout = C.sb("wout", [128, 16, 1024], BF16, stk)
    for kt in range(16):
        S.dma(wout[:, kt, :], wout_ap[kt], q="pool")
    mts = [C.sb(f"mt{i}", [128, 16, 512], BF16, stk) for i in range(2)]
    xts = [C.sb(f"xo{i}", [128, 1024], F32, stk) for i in range(3)]
    junk = C.sb("junk2", [128, 512], BF16, stk)
    ss2_s = [C.sb("ss2", [128, 2], F32, stk) for _ in range(2)]
    rs_s = [C.sb("rs", [128, 1], F32, stk) for _ in range(2)]
    tmp_s = [C.sb("tmpo", [128, 1024], F32, stk) for _ in range(2)]
    outs = []
    if not isinstance(mixg, Chunked):
        mixg = Chunked.wrap(mixg)
    cur = {"t0": None, "mt": None, "n": 0}
    for idx, (i, tok0, jj) in enumerate(tiles):
        g4 = tok0 // 512
        if cur["t0"] != g4:
            mt = mts[cur["n"] % 2]
            cur["n"] += 1
            ncol = min(512, mixg.ntok - g4 * 512)
            S.dma(mt[:, :, 0:ncol], mixg.cols(0, mixg.rows, g4 * 512, ncol).rearrange("(kt p) t -> p kt t", p=128), q="sp")
            cur["t0"], cur["mt"] = g4, mt
        mt = cur["mt"]
        c0 = tok0 - g4 * 512
        xt = xts[idx % 3]
        ss2, rs, tmp = ss2_s[idx % 2], rs_s[idx % 2], tmp_s[idx % 2]
        PY = (PS[0], PS[1]) if idx % 2 == 0 else (PS[4], PS[5])
        S.dma(xt[:], x_in[tok0:tok0 + 128, :], q="act")
        for nb in range(2):
            for kt in range(16):
                S.mm(PY[nb][:, :], mt[:, kt, c0:c0 + 128], wout[:, kt, nb * 512:(nb + 1) * 512],
                     start=(kt == 0), stop=(kt == 15))
            S.act(junk[:], PY[nb][:, :], AF.Square, accum_out=ss2[:, nb:nb + 1])
        S.tt(rs[:], ss2[:, 0:1], ss2[:, 1:2], ALU.add)
        S.act(rs[:], rs[:], AF.Sqrt, bias=EPS, scale=1.0 / D)
        S.recip(rs[:], rs[:])
        for nb in range(2):
            sl = slice(nb * 512, (nb + 1) * 512)
            S.stt(tmp[:, sl], PY[nb][:, :], rs[:, 0:1], gg[jj][:, sl], ALU.mult, ALU.mult)
        S.tt(xt[:], xt[:], tmp[:], ALU.add, eng="pool")
        if x_out is not None:
            outs.append(S.dma(x_out[i * 128:(i + 1) * 128, :], xt[:], q="sp"))
        if post is not None:
            post(i, xt, jj)
    return outs


def adaln_shift_scale(S, C, PS, A, pre, scond, stk):
    bmodT = C.sb("bmodT", [128, 16], F32, stk); S.dma(bmodT[:], A[pre + "bmodT"])
    gpreT = C.sb("gpreT", [128, 8], F32, stk); S.dma(gpreT[:], A[pre + "gpreT"])
    modT = C.sb("modT", [128, 16, 2], F32, stk)
    wm = [C.sb(f"wm{i}", [128, 8, 128], F32, stk) for i in range(2)]
    for fb in range(16):
        w = wm[fb % 2]
        S.dma(w[:], A[pre + "wmod"][fb], q="sp" if fb % 2 == 0 else "act")
        for kt in range(8):
            S.mm(PS[4 + fb % 2][:, 0:2], w[:, kt, :], scond[:, kt, :], start=(kt == 0), stop=(kt == 7))
        S.ts(modT[:, fb, :], PS[4 + fb % 2][:, 0:2], bmodT[:, fb:fb + 1], None, ALU.add)
    Amod = C.sb("Amod", [128, 8, 2], F32, stk)
    S.ts(Amod[:], modT[:, 8:16, :], 1.0, None, ALU.add)
    S.tt(Amod[:], Amod[:], gpreT[:, :].unsqueeze(2).broadcast_to([128, 8, 2]), ALU.mult)
    return Amod, modT


def p2_host_inputs(inp, b, hf, mix0g):
    f = lambda a: np.ascontiguousarray(a, dtype=np.float32)
    o = {}
    o["mix0g"] = np.ascontiguousarray(mix0g)
    o["x_all"] = f(np.concatenate([inp["x"][b], inp["ctx"][b]], 0))
    cond = np.stack([inp["c"][b], inp["c_ctx"]], -1)
    o["condT"] = f(cond.reshape(8, 128, 2).transpose(1, 0, 2))
    o["l0_wgate"] = f(inp["w_mod"][0][:, 2048:3072].reshape(8, 128, 1024))
    o["l0_bgate2"] = f(np.broadcast_to(inp["b_mod"][0][2048:3072][None], (2, 1024)))
    o["l0_gpost2"] = f(np.broadcast_to(inp["g_post"][0][None], (2, 1024)))
    sel2 = np.zeros((2, 2, 128), np.float32); sel2[0, 0] = 1; sel2[1, 1] = 1
    o["sel2"] = sel2
    wo = inp["e_w_out"][0]
    order = np.concatenate([np.arange(r * 512, r * 512 + 512) if part == 0 else np.arange(1024 + r * 512, 1024 + r * 512 + 512)
                            for r in range(2) for part in range(2)])
    o["wout0"] = f(wo[order].reshape(16, 128, 1024))
    wm = inp["w_mod"][1][:, :2048]
    o["l1_wmod"] = f(wm.reshape(8, 128, 16, 128).transpose(2, 1, 0, 3))
    o["l1_bmodT"] = f(inp["b_mod"][1][:2048].reshape(16, 128).T)
    o["l1_gpreT"] = f(inp["g_pre"][1].reshape(8, 128).T)
    W = inp["o_w_in"][0]
    heads = [4 * hf + h for h in range(4)]
    def tl(cols):
        return W[:, cols].reshape(8, 128, len(cols)).transpose(1, 0, 2)
    o["wq"] = f(np.stack([tl(np.arange(hd * 128, hd * 128 + 128)) for hd in heads]))
    o["wk"] = f(np.stack([tl(np.arange(1024 + hd * 128, 1024 + hd * 128 + 128)) for hd in heads]))
    o["wv"] = f(np.stack([tl(np.arange(2048 + hd * 256, 2048 + hd * 256 + 256)) for hd in heads]))
    o["woz"] = f(np.stack([tl(np.concatenate([np.arange(4096 + hd * 256, 4096 + hd * 256 + 256),
                                              np.arange(6144 + hd * 256, 6144 + hd * 256 + 256)])) for hd in heads]))
    gcols = np.array([8192 + g * 8 + hd for g in range(4) for hd in heads])
    o["wg"] = f(tl(gcols))
    cw = inp["o_conv_w"][0]; cb = inp["o_conv_b"][0]
    convw = np.zeros((128, 4, 2, 3), np.float32); convb = np.zeros((128, 4, 2), np.float32)
    for h, hd in enumerate(heads):
        for s in range(2):
            c0 = s * 1024 + hd * 128
            convw[:, h, s, :] = cw[:, c0:c0 + 128].T
            convb[:, h, s] = cb[c0:c0 + 128]
    o["convw1"] = convw; o["convb1"] = convb
    gb = inp["o_gate_b"][0].reshape(4, 8)[:, heads]
    o["gateb"] = f(gb.T)
    hn = inp["o_head_norm"][0].reshape(8, 256)[heads]
    o["hn"] = f(np.broadcast_to(hn[None], (128, 4, 256)))
    t = np.arange(128)
    same = (t[:, None] // 64) == (t[None, :] // 64)
    o["maskF"] = f((same & (t[:, None] <= t[None, :])))
    o["maskB"] = f((same & (t[:, None] >= t[None, :])))
    sel4 = np.zeros((4, 4, 128), np.float32)
    for h in range(4):
        sel4[h, h] = 1
    o["sel4"] = sel4
    o["identF"] = np.eye(128, dtype=np.float32)
    return o


def build_p2(nc, sample, n_heads=4, stop=None, shared=None):
    if shared is None:
        A = _declare_inputs(nc, sample)
        x1o = nc.dram_tensor("x1o", [NTOK, D], F32, kind="ExternalOutput").ap()
        mix1 = Chunked.wrap(nc.dram_tensor("mix1", [1024, NLAT], BF16, kind="ExternalOutput").ap())
    else:
        A = shared["A"]
        x1o = shared["x1o"]
        mix1 = shared["mix1"]
    outs = []
    NCH = NTOK // 64
    with ExitStack() as st:
        if shared is None:
            S = Sched(nc)
            C = Ctx(nc, st)
            PS = [C.ps(f"pb{i}", [128, 512], F32) for i in range(7)]
            pTb = C.ps("pTb", [128, 1024], BF16)
        else:
            S, C, PS, pTb = shared["S"], shared["C"], shared["PS"], shared["pTb"]
            C.stack = st
        identF = C.sb("identF", [128, 128], F32); S.dma(identF[:], A["identF"])
        identB = C.sb("identB", [128, 128], BF16); S.dma(identB[:], A["identF"], q="pool")
        onesF = C.sb("onesF", [128, 128], F32); S.memset(onesF[:], 1.0)
        condT = C.sb("condT", [128, 8, 2], F32); S.dma(condT[:], A["condT"])
        scond = C.sb("scond", [128, 8, 2], F32)
        S.act(scond[:], condT[:], AF.Silu)
        h1T = C.sb("h1T", [128, 8, NTOK], BF16)
        tokq = C.sb("tokq", [128, 34, 24], F32)
        ebB = C.sb("ebB", [128, 2, 4, NCH], F32)
        with ExitStack() as sa:
            Amod, modT = adaln_shift_scale(S, C, PS, A, "l1_", scond, sa)
            gg = gate_rows(S, C, PS, A, "l0_", scond, onesF, sa)
            ssq_s = [C.sb("ssq", [128, 1], F32, sa) for _ in range(2)]
            rstd_s = [C.sb("rstd", [128, 1], F32, sa) for _ in range(2)]
            junk = C.sb("junk", [128, 1024], BF16, sa)
            xh_s = [C.sb("xh", [128, 1024], F32, sa) for _ in range(2)]

            def post(i, xt, jj):
                ssq, rstd, xh = ssq_s[i % 2], rstd_s[i % 2], xh_s[i % 2]
                S.act(junk[:], xt[:], AF.Square, accum_out=ssq[:, 0:1])
                S.act(rstd[:], ssq[:], AF.Sqrt, bias=EPS, scale=1.0 / D)
                S.recip(rstd[:], rstd[:])
                S.ts(xh[:], xt[:], rstd[:, 0:1], None, ALU.mult)
                for half in range(2):
                    pt = PS[2 + half]
                    for k4 in range(4):
                        kt = half * 4 + k4
                        S.transpose(pt[:, k4 * 128:(k4 + 1) * 128], xh[:, kt * 128:(kt + 1) * 128], identF[:])
                    for k4 in range(4):
                        kt = half * 4 + k4
                        S.ts(h1T[:, kt, i * 128:(i + 1) * 128], pt[:, k4 * 128:(k4 + 1) * 128],
                             Amod[:, kt, jj:jj + 1], modT[:, kt, jj:jj + 1], ALU.mult, ALU.add)
            tiles = [(i, i * 128, 0 if i < 32 else 1) for i in range(34)]
            outs += outproj_residual(S, C, PS, A, sa, A["mix0g"], A["wout0"], A["x_all"], tiles, gg, x1o, identF, post)
        S.fence()
        if stop == "A":
            dbg = nc.dram_tensor("h1To", [128, 8, NTOK], BF16, kind="ExternalOutput").ap()
            outs.append(S.dma(dbg, h1T[:]))
            S.emit(st, final_wait=outs)
            return nc
        blocks = [(tb * 512, 512) for tb in range(8)] + [(4096, 256)]
        with ExitStack() as sb_:
            wg = C.sb("wg", [128, 8, 16], BF16, sb_); S.dma(wg[:], A["wg"], q="pool")
            gateb = C.sb("gateb", [4, 4], F32, sb_); S.dma(gateb[:], A["gateb"])
            sel4 = C.sb("sel4", [4, 4, 128], F32, sb_); S.dma(sel4[:], A["sel4"])
            gi = C.sb("gi", [4, NTOK], F32, sb_)
            gf = C.sb("gf", [4, NTOK], F32, sb_)
            cumB = C.sb("cumB", [4, NTOK], F32, sb_)
            e1p = C.sb("e1p", [4, NTOK], F32, sb_)
            ebr = C.sb("ebr", [4, NCH], F32, sb_)
            v3 = lambda t_: t_[:, :].rearrange("p (c j) -> p c j", j=64)
            for d in range(2):
                for gq_, dstg in ((2 * d, gi), (2 * d + 1, gf)):
                    for bi, (t0, n) in enumerate(blocks):
                        pt = PS[bi % 2]
                        for kt in range(8):
                            S.mm(pt[0:4, 0:n], wg[:, kt, gq_ * 4:(gq_ + 1) * 4], h1T[:, kt, t0:t0 + n], start=(kt == 0), stop=(kt == 7))
                        S.ts(dstg[:, t0:t0 + n], pt[0:4, 0:n], gateb[:, gq_:gq_ + 1], None, ALU.add)
                S.act(gf[:], gf[:], AF.Exp, scale=-1.0)
                S.act(gf[:], gf[:], AF.Ln, bias=1.0)
                src, dst = gf, cumB
                for k in range(6):
                    sh = 1 << k
                    s3, d3 = v3(src), v3(dst)
                    if d == 0:
                        S.tt(d3[:, :, sh:64], s3[:, :, sh:64], s3[:, :, 0:64 - sh], ALU.add)
                        S.copy(d3[:, :, 0:sh], s3[:, :, 0:sh], eng="pool")
                    else:
                        S.tt(d3[:, :, 0:64 - sh], s3[:, :, 0:64 - sh], s3[:, :, sh:64], ALU.add)
                        S.copy(d3[:, :, 64 - sh:64], s3[:, :, 64 - sh:64], eng="pool")
                    src, dst = dst, src
                cum, thr = src, dst
                e1 = gi
                endpos = 63 if d == 0 else 0
                S.act(ebr[:], v3(cum)[:, :, endpos], AF.Exp, scale=-1.0)
                S.tt(e1[:], gi[:], cum[:], ALU.add)
                S.act(e1[:], e1[:], AF.Exp)
                S.ts(e1[:], e1[:], 128.0 ** -0.5, None, ALU.mult)
                S.act(thr[:], cum[:], AF.Exp)
                S.tt(v3(e1p), v3(e1), ebr[:, :].unsqueeze(2).broadcast_to([4, NCH, 64]), ALU.mult)
                for h in range(4):
                    S.mm(PS[2][:, 0:NCH], sel4[0:4, h, :], ebr[0:4, :])
                    S.copy(ebB[:, d, h, :], PS[2][:, 0:NCH], eng="act")
                for i in range(34):
                    pt = PS[3 + i % 2]
                    for qi, src_t in enumerate((e1, e1p, thr)):
                        S.transpose(pt[:, qi * 4:(qi + 1) * 4], src_t[0:4, i * 128:(i + 1) * 128], identF[0:4, 0:4])
                    S.copy(tokq[:, i, d * 12:(d + 1) * 12], pt[:, 0:12], eng="act")
        S.fence()
        if stop == "B":
            dbg = nc.dram_tensor("tokqo", [128, 34, 24], F32, kind="ExternalOutput").ap()
            outs.append(S.dma(dbg, tokq[:]))
            dbg2 = nc.dram_tensor("ebBo", [128, 2, 4, NCH], F32, kind="ExternalOutput").ap()
            outs.append(S.dma(dbg2, ebB[:]))
            S.emit(st, final_wait=outs)
            return nc
        with ExitStack() as sc:
            convw = C.sb("convw1", [128, 4, 2, 3], F32, sc); S.dma(convw[:], A["convw1"])
            convb = C.sb("convb1", [128, 4, 2], F32, sc); S.dma(convb[:], A["convb1"])
            hn = C.sb("hn", [128, 256], F32, sc)
            maskF = C.sb("maskF", [128, 128], F32, sc); S.dma(maskF[:], A["maskF"])
            maskB = C.sb("maskB", [128, 128], F32, sc); S.dma(maskB[:], A["maskB"])
            masks = (maskF, maskB)
            wbuf = C.sb("wbuf", [128, 8, 512], BF16, sc)
            wq = wbuf[:, :, 0:128]
            wk = wbuf[:, :, 128:256]
            wv = wbuf[:, :, 256:512]
            woz = wbuf
            SEG = NLAT + 2 + NCTX + 2
            ust = C.sb("ust", [128, SEG], BF16, sc); S.memset(ust[:], 0.0)
            ctmp = C.sb("ctmp", [128, 1024], F32, sc)
            qT = C.sb("qT", [128, NTOK], BF16, sc)
            kT = C.sb("kT", [128, NTOK], BF16, sc)
            ktD = [C.sb(f"ktD{d}", [128, 34, 128], BF16, sc) for d in range(2)]
            vaug = C.sb("vaug", [128, 34, 257], BF16, sc)
            S.memset(vaug[:, :, 256:257], 1.0)
            hsum = C.sb("hsum", [128, 32, 256], F32, sc)
            Cst = [C.sb(f"Cst{d}", [128, 257], F32, sc) for d in range(2)]
            Cbf = [C.sb(f"Cbf{d}", [128, 257], BF16, sc) for d in range(2)]
            Sm = [C.sb(f"Sm{d}", [128, 128], BF16, sc) for d in range(2)]
            dm = [C.sb(f"dm{d}", [128, 1], F32, sc) for d in range(2)]
            fin = [(C.sb("so", [128, 256], F32, sc), C.sb("sz", [128, 256], F32, sc), C.sb("hh", [128, 256], F32, sc),
                    C.sb("junk3", [128, 256], BF16, sc), C.sb("ssq3", [128, 1], F32, sc), C.sb("mixv", [128, 256], BF16, sc))
                   for _ in range(2)]
            mob = [C.sb(f"mob{i}", [128, 2, 512], BF16, sc) for i in range(2)]
            segs = [(k * 1024, k * 1024 + 1, 1024) for k in range(4)] + [(4096, NLAT + 3, 256)]

            def ust_cols(blk):
                t0, n = blk
                return slice(t0 + 1, t0 + 1 + n) if t0 < NLAT else slice(NLAT + 3, NLAT + 3 + n)

            for h in range(n_heads):
                S.dma(wq, A["wq"][h], q="pool")
                S.dma(wk, A["wk"][h], q="pool")
                S.dma(wv, A["wv"][h], q="pool")
                S.dma(hn[:], A["hn"][:, h, :])
                S.memset(hsum[:], 0.0, eng="pool")
                for (w, s, dstT) in ((wq, 0, qT), (wk, 1, kT)):
                    for bi, blk in enumerate(blocks):
                        t0, n = blk
                        pt = PS[bi % 2]
                        for kt in range(8):
                            S.mm(pt[:, 0:n], w[:, kt, :], h1T[:, kt, t0:t0 + n], start=(kt == 0), stop=(kt == 7))
                        S.act(ust[:, ust_cols(blk)], pt[:, 0:n], AF.Copy)
                    for (t0, u0, n) in segs:
                        S.act(ctmp[:, 0:n], ust[:, u0:u0 + n], AF.Identity, bias=convb[:, h, s:s + 1], scale=convw[:, h, s, 1:2])
                        S.stt(ctmp[:, 0:n], ust[:, u0 - 1:u0 - 1 + n], convw[:, h, s, 0:1], ctmp[:, 0:n], ALU.mult, ALU.add)
                        S.stt(ctmp[:, 0:n], ust[:, u0 + 1:u0 + 1 + n], convw[:, h, s, 2:3], ctmp[:, 0:n], ALU.mult, ALU.add)
                        S.act(dstT[:, t0:t0 + n], ctmp[:, 0:n], AF.Silu)
                for i4 in range(0, 34, 8):
                    nt = min(8, 34 - i4)
                    for j in range(nt):
                        i = i4 + j
                        S.transpose(pTb[:, j * 128:(j + 1) * 128], kT[:, i * 128:(i + 1) * 128], identB[:])
                    for j in range(nt):
                        i = i4 + j
                        for d in range(2):
                            S.act(ktD[d][:, i, :], pTb[:, j * 128:(j + 1) * 128], AF.Identity, scale=tokq[:, i, d * 12 + 4 + h:d * 12 + 5 + h])
                for i in range(34):
                    pt = PS[i % 2]
                    for kt in range(8):
                        S.mm(pt[:, 0:256], h1T[:, kt, i * 128:(i + 1) * 128], wbuf[:, kt, 256:512], start=(kt == 0), stop=(kt == 7))
                    S.act(vaug[:, i, 0:256], pt[:, 0:256], AF.Copy)
                for d in range(2):
                    S.memset(Cst[d][:], 0.0)
                    S.memset(Cbf[d][:], 0.0, eng="pool")
                orderF = [32, 33] + list(range(32))
                orderB = [33, 32] + list(range(31, -1, -1))
                for step in range(34):
                    for d, tile_i in ((0, orderF[step]), (1, orderB[step])):
                        i = tile_i
                        is_lat = i < 32
                        pS, pN, pU = PS[3 * d], PS[3 * d + 1], PS[3 * d + 2]
                        tk = slice(i * 128, (i + 1) * 128)
                        chunks = (0, 1) if d == 0 else (1, 0)
                        if is_lat:
                            S.mm(pS[:, 0:128], kT[:, tk], qT[:, tk])
                            S.stt(Sm[d][:], pS[:, 0:128], tokq[:, i, d * 12 + h:d * 12 + h + 1], masks[d][:], ALU.mult, ALU.mult)
                            S.mm(pN[:, 0:257], Sm[d][:], vaug[:, i, :], start=True, stop=False)
                        for ci, c in enumerate(chunks):
                            rows = slice(c * 64, (c + 1) * 64)
                            ch = i * 2 + c
                            if is_lat:
                                S.mm(pN[rows, 0:257], qT[:, i * 128 + c * 64:i * 128 + (c + 1) * 64], Cbf[d][:, :],
                                     start=False, stop=True)
                            S.mm(pU[:, 0:257], ktD[d][rows, i, :], vaug[rows, i, :])
                            S.stt(Cst[d][:], Cst[d][:], ebB[:, d, h, ch:ch + 1], pU[:, 0:257], ALU.mult, ALU.add)
                            S.act(Cbf[d][:], Cst[d][:], AF.Copy)
                        if is_lat:
                            S.act(dm[d][:], pN[:, 256:257], AF.Abs)
                            S.ts(dm[d][:], dm[d][:], tokq[:, i, d * 12 + 8 + h:d * 12 + 9 + h], None, ALU.max)
                            S.recip(dm[d][:], dm[d][:])
                            S.stt(hsum[:, i, :], pN[:, 0:256], dm[d][:, 0:1], hsum[:, i, :], ALU.mult, ALU.add)
                S.dma(woz[:], A["woz"][h], q="pool")
                for i in range(32):
                    pt = PS[i % 6]
                    so, sz, hh, junk3, ssq3, mixv = fin[i % 2]
                    for kt in range(8):
                        S.mm(pt[:, :], h1T[:, kt, i * 128:(i + 1) * 128], woz[:, kt, :], start=(kt == 0), stop=(kt == 7))
                    S.act(so[:], pt[:, 0:256], AF.Sigmoid)
                    S.act(sz[:], pt[:, 256:512], AF.Silu)
                    S.tt(hh[:], hsum[:, i, :], so[:], ALU.mult)
                    S.act(junk3[:], hh[:], AF.Square, accum_out=ssq3[:, 0:1])
                    S.act(ssq3[:], ssq3[:], AF.Sqrt, bias=EPS, scale=1.0 / 256)
                    S.recip(ssq3[:], ssq3[:])
                    S.tt(sz[:], sz[:], hn[:], ALU.mult, eng="pool")
                    S.stt(mixv[:], hh[:], ssq3[:, 0:1], sz[:], ALU.mult, ALU.mult)
                    mo_ = mob[(i // 4) % 2]
                    for j in range(2):
                        S.transpose(pTb[:, j * 128:(j + 1) * 128], mixv[:, j * 128:(j + 1) * 128], identB[:])
                    S.copy(mo_[:, :, (i % 4) * 128:(i % 4 + 1) * 128],
                           pTb[:, 0:256].rearrange("p (j t) -> p j t", j=2), eng="act")
                    if i % 4 == 3:
                        t0 = (i // 4) * 512
                        dst = mix1.cols(h * 256, (h + 1) * 256, t0, 512).rearrange("(j p) t -> p j t", p=128)
                        outs.append(S.dma(dst, mo_[:, :, :], q="sp"))
        if shared is None:
            S.emit(st, final_wait=outs)
    return nc


def p3_host_inputs(inp, b, hf, mix1g, x1):
    f = lambda a: np.ascontiguousarray(a, dtype=np.float32)
    o = {}
    o["mix1g"] = np.ascontiguousarray(mix1g[:, hf * 2048:(hf + 1) * 2048])
    o["x1loc"] = f(x1[hf * 2048:(hf + 1) * 2048])
    cond = np.stack([inp["c"][b], inp["c_ctx"]], -1)
    o["condT"] = f(cond.reshape(8, 128, 2).transpose(1, 0, 2))
    o["l1_wgate"] = f(inp["w_mod"][1][:, 2048:3072].reshape(8, 128, 1024))
    o["l1_bgate2"] = f(np.broadcast_to(inp["b_mod"][1][2048:3072][None], (2, 1024)))
    o["l1_gpost2"] = f(np.broadcast_to(inp["g_post"][1][None], (2, 1024)))
    sel2 = np.zeros((2, 2, 128), np.float32); sel2[0, 0] = 1; sel2[1, 1] = 1
    o["sel2"] = sel2
    o["wout1"] = f(inp["o_w_out"][0].reshape(16, 128, 1024))
    o["identF"] = np.eye(128, dtype=np.float32)
    return o


def build_p3(nc, sample, shared=None):
    if shared is None:
        A = _declare_inputs(nc, sample)
        yo = nc.dram_tensor("yo", [2048, D], F32, kind="ExternalOutput").ap()
        ntile = 16
    else:
        A = shared["A"]
        yo = shared["yo"]
        ntile = 32
    with ExitStack() as st:
        if shared is None:
            S = Sched(nc)
            C = Ctx(nc, st)
            PS = [C.ps(f"pb{i}", [128, 512], F32) for i in range(7)]
        else:
            S, C, PS = shared["S"], shared["C"], shared["PS"]
            C.stack = st
        identF = C.sb("identF", [128, 128], F32); S.dma(identF[:], A["identF"])
        onesF = C.sb("onesF", [128, 128], F32); S.memset(onesF[:], 1.0)
        condT = C.sb("condT", [128, 8, 2], F32); S.dma(condT[:], A["condT"])
        scond = C.sb("scond", [128, 8, 2], F32)
        S.act(scond[:], condT[:], AF.Silu)
        gg = gate_rows(S, C, PS, A, "l1_", scond, onesF, st)
        tiles = [(i, i * 128, 0) for i in range(ntile)]
        outs = outproj_residual(S, C, PS, A, st, A["mix1g"], A["wout1"], A["x1loc"], tiles, gg, yo, identF, None)
        if shared is None:
            S.emit(st, final_wait=outs)
        else:
            shared["outs"] += outs
    return nc


CORES = [(b, hf) for b in range(4) for hf in range(2)]


def _launch(build, maps, **kw):
    nc = bass.Bass("TRN2", target_bir_lowering=False)
    build(nc, maps[0], **kw)
    res = run_bass_kernel_spmd(nc, maps, core_ids=list(range(len(maps))))
    return res.results


GROUPS = [[0, 1], [2, 3], [4, 5], [6, 7]]


def fused_host_inputs(inp, b, hf):
    o = {}
    dummy_mix0 = np.zeros((2048, NTOK), NPBF)
    p2 = p2_host_inputs(inp, b, hf, dummy_mix0)
    p3 = p3_host_inputs(inp, b, hf, np.zeros((2048, NLAT), NPBF), np.zeros((NLAT, D), np.float32))
    for d in (p3, p2, p1_host_inputs(inp, b, hf)):
        o.update(d)
    for k in ("mix0g", "mix1g", "x1loc"):
        o.pop(k)
    return o


def build_fused(nc, sample):
    A = _declare_inputs(nc, sample)
    yo = nc.dram_tensor("yo", [NLAT, D], F32, kind="ExternalOutput").ap()
    CS = 1024
    mix0 = Chunked.make(nc, "mix0", 1024, NTOK, BF16, CS, kind="Internal")
    mix0g = Chunked.make(nc, "mix0g", 2048, NTOK, BF16, CS, kind="Internal", addr_space="Local")
    x1o = nc.dram_tensor("x1o", [NTOK, D], F32, kind="Internal").ap()
    mix1 = Chunked.make(nc, "mix1", 1024, NLAT, BF16, CS, kind="Internal")
    mix1g = Chunked.make(nc, "mix1g", 2048, NLAT, BF16, CS, kind="Internal", addr_space="Local")
    A["mix0g"] = mix0g
    A["mix1g"] = mix1g
    A["x1loc"] = x1o[0:NLAT, :]
    with ExitStack() as st:
        S = Sched(nc)
        C = Ctx(nc, st)
        PS = [C.ps(f"pb{i}", [128, 512], F32) for i in range(7)]
        pTb = C.ps("pTb", [128, 1024], BF16)
        sh = dict(S=S, C=C, PS=PS, pTb=pTb, A=A, mix0=mix0, x1o=x1o, mix1=mix1, yo=yo, outs=[])
        import os
        nocc = os.environ.get("K_FUSE_NOCC") == "1"
        light = os.environ.get("K_P1_LIGHT") == "1"

        def exchange(gch, lch):
            for (_, _, go), (_, _, gi_) in zip(gch.chunks, lch.chunks):
                if nocc:
                    S.dma(go[0:1024, :], gi_, q="sp")
                    S.dma(go[1024:2048, :], gi_, q="act")
                else:
                    S.allgather(go, gi_, GROUPS)
        if light:
            build_p1(nc, sample, shared=sh, n_ct=1, do_att=False)
        else:
            build_p1(nc, sample, shared=sh)
        S.fence()
        exchange(mix0g, mix0)
        build_p2(nc, sample, shared=sh, n_heads=(1 if light else 4))
        S.fence()
        exchange(mix1g, mix1)
        build_p3(nc, sample, shared=sh)
        C.stack = st
        S.emit(st, final_wait=sh["outs"])
        print("fused stats", S.stats, flush=True)
    return nc


def kernel(**inputs):
    inp = {k: np.asarray(v) for k, v in inputs.items()}
    res = _launch(build_fused, [fused_host_inputs(inp, b, hf) for (b, hf) in CORES])
    out = np.zeros((4, NLAT, D), np.float32)
    for ci, (b, hf) in enumerate(CORES):
        out[b, hf * 2048:(hf + 1) * 2048] = np.asarray(res[ci]["yo"])[hf * 2048:(hf + 1) * 2048]
    return out
```

```python
import math
from contextlib import ExitStack
import numpy as np
import ml_dtypes
import concourse.bass as bass
import concourse.mybir as mybir
from concourse.bass_utils import run_bass_kernel_spmd

F32 = mybir.dt.float32
BF16 = mybir.dt.bfloat16
AF = mybir.ActivationFunctionType
ALU = mybir.AluOpType
AX = mybir.AxisListType
NPBF = ml_dtypes.bfloat16

D = 1024
NLAT = 4096
NCTX = 256
NTOK = NLAT + NCTX
EPS = 1e-6

ENGS = ("pe", "act", "dve", "pool", "sp")
SEM_ROT = 12000
ND = 8
CC_INC = 1
_DT_SIZE = {}


def _dsize(dt):
    if dt not in _DT_SIZE:
        _DT_SIZE[dt] = mybir.dt.size(dt)
    return _DT_SIZE[dt]


def _box(ap):
    t = ap.tensor
    dims = list(ap.ap)
    off = ap.offset
    sp = str(ap.space)
    if sp in ("SB", "PSUM"):
        pstep = 1
        for s in t.shape[1:]:
            pstep *= s
        p0 = off // pstep
        f0 = off % pstep
        pd = dims[0]
        npart = 1 if pd[0] == 0 else pd[1]
        ext = 0
        for st, cnt in dims[1:]:
            ext += (cnt - 1) * abs(st)
        if sp == "PSUM":
            return (t.name, 0, 128, 0, pstep)
        return (t.name, p0, p0 + npart, f0, f0 + ext + 1)
    ext = 0
    for st, cnt in dims:
        ext += (cnt - 1) * abs(st)
    return (t.name, 0, 1, off, off + ext + 1)


class Sched:
    def __init__(self, nc, same_engine_sync=None):
        import os
        if same_engine_sync is None:
            same_engine_sync = os.environ.get('K_SAME', '1') == '1'
        self.nc = nc
        self.ins = []
        self.track = {}
        self.same = same_engine_sync
        self.last_cp = {e: None for e in ENGS}
        self.last_dm = {e: [] for e in ENGS}
        self.pending = {e: set() for e in ENGS}

    def _deps(self, reads, writes, idx):
        deps = set()
        rb = [_box(a) for a in reads]
        wb = [_box(a) for a in writes]
        for b in rb:
            for ent in self.track.get(b[0], ()):
                e = ent[0]
                if e[1] < b[2] and b[1] < e[2] and e[3] < b[4] and b[3] < e[4]:
                    if ent[1] is not None:
                        deps.add(ent[1])
        for b in wb:
            for ent in self.track.get(b[0], ()):
                e = ent[0]
                if e[1] < b[2] and b[1] < e[2] and e[3] < b[4] and b[3] < e[4]:
                    if ent[1] is not None:
                        deps.add(ent[1])
                    deps.update(ent[2])
        for b in rb:
            lst = self.track.setdefault(b[0], [])
            for ent in lst:
                if ent[0] == b:
                    ent[2].append(idx)
                    break
            else:
                lst.append([b, None, [idx]])
        for b in wb:
            lst = self.track.get(b[0], [])
            keep = []
            for ent in lst:
                e = ent[0]
                if b[1] <= e[1] and e[2] <= b[2] and b[3] <= e[3] and e[4] <= b[4]:
                    continue
                keep.append(ent)
            keep.append([b, idx, []])
            self.track[b[0]] = keep
        deps.discard(idx)
        return deps

    def add(self, eng, fn, r=(), w=(), dma=False):
        idx = len(self.ins)
        deps = self._deps(list(r), list(w), idx)
        if self.pending[eng]:
            deps |= self.pending[eng]
            self.pending[eng] = set()
        self.ins.append(dict(eng=eng, fn=fn, deps=deps, dma=dma, users=set()))
        if dma:
            lo = self.last_dm[eng]
            lo.append(idx)
            if len(lo) > ND:
                lo.pop(0)
        else:
            self.last_cp[eng] = idx
        return idx

    def fence(self):
        allp = set()
        for e in ENGS:
            allp.update(self.last_dm[e])
            if self.last_cp[e] is not None:
                allp.add(self.last_cp[e])
        for e in ENGS:
            self.pending[e] = set(allp) | self.pending[e]
        self.track = {}

    def mm(self, out, lhsT, rhs, start=True, stop=True, **kw):
        r = [lhsT, rhs]
        if not start:
            r.append(out)
        return self.add("pe", lambda e: e.matmul(out, lhsT, rhs, start=start, stop=stop, **kw), r=r, w=[out])

    def transpose(self, out, in_, ident):
        return self.add("pe", lambda e: e.transpose(out, in_, ident), r=[in_, ident], w=[out])

    def act(self, out, in_, func, bias=None, scale=None, accum_out=None):
        r = [in_]
        kw = {}
        if bias is not None:
            kw["bias"] = bias
            if not isinstance(bias, (int, float)):
                r.append(bias)
        if scale is not None:
            kw["scale"] = scale
            if not isinstance(scale, (int, float)):
                r.append(scale)
        w = [out]
        if accum_out is not None:
            kw["accum_out"] = accum_out
            w.append(accum_out)
        return self.add("act", lambda e: e.activation(out, in_, func, **kw), r=r, w=w)

    def tt(self, out, in0, in1, op, eng="dve"):
        return self.add(eng, lambda e: e.tensor_tensor(out, in0, in1, op), r=[in0, in1], w=[out])

    def ts(self, out, in0, s1, s2, op0, op1=None, eng="dve"):
        r = [in0]
        if not isinstance(s1, (int, float)):
            r.append(s1)
        if s2 is not None and not isinstance(s2, (int, float)):
            r.append(s2)
        if op1 is None:
            return self.add(eng, lambda e: e.tensor_scalar(out, in0, s1, None, op0), r=r, w=[out])
        return self.add(eng, lambda e: e.tensor_scalar(out, in0, s1, s2, op0, op1), r=r, w=[out])

    def stt(self, out, in0, scalar, in1, op0, op1):
        r = [in0, in1]
        if not isinstance(scalar, (int, float)):
            r.append(scalar)
        return self.add("dve", lambda e: e.scalar_tensor_tensor(out, in0, scalar, in1, op0, op1), r=r, w=[out])

    def copy(self, out, in_, eng="dve"):
        if eng == "act":
            return self.act(out, in_, AF.Copy)
        return self.add(eng, lambda e: e.tensor_copy(out, in_), r=[in_], w=[out])

    def memset(self, ap, val, eng="dve"):
        return self.add(eng, lambda e: e.memset(ap, val), r=[], w=[ap])

    def recip(self, out, in_):
        return self.add("dve", lambda e: e.reciprocal(out, in_), r=[in_], w=[out])

    def reduce(self, out, in_, op, axis=AX.X):
        return self.add("dve", lambda e: e.tensor_reduce(out, in_, axis, op), r=[in_], w=[out])

    def wrap(self, out, in_, t1, t2):
        self.ts(t1, in_, -math.pi, 2 * math.pi, ALU.is_lt, ALU.mult)
        self.ts(t2, in_, math.pi, -2 * math.pi, ALU.is_gt, ALU.mult)
        self.tt(t1, t1, t2, ALU.add, eng="pool")
        return self.tt(out, in_, t1, ALU.add)

    def allgather(self, out, in_, groups):
        idx = self.add("pool", lambda e: e.collective_compute("AllGather", ALU.bypass, replica_groups=groups,
                                                              ins=[in_], outs=[out]), r=[in_], w=[out], dma=True)
        self.ins[idx]["cc"] = True
        return idx

    def dma(self, out, in_, q="sp", **kw):
        return self.add(q, lambda e: e.dma_start(out, in_, **kw), r=[in_], w=[out], dma=True)

    def emit(self, stack, final_wait=()):
        nc = self.nc
        ins = self.ins
        for i, it in enumerate(ins):
            for d in it["deps"]:
                ins[d]["users"].add(i)

        def new_sem(tag):
            return stack.enter_context(nc.semaphore(tag))

        eng_sem = {e: [new_sem(f"s_{e}_0")] for e in ENGS}
        eng_cnt = {e: 0 for e in ENGS}
        dma_sems = {e: [new_sem(f"d_{e}_{k}") for k in range(ND)] for e in ("sp", "pool", "act")}
        dma_cnt = {e: 0 for e in ENGS}
        for i, it in enumerate(ins):
            e = it["eng"]
            it["sig"] = None
            it["pre"] = []
            if it.get("cc"):
                s = new_sem(f"cc_{i}")
                it["sig"] = (s, CC_INC, CC_INC)
            elif it["dma"]:
                k = dma_cnt[e]
                s = dma_sems[e][k % ND]
                it["sig"] = (s, 16 * (k // ND + 1), 16)
                if k >= ND:
                    it["pre"].append((s, 16 * (k // ND)))
                dma_cnt[e] = k + 1
            else:
                need = False
                for u in it["users"]:
                    ue = ins[u]["eng"]
                    if ue != e or ins[u]["dma"]:
                        need = True
                    elif self.same and e != "pe":
                        need = True
                if need:
                    if eng_cnt[e] >= SEM_ROT:
                        eng_sem[e].append(new_sem(f"s_{e}_{len(eng_sem[e])}"))
                        eng_cnt[e] = 0
                    eng_cnt[e] += 1
                    it["sig"] = (eng_sem[e][-1], eng_cnt[e], 1)
        self.stats = dict(n_ins=len(ins), sig={e: (len(eng_sem[e]) - 1) * SEM_ROT + eng_cnt[e] for e in ENGS},
                          dma=dict(dma_cnt), per_eng={e: sum(1 for it in ins if it["eng"] == e) for e in ENGS})
        waited = {e: {} for e in ENGS}
        streams = {e: [] for e in ENGS}
        for i, it in enumerate(ins):
            e = it["eng"]
            waits = list(it["pre"])
            for d in sorted(it["deps"]):
                pd = ins[d]
                if pd["sig"] is None:
                    continue
                if pd["eng"] == e and not pd["dma"] and (e == "pe" or not self.same):
                    continue
                waits.append((pd["sig"][0], pd["sig"][1]))
            ww = []
            for s, v in waits:
                key = id(s)
                if waited[e].get(key, 0) >= v:
                    continue
                waited[e][key] = v
                ww.append((s, v))
            streams[e].append((ww, it))
        fw = [(ins[d]["sig"][0], ins[d]["sig"][1]) for d in final_wait]
        self.n_ins = len(ins)

        def run(handle, lst, extra=()):
            for ww, it in lst:
                for s, v in ww:
                    handle.wait_ge(s, v)
                inst = it["fn"](handle)
                if it["sig"] is not None:
                    inst.then_inc(it["sig"][0], it["sig"][2])
            for s, v in extra:
                handle.wait_ge(s, v)

        block = stack.enter_context(nc.Block())

        @block.tensor
        def _(e):
            run(e, streams["pe"])

        @block.scalar
        def _(e):
            run(e, streams["act"])

        @block.vector
        def _(e):
            run(e, streams["dve"])

        @block.gpsimd
        def _(e):
            run(e, streams["pool"])

        @block.sync
        def _(e):
            run(e, streams["sp"], fw)


def fft_tables(N1):
    N = 64 * N1
    H = N1 // 2
    n1 = np.arange(N1)
    k1 = np.arange(N1)
    a1 = 2 * np.pi * np.outer(n1, k1) / N1
    CS1 = np.concatenate([np.cos(a1), -np.sin(a1)], 1)
    m = np.arange(128)
    n2m = m // 2
    c2m = m % 2
    aT = 2 * np.pi * np.outer(n2m, k1) / N
    a64 = 2 * np.pi * np.outer(n2m, n2m) / 64
    same = (c2m[:, None] == c2m[None, :])
    BdC = np.cos(a64) * same
    BdS = np.sin(a64) * same
    aI = 2 * np.pi * np.outer(k1, np.arange(64)) / N
    ai = 2 * np.pi * np.outer(k1, np.arange(H)) / N1
    f = lambda a: np.ascontiguousarray(a, dtype=np.float32)
    return dict(CS1=f(CS1), TwFc=f(np.cos(aT)), TwFs=f(np.sin(aT)), BdC=f(BdC), BdS=f(BdS), BdSn=f(-BdS),
                R1=f(np.concatenate([BdC, BdS], 1)), R2=f(np.concatenate([-BdS, BdC], 1)),
                TwIc=f(np.cos(aI)), TwIs=f(np.sin(aI)), Ci=f(np.cos(ai) / N), Sin_=f(-np.sin(ai) / N))


def filter_pos_tables(n):
    N = 2 * n
    N1 = N // 64
    t = np.linspace(0.0, 1.0, n, dtype=np.float32)
    bands = np.linspace(1e-4, 15.0, 16, dtype=np.float32)
    ang = (np.float32(2.0 * math.pi / n) * np.arange(n, dtype=np.float32)[:, None] * bands).astype(np.float32)
    z = np.concatenate([t[:, None], np.cos(ang), -np.sin(ang)], axis=-1).astype(np.float32)
    idx = np.arange(N)
    d = np.where(idx < n, idx, N - idx)
    d[n] = 0
    z2 = z[d]
    BIG = 1.0e4
    negF = np.where(idx < n, -t[d], -BIG).astype(np.float32)
    negB = np.where(idx > n, -t[d], -BIG).astype(np.float32)
    negt = np.stack([negF.reshape(N1, 64), negB.reshape(N1, 64)], axis=1)
    return np.ascontiguousarray(z2.T), np.ascontiguousarray(negt)


def rope_tables():
    nf = 32
    inv = (10000.0 ** (-np.arange(nf, dtype=np.float32) / nf)).astype(np.float32)
    j = np.arange(64, dtype=np.float32)
    dd = np.arange(128)
    ang = j[None, :] * inv[dd % 32][:, None]
    R = np.zeros((128, 128), np.float32)
    for dp in range(128):
        if dp % 64 < 32:
            R[dp, dp + 32] = -1.0
        else:
            R[dp, dp - 32] = 1.0
    return np.cos(ang).astype(np.float32), np.sin(ang).astype(np.float32), np.ascontiguousarray(R.T)


P1_SPEC = None


def p1_host_inputs(inp, b, hf):
    f = lambda a: np.ascontiguousarray(a, dtype=np.float32)
    o = {}
    o["x_all"] = f(np.concatenate([inp["x"][b], inp["ctx"][b]], 0))
    cond = np.stack([inp["c"][b], inp["c_ctx"]], -1)
    o["condT"] = f(cond.reshape(8, 128, 2).transpose(1, 0, 2))
    wm = inp["w_mod"][0][:, :2048]
    o["wmod"] = f(wm.reshape(8, 128, 16, 128).transpose(2, 1, 0, 3))
    o["bmodT"] = f(inp["b_mod"][0][:2048].reshape(16, 128).T)
    o["gpreT"] = f(inp["g_pre"][0].reshape(8, 128).T)
    W = inp["e_w_in"][0]
    cols = []
    cols.append(np.arange(5120 + hf * 128, 5120 + hf * 128 + 128))
    cols.append(np.arange(5376 + hf * 128, 5376 + hf * 128 + 128))
    for h in range(4):
        hd = 4 * hf + h
        cols.append(np.arange(4096 + hd * 128, 4096 + hd * 128 + 128))
        cols.append(np.arange(5632 + hd * 128, 5632 + hd * 128 + 128))
    for ct in range(4):
        c0 = hf * 512 + ct * 128
        cols.append(np.arange(1024 + c0, 1024 + c0 + 128))
        cols.append(np.arange(2048 + c0, 2048 + c0 + 128))
        cols.append(np.arange(c0, c0 + 128))
        cols.append(np.arange(3072 + c0, 3072 + c0 + 128))
    wt = np.stack([W[:, c] for c in cols], 0)
    o["w_in"] = f(wt.reshape(26, 8, 128, 128).transpose(0, 2, 1, 3))
    cw = inp["e_conv_w"][0]
    cb = inp["e_conv_b"][0]
    convw = np.zeros((128, 4, 3, 3), np.float32)
    convb = np.zeros((128, 4, 3), np.float32)
    for ct in range(4):
        c0 = hf * 512 + ct * 128
        for s in range(3):
            convw[:, ct, s, :] = cw[:, s * 1024 + c0: s * 1024 + c0 + 128].T
            convb[:, ct, s] = cb[s * 1024 + c0: s * 1024 + c0 + 128]
    o["convw"] = convw
    o["convb"] = convb
    o["gq"] = f(inp["e_q_norm"][0].reshape(128, 1))
    o["gk"] = f(inp["e_k_norm"][0].reshape(128, 1))
    rc, rs, rT = rope_tables()
    o["ropec"], o["ropes"], o["RmatT"] = rc, rs, rT
    z2l, ntl = filter_pos_tables(NLAT)
    z2c, ntc = filter_pos_tables(NCTX)
    o["z2T_l"], o["negt_l"], o["z2T_c"], o["negt_c"] = z2l, ntl, z2c, ntc
    o["fw1"] = f(inp["e_filt_w1"][0])
    o["fb1"] = f(inp["e_filt_b1"][0].reshape(64, 1))
    o["ffr"] = f(inp["e_filt_freq"][0].reshape(64, 1))
    o["fw2"] = f(inp["e_filt_w2"][0])
    o["fb2"] = f(inp["e_filt_b2"][0].reshape(64, 1))
    w3 = inp["e_filt_w3"][0].reshape(64, 2, 1024)
    o["fw3"] = f(w3[:, :, hf * 512:(hf + 1) * 512])
    min_decay = math.log(1e-2) / 1.5
    max_decay = math.log(1e-2) / 0.3
    deltas = np.abs(np.linspace(min_decay, max_decay, 1024, dtype=np.float32))
    o["adel"] = f(np.broadcast_to(deltas[hf * 512:(hf + 1) * 512][None, :], (128, 512)))
    o["hyb"] = f(inp["e_hy_bias"][0][hf * 512:(hf + 1) * 512].reshape(1, 512))
    for N1, tag in ((128, "l"), (8, "c")):
        for k, v in fft_tables(N1).items():
            o[f"ft_{tag}_{k}"] = v
    o["identF"] = np.eye(128, dtype=np.float32)
    return o


def _declare_inputs(nc, sample):
    aps = {}
    for k, v in sample.items():
        dt = F32 if v.dtype == np.float32 else BF16
        aps[k] = nc.dram_tensor(k, list(v.shape), dt, kind="ExternalInput").ap()
    return aps


class Chunked:
    def __init__(self, chunks, rows, ntok):
        self.chunks = chunks
        self.rows = rows
        self.ntok = ntok

    @staticmethod
    def make(nc, name, rows, ntok, dt, csize, **kw):
        ch = []
        for k, t0 in enumerate(range(0, ntok, csize)):
            n = min(csize, ntok - t0)
            ch.append((t0, n, nc.dram_tensor(f"{name}_{k}", [rows, n], dt, **kw).ap()))
        return Chunked(ch, rows, ntok)

    @staticmethod
    def wrap(ap):
        return Chunked([(0, ap.shape[1], ap)], ap.shape[0], ap.shape[1])

    def cols(self, r0, r1, t0, n):
        for (c0, cn, ap) in self.chunks:
            if c0 <= t0 and t0 + n <= c0 + cn:
                return ap[r0:r1, t0 - c0:t0 - c0 + n]
        raise AssertionError(("straddles chunks", t0, n))

    def pieces(self, r0, r1):
        for (c0, cn, ap) in self.chunks:
            yield c0, cn, ap[r0:r1, :]


class Ctx:
    def __init__(self, nc, stack):
        self.nc = nc
        self.stack = stack
        self.n = 0

    def sb(self, name, shape, dt, stack=None):
        self.n += 1
        return (stack or self.stack).enter_context(self.nc.sbuf_tensor(f"{name}_{self.n}", list(shape), dt))

    def ps(self, name, shape, dt, stack=None):
        self.n += 1
        return (stack or self.stack).enter_context(self.nc.psum_tensor(f"{name}_{self.n}", list(shape), dt))


def load_fft_tabs(S, C, A, tag, N1):
    H = N1 // 2
    T = {}
    def ld(name, shape, dt, q):
        t = C.sb(f"ft{tag}{name}", shape, dt)
        S.dma(t[:], A[f"ft_{tag}_{name}"], q=q)
        return t
    T["CS1"] = ld("CS1", [N1, 2 * N1], BF16, "pool")
    T["TwFc"] = ld("TwFc", [128, N1], F32, "sp")
    T["TwFs"] = ld("TwFs", [128, N1], F32, "sp")
    T["BdC"] = ld("BdC", [128, 128], BF16, "pool")
    T["BdS"] = ld("BdS", [128, 128], BF16, "pool")
    T["BdSn"] = ld("BdSn", [128, 128], BF16, "pool")
    T["R1"] = ld("R1", [128, 256], BF16, "pool")
    T["R2"] = ld("R2", [128, 256], BF16, "pool")
    T["TwIc"] = ld("TwIc", [N1, 64], F32, "sp")
    T["TwIs"] = ld("TwIs", [N1, 64], F32, "sp")
    T["Ci"] = ld("Ci", [N1, H], BF16, "pool")
    T["Sin_"] = ld("Sin_", [N1, H], BF16, "pool")
    return T


def fft_conv_core(S, C, PS, T, N1, X, Xf, Ball_re, Ball_im, tmps, evac):
    H = N1 // 2
    W2 = 2 * N1
    Gf = 512 // W2
    NG = 64 // Gf
    GN = Gf * N1
    tc = T["TwFc"][:, :].unsqueeze(1).broadcast_to([128, 2 * Gf, N1])
    tsn = T["TwFs"][:, :].unsqueeze(1).broadcast_to([128, 2 * Gf, N1])
    tic = T["TwIc"][:, :].unsqueeze(1).unsqueeze(3).broadcast_to([N1, 4, 64, 2])
    tis = T["TwIs"][:, :].unsqueeze(1).unsqueeze(3).broadcast_to([N1, 4, 64, 2])

    def stageA(g):
        m1, m2, Ar, Ai, Br_, Bi_, Fh, Yr, Yi = tmps[g % 2]
        pg1, pf1 = PS[2 * (g % 2)], PS[2 * (g % 2) + 1]
        p0 = g * Gf
        for j in range(Gf):
            S.mm(pg1[:, j * W2:(j + 1) * W2], X[0:H, p0 + j, :], T["CS1"][0:H, :])
        for j in range(Gf):
            S.mm(pf1[:, j * W2:(j + 1) * W2], Xf[0:N1, p0 + j, :], T["CS1"][0:N1, :])
        m1a = m1[:, :].rearrange("p (a n) -> p a n", n=N1)
        m2a = m2[:, :].rearrange("p (a n) -> p a n", n=N1)
        m1v = m1[:, :].rearrange("p (a r n) -> p a r n", r=2, n=N1)
        m2v = m2[:, :].rearrange("p (a r n) -> p a r n", r=2, n=N1)
        for (pp, ar, ai) in ((pg1, Ar, Ai), (pf1, Br_, Bi_)):
            v = pp[:, :].rearrange("p (a n) -> p a n", n=N1)
            S.tt(m1a, v, tc, ALU.mult)
            S.tt(m2a, v, tsn, ALU.mult)
            arv = ar[:, :].rearrange("p (a n) -> p a n", n=N1)
            aiv = ai[:, :].rearrange("p (a n) -> p a n", n=N1)
            S.tt(arv, m1v[:, :, 0, :], m2v[:, :, 1, :], ALU.add, eng="pool")
            S.tt(aiv, m1v[:, :, 1, :], m2v[:, :, 0, :], ALU.subtract, eng=("dve" if pp is pf1 else "pool"))

    def stageB(g):
        m1, m2, Ar, Ai, Br_, Bi_, Fh, Yr, Yi = tmps[g % 2]
        pg3, pf3 = PS[4], PS[5]
        for (pp, ar, ai) in ((pg3, Ar, Ai), (pf3, Br_, Bi_)):
            S.mm(pp[:, 0:GN], T["BdC"][:], ar[:, :], start=True, stop=False)
            S.mm(pp[:, 0:GN], T["BdS"][:], ai[:, :], start=False, stop=True)
            S.mm(pp[:, GN:2 * GN], T["BdC"][:], ai[:, :], start=True, stop=False)
            S.mm(pp[:, GN:2 * GN], T["BdSn"][:], ar[:, :], start=False, stop=True)
        S.act(Fh[:, :], pf3[:, :], AF.Copy)
        gv = pg3[:, :].rearrange("p (r n) -> p r n", r=2)
        fr = Fh[:, 0:GN].unsqueeze(1).broadcast_to([128, 2, GN])
        fi = Fh[:, GN:2 * GN].unsqueeze(1).broadcast_to([128, 2, GN])
        q1 = m1[:, :].rearrange("p (r n) -> p r n", r=2)
        q2 = m2[:, :].rearrange("p (r n) -> p r n", r=2)
        S.tt(q1, gv, fr, ALU.mult)
        S.tt(q2, gv, fi, ALU.mult)
        S.tt(Yr[:, :], q1[:, 0, :], q2[:, 1, :], ALU.subtract, eng="pool")
        S.tt(Yi[:, :], q2[:, 0, :], q1[:, 1, :], ALU.add, eng="dve")

    def stageC(g):
        m1, m2, Ar, Ai, Br_, Bi_, Fh, Yr, Yi = tmps[g % 2]
        pI = PS[6]
        for sub in range(Gf // 2):
            pr0 = g * Gf + 2 * sub
            for jj in range(2):
                j = 2 * sub + jj
                S.mm(pI[0:N1, jj * 256:(jj + 1) * 256], Yr[:, j * N1:(j + 1) * N1], T["R1"][:], start=True, stop=False)
                S.mm(pI[0:N1, jj * 256:(jj + 1) * 256], Yi[:, j * N1:(j + 1) * N1], T["R2"][:], start=False, stop=True)
            iv = pI[0:N1, :].rearrange("p (a n c) -> p a n c", a=4, c=2)
            n1v = m1[0:N1, :].rearrange("p (a n c) -> p a n c", a=4, c=2)
            n2v = m2[0:N1, :].rearrange("p (a n c) -> p a n c", a=4, c=2)
            S.tt(n1v, iv, tic, ALU.mult)
            S.tt(n2v, iv, tis, ALU.mult)
            n1p = m1[0:N1, :].rearrange("p (j r n c) -> p j r n c", j=2, r=2, c=2)
            n2p = m2[0:N1, :].rearrange("p (j r n c) -> p j r n c", j=2, r=2, c=2)
            ore = Ball_re[0:N1, :, 2 * pr0:2 * pr0 + 4].rearrange("p n (j c) -> p j n c", c=2)
            oim = Ball_im[0:N1, :, 2 * pr0:2 * pr0 + 4].rearrange("p n (j c) -> p j n c", c=2)
            S.tt(ore, n1p[:, :, 0, :, :], n2p[:, :, 1, :, :], ALU.subtract, eng="pool")
            S.tt(oim, n2p[:, :, 0, :, :], n1p[:, :, 1, :, :], ALU.add, eng="pool")

    stageA(0)
    for g in range(NG):
        if g + 1 < NG:
            stageA(g + 1)
        stageB(g)
        stageC(g)
    ng = min(64, 512 // H)
    k = 0
    for n2_0 in range(0, 64, ng):
        po = PS[k % 2]
        k += 1
        for j in range(ng):
            n2 = n2_0 + j
            S.mm(po[:, j * H:(j + 1) * H], Ball_re[0:N1, n2, :], T["Ci"][0:N1, :], start=True, stop=False)
            S.mm(po[:, j * H:(j + 1) * H], Ball_im[0:N1, n2, :], T["Sin_"][0:N1, :], start=False, stop=True)
        evac(po, n2_0, ng)


def build_p1(nc, sample, n_ct=4, do_att=True, do_hy=True, dbg=None, stop=None, shared=None):
    if shared is None:
        A = _declare_inputs(nc, sample)
        mix = Chunked.wrap(nc.dram_tensor("mix", [1024, NTOK], BF16, kind="ExternalOutput").ap())
    else:
        A = shared["A"]
        mix = shared["mix0"]
    xf_l = nc.dram_tensor("xf_l", [4, 128, 8192], BF16, kind="Internal").ap()
    xf_c = nc.dram_tensor("xf_c", [4, 8, 8192], BF16, kind="Internal").ap()
    outs = []
    with ExitStack() as st:
        if shared is None:
            S = Sched(nc)
            C = Ctx(nc, st)
            PS = [C.ps(f"pb{i}", [128, 512], F32) for i in range(7)]
            pTb = C.ps("pTb", [128, 1024], BF16)
        else:
            S, C, PS, pTb = shared["S"], shared["C"], shared["PS"], shared["pTb"]
            C.stack = st
        identF = C.sb("identF", [128, 128], F32)
        identB = C.sb("identB", [128, 128], BF16)
        onesF = C.sb("onesF", [128, 128], F32)
        onesB = C.sb("onesB", [128, 128], BF16)
        S.dma(identF[:], A["identF"])
        S.dma(identB[:], A["identF"], q="pool")
        S.memset(onesF[:], 1.0)
        S.memset(onesB[:], 1.0, eng="pool")
        rnT = C.sb("rnT", [128, 4, 2], F32)
        TL = load_fft_tabs(S, C, A, "l", 128)
        TC = load_fft_tabs(S, C, A, "c", 8)
        ftmp = []
        for par in range(2):
            ftmp.append((C.sb("m1", [128, 512], F32), C.sb("m2", [128, 512], F32),
                         C.sb("Ar", [128, 256], BF16), C.sb("Ai", [128, 256], BF16),
                         C.sb("Br", [128, 256], BF16), C.sb("Bi", [128, 256], BF16),
                         C.sb("Fh", [128, 512], F32),
                         C.sb("Yr", [128, 256], BF16), C.sb("Yi", [128, 256], BF16)))
        if stop == 'const':
            S.emit(st, final_wait=[])
            return nc

        import os
        SKIP = os.environ.get("K_SKIP", "")
        if do_hy and "F" not in SKIP:
            with ExitStack() as s0:
                fw1 = C.sb("fw1", [33, 64], F32, s0); S.dma(fw1[:], A["fw1"])
                fw2 = C.sb("fw2", [64, 64], F32, s0); S.dma(fw2[:], A["fw2"])
                fb1 = C.sb("fb1", [64, 1], F32, s0); S.dma(fb1[:], A["fb1"])
                fb2 = C.sb("fb2", [64, 1], F32, s0); S.dma(fb2[:], A["fb2"])
                ffr = C.sb("ffr", [64, 1], F32, s0); S.dma(ffr[:], A["ffr"])
                fw3 = C.sb("fw3", [64, 2, 512], BF16, s0); S.dma(fw3[:], A["fw3"], q="pool")
                adel = C.sb("adel", [128, 512], F32, s0); S.dma(adel[:], A["adel"])
                hyb = C.sb("hyb", [1, 512], F32, s0); S.dma(hyb[:], A["hyb"])
                frb1 = C.sb("frb1", [64, 1], F32, s0)
                frb2 = C.sb("frb2", [64, 1], F32, s0)
                S.tt(frb1[:], fb1[:], ffr[:], ALU.mult)
                S.tt(frb2[:], fb2[:], ffr[:], ALU.mult)
                hdn2 = C.sb("hdn2", [64, 8192], BF16, s0)
                z2 = C.sb("z2", [33, 512], F32, s0)
                u1 = C.sb("u1", [64, 512], F32, s0)
                h1 = C.sb("h1", [64, 512], F32, s0)
                wt1 = C.sb("wt1", [64, 512], F32, s0)
                wt2 = C.sb("wt2", [64, 512], F32, s0)
                negt = C.sb("negt", [128, 2, 64], F32, s0)
                wbuf_s = [C.sb("wbuf", [128, 2, 512], F32, s0) for _ in range(2)]
                fgrp_s = [C.sb("fgrp", [128, 512], F32, s0) for _ in range(2)]
                fgr2_s = [C.sb("fgr2", [128, 512], F32, s0) for _ in range(2)]
                acc = C.sb("acc", [128, 512], F32, s0)
                nrow = C.sb("nrow", [1, 512], F32, s0)
                ncol = C.sb("ncol", [128, 1], F32, s0)
                Xfo = C.sb("Xfo", [128, 4, 64, 128], BF16, s0)
                for (tag, N1, z2T, ntab, xfd, li) in (("l", 128, A["z2T_l"], A["negt_l"], xf_l, 0),
                                                      ("c", 8, A["z2T_c"], A["negt_c"], xf_c, 1)):
                    N = 64 * N1
                    S.dma(negt[0:N1, :, :], ntab)
                    for blk in range(N // 512):
                        sl = slice(blk * 512, (blk + 1) * 512)
                        S.dma(z2[:], z2T[:, sl])
                        S.mm(PS[5][0:64, :], fw1[:], z2[:])
                        S.ts(u1[:], PS[5][0:64, :], ffr[:, 0:1], frb1[:, 0:1], ALU.mult, ALU.add)
                        S.wrap(u1[:], u1[:], wt1[:], wt2[:])
                        S.wrap(u1[:], u1[:], wt1[:], wt2[:])
                        S.act(h1[:], u1[:], AF.Sin)
                        S.mm(PS[6][0:64, :], fw2[:], h1[:])
                        S.ts(u1[:], PS[6][0:64, :], ffr[:, 0:1], frb2[:, 0:1], ALU.mult, ALU.add)
                        S.wrap(u1[:], u1[:], wt1[:], wt2[:])
                        S.wrap(u1[:], u1[:], wt1[:], wt2[:])
                        S.act(hdn2[:, sl], u1[:], AF.Sin)
                    S.memset(acc[0:N1, :], 0.0)
                    for n2 in range(64):
                        wbuf, fgrp, fgr2 = wbuf_s[n2 % 2], fgrp_s[n2 % 2], fgr2_s[n2 % 2]
                        pF = (PS[1 + 2 * (n2 % 2)], PS[2 + 2 * (n2 % 2)])
                        for dr in range(2):
                            S.mm(pF[dr][0:N1, :], hdn2[:, n2:N:64], fw3[:, dr, :])
                            S.act(wbuf[0:N1, dr, :], adel[0:N1, :], AF.Exp, scale=negt[0:N1, dr, n2:n2 + 1])
                        S.tt(fgrp[0:N1, :], pF[0][0:N1, :], wbuf[0:N1, 0, :], ALU.mult)
                        S.tt(fgr2[0:N1, :], pF[1][0:N1, :], wbuf[0:N1, 1, :], ALU.mult)
                        S.tt(fgrp[0:N1, :], fgrp[0:N1, :], fgr2[0:N1, :], ALU.add, eng="pool")
                        S.act(fgr2[0:N1, :], fgrp[0:N1, :], AF.Abs)
                        S.tt(acc[0:N1, :], acc[0:N1, :], fgr2[0:N1, :], ALU.add, eng="pool")
                        ov = Xfo[0:N1, :, :, 2 * n2:2 * n2 + 2]
                        iv = fgrp[0:N1, :].rearrange("p (t a c) -> p t a c", t=4, c=2)
                        S.copy(ov, iv, eng="act")
                    S.mm(PS[6][0:1, 0:512], onesF[0:N1, 0:1], acc[0:N1, :])
                    S.tt(nrow[:], PS[6][0:1, 0:512], hyb[0:1, :], ALU.mult)
                    for ct in range(n_ct):
                        S.mm(PS[5][0:128, 0:1], acc[0:N1, ct * 128:(ct + 1) * 128], onesF[0:N1, 0:1])
                        S.copy(ncol[:], PS[5][0:128, 0:1])
                        S.recip(rnT[:, ct, li:li + 1], ncol[:])
                        tap = Xfo[0:1, ct, :, 0:2]
                        S.tt(tap, tap, nrow[0:1, ct * 128:(ct + 1) * 128].rearrange("p (a c) -> p a c", c=2), ALU.add)
                        S.dma(xfd[ct, :, :], Xfo[0:N1, ct, :, :].rearrange("p a n -> p (a n)"))
            S.fence()

        hT = C.sb("hT", [128, 8, NTOK], BF16)
        with ExitStack() as s1:
            condT = C.sb("condT", [128, 8, 2], F32, s1); S.dma(condT[:], A["condT"])
            scond = C.sb("scond", [128, 8, 2], F32, s1)
            S.act(scond[:], condT[:], AF.Silu)
            bmodT = C.sb("bmodT", [128, 16], F32, s1); S.dma(bmodT[:], A["bmodT"])
            gpreT = C.sb("gpreT", [128, 8], F32, s1); S.dma(gpreT[:], A["gpreT"])
            modT = C.sb("modT", [128, 16, 2], F32, s1)
            wm = [C.sb(f"wm{i}", [128, 8, 128], F32, s1) for i in range(2)]
            for fb in range(16):
                w = wm[fb % 2]
                S.dma(w[:], A["wmod"][fb], q="sp" if fb % 2 == 0 else "act")
                for kt in range(8):
                    S.mm(PS[fb % 2][:, 0:2], w[:, kt, :], scond[:, kt, :], start=(kt == 0), stop=(kt == 7))
                S.ts(modT[:, fb, :], PS[fb % 2][:, 0:2], bmodT[:, fb:fb + 1], None, ALU.add)
            if stop == 'mod':
                S.emit(st, final_wait=[])
                return nc
            Amod = C.sb("Amod", [128, 8, 2], F32, s1)
            S.ts(Amod[:], modT[:, 8:16, :], 1.0, None, ALU.add)
            S.tt(Amod[:], Amod[:], gpreT[:, :].unsqueeze(2).broadcast_to([128, 8, 2]), ALU.mult)
            xts = [C.sb(f"xt{i}", [128, 1024], F32, s1) for i in range(3)]
            junk = C.sb("junk", [128, 1024], BF16, s1)
            ssq = C.sb("ssq", [128, 34], F32, s1)
            rstd = C.sb("rstd", [128, 34], F32, s1)
            import os
            for i in range(int(os.environ.get('K_NT', '34'))):
                xt = xts[i % 3]
                S.dma(xt[:], A["x_all"][i * 128:(i + 1) * 128, :], q="sp" if i % 2 == 0 else "act")
                import os
                lvl = int(os.environ.get("K_DEBUG", "9"))
                if lvl < 1:
                    continue
                S.act(junk[:], xt[:], AF.Square, accum_out=ssq[:, i:i + 1])
                if lvl < 2:
                    continue
                S.act(rstd[:, i:i + 1], ssq[:, i:i + 1], AF.Sqrt, bias=EPS, scale=1.0 / D)
                if lvl < 3:
                    continue
                S.recip(rstd[:, i:i + 1], rstd[:, i:i + 1])
                S.ts(xt[:], xt[:], rstd[:, i:i + 1], None, ALU.mult)
                if lvl < 4:
                    continue
                jj = 0 if i < 32 else 1
                for half in range(2):
                    pt = PS[2 + half]
                    for k4 in range(4):
                        kt = half * 4 + k4
                        S.transpose(pt[:, k4 * 128:(k4 + 1) * 128], xt[:, kt * 128:(kt + 1) * 128], identF[:])
                    if lvl < 5:
                        continue
                    for k4 in range(4):
                        kt = half * 4 + k4
                        S.ts(hT[:, kt, i * 128:(i + 1) * 128], pt[:, k4 * 128:(k4 + 1) * 128],
                             Amod[:, kt, jj:jj + 1], modT[:, kt, jj:jj + 1], ALU.mult, ALU.add)
        S.fence()
        if dbg == "hT":
            hTo = nc.dram_tensor("hTo", [128, 8, NTOK], BF16, kind="ExternalOutput").ap()
            outs.append(S.dma(hTo, hT[:]))

        wbufs = [C.sb(f"wcol{i}", [128, 8, 128], BF16) for i in range(2)]
        wstate = {"n": 0}

        def load_w(tile_idx):
            w = wbufs[wstate["n"] % 2]
            wstate["n"] += 1
            S.dma(w[:], A["w_in"][tile_idx], q="pool")
            return w

        blocks = [(tb * 512, 512) for tb in range(8)] + [(4096, 256)]

        def proj_fm(w, blk, pt):
            t0, n = blk
            for kt in range(8):
                S.mm(pt[:, 0:n], w[:, kt, :], hT[:, kt, t0:t0 + n], start=(kt == 0), stop=(kt == 7))

        if do_att:
            with ExitStack() as s2:
                gq = C.sb("gq", [128, 1], F32, s2); S.dma(gq[:], A["gq"])
                gk = C.sb("gk", [128, 1], F32, s2); S.dma(gk[:], A["gk"])
                S.ts(gq[:], gq[:], 128.0 ** -0.5, None, ALU.mult)
                ropec = C.sb("ropec", [128, 64], F32, s2); S.dma(ropec[:], A["ropec"])
                ropes = C.sb("ropes", [128, 64], F32, s2); S.dma(ropes[:], A["ropes"])
                RmT = C.sb("RmT", [128, 128], F32, s2); S.dma(RmT[:], A["RmatT"])
                KT = C.sb("KT", [128, NTOK], BF16, s2)
                V = C.sb("V", [128, 34, 128], BF16, s2)
                sq = C.sb("sq", [128, 512], BF16, s2)
                rinv = C.sb("rinv", [128, 512], F32, s2)
                kn = C.sb("kn", [128, 512], F32, s2)
                t1 = C.sb("t1", [128, 512], F32, s2)
                t2 = C.sb("t2", [128, 512], F32, s2)

                def norm_rope(pt, blk, g, out_ap):
                    t0, n = blk
                    S.act(sq[:, 0:n], pt[:, 0:n], AF.Square)
                    S.mm(PS[2][:, 0:n], onesB[:], sq[:, 0:n])
                    S.act(rinv[:, 0:n], PS[2][:, 0:n], AF.Sqrt, bias=EPS, scale=1.0 / 128)
                    S.recip(rinv[:, 0:n], rinv[:, 0:n])
                    if t0 >= NLAT:
                        S.stt(out_ap, pt[:, 0:n], g[:, 0:1], rinv[:, 0:n], ALU.mult, ALU.mult)
                        return
                    S.stt(kn[:, 0:n], pt[:, 0:n], g[:, 0:1], rinv[:, 0:n], ALU.mult, ALU.mult)
                    S.mm(PS[2][:, 0:n], RmT[:], kn[:, 0:n])
                    r0 = t0 // 64
                    for (p0, p1) in ((0, 64), (64, 128)):
                        if p0 == 0:
                            cv = ropec[p0:p1, r0:r0 + 8].unsqueeze(2).broadcast_to([64, 8, 64])
                            sv = ropes[p0:p1, r0:r0 + 8].unsqueeze(2).broadcast_to([64, 8, 64])
                        else:
                            cv = ropec[p0:p1, :].unsqueeze(1).broadcast_to([64, 8, 64])
                            sv = ropes[p0:p1, :].unsqueeze(1).broadcast_to([64, 8, 64])
                        v3 = lambda a: a[p0:p1, 0:512].rearrange("p (r c) -> p r c", c=64)
                        S.tt(v3(t1), v3(kn), cv, ALU.mult, eng="pool")
                        S.tt(v3(t2), v3(PS[2]), sv, ALU.mult)
                    S.tt(out_ap, t1[:, 0:n], t2[:, 0:n], ALU.add)

                wk = load_w(0)
                for bi, blk in enumerate(blocks):
                    proj_fm(wk, blk, PS[bi % 2])
                    norm_rope(PS[bi % 2], blk, gk, KT[:, blk[0]:blk[0] + blk[1]])
                wv = load_w(1)
                for i4 in range(0, 34, 4):
                    nt = min(4, 34 - i4)
                    pt = PS[(i4 // 4) % 2]
                    for j in range(nt):
                        i = i4 + j
                        for kt in range(8):
                            S.mm(pt[:, j * 128:(j + 1) * 128], hT[:, kt, i * 128:(i + 1) * 128], wv[:, kt, :],
                                 start=(kt == 0), stop=(kt == 7))
                    S.act(V[:, i4:i4 + nt, :].rearrange("p a d -> p (a d)"), pt[:, 0:nt * 128], AF.Copy)
                qTb = [C.sb(f"qTb{i}", [128, 512], BF16, s2) for i in range(2)]
                gab = [C.sb(f"gab{i}", [128, 512], F32, s2) for i in range(2)]
                Pb = [C.sb(f"Pb{i}", [128, 512], BF16, s2) for i in range(3)]
                rden = C.sb("rden", [1, 512], F32, s2)
                o1 = C.sb("o1", [128, 512], F32, s2)
                mo = [C.sb(f"mo{i}", [128, 512], BF16, s2) for i in range(2)]
                work = [(h, bi) for h in range(4) for bi in range(len(blocks))]
                wq = {}

                def prep(idx):
                    h, bi = work[idx]
                    blk = blocks[bi]
                    if bi == 0:
                        wq["q"] = load_w(2 + 2 * h)
                        wq["g"] = load_w(3 + 2 * h)
                    proj_fm(wq["q"], blk, PS[0])
                    norm_rope(PS[0], blk, gq, qTb[idx % 2][:, 0:blk[1]])
                    proj_fm(wq["g"], blk, PS[1])
                    S.act(gab[idx % 2][:, 0:blk[1]], PS[1][:, 0:blk[1]], AF.Silu)

                def attend(idx):
                    h, bi = work[idx]
                    t0, n = blocks[bi]
                    ktiles = list(range(34)) if t0 < NLAT else [32, 33]
                    pO, pD = PS[5], PS[6]
                    nk = len(ktiles)

                    def smm(ki):
                        kt_i = ktiles[ki]
                        S.mm(PS[3 + ki % 2][:, 0:n], KT[:, kt_i * 128:(kt_i + 1) * 128], qTb[idx % 2][:, 0:n])
                    smm(0)
                    if nk > 1:
                        smm(1)
                    for ki, kt_i in enumerate(ktiles):
                        pS = PS[3 + ki % 2]
                        P = Pb[ki % 3]
                        S.act(P[:, 0:n], pS[:, 0:n], AF.Exp)
                        S.mm(pO[:, 0:n], V[:, kt_i, :], P[:, 0:n], start=(ki == 0), stop=(ki == nk - 1))
                        S.mm(pD[0:1, 0:n], onesB[:, 0:1], P[:, 0:n], start=(ki == 0), stop=(ki == nk - 1))
                        if ki + 2 < nk:
                            smm(ki + 2)
                    S.recip(rden[0:1, 0:n], pD[0:1, 0:n])
                    S.mm(PS[2][:, 0:n], onesF[0:1, :], rden[0:1, 0:n])
                    S.tt(o1[:, 0:n], pO[:, 0:n], gab[idx % 2][:, 0:n], ALU.mult)
                    m = mo[idx % 2]
                    S.tt(m[:, 0:n], o1[:, 0:n], PS[2][:, 0:n], ALU.mult)
                    outs.append(S.dma(mix.cols(512 + h * 128, 512 + (h + 1) * 128, t0, n), m[:, 0:n], q="sp"))

                prep(0)
                for idx in range(len(work)):
                    if idx + 1 < len(work):
                        prep(idx + 1)
                    attend(idx)
            S.fence()

        if do_hy:
            with ExitStack() as s3:
                convw = C.sb("convw", [128, 4, 3, 3], F32, s3); S.dma(convw[:], A["convw"])
                convb = C.sb("convb", [128, 4, 3], F32, s3); S.dma(convb[:], A["convb"])
                SEG = NLAT + 2 + NCTX + 2
                ust = C.sb("ust", [128, SEG], BF16, s3)
                S.memset(ust[:], 0.0)
                ctmp = C.sb("ctmp", [128, 1024], F32, s3)
                x1c = C.sb("x1c", [128, NTOK], BF16, s3)
                gT = C.sb("gT", [128, NTOK], BF16, s3)
                xg = C.sb("xg", [128, NTOK], BF16, s3)
                Xl = C.sb("Xl", [64, 64, 128], BF16, s3)
                Xfl = C.sb("Xfl", [128, 64, 128], BF16, s3)
                Bre = C.sb("Bre", [128, 64, 128], BF16, s3)
                Bim = C.sb("Bim", [128, 64, 128], BF16, s3)
                segs = [(k * 1024, k * 1024 + 1, 1024) for k in range(4)] + [(4096, NLAT + 3, 256)]

                def conv_stream(ct, s, out_fn):
                    for (t0, u0, n) in segs:
                        S.act(ctmp[:, 0:n], ust[:, u0:u0 + n], AF.Identity, bias=convb[:, ct, s:s + 1],
                              scale=convw[:, ct, s, 1:2])
                        S.stt(ctmp[:, 0:n], ust[:, u0 - 1:u0 - 1 + n], convw[:, ct, s, 0:1], ctmp[:, 0:n],
                              ALU.mult, ALU.add)
                        out_fn(t0, n, ust[:, u0 + 1:u0 + 1 + n], convw[:, ct, s, 2:3], ctmp[:, 0:n])

                def ust_cols(blk):
                    t0, n = blk
                    return slice(t0 + 1, t0 + 1 + n) if t0 < NLAT else slice(NLAT + 3, NLAT + 3 + n)

                for ct in range(n_ct):
                    wbase = 10 + 4 * ct
                    S.dma(Xfl[:, :, :].rearrange("p a n -> p (a n)"), xf_l[ct, :, :], q="act")
                    w = load_w(wbase + 0)
                    for bi, blk in enumerate(blocks):
                        proj_fm(w, blk, PS[bi % 2])
                        S.act(ust[:, ust_cols(blk)], PS[bi % 2][:, 0:blk[1]], AF.Copy)
                    conv_stream(ct, 1, lambda t0, n, a, sc, b: S.stt(x1c[:, t0:t0 + n], a, sc, b, ALU.mult, ALU.add))
                    w = load_w(wbase + 1)
                    for bi, blk in enumerate(blocks):
                        proj_fm(w, blk, PS[bi % 2])
                        S.act(ust[:, ust_cols(blk)], PS[bi % 2][:, 0:blk[1]], AF.Copy)

                    def gfn(t0, n, a, sc, b):
                        S.stt(b, a, sc, b, ALU.mult, ALU.add)
                        S.tt(gT[:, t0:t0 + n], b, x1c[:, t0:t0 + n], ALU.mult, eng="pool")
                    conv_stream(ct, 2, gfn)
                    w = load_w(wbase + 3)
                    for bi, blk in enumerate(blocks):
                        proj_fm(w, blk, PS[bi % 2])
                        S.act(xg[:, blk[0]:blk[0] + blk[1]], PS[bi % 2][:, 0:blk[1]], AF.Silu)
                    w = load_w(wbase + 2)
                    for bi, blk in enumerate(blocks):
                        proj_fm(w, blk, PS[bi % 2])
                        S.act(ust[:, ust_cols(blk)], PS[bi % 2][:, 0:blk[1]], AF.Copy)

                    def xfn(t0, n, a, sc, b):
                        S.stt(b, a, sc, b, ALU.mult, ALU.add)
                        S.tt(xg[:, t0:t0 + n], b, xg[:, t0:t0 + n], ALU.mult, eng="pool")
                    conv_stream(ct, 0, xfn)
                    for n2_0 in range(0, 64, 8):
                        for j in range(8):
                            S.transpose(pTb[0:64, j * 128:(j + 1) * 128], gT[:, n2_0 + j:NLAT:64], identB[:])
                        ov = Xl[0:64, :, :].rearrange("p a (n c) -> p n a c", c=2)[:, n2_0:n2_0 + 8, :, :]
                        iv = pTb[0:64, :].rearrange("p (n a c) -> p n a c", n=8, c=2)
                        S.copy(ov, iv, eng="act")

                    def mk_evac(tok0, H, li):
                        def evac(po, n2_0, ng):
                            ov = gT[:, tok0:tok0 + 64 * H].rearrange("p (a n) -> p n a", n=64)[:, n2_0:n2_0 + ng, :]
                            xv = xg[:, tok0:tok0 + 64 * H].rearrange("p (a n) -> p n a", n=64)[:, n2_0:n2_0 + ng, :]
                            iv = po[:, 0:ng * H].rearrange("p (n a) -> p n a", a=H)
                            S.stt(ov, iv, rnT[:, ct, li:li + 1], xv, ALU.mult, ALU.mult)
                        return evac
                    if "L" not in SKIP:
                        fft_conv_core(S, C, PS, TL, 128, Xl, Xfl, Bre, Bim, ftmp, mk_evac(0, 64, 0))
                    S.dma(Xfl[0:8, :, :].rearrange("p a n -> p (a n)"), xf_c[ct, :, :], q="act")
                    for n2_0 in range(0, 64, 8):
                        for j in range(8):
                            S.transpose(pTb[0:4, j * 128:(j + 1) * 128], gT[:, NLAT + n2_0 + j:NTOK:64], identB[:])
                        ov = Xl[0:4, :, :].rearrange("p a (n c) -> p n a c", c=2)[:, n2_0:n2_0 + 8, :, :]
                        iv = pTb[0:4, :].rearrange("p (n a c) -> p n a c", n=8, c=2)
                        S.copy(ov, iv, eng="act")
                    if "C" not in SKIP:
                        fft_conv_core(S, C, PS, TC, 8, Xl, Xfl, Bre, Bim, ftmp, mk_evac(NLAT, 4, 1))
                    for (c0, cn, cap) in mix.pieces(ct * 128, (ct + 1) * 128):
                        outs.append(S.dma(cap, gT[:, c0:c0 + cn], q="sp"))
        if shared is None:
            S.emit(st, final_wait=outs)
    return nc


def gate_rows(S, C, PS, A, pre, scond, onesF, stk):
    rows = C.sb("grow", [2, 1024], F32, stk)
    bro = C.sb("gbro", [2, 1024], F32, stk); S.dma(bro[:], A[pre + "bgate2"])
    gpo = C.sb("ggpo", [2, 1024], F32, stk); S.dma(gpo[:], A[pre + "gpost2"])
    sel2 = C.sb("sel2", [2, 2, 128], F32, stk); S.dma(sel2[:], A["sel2"])
    wg = [C.sb(f"wgate{i}", [128, 1024], F32, stk) for i in range(2)]
    for kt in range(8):
        w = wg[kt % 2]
        S.dma(w[:], A[pre + "wgate"][kt], q="sp" if kt % 2 == 0 else "act")
        for nb in range(2):
            S.mm(PS[nb][0:2, :], scond[:, kt, 0:2], w[:, nb * 512:(nb + 1) * 512], start=(kt == 0), stop=(kt == 7))
    for nb in range(2):
        S.tt(rows[:, nb * 512:(nb + 1) * 512], PS[nb][0:2, :], bro[:, nb * 512:(nb + 1) * 512], ALU.add)
    S.tt(rows[:], rows[:], gpo[:], ALU.mult)
    gg = []
    for j in range(2):
        g = C.sb(f"gg{j}", [128, 1024], F32, stk)
        for nb in range(2):
            S.mm(PS[2 + nb][:, :], sel2[0:2, j, :], rows[0:2, nb * 512:(nb + 1) * 512])
            S.copy(g[:, nb * 512:(nb + 1) * 512], PS[2 + nb][:, :], eng="act")
        gg.append(g)
    return gg


def outproj_residual(S, C, PS, A, stk, mixg, wout_ap, x_in, tiles, gg, x_out, identF, post=None):
    wout = C.sb("wout", [128, 16, 1024], BF16, stk)
    for kt in range(16):
        S.dma(wout[:, kt, :], wout_ap[kt], q="pool")
    mts = [C.sb(f"mt{i}", [128, 16, 512], BF16, stk) for i in range(2)]
    xts = [C.sb(f"xo{i}", [128, 1024], F32, stk) for i in range(3)]
    junk = C.sb("junk2", [128, 512], BF16, stk)
    ss2_s = [C.sb("ss2", [128, 2], F32, stk) for _ in range(2)]
    rs_s = [C.sb("rs", [128, 1], F32, stk) for _ in range(2)]
    tmp_s = [C.sb("tmpo", [128, 1024], F32, stk) for _ in range(2)]
    outs = []
    if not isinstance(mixg, Chunked):
        mixg = Chunked.wrap(mixg)
    cur = {"t0": None, "mt": None, "n": 0}
    for idx, (i, tok0, jj) in enumerate(tiles):
        g4 = tok0 // 512
        if cur["t0"] != g4:
            mt = mts[cur["n"] % 2]
            cur["n"] += 1
            ncol = min(512, mixg.ntok - g4 * 512)
            S.dma(mt[:, :, 0:ncol], mixg.cols(0, mixg.rows, g4 * 512, ncol).rearrange("(kt p) t -> p kt t", p=128), q="sp")
            cur["t0"], cur["mt"] = g4, mt
        mt = cur["mt"]
        c0 = tok0 - g4 * 512
        xt = xts[idx % 3]
        ss2, rs, tmp = ss2_s[idx % 2], rs_s[idx % 2], tmp_s[idx % 2]
        PY = (PS[0], PS[1]) if idx % 2 == 0 else (PS[4], PS[5])
        S.dma(xt[:], x_in[tok0:tok0 + 128, :], q="act")
        for nb in range(2):
            for kt in range(16):
                S.mm(PY[nb][:, :], mt[:, kt, c0:c0 + 128], wout[:, kt, nb * 512:(nb + 1) * 512],
                     start=(kt == 0), stop=(kt == 15))
            S.act(junk[:], PY[nb][:, :], AF.Square, accum_out=ss2[:, nb:nb + 1])
        S.tt(rs[:], ss2[:, 0:1], ss2[:, 1:2], ALU.add)
        S.act(rs[:], rs[:], AF.Sqrt, bias=EPS, scale=1.0 / D)
        S.recip(rs[:], rs[:])
        for nb in range(2):
            sl = slice(nb * 512, (nb + 1) * 512)
            S.stt(tmp[:, sl], PY[nb][:, :], rs[:, 0:1], gg[jj][:, sl], ALU.mult, ALU.mult)
        S.tt(xt[:], xt[:], tmp[:], ALU.add, eng="pool")
        if x_out is not None:
            outs.append(S.dma(x_out[i * 128:(i + 1) * 128, :], xt[:], q="sp"))
        if post is not None:
            post(i, xt, jj)
    return outs


def adaln_shift_scale(S, C, PS, A, pre, scond, stk):
    bmodT = C.sb("bmodT", [128, 16], F32, stk); S.dma(bmodT[:], A[pre + "bmodT"])
    gpreT = C.sb("gpreT", [128, 8], F32, stk); S.dma(gpreT[:], A[pre + "gpreT"])
    modT = C.sb("modT", [128, 16, 2], F32, stk)
    wm = [C.sb(f"wm{i}", [128, 8, 128], F32, stk) for i in range(2)]
    for fb in range(16):
        w = wm[fb % 2]
        S.dma(w[:], A[pre + "wmod"][fb], q="sp" if fb % 2 == 0 else "act")
        for kt in range(8):
            S.mm(PS[4 + fb % 2][:, 0:2], w[:, kt, :], scond[:, kt, :], start=(kt == 0), stop=(kt == 7))
        S.ts(modT[:, fb, :], PS[4 + fb % 2][:, 0:2], bmodT[:, fb:fb + 1], None, ALU.add)
    Amod = C.sb("Amod", [128, 8, 2], F32, stk)
    S.ts(Amod[:], modT[:, 8:16, :], 1.0, None, ALU.add)
    S.tt(Amod[:], Amod[:], gpreT[:, :].unsqueeze(2).broadcast_to([128, 8, 2]), ALU.mult)
    return Amod, modT


def p2_host_inputs(inp, b, hf, mix0g):
    f = lambda a: np.ascontiguousarray(a, dtype=np.float32)
    o = {}
    o["mix0g"] = np.ascontiguousarray(mix0g)
    o["x_all"] = f(np.concatenate([inp["x"][b], inp["ctx"][b]], 0))
    cond = np.stack([inp["c"][b], inp["c_ctx"]], -1)
    o["condT"] = f(cond.reshape(8, 128, 2).transpose(1, 0, 2))
    o["l0_wgate"] = f(inp["w_mod"][0][:, 2048:3072].reshape(8, 128, 1024))
    o["l0_bgate2"] = f(np.broadcast_to(inp["b_mod"][0][2048:3072][None], (2, 1024)))
    o["l0_gpost2"] = f(np.broadcast_to(inp["g_post"][0][None], (2, 1024)))
    sel2 = np.zeros((2, 2, 128), np.float32); sel2[0, 0] = 1; sel2[1, 1] = 1
    o["sel2"] = sel2
    wo = inp["e_w_out"][0]
    order = np.concatenate([np.arange(r * 512, r * 512 + 512) if part == 0 else np.arange(1024 + r * 512, 1024 + r * 512 + 512)
                            for r in range(2) for part in range(2)])
    o["wout0"] = f(wo[order].reshape(16, 128, 1024))
    wm = inp["w_mod"][1][:, :2048]
    o["l1_wmod"] = f(wm.reshape(8, 128, 16, 128).transpose(2, 1, 0, 3))
    o["l1_bmodT"] = f(inp["b_mod"][1][:2048].reshape(16, 128).T)
    o["l1_gpreT"] = f(inp["g_pre"][1].reshape(8, 128).T)
    W = inp["o_w_in"][0]
    heads = [4 * hf + h for h in range(4)]
    def tl(cols):
        return W[:, cols].reshape(8, 128, len(cols)).transpose(1, 0, 2)
    o["wq"] = f(np.stack([tl(np.arange(hd * 128, hd * 128 + 128)) for hd in heads]))
    o["wk"] = f(np.stack([tl(np.arange(1024 + hd * 128, 1024 + hd * 128 + 128)) for hd in heads]))
    o["wv"] = f(np.stack([tl(np.arange(2048 + hd * 256, 2048 + hd * 256 + 256)) for hd in heads]))
    o["woz"] = f(np.stack([tl(np.concatenate([np.arange(4096 + hd * 256, 4096 + hd * 256 + 256),
                                              np.arange(6144 + hd * 256, 6144 + hd * 256 + 256)])) for hd in heads]))
    gcols = np.array([8192 + g * 8 + hd for g in range(4) for hd in heads])
    o["wg"] = f(tl(gcols))
    cw = inp["o_conv_w"][0]; cb = inp["o_conv_b"][0]
    convw = np.zeros((128, 4, 2, 3), np.float32); convb = np.zeros((128, 4, 2), np.float32)
    for h, hd in enumerate(heads):
        for s in range(2):
            c0 = s * 1024 + hd * 128
            convw[:, h, s, :] = cw[:, c0:c0 + 128].T
            convb[:, h, s] = cb[c0:c0 + 128]
    o["convw1"] = convw; o["convb1"] = convb
    gb = inp["o_gate_b"][0].reshape(4, 8)[:, heads]
    o["gateb"] = f(gb.T)
    hn = inp["o_head_norm"][0].reshape(8, 256)[heads]
    o["hn"] = f(np.broadcast_to(hn[None], (128, 4, 256)))
    t = np.arange(128)
    same = (t[:, None] // 64) == (t[None, :] // 64)
    o["maskF"] = f((same & (t[:, None] <= t[None, :])))
    o["maskB"] = f((same & (t[:, None] >= t[None, :])))
    sel4 = np.zeros((4, 4, 128), np.float32)
    for h in range(4):
        sel4[h, h] = 1
    o["sel4"] = sel4
    o["identF"] = np.eye(128, dtype=np.float32)
    return o


def build_p2(nc, sample, n_heads=4, stop=None, shared=None):
    if shared is None:
        A = _declare_inputs(nc, sample)
        x1o = nc.dram_tensor("x1o", [NTOK, D], F32, kind="ExternalOutput").ap()
        mix1 = Chunked.wrap(nc.dram_tensor("mix1", [1024, NLAT], BF16, kind="ExternalOutput").ap())
    else:
        A = shared["A"]
        x1o = shared["x1o"]
        mix1 = shared["mix1"]
    outs = []
    NCH = NTOK // 64
    with ExitStack() as st:
        if shared is None:
            S = Sched(nc)
            C = Ctx(nc, st)
            PS = [C.ps(f"pb{i}", [128, 512], F32) for i in range(7)]
            pTb = C.ps("pTb", [128, 1024], BF16)
        else:
            S, C, PS, pTb = shared["S"], shared["C"], shared["PS"], shared["pTb"]
            C.stack = st
        identF = C.sb("identF", [128, 128], F32); S.dma(identF[:], A["identF"])
        identB = C.sb("identB", [128, 128], BF16); S.dma(identB[:], A["identF"], q="pool")
        onesF = C.sb("onesF", [128, 128], F32); S.memset(onesF[:], 1.0)
        condT = C.sb("condT", [128, 8, 2], F32); S.dma(condT[:], A["condT"])
        scond = C.sb("scond", [128, 8, 2], F32)
        S.act(scond[:], condT[:], AF.Silu)
        h1T = C.sb("h1T", [128, 8, NTOK], BF16)
        tokq = C.sb("tokq", [128, 34, 24], F32)
        ebB = C.sb("ebB", [128, 2, 4, NCH], F32)
        with ExitStack() as sa:
            Amod, modT = adaln_shift_scale(S, C, PS, A, "l1_", scond, sa)
            gg = gate_rows(S, C, PS, A, "l0_", scond, onesF, sa)
            ssq_s = [C.sb("ssq", [128, 1], F32, sa) for _ in range(2)]
            rstd_s = [C.sb("rstd", [128, 1], F32, sa) for _ in range(2)]
            junk = C.sb("junk", [128, 1024], BF16, sa)
            xh_s = [C.sb("xh", [128, 1024], F32, sa) for _ in range(2)]

            def post(i, xt, jj):
                ssq, rstd, xh = ssq_s[i % 2], rstd_s[i % 2], xh_s[i % 2]
                S.act(junk[:], xt[:], AF.Square, accum_out=ssq[:, 0:1])
                S.act(rstd[:], ssq[:], AF.Sqrt, bias=EPS, scale=1.0 / D)
                S.recip(rstd[:], rstd[:])
                S.ts(xh[:], xt[:], rstd[:, 0:1], None, ALU.mult)
                for half in range(2):
                    pt = PS[2 + half]
                    for k4 in range(4):
                        kt = half * 4 + k4
                        S.transpose(pt[:, k4 * 128:(k4 + 1) * 128], xh[:, kt * 128:(kt + 1) * 128], identF[:])
                    for k4 in range(4):
                        kt = half * 4 + k4
                        S.ts(h1T[:, kt, i * 128:(i + 1) * 128], pt[:, k4 * 128:(k4 + 1) * 128],
                             Amod[:, kt, jj:jj + 1], modT[:, kt, jj:jj + 1], ALU.mult, ALU.add)
            tiles = [(i, i * 128, 0 if i < 32 else 1) for i in range(34)]
            outs += outproj_residual(S, C, PS, A, sa, A["mix0g"], A["wout0"], A["x_all"], tiles, gg, x1o, identF, post)
        S.fence()
        if stop == "A":
            dbg = nc.dram_tensor("h1To", [128, 8, NTOK], BF16, kind="ExternalOutput").ap()
            outs.append(S.dma(dbg, h1T[:]))
            S.emit(st, final_wait=outs)
            return nc
        blocks = [(tb * 512, 512) for tb in range(8)] + [(4096, 256)]
        with ExitStack() as sb_:
            wg = C.sb("wg", [128, 8, 16], BF16, sb_); S.dma(wg[:], A["wg"], q="pool")
            gateb = C.sb("gateb", [4, 4], F32, sb_); S.dma(gateb[:], A["gateb"])
            sel4 = C.sb("sel4", [4, 4, 128], F32, sb_); S.dma(sel4[:], A["sel4"])
            gi = C.sb("gi", [4, NTOK], F32, sb_)
            gf = C.sb("gf", [4, NTOK], F32, sb_)
            cumB = C.sb("cumB", [4, NTOK], F32, sb_)
            e1p = C.sb("e1p", [4, NTOK], F32, sb_)
            ebr = C.sb("ebr", [4, NCH], F32, sb_)
            v3 = lambda t_: t_[:, :].rearrange("p (c j) -> p c j", j=64)
            for d in range(2):
                for gq_, dstg in ((2 * d, gi), (2 * d + 1, gf)):
                    for bi, (t0, n) in enumerate(blocks):
                        pt = PS[bi % 2]
                        for kt in range(8):
                            S.mm(pt[0:4, 0:n], wg[:, kt, gq_ * 4:(gq_ + 1) * 4], h1T[:, kt, t0:t0 + n], start=(kt == 0), stop=(kt == 7))
                        S.ts(dstg[:, t0:t0 + n], pt[0:4, 0:n], gateb[:, gq_:gq_ + 1], None, ALU.add)
                S.act(gf[:], gf[:], AF.Exp, scale=-1.0)
                S.act(gf[:], gf[:], AF.Ln, bias=1.0)
                src, dst = gf, cumB
                for k in range(6):
                    sh = 1 << k
                    s3, d3 = v3(src), v3(dst)
                    if d == 0:
                        S.tt(d3[:, :, sh:64], s3[:, :, sh:64], s3[:, :, 0:64 - sh], ALU.add)
                        S.copy(d3[:, :, 0:sh], s3[:, :, 0:sh], eng="pool")
                    else:
                        S.tt(d3[:, :, 0:64 - sh], s3[:, :, 0:64 - sh], s3[:, :, sh:64], ALU.add)
                        S.copy(d3[:, :, 64 - sh:64], s3[:, :, 64 - sh:64], eng="pool")
                    src, dst = dst, src
                cum, thr = src, dst
                e1 = gi
                endpos = 63 if d == 0 else 0
                S.act(ebr[:], v3(cum)[:, :, endpos], AF.Exp, scale=-1.0)
                S.tt(e1[:], gi[:], cum[:], ALU.add)
                S.act(e1[:], e1[:], AF.Exp)
                S.ts(e1[:], e1[:], 128.0 ** -0.5, None, ALU.mult)
                S.act(thr[:], cum[:], AF.Exp)
                S.tt(v3(e1p), v3(e1), ebr[:, :].unsqueeze(2).broadcast_to([4, NCH, 64]), ALU.mult)
                for h in range(4):
                    S.mm(PS[2][:, 0:NCH], sel4[0:4, h, :], ebr[0:4, :])
                    S.copy(ebB[:, d, h, :], PS[2][:, 0:NCH], eng="act")
                for i in range(34):
                    pt = PS[3 + i % 2]
                    for qi, src_t in enumerate((e1, e1p, thr)):
                        S.transpose(pt[:, qi * 4:(qi + 1) * 4], src_t[0:4, i * 128:(i + 1) * 128], identF[0:4, 0:4])
                    S.copy(tokq[:, i, d * 12:(d + 1) * 12], pt[:, 0:12], eng="act")
        S.fence()
        if stop == "B":
            dbg = nc.dram_tensor("tokqo", [128, 34, 24], F32, kind="ExternalOutput").ap()
            outs.append(S.dma(dbg, tokq[:]))
            dbg2 = nc.dram_tensor("ebBo", [128, 2, 4, NCH], F32, kind="ExternalOutput").ap()
            outs.append(S.dma(dbg2, ebB[:]))
            S.emit(st, final_wait=outs)
            return nc
        with ExitStack() as sc:
            convw = C.sb("convw1", [128, 4, 2, 3], F32, sc); S.dma(convw[:], A["convw1"])
            convb = C.sb("convb1", [128, 4, 2], F32, sc); S.dma(convb[:], A["convb1"])
            hn = C.sb("hn", [128, 256], F32, sc)
            maskF = C.sb("maskF", [128, 128], F32, sc); S.dma(maskF[:], A["maskF"])
            maskB = C.sb("maskB", [128, 128], F32, sc); S.dma(maskB[:], A["maskB"])
            masks = (maskF, maskB)
            wbuf = C.sb("wbuf", [128, 8, 512], BF16, sc)
            wq = wbuf[:, :, 0:128]
            wk = wbuf[:, :, 128:256]
            wv = wbuf[:, :, 256:512]
            woz = wbuf
            SEG = NLAT + 2 + NCTX + 2
            ust = C.sb("ust", [128, SEG], BF16, sc); S.memset(ust[:], 0.0)
            ctmp = C.sb("ctmp", [128, 1024], F32, sc)
            qT = C.sb("qT", [128, NTOK], BF16, sc)
            kT = C.sb("kT", [128, NTOK], BF16, sc)
            ktD = [C.sb(f"ktD{d}", [128, 34, 128], BF16, sc) for d in range(2)]
            vaug = C.sb("vaug", [128, 34, 257], BF16, sc)
            S.memset(vaug[:, :, 256:257], 1.0)
            hsum = C.sb("hsum", [128, 32, 256], F32, sc)
            Cst = [C.sb(f"Cst{d}", [128, 257], F32, sc) for d in range(2)]
            Cbf = [C.sb(f"Cbf{d}", [128, 257], BF16, sc) for d in range(2)]
            Sm = [C.sb(f"Sm{d}", [128, 128], BF16, sc) for d in range(2)]
            dm = [C.sb(f"dm{d}", [128, 1], F32, sc) for d in range(2)]
            fin = [(C.sb("so", [128, 256], F32, sc), C.sb("sz", [128, 256], F32, sc), C.sb("hh", [128, 256], F32, sc),
                    C.sb("junk3", [128, 256], BF16, sc), C.sb("ssq3", [128, 1], F32, sc), C.sb("mixv", [128, 256], BF16, sc))
                   for _ in range(2)]
            mob = [C.sb(f"mob{i}", [128, 2, 512], BF16, sc) for i in range(2)]
            segs = [(k * 1024, k * 1024 + 1, 1024) for k in range(4)] + [(4096, NLAT + 3, 256)]

            def ust_cols(blk):
                t0, n = blk
                return slice(t0 + 1, t0 + 1 + n) if t0 < NLAT else slice(NLAT + 3, NLAT + 3 + n)

            for h in range(n_heads):
                S.dma(wq, A["wq"][h], q="pool")
                S.dma(wk, A["wk"][h], q="pool")
                S.dma(wv, A["wv"][h], q="pool")
                S.dma(hn[:], A["hn"][:, h, :])
                S.memset(hsum[:], 0.0, eng="pool")
                for (w, s, dstT) in ((wq, 0, qT), (wk, 1, kT)):
                    for bi, blk in enumerate(blocks):
                        t0, n = blk
                        pt = PS[bi % 2]
                        for kt in range(8):
                            S.mm(pt[:, 0:n], w[:, kt, :], h1T[:, kt, t0:t0 + n], start=(kt == 0), stop=(kt == 7))
                        S.act(ust[:, ust_cols(blk)], pt[:, 0:n], AF.Copy)
                    for (t0, u0, n) in segs:
                        S.act(ctmp[:, 0:n], ust[:, u0:u0 + n], AF.Identity, bias=convb[:, h, s:s + 1], scale=convw[:, h, s, 1:2])
                        S.stt(ctmp[:, 0:n], ust[:, u0 - 1:u0 - 1 + n], convw[:, h, s, 0:1], ctmp[:, 0:n], ALU.mult, ALU.add)
                        S.stt(ctmp[:, 0:n], ust[:, u0 + 1:u0 + 1 + n], convw[:, h, s, 2:3], ctmp[:, 0:n], ALU.mult, ALU.add)
                        S.act(dstT[:, t0:t0 + n], ctmp[:, 0:n], AF.Silu)
                for i4 in range(0, 34, 8):
                    nt = min(8, 34 - i4)
                    for j in range(nt):
                        i = i4 + j
                        S.transpose(pTb[:, j * 128:(j + 1) * 128], kT[:, i * 128:(i + 1) * 128], identB[:])
                    for j in range(nt):
                        i = i4 + j
                        for d in range(2):
                            S.act(ktD[d][:, i, :], pTb[:, j * 128:(j + 1) * 128], AF.Identity, scale=tokq[:, i, d * 12 + 4 + h:d * 12 + 5 + h])
                for i in range(34):
                    pt = PS[i % 2]
                    for kt in range(8):
                        S.mm(pt[:, 0:256], h1T[:, kt, i * 128:(i + 1) * 128], wbuf[:, kt, 256:512], start=(kt == 0), stop=(kt == 7))
                    S.act(vaug[:, i, 0:256], pt[:, 0:256], AF.Copy)
                S.dma(woz[:], A["woz"][h], q="pool")

                def finish_gen(i, fi):
                    pt = PS[6]
                    so, sz, hh, junk3, ssq3, mixv = fin[fi % 2]
                    for kt in range(8):
                        S.mm(pt[:, :], h1T[:, kt, i * 128:(i + 1) * 128], woz[:, kt, :], start=(kt == 0), stop=(kt == 7))
                        if kt % 2 == 1:
                            yield
                    S.act(so[:], pt[:, 0:256], AF.Sigmoid)
                    S.act(sz[:], pt[:, 256:512], AF.Silu)
                    S.tt(hh[:], hsum[:, i, :], so[:], ALU.mult)
                    S.act(junk3[:], hh[:], AF.Square, accum_out=ssq3[:, 0:1])
                    S.act(ssq3[:], ssq3[:], AF.Sqrt, bias=EPS, scale=1.0 / 256)
                    S.recip(ssq3[:], ssq3[:])
                    S.tt(sz[:], sz[:], hn[:], ALU.mult, eng="pool")
                    S.stt(mixv[:], hh[:], ssq3[:, 0:1], sz[:], ALU.mult, ALU.mult)
                    yield
                    slot = fin_slot[i]
                    mo_ = mob[(slot // 4) % 2]
                    for j in range(2):
                        S.transpose(pTb[:, j * 128:(j + 1) * 128], mixv[:, j * 128:(j + 1) * 128], identB[:])
                    S.copy(mo_[:, :, (slot % 4) * 128:(slot % 4 + 1) * 128],
                           pTb[:, 0:256].rearrange("p (j t) -> p j t", j=2), eng="act")
                    outs.append(S.dma(mix1.cols(h * 256, (h + 1) * 256, i * 128, 128).rearrange("(j p) t -> p j t", p=128),
                                      mo_[:, :, (slot % 4) * 128:(slot % 4 + 1) * 128], q="sp"))

                fin_slot = {}
                pending_fin = []
                done_cnt = [0] * 32
                n_fin = [0]
                for d in range(2):
                    S.memset(Cst[d][:], 0.0)
                    S.memset(Cbf[d][:], 0.0, eng="pool")
                orderF = [32, 33] + list(range(32))
                orderB = [33, 32] + list(range(31, -1, -1))
                for step in range(34):
                    for d, tile_i in ((0, orderF[step]), (1, orderB[step])):
                        i = tile_i
                        is_lat = i < 32
                        pS, pN, pU = PS[3 * d], PS[3 * d + 1], PS[3 * d + 2]
                        tk = slice(i * 128, (i + 1) * 128)
                        chunks = (0, 1) if d == 0 else (1, 0)
                        if is_lat:
                            S.mm(pS[:, 0:128], kT[:, tk], qT[:, tk])
                            S.stt(Sm[d][:], pS[:, 0:128], tokq[:, i, d * 12 + h:d * 12 + h + 1], masks[d][:], ALU.mult, ALU.mult)
                            S.mm(pN[:, 0:257], Sm[d][:], vaug[:, i, :], start=True, stop=False)
                        for ci, c in enumerate(chunks):
                            rows = slice(c * 64, (c + 1) * 64)
                            ch = i * 2 + c
                            if is_lat:
                                S.mm(pN[rows, 0:257], qT[:, i * 128 + c * 64:i * 128 + (c + 1) * 64], Cbf[d][:, :],
                                     start=False, stop=True)
                            S.mm(pU[:, 0:257], ktD[d][rows, i, :], vaug[rows, i, :])
                            S.stt(Cst[d][:], Cst[d][:], ebB[:, d, h, ch:ch + 1], pU[:, 0:257], ALU.mult, ALU.add)
                            S.act(Cbf[d][:], Cst[d][:], AF.Copy)
                        if is_lat:
                            S.act(dm[d][:], pN[:, 256:257], AF.Abs)
                            S.ts(dm[d][:], dm[d][:], tokq[:, i, d * 12 + 8 + h:d * 12 + 9 + h], None, ALU.max)
                            S.recip(dm[d][:], dm[d][:])
                            S.stt(hsum[:, i, :], pN[:, 0:256], dm[d][:, 0:1], hsum[:, i, :], ALU.mult, ALU.add)
                            done_cnt[i] += 1
                            if done_cnt[i] == 2:
                                fin_slot[i] = n_fin[0]
                                pending_fin.append(finish_gen(i, n_fin[0]))
                                n_fin[0] += 1
                        if pending_fin:
                            if next(pending_fin[0], "end") == "end":
                                pending_fin.pop(0)
                for g_ in pending_fin:
                    for _ in g_:
                        pass
        if shared is None:
            S.emit(st, final_wait=outs)
    return nc


def p3_host_inputs(inp, b, hf, mix1g, x1):
    f = lambda a: np.ascontiguousarray(a, dtype=np.float32)
    o = {}
    o["mix1g"] = np.ascontiguousarray(mix1g[:, hf * 2048:(hf + 1) * 2048])
    o["x1loc"] = f(x1[hf * 2048:(hf + 1) * 2048])
    cond = np.stack([inp["c"][b], inp["c_ctx"]], -1)
    o["condT"] = f(cond.reshape(8, 128, 2).transpose(1, 0, 2))
    o["l1_wgate"] = f(inp["w_mod"][1][:, 2048:3072].reshape(8, 128, 1024))
    o["l1_bgate2"] = f(np.broadcast_to(inp["b_mod"][1][2048:3072][None], (2, 1024)))
    o["l1_gpost2"] = f(np.broadcast_to(inp["g_post"][1][None], (2, 1024)))
    sel2 = np.zeros((2, 2, 128), np.float32); sel2[0, 0] = 1; sel2[1, 1] = 1
    o["sel2"] = sel2
    o["wout1"] = f(inp["o_w_out"][0].reshape(16, 128, 1024))
    o["identF"] = np.eye(128, dtype=np.float32)
    return o


def build_p3(nc, sample, shared=None):
    if shared is None:
        A = _declare_inputs(nc, sample)
        yo = nc.dram_tensor("yo", [2048, D], F32, kind="ExternalOutput").ap()
        ntile = 16
    else:
        A = shared["A"]
        yo = shared["yo"]
        ntile = 32
    with ExitStack() as st:
        if shared is None:
            S = Sched(nc)
            C = Ctx(nc, st)
            PS = [C.ps(f"pb{i}", [128, 512], F32) for i in range(7)]
        else:
            S, C, PS = shared["S"], shared["C"], shared["PS"]
            C.stack = st
        identF = C.sb("identF", [128, 128], F32); S.dma(identF[:], A["identF"])
        onesF = C.sb("onesF", [128, 128], F32); S.memset(onesF[:], 1.0)
        condT = C.sb("condT", [128, 8, 2], F32); S.dma(condT[:], A["condT"])
        scond = C.sb("scond", [128, 8, 2], F32)
        S.act(scond[:], condT[:], AF.Silu)
        gg = gate_rows(S, C, PS, A, "l1_", scond, onesF, st)
        tiles = [(i, i * 128, 0) for i in range(ntile)]
        outs = outproj_residual(S, C, PS, A, st, A["mix1g"], A["wout1"], A["x1loc"], tiles, gg, yo, identF, None)
        if shared is None:
            S.emit(st, final_wait=outs)
        else:
            shared["outs"] += outs
    return nc


CORES = [(b, hf) for b in range(4) for hf in range(2)]


def _launch(build, maps, **kw):
    nc = bass.Bass("TRN2", target_bir_lowering=False)
    build(nc, maps[0], **kw)
    res = run_bass_kernel_spmd(nc, maps, core_ids=list(range(len(maps))))
    return res.results


GROUPS = [[0, 1], [2, 3], [4, 5], [6, 7]]


def fused_host_inputs(inp, b, hf):
    o = {}
    dummy_mix0 = np.zeros((2048, NTOK), NPBF)
    p2 = p2_host_inputs(inp, b, hf, dummy_mix0)
    p3 = p3_host_inputs(inp, b, hf, np.zeros((2048, NLAT), NPBF), np.zeros((NLAT, D), np.float32))
    for d in (p3, p2, p1_host_inputs(inp, b, hf)):
        o.update(d)
    for k in ("mix0g", "mix1g", "x1loc"):
        o.pop(k)
    return o


def build_fused(nc, sample):
    A = _declare_inputs(nc, sample)
    yo = nc.dram_tensor("yo", [NLAT, D], F32, kind="ExternalOutput").ap()
    CS = 1024
    mix0 = Chunked.make(nc, "mix0", 1024, NTOK, BF16, CS, kind="Internal")
    mix0g = Chunked.make(nc, "mix0g", 2048, NTOK, BF16, CS, kind="Internal", addr_space="Local")
    x1o = nc.dram_tensor("x1o", [NTOK, D], F32, kind="Internal").ap()
    mix1 = Chunked.make(nc, "mix1", 1024, NLAT, BF16, CS, kind="Internal")
    mix1g = Chunked.make(nc, "mix1g", 2048, NLAT, BF16, CS, kind="Internal", addr_space="Local")
    A["mix0g"] = mix0g
    A["mix1g"] = mix1g
    A["x1loc"] = x1o[0:NLAT, :]
    with ExitStack() as st:
        S = Sched(nc)
        C = Ctx(nc, st)
        PS = [C.ps(f"pb{i}", [128, 512], F32) for i in range(7)]
        pTb = C.ps("pTb", [128, 1024], BF16)
        sh = dict(S=S, C=C, PS=PS, pTb=pTb, A=A, mix0=mix0, x1o=x1o, mix1=mix1, yo=yo, outs=[])
        import os
        nocc = os.environ.get("K_FUSE_NOCC") == "1"
        light = os.environ.get("K_P1_LIGHT") == "1"

        def exchange(gch, lch):
            for (_, _, go), (_, _, gi_) in zip(gch.chunks, lch.chunks):
                if nocc:
                    S.dma(go[0:1024, :], gi_, q="sp")
                    S.dma(go[1024:2048, :], gi_, q="act")
                else:
                    S.allgather(go, gi_, GROUPS)
        if light:
            build_p1(nc, sample, shared=sh, n_ct=1, do_att=False)
        else:
            build_p1(nc, sample, shared=sh)
        S.fence()
        exchange(mix0g, mix0)
        build_p2(nc, sample, shared=sh, n_heads=(1 if light else 4))
        S.fence()
        exchange(mix1g, mix1)
        build_p3(nc, sample, shared=sh)
        C.stack = st
        S.emit(st, final_wait=sh["outs"])
        print("fused stats", S.stats, flush=True)
    return nc


def kernel(**inputs):
    inp = {k: np.asarray(v) for k, v in inputs.items()}
    res = _launch(build_fused, [fused_host_inputs(inp, b, hf) for (b, hf) in CORES])
    out = np.zeros((4, NLAT, D), np.float32)
    for ci, (b, hf) in enumerate(CORES):
        out[b, hf * 2048:(hf + 1) * 2048] = np.asarray(res[ci]["yo"])[hf * 2048:(hf + 1) * 2048]
    return out
```

```python
import math
from contextlib import ExitStack
import numpy as np
import ml_dtypes
import concourse.bass as bass
import concourse.mybir as mybir
from concourse.bass_utils import run_bass_kernel_spmd

F32 = mybir.dt.float32
BF16 = mybir.dt.bfloat16
AF = mybir.ActivationFunctionType
ALU = mybir.AluOpType
AX = mybir.AxisListType
NPBF = ml_dtypes.bfloat16

D = 1024
NLAT = 4096
NCTX = 256
NTOK = NLAT + NCTX
EPS = 1e-6

ENGS = ("pe", "act", "dve", "pool", "sp")
SEM_ROT = 12000
ND = 8
CC_INC = 1
_DT_SIZE = {}


def _dsize(dt):
    if dt not in _DT_SIZE:
        _DT_SIZE[dt] = mybir.dt.size(dt)
    return _DT_SIZE[dt]


def _box(ap):
    t = ap.tensor
    dims = list(ap.ap)
    off = ap.offset
    sp = str(ap.space)
    if sp in ("SB", "PSUM"):
        pstep = 1
        for s in t.shape[1:]:
            pstep *= s
        p0 = off // pstep
        f0 = off % pstep
        pd = dims[0]
        npart = 1 if pd[0] == 0 else pd[1]
        ext = 0
        for st, cnt in dims[1:]:
            ext += (cnt - 1) * abs(st)
        if sp == "PSUM":
            return (t.name, 0, 128, 0, pstep)
        return (t.name, p0, p0 + npart, f0, f0 + ext + 1)
    ext = 0
    for st, cnt in dims:
        ext += (cnt - 1) * abs(st)
    return (t.name, 0, 1, off, off + ext + 1)


class Sched:
    def __init__(self, nc, same_engine_sync=None):
        import os
        if same_engine_sync is None:
            same_engine_sync = os.environ.get('K_SAME', '1') == '1'
        self.nc = nc
        self.ins = []
        self.track = {}
        self.same = same_engine_sync
        self.last_cp = {e: None for e in ENGS}
        self.last_dm = {e: [] for e in ENGS}
        self.pending = {e: set() for e in ENGS}

    def _deps(self, reads, writes, idx):
        deps = set()
        rb = [_box(a) for a in reads]
        wb = [_box(a) for a in writes]
        for b in rb:
            for ent in self.track.get(b[0], ()):
                e = ent[0]
                if e[1] < b[2] and b[1] < e[2] and e[3] < b[4] and b[3] < e[4]:
                    if ent[1] is not None:
                        deps.add(ent[1])
        for b in wb:
            for ent in self.track.get(b[0], ()):
                e = ent[0]
                if e[1] < b[2] and b[1] < e[2] and e[3] < b[4] and b[3] < e[4]:
                    if ent[1] is not None:
                        deps.add(ent[1])
                    deps.update(ent[2])
        for b in rb:
            lst = self.track.setdefault(b[0], [])
            for ent in lst:
                if ent[0] == b:
                    ent[2].append(idx)
                    break
            else:
                lst.append([b, None, [idx]])
        for b in wb:
            lst = self.track.get(b[0], [])
            keep = []
            for ent in lst:
                e = ent[0]
                if b[1] <= e[1] and e[2] <= b[2] and b[3] <= e[3] and e[4] <= b[4]:
                    continue
                keep.append(ent)
            keep.append([b, idx, []])
            self.track[b[0]] = keep
        deps.discard(idx)
        return deps

    def add(self, eng, fn, r=(), w=(), dma=False):
        idx = len(self.ins)
        deps = self._deps(list(r), list(w), idx)
        if self.pending[eng]:
            deps |= self.pending[eng]
            self.pending[eng] = set()
        self.ins.append(dict(eng=eng, fn=fn, deps=deps, dma=dma, users=set()))
        if dma:
            lo = self.last_dm[eng]
            lo.append(idx)
            if len(lo) > ND:
                lo.pop(0)
        else:
            self.last_cp[eng] = idx
        return idx

    def fence(self):
        allp = set()
        for e in ENGS:
            allp.update(self.last_dm[e])
            if self.last_cp[e] is not None:
                allp.add(self.last_cp[e])
        for e in ENGS:
            self.pending[e] = set(allp) | self.pending[e]
        self.track = {}

    def mm(self, out, lhsT, rhs, start=True, stop=True, **kw):
        r = [lhsT, rhs]
        if not start:
            r.append(out)
        return self.add("pe", lambda e: e.matmul(out, lhsT, rhs, start=start, stop=stop, **kw), r=r, w=[out])

    def transpose(self, out, in_, ident):
        return self.add("pe", lambda e: e.transpose(out, in_, ident), r=[in_, ident], w=[out])

    def act(self, out, in_, func, bias=None, scale=None, accum_out=None):
        r = [in_]
        kw = {}
        if bias is not None:
            kw["bias"] = bias
            if not isinstance(bias, (int, float)):
                r.append(bias)
        if scale is not None:
            kw["scale"] = scale
            if not isinstance(scale, (int, float)):
                r.append(scale)
        w = [out]
        if accum_out is not None:
            kw["accum_out"] = accum_out
            w.append(accum_out)
        return self.add("act", lambda e: e.activation(out, in_, func, **kw), r=r, w=w)

    def tt(self, out, in0, in1, op, eng="dve"):
        return self.add(eng, lambda e: e.tensor_tensor(out, in0, in1, op), r=[in0, in1], w=[out])

    def ts(self, out, in0, s1, s2, op0, op1=None, eng="dve"):
        r = [in0]
        if not isinstance(s1, (int, float)):
            r.append(s1)
        if s2 is not None and not isinstance(s2, (int, float)):
            r.append(s2)
        if op1 is None:
            return self.add(eng, lambda e: e.tensor_scalar(out, in0, s1, None, op0), r=r, w=[out])
        return self.add(eng, lambda e: e.tensor_scalar(out, in0, s1, s2, op0, op1), r=r, w=[out])

    def stt(self, out, in0, scalar, in1, op0, op1):
        r = [in0, in1]
        if not isinstance(scalar, (int, float)):
            r.append(scalar)
        return self.add("dve", lambda e: e.scalar_tensor_tensor(out, in0, scalar, in1, op0, op1), r=r, w=[out])

    def copy(self, out, in_, eng="dve"):
        if eng == "act":
            return self.act(out, in_, AF.Copy)
        return self.add(eng, lambda e: e.tensor_copy(out, in_), r=[in_], w=[out])

    def memset(self, ap, val, eng="dve"):
        return self.add(eng, lambda e: e.memset(ap, val), r=[], w=[ap])

    def recip(self, out, in_):
        return self.add("dve", lambda e: e.reciprocal(out, in_), r=[in_], w=[out])

    def reduce(self, out, in_, op, axis=AX.X):
        return self.add("dve", lambda e: e.tensor_reduce(out, in_, axis, op), r=[in_], w=[out])

    def wrap(self, out, in_, t1, t2):
        self.ts(t1, in_, -math.pi, 2 * math.pi, ALU.is_lt, ALU.mult)
        self.ts(t2, in_, math.pi, -2 * math.pi, ALU.is_gt, ALU.mult)
        self.tt(t1, t1, t2, ALU.add, eng="pool")
        return self.tt(out, in_, t1, ALU.add)

    def allgather(self, out, in_, groups):
        idx = self.add("pool", lambda e: e.collective_compute("AllGather", ALU.bypass, replica_groups=groups,
                                                              ins=[in_], outs=[out]), r=[in_], w=[out], dma=True)
        self.ins[idx]["cc"] = True
        return idx

    def dma(self, out, in_, q="sp", **kw):
        return self.add(q, lambda e: e.dma_start(out, in_, **kw), r=[in_], w=[out], dma=True)

    def emit(self, stack, final_wait=()):
        nc = self.nc
        ins = self.ins
        for i, it in enumerate(ins):
            for d in it["deps"]:
                ins[d]["users"].add(i)

        def new_sem(tag):
            return stack.enter_context(nc.semaphore(tag))

        eng_sem = {e: [new_sem(f"s_{e}_0")] for e in ENGS}
        eng_cnt = {e: 0 for e in ENGS}
        dma_sems = {e: [new_sem(f"d_{e}_{k}") for k in range(ND)] for e in ("sp", "pool", "act")}
        dma_cnt = {e: 0 for e in ENGS}
        for i, it in enumerate(ins):
            e = it["eng"]
            it["sig"] = None
            it["pre"] = []
            if it.get("cc"):
                s = new_sem(f"cc_{i}")
                it["sig"] = (s, CC_INC, CC_INC)
            elif it["dma"]:
                k = dma_cnt[e]
                s = dma_sems[e][k % ND]
                it["sig"] = (s, 16 * (k // ND + 1), 16)
                if k >= ND:
                    it["pre"].append((s, 16 * (k // ND)))
                dma_cnt[e] = k + 1
            else:
                need = False
                for u in it["users"]:
                    ue = ins[u]["eng"]
                    if ue != e or ins[u]["dma"]:
                        need = True
                    elif self.same and e != "pe":
                        need = True
                if need:
                    if eng_cnt[e] >= SEM_ROT:
                        eng_sem[e].append(new_sem(f"s_{e}_{len(eng_sem[e])}"))
                        eng_cnt[e] = 0
                    eng_cnt[e] += 1
                    it["sig"] = (eng_sem[e][-1], eng_cnt[e], 1)
        self.stats = dict(n_ins=len(ins), sig={e: (len(eng_sem[e]) - 1) * SEM_ROT + eng_cnt[e] for e in ENGS},
                          dma=dict(dma_cnt), per_eng={e: sum(1 for it in ins if it["eng"] == e) for e in ENGS})
        waited = {e: {} for e in ENGS}
        streams = {e: [] for e in ENGS}
        for i, it in enumerate(ins):
            e = it["eng"]
            waits = list(it["pre"])
            for d in sorted(it["deps"]):
                pd = ins[d]
                if pd["sig"] is None:
                    continue
                if pd["eng"] == e and not pd["dma"] and (e == "pe" or not self.same):
                    continue
                waits.append((pd["sig"][0], pd["sig"][1]))
            ww = []
            for s, v in waits:
                key = id(s)
                if waited[e].get(key, 0) >= v:
                    continue
                waited[e][key] = v
                ww.append((s, v))
            streams[e].append((ww, it))
        fw = [(ins[d]["sig"][0], ins[d]["sig"][1]) for d in final_wait]
        self.n_ins = len(ins)

        def run(handle, lst, extra=()):
            for ww, it in lst:
                for s, v in ww:
                    handle.wait_ge(s, v)
                inst = it["fn"](handle)
                if it["sig"] is not None:
                    inst.then_inc(it["sig"][0], it["sig"][2])
            for s, v in extra:
                handle.wait_ge(s, v)

        block = stack.enter_context(nc.Block())

        @block.tensor
        def _(e):
            run(e, streams["pe"])

        @block.scalar
        def _(e):
            run(e, streams["act"])

        @block.vector
        def _(e):
            run(e, streams["dve"])

        @block.gpsimd
        def _(e):
            run(e, streams["pool"])

        @block.sync
        def _(e):
            run(e, streams["sp"], fw)


def fft_tables(N1):
    N = 64 * N1
    H = N1 // 2
    n1 = np.arange(N1)
    k1 = np.arange(N1)
    a1 = 2 * np.pi * np.outer(n1, k1) / N1
    CS1 = np.concatenate([np.cos(a1), -np.sin(a1)], 1)
    m = np.arange(128)
    n2m = m // 2
    c2m = m % 2
    aT = 2 * np.pi * np.outer(n2m, k1) / N
    a64 = 2 * np.pi * np.outer(n2m, n2m) / 64
    same = (c2m[:, None] == c2m[None, :])
    BdC = np.cos(a64) * same
    BdS = np.sin(a64) * same
    aI = 2 * np.pi * np.outer(k1, np.arange(64)) / N
    ai = 2 * np.pi * np.outer(k1, np.arange(H)) / N1
    f = lambda a: np.ascontiguousarray(a, dtype=np.float32)
    return dict(CS1=f(CS1), TwFc=f(np.cos(aT)), TwFs=f(np.sin(aT)), BdC=f(BdC), BdS=f(BdS), BdSn=f(-BdS),
                R1=f(np.concatenate([BdC, BdS], 1)), R2=f(np.concatenate([-BdS, BdC], 1)),
                TwIc=f(np.cos(aI)), TwIs=f(np.sin(aI)), Ci=f(np.cos(ai) / N), Sin_=f(-np.sin(ai) / N))


def filter_pos_tables(n):
    N = 2 * n
    N1 = N // 64
    t = np.linspace(0.0, 1.0, n, dtype=np.float32)
    bands = np.linspace(1e-4, 15.0, 16, dtype=np.float32)
    ang = (np.float32(2.0 * math.pi / n) * np.arange(n, dtype=np.float32)[:, None] * bands).astype(np.float32)
    z = np.concatenate([t[:, None], np.cos(ang), -np.sin(ang)], axis=-1).astype(np.float32)
    idx = np.arange(N)
    d = np.where(idx < n, idx, N - idx)
    d[n] = 0
    z2 = z[d]
    BIG = 1.0e4
    negF = np.where(idx < n, -t[d], -BIG).astype(np.float32)
    negB = np.where(idx > n, -t[d], -BIG).astype(np.float32)
    negt = np.stack([negF.reshape(N1, 64), negB.reshape(N1, 64)], axis=1)
    return np.ascontiguousarray(z2.T), np.ascontiguousarray(negt)


def rope_tables():
    nf = 32
    inv = (10000.0 ** (-np.arange(nf, dtype=np.float32) / nf)).astype(np.float32)
    j = np.arange(64, dtype=np.float32)
    dd = np.arange(128)
    ang = j[None, :] * inv[dd % 32][:, None]
    R = np.zeros((128, 128), np.float32)
    for dp in range(128):
        if dp % 64 < 32:
            R[dp, dp + 32] = -1.0
        else:
            R[dp, dp - 32] = 1.0
    return np.cos(ang).astype(np.float32), np.sin(ang).astype(np.float32), np.ascontiguousarray(R.T)


P1_SPEC = None


def p1_host_inputs(inp, b, hf):
    f = lambda a: np.ascontiguousarray(a, dtype=np.float32)
    o = {}
    o["x_all"] = f(np.concatenate([inp["x"][b], inp["ctx"][b]], 0))
    cond = np.stack([inp["c"][b], inp["c_ctx"]], -1)
    o["condT"] = f(cond.reshape(8, 128, 2).transpose(1, 0, 2))
    wm = inp["w_mod"][0][:, :2048]
    o["wmod"] = f(wm.reshape(8, 128, 16, 128).transpose(2, 1, 0, 3))
    o["bmodT"] = f(inp["b_mod"][0][:2048].reshape(16, 128).T)
    o["gpreT"] = f(inp["g_pre"][0].reshape(8, 128).T)
    W = inp["e_w_in"][0]
    cols = []
    cols.append(np.arange(5120 + hf * 128, 5120 + hf * 128 + 128))
    cols.append(np.arange(5376 + hf * 128, 5376 + hf * 128 + 128))
    for h in range(4):
        hd = 4 * hf + h
        cols.append(np.arange(4096 + hd * 128, 4096 + hd * 128 + 128))
        cols.append(np.arange(5632 + hd * 128, 5632 + hd * 128 + 128))
    for ct in range(4):
        c0 = hf * 512 + ct * 128
        cols.append(np.arange(1024 + c0, 1024 + c0 + 128))
        cols.append(np.arange(2048 + c0, 2048 + c0 + 128))
        cols.append(np.arange(c0, c0 + 128))
        cols.append(np.arange(3072 + c0, 3072 + c0 + 128))
    wt = np.stack([W[:, c] for c in cols], 0)
    o["w_in"] = f(wt.reshape(26, 8, 128, 128).transpose(0, 2, 1, 3))
    cw = inp["e_conv_w"][0]
    cb = inp["e_conv_b"][0]
    convw = np.zeros((128, 4, 3, 3), np.float32)
    convb = np.zeros((128, 4, 3), np.float32)
    for ct in range(4):
        c0 = hf * 512 + ct * 128
        for s in range(3):
            convw[:, ct, s, :] = cw[:, s * 1024 + c0: s * 1024 + c0 + 128].T
            convb[:, ct, s] = cb[s * 1024 + c0: s * 1024 + c0 + 128]
    o["convw"] = convw
    o["convb"] = convb
    o["gq"] = f(inp["e_q_norm"][0].reshape(128, 1))
    o["gk"] = f(inp["e_k_norm"][0].reshape(128, 1))
    rc, rs, rT = rope_tables()
    o["ropec"], o["ropes"], o["RmatT"] = rc, rs, rT
    z2l, ntl = filter_pos_tables(NLAT)
    z2c, ntc = filter_pos_tables(NCTX)
    o["z2T_l"], o["negt_l"], o["z2T_c"], o["negt_c"] = z2l, ntl, z2c, ntc
    o["fw1"] = f(inp["e_filt_w1"][0])
    o["fb1"] = f(inp["e_filt_b1"][0].reshape(64, 1))
    o["ffr"] = f(inp["e_filt_freq"][0].reshape(64, 1))
    o["fw2"] = f(inp["e_filt_w2"][0])
    o["fb2"] = f(inp["e_filt_b2"][0].reshape(64, 1))
    w3 = inp["e_filt_w3"][0].reshape(64, 2, 1024)
    o["fw3"] = f(w3[:, :, hf * 512:(hf + 1) * 512])
    min_decay = math.log(1e-2) / 1.5
    max_decay = math.log(1e-2) / 0.3
    deltas = np.abs(np.linspace(min_decay, max_decay, 1024, dtype=np.float32))
    o["adel"] = f(np.broadcast_to(deltas[hf * 512:(hf + 1) * 512][None, :], (128, 512)))
    o["hyb"] = f(inp["e_hy_bias"][0][hf * 512:(hf + 1) * 512].reshape(1, 512))
    for N1, tag in ((128, "l"), (8, "c")):
        for k, v in fft_tables(N1).items():
            o[f"ft_{tag}_{k}"] = v
    o["identF"] = np.eye(128, dtype=np.float32)
    return o


def _declare_inputs(nc, sample):
    aps = {}
    for k, v in sample.items():
        dt = F32 if v.dtype == np.float32 else BF16
        aps[k] = nc.dram_tensor(k, list(v.shape), dt, kind="ExternalInput").ap()
    return aps


class Chunked:
    def __init__(self, chunks, rows, ntok):
        self.chunks = chunks
        self.rows = rows
        self.ntok = ntok

    @staticmethod
    def make(nc, name, rows, ntok, dt, csize, **kw):
        ch = []
        for k, t0 in enumerate(range(0, ntok, csize)):
            n = min(csize, ntok - t0)
            ch.append((t0, n, nc.dram_tensor(f"{name}_{k}", [rows, n], dt, **kw).ap()))
        return Chunked(ch, rows, ntok)

    @staticmethod
    def wrap(ap):
        return Chunked([(0, ap.shape[1], ap)], ap.shape[0], ap.shape[1])

    def cols(self, r0, r1, t0, n):
        for (c0, cn, ap) in self.chunks:
            if c0 <= t0 and t0 + n <= c0 + cn:
                return ap[r0:r1, t0 - c0:t0 - c0 + n]
        raise AssertionError(("straddles chunks", t0, n))

    def pieces(self, r0, r1):
        for (c0, cn, ap) in self.chunks:
            yield c0, cn, ap[r0:r1, :]


class Ctx:
    def __init__(self, nc, stack):
        self.nc = nc
        self.stack = stack
        self.n = 0

    def sb(self, name, shape, dt, stack=None):
        self.n += 1
        return (stack or self.stack).enter_context(self.nc.sbuf_tensor(f"{name}_{self.n}", list(shape), dt))

    def ps(self, name, shape, dt, stack=None):
        self.n += 1
        return (stack or self.stack).enter_context(self.nc.psum_tensor(f"{name}_{self.n}", list(shape), dt))


def load_fft_tabs(S, C, A, tag, N1):
    H = N1 // 2
    T = {}
    def ld(name, shape, dt, q):
        t = C.sb(f"ft{tag}{name}", shape, dt)
        S.dma(t[:], A[f"ft_{tag}_{name}"], q=q)
        return t
    T["CS1"] = ld("CS1", [N1, 2 * N1], BF16, "pool")
    T["TwFc"] = ld("TwFc", [128, N1], F32, "sp")
    T["TwFs"] = ld("TwFs", [128, N1], F32, "sp")
    T["BdC"] = ld("BdC", [128, 128], BF16, "pool")
    T["BdS"] = ld("BdS", [128, 128], BF16, "pool")
    T["BdSn"] = ld("BdSn", [128, 128], BF16, "pool")
    T["R1"] = ld("R1", [128, 256], BF16, "pool")
    T["R2"] = ld("R2", [128, 256], BF16, "pool")
    T["TwIc"] = ld("TwIc", [N1, 64], F32, "sp")
    T["TwIs"] = ld("TwIs", [N1, 64], F32, "sp")
    T["Ci"] = ld("Ci", [N1, H], BF16, "pool")
    T["Sin_"] = ld("Sin_", [N1, H], BF16, "pool")
    return T


def fft_conv_core(S, C, PS, T, N1, X, Xf, Ball_re, Ball_im, tmps, evac):
    H = N1 // 2
    W2 = 2 * N1
    Gf = 512 // W2
    NG = 64 // Gf
    GN = Gf * N1
    tc = T["TwFc"][:, :].unsqueeze(1).broadcast_to([128, 2 * Gf, N1])
    tsn = T["TwFs"][:, :].unsqueeze(1).broadcast_to([128, 2 * Gf, N1])
    tic = T["TwIc"][:, :].unsqueeze(1).unsqueeze(3).broadcast_to([N1, 4, 64, 2])
    tis = T["TwIs"][:, :].unsqueeze(1).unsqueeze(3).broadcast_to([N1, 4, 64, 2])

    def stageA(g):
        m1, m2, Ar, Ai, Br_, Bi_, Fh, Yr, Yi = tmps[g % 2]
        pg1, pf1 = PS[2 * (g % 2)], PS[2 * (g % 2) + 1]
        p0 = g * Gf
        for j in range(Gf):
            S.mm(pg1[:, j * W2:(j + 1) * W2], X[0:H, p0 + j, :], T["CS1"][0:H, :])
        for j in range(Gf):
            S.mm(pf1[:, j * W2:(j + 1) * W2], Xf[0:N1, p0 + j, :], T["CS1"][0:N1, :])
        m1a = m1[:, :].rearrange("p (a n) -> p a n", n=N1)
        m2a = m2[:, :].rearrange("p (a n) -> p a n", n=N1)
        m1v = m1[:, :].rearrange("p (a r n) -> p a r n", r=2, n=N1)
        m2v = m2[:, :].rearrange("p (a r n) -> p a r n", r=2, n=N1)
        for (pp, ar, ai) in ((pg1, Ar, Ai), (pf1, Br_, Bi_)):
            v = pp[:, :].rearrange("p (a n) -> p a n", n=N1)
            S.tt(m1a, v, tc, ALU.mult)
            S.tt(m2a, v, tsn, ALU.mult)
            arv = ar[:, :].rearrange("p (a n) -> p a n", n=N1)
            aiv = ai[:, :].rearrange("p (a n) -> p a n", n=N1)
            S.tt(arv, m1v[:, :, 0, :], m2v[:, :, 1, :], ALU.add, eng="pool")
            S.tt(aiv, m1v[:, :, 1, :], m2v[:, :, 0, :], ALU.subtract, eng=("dve" if pp is pf1 else "pool"))

    def stageB(g):
        m1, m2, Ar, Ai, Br_, Bi_, Fh, Yr, Yi = tmps[g % 2]
        pg3, pf3 = PS[4], PS[5]
        for (pp, ar, ai) in ((pg3, Ar, Ai), (pf3, Br_, Bi_)):
            S.mm(pp[:, 0:GN], T["BdC"][:], ar[:, :], start=True, stop=False)
            S.mm(pp[:, 0:GN], T["BdS"][:], ai[:, :], start=False, stop=True)
            S.mm(pp[:, GN:2 * GN], T["BdC"][:], ai[:, :], start=True, stop=False)
            S.mm(pp[:, GN:2 * GN], T["BdSn"][:], ar[:, :], start=False, stop=True)
        S.act(Fh[:, :], pf3[:, :], AF.Copy)
        gv = pg3[:, :].rearrange("p (r n) -> p r n", r=2)
        fr = Fh[:, 0:GN].unsqueeze(1).broadcast_to([128, 2, GN])
        fi = Fh[:, GN:2 * GN].unsqueeze(1).broadcast_to([128, 2, GN])
        q1 = m1[:, :].rearrange("p (r n) -> p r n", r=2)
        q2 = m2[:, :].rearrange("p (r n) -> p r n", r=2)
        S.tt(q1, gv, fr, ALU.mult)
        S.tt(q2, gv, fi, ALU.mult)
        S.tt(Yr[:, :], q1[:, 0, :], q2[:, 1, :], ALU.subtract, eng="pool")
        S.tt(Yi[:, :], q2[:, 0, :], q1[:, 1, :], ALU.add, eng="dve")

    def stageC(g):
        m1, m2, Ar, Ai, Br_, Bi_, Fh, Yr, Yi = tmps[g % 2]
        pI = PS[6]
        for sub in range(Gf // 2):
            pr0 = g * Gf + 2 * sub
            for jj in range(2):
                j = 2 * sub + jj
                S.mm(pI[0:N1, jj * 256:(jj + 1) * 256], Yr[:, j * N1:(j + 1) * N1], T["R1"][:], start=True, stop=False)
                S.mm(pI[0:N1, jj * 256:(jj + 1) * 256], Yi[:, j * N1:(j + 1) * N1], T["R2"][:], start=False, stop=True)
            iv = pI[0:N1, :].rearrange("p (a n c) -> p a n c", a=4, c=2)
            n1v = m1[0:N1, :].rearrange("p (a n c) -> p a n c", a=4, c=2)
            n2v = m2[0:N1, :].rearrange("p (a n c) -> p a n c", a=4, c=2)
            S.tt(n1v, iv, tic, ALU.mult)
            S.tt(n2v, iv, tis, ALU.mult)
            n1p = m1[0:N1, :].rearrange("p (j r n c) -> p j r n c", j=2, r=2, c=2)
            n2p = m2[0:N1, :].rearrange("p (j r n c) -> p j r n c", j=2, r=2, c=2)
            ore = Ball_re[0:N1, :, 2 * pr0:2 * pr0 + 4].rearrange("p n (j c) -> p j n c", c=2)
            oim = Ball_im[0:N1, :, 2 * pr0:2 * pr0 + 4].rearrange("p n (j c) -> p j n c", c=2)
            S.tt(ore, n1p[:, :, 0, :, :], n2p[:, :, 1, :, :], ALU.subtract, eng="pool")
            S.tt(oim, n2p[:, :, 0, :, :], n1p[:, :, 1, :, :], ALU.add, eng="pool")

    stageA(0)
    for g in range(NG):
        if g + 1 < NG:
            stageA(g + 1)
        stageB(g)
        stageC(g)
    ng = min(64, 512 // H)
    k = 0
    for n2_0 in range(0, 64, ng):
        po = PS[k % 2]
        k += 1
        for j in range(ng):
            n2 = n2_0 + j
            S.mm(po[:, j * H:(j + 1) * H], Ball_re[0:N1, n2, :], T["Ci"][0:N1, :], start=True, stop=False)
            S.mm(po[:, j * H:(j + 1) * H], Ball_im[0:N1, n2, :], T["Sin_"][0:N1, :], start=False, stop=True)
        evac(po, n2_0, ng)


def build_p1(nc, sample, n_ct=4, do_att=True, do_hy=True, dbg=None, stop=None, shared=None):
    if shared is None:
        A = _declare_inputs(nc, sample)
        mix = Chunked.wrap(nc.dram_tensor("mix", [1024, NTOK], BF16, kind="ExternalOutput").ap())
    else:
        A = shared["A"]
        mix = shared["mix0"]
    xf_l = nc.dram_tensor("xf_l", [4, 128, 8192], BF16, kind="Internal").ap()
    xf_c = nc.dram_tensor("xf_c", [4, 8, 8192], BF16, kind="Internal").ap()
    outs = []
    with ExitStack() as st:
        if shared is None:
            S = Sched(nc)
            C = Ctx(nc, st)
            PS = [C.ps(f"pb{i}", [128, 512], F32) for i in range(7)]
            pTb = C.ps("pTb", [128, 1024], BF16)
        else:
            S, C, PS, pTb = shared["S"], shared["C"], shared["PS"], shared["pTb"]
            C.stack = st
        identF = C.sb("identF", [128, 128], F32)
        identB = C.sb("identB", [128, 128], BF16)
        onesF = C.sb("onesF", [128, 128], F32)
        onesB = C.sb("onesB", [128, 128], BF16)
        S.dma(identF[:], A["identF"])
        S.dma(identB[:], A["identF"], q="pool")
        S.memset(onesF[:], 1.0)
        S.memset(onesB[:], 1.0, eng="pool")
        rnT = C.sb("rnT", [128, 4, 2], F32)
        TL = load_fft_tabs(S, C, A, "l", 128)
        TC = load_fft_tabs(S, C, A, "c", 8)
        ftmp = []
        for par in range(2):
            ftmp.append((C.sb("m1", [128, 512], F32), C.sb("m2", [128, 512], F32),
                         C.sb("Ar", [128, 256], BF16), C.sb("Ai", [128, 256], BF16),
                         C.sb("Br", [128, 256], BF16), C.sb("Bi", [128, 256], BF16),
                         C.sb("Fh", [128, 512], F32),
                         C.sb("Yr", [128, 256], BF16), C.sb("Yi", [128, 256], BF16)))
        if stop == 'const':
            S.emit(st, final_wait=[])
            return nc

        import os
        SKIP = os.environ.get("K_SKIP", "")
        if do_hy and "F" not in SKIP:
            with ExitStack() as s0:
                fw1 = C.sb("fw1", [33, 64], F32, s0); S.dma(fw1[:], A["fw1"])
                fw2 = C.sb("fw2", [64, 64], F32, s0); S.dma(fw2[:], A["fw2"])
                fb1 = C.sb("fb1", [64, 1], F32, s0); S.dma(fb1[:], A["fb1"])
                fb2 = C.sb("fb2", [64, 1], F32, s0); S.dma(fb2[:], A["fb2"])
                ffr = C.sb("ffr", [64, 1], F32, s0); S.dma(ffr[:], A["ffr"])
                fw3 = C.sb("fw3", [64, 2, 512], BF16, s0); S.dma(fw3[:], A["fw3"], q="pool")
                adel = C.sb("adel", [128, 512], F32, s0); S.dma(adel[:], A["adel"])
                hyb = C.sb("hyb", [1, 512], F32, s0); S.dma(hyb[:], A["hyb"])
                frb1 = C.sb("frb1", [64, 1], F32, s0)
                frb2 = C.sb("frb2", [64, 1], F32, s0)
                S.tt(frb1[:], fb1[:], ffr[:], ALU.mult)
                S.tt(frb2[:], fb2[:], ffr[:], ALU.mult)
                hdn2 = C.sb("hdn2", [64, 8192], BF16, s0)
                z2 = C.sb("z2", [33, 512], F32, s0)
                u1 = C.sb("u1", [64, 512], F32, s0)
                h1 = C.sb("h1", [64, 512], F32, s0)
                wt1 = C.sb("wt1", [64, 512], F32, s0)
                wt2 = C.sb("wt2", [64, 512], F32, s0)
                negt = C.sb("negt", [128, 2, 64], F32, s0)
                wbuf_s = [C.sb("wbuf", [128, 2, 512], F32, s0) for _ in range(2)]
                fgrp_s = [C.sb("fgrp", [128, 512], F32, s0) for _ in range(2)]
                fgr2_s = [C.sb("fgr2", [128, 512], F32, s0) for _ in range(2)]
                acc = C.sb("acc", [128, 512], F32, s0)
                nrow = C.sb("nrow", [1, 512], F32, s0)
                ncol = C.sb("ncol", [128, 1], F32, s0)
                Xfo = C.sb("Xfo", [128, 4, 64, 128], BF16, s0)
                for (tag, N1, z2T, ntab, xfd, li) in (("l", 128, A["z2T_l"], A["negt_l"], xf_l, 0),
                                                      ("c", 8, A["z2T_c"], A["negt_c"], xf_c, 1)):
                    N = 64 * N1
                    S.dma(negt[0:N1, :, :], ntab)
                    for blk in range(N // 512):
                        sl = slice(blk * 512, (blk + 1) * 512)
                        S.dma(z2[:], z2T[:, sl])
                        S.mm(PS[5][0:64, :], fw1[:], z2[:])
                        S.ts(u1[:], PS[5][0:64, :], ffr[:, 0:1], frb1[:, 0:1], ALU.mult, ALU.add)
                        S.wrap(u1[:], u1[:], wt1[:], wt2[:])
                        S.wrap(u1[:], u1[:], wt1[:], wt2[:])
                        S.act(h1[:], u1[:], AF.Sin)
                        S.mm(PS[6][0:64, :], fw2[:], h1[:])
                        S.ts(u1[:], PS[6][0:64, :], ffr[:, 0:1], frb2[:, 0:1], ALU.mult, ALU.add)
                        S.wrap(u1[:], u1[:], wt1[:], wt2[:])
                        S.wrap(u1[:], u1[:], wt1[:], wt2[:])
                        S.act(hdn2[:, sl], u1[:], AF.Sin)
                    S.memset(acc[0:N1, :], 0.0)
                    for n2 in range(64):
                        wbuf, fgrp, fgr2 = wbuf_s[n2 % 2], fgrp_s[n2 % 2], fgr2_s[n2 % 2]
                        pF = (PS[1 + 2 * (n2 % 2)], PS[2 + 2 * (n2 % 2)])
                        for dr in range(2):
                            S.mm(pF[dr][0:N1, :], hdn2[:, n2:N:64], fw3[:, dr, :])
                            S.act(wbuf[0:N1, dr, :], adel[0:N1, :], AF.Exp, scale=negt[0:N1, dr, n2:n2 + 1])
                        S.tt(fgrp[0:N1, :], pF[0][0:N1, :], wbuf[0:N1, 0, :], ALU.mult)
                        S.tt(fgr2[0:N1, :], pF[1][0:N1, :], wbuf[0:N1, 1, :], ALU.mult)
                        S.tt(fgrp[0:N1, :], fgrp[0:N1, :], fgr2[0:N1, :], ALU.add, eng="pool")
                        S.act(fgr2[0:N1, :], fgrp[0:N1, :], AF.Abs)
                        S.tt(acc[0:N1, :], acc[0:N1, :], fgr2[0:N1, :], ALU.add, eng="pool")
                        ov = Xfo[0:N1, :, :, 2 * n2:2 * n2 + 2]
                        iv = fgrp[0:N1, :].rearrange("p (t a c) -> p t a c", t=4, c=2)
                        S.copy(ov, iv, eng="act")
                    S.mm(PS[6][0:1, 0:512], onesF[0:N1, 0:1], acc[0:N1, :])
                    S.tt(nrow[:], PS[6][0:1, 0:512], hyb[0:1, :], ALU.mult)
                    for ct in range(n_ct):
                        S.mm(PS[5][0:128, 0:1], acc[0:N1, ct * 128:(ct + 1) * 128], onesF[0:N1, 0:1])
                        S.copy(ncol[:], PS[5][0:128, 0:1])
                        S.recip(rnT[:, ct, li:li + 1], ncol[:])
                        tap = Xfo[0:1, ct, :, 0:2]
                        S.tt(tap, tap, nrow[0:1, ct * 128:(ct + 1) * 128].rearrange("p (a c) -> p a c", c=2), ALU.add)
                        S.dma(xfd[ct, :, :], Xfo[0:N1, ct, :, :].rearrange("p a n -> p (a n)"))
            S.fence()

        hT = C.sb("hT", [128, 8, NTOK], BF16)
        with ExitStack() as s1:
            condT = C.sb("condT", [128, 8, 2], F32, s1); S.dma(condT[:], A["condT"])
            scond = C.sb("scond", [128, 8, 2], F32, s1)
            S.act(scond[:], condT[:], AF.Silu)
            bmodT = C.sb("bmodT", [128, 16], F32, s1); S.dma(bmodT[:], A["bmodT"])
            gpreT = C.sb("gpreT", [128, 8], F32, s1); S.dma(gpreT[:], A["gpreT"])
            modT = C.sb("modT", [128, 16, 2], F32, s1)
            wm = [C.sb(f"wm{i}", [128, 8, 128], F32, s1) for i in range(2)]
            for fb in range(16):
                w = wm[fb % 2]
                S.dma(w[:], A["wmod"][fb], q="sp" if fb % 2 == 0 else "act")
                for kt in range(8):
                    S.mm(PS[fb % 2][:, 0:2], w[:, kt, :], scond[:, kt, :], start=(kt == 0), stop=(kt == 7))
                S.ts(modT[:, fb, :], PS[fb % 2][:, 0:2], bmodT[:, fb:fb + 1], None, ALU.add)
            if stop == 'mod':
                S.emit(st, final_wait=[])
                return nc
            Amod = C.sb("Amod", [128, 8, 2], F32, s1)
            S.ts(Amod[:], modT[:, 8:16, :], 1.0, None, ALU.add)
            S.tt(Amod[:], Amod[:], gpreT[:, :].unsqueeze(2).broadcast_to([128, 8, 2]), ALU.mult)
            xts = [C.sb(f"xt{i}", [128, 1024], F32, s1) for i in range(3)]
            junk = C.sb("junk", [128, 1024], BF16, s1)
            ssq = C.sb("ssq", [128, 34], F32, s1)
            rstd = C.sb("rstd", [128, 34], F32, s1)
            import os
            for i in range(int(os.environ.get('K_NT', '34'))):
                xt = xts[i % 3]
                S.dma(xt[:], A["x_all"][i * 128:(i + 1) * 128, :], q="sp" if i % 2 == 0 else "act")
                import os
                lvl = int(os.environ.get("K_DEBUG", "9"))
                if lvl < 1:
                    continue
                S.act(junk[:], xt[:], AF.Square, accum_out=ssq[:, i:i + 1])
                if lvl < 2:
                    continue
                S.act(rstd[:, i:i + 1], ssq[:, i:i + 1], AF.Sqrt, bias=EPS, scale=1.0 / D)
                if lvl < 3:
                    continue
                S.recip(rstd[:, i:i + 1], rstd[:, i:i + 1])
                S.ts(xt[:], xt[:], rstd[:, i:i + 1], None, ALU.mult)
                if lvl < 4:
                    continue
                jj = 0 if i < 32 else 1
                for half in range(2):
                    pt = PS[2 + half]
                    for k4 in range(4):
                        kt = half * 4 + k4
                        S.transpose(pt[:, k4 * 128:(k4 + 1) * 128], xt[:, kt * 128:(kt + 1) * 128], identF[:])
                    if lvl < 5:
                        continue
                    for k4 in range(4):
                        kt = half * 4 + k4
                        S.ts(hT[:, kt, i * 128:(i + 1) * 128], pt[:, k4 * 128:(k4 + 1) * 128],
                             Amod[:, kt, jj:jj + 1], modT[:, kt, jj:jj + 1], ALU.mult, ALU.add)
        S.fence()
        if dbg == "hT":
            hTo = nc.dram_tensor("hTo", [128, 8, NTOK], BF16, kind="ExternalOutput").ap()
            outs.append(S.dma(hTo, hT[:]))

        wbufs = [C.sb(f"wcol{i}", [128, 8, 128], BF16) for i in range(2)]
        wstate = {"n": 0}

        def load_w(tile_idx):
            w = wbufs[wstate["n"] % 2]
            wstate["n"] += 1
            S.dma(w[:], A["w_in"][tile_idx], q="pool")
            return w

        blocks = [(tb * 512, 512) for tb in range(8)] + [(4096, 256)]

        def proj_fm(w, blk, pt):
            t0, n = blk
            for kt in range(8):
                S.mm(pt[:, 0:n], w[:, kt, :], hT[:, kt, t0:t0 + n], start=(kt == 0), stop=(kt == 7))

        if do_att:
            with ExitStack() as s2:
                gq = C.sb("gq", [128, 1], F32, s2); S.dma(gq[:], A["gq"])
                gk = C.sb("gk", [128, 1], F32, s2); S.dma(gk[:], A["gk"])
                S.ts(gq[:], gq[:], 128.0 ** -0.5, None, ALU.mult)
                ropec = C.sb("ropec", [128, 64], F32, s2); S.dma(ropec[:], A["ropec"])
                ropes = C.sb("ropes", [128, 64], F32, s2); S.dma(ropes[:], A["ropes"])
                RmT = C.sb("RmT", [128, 128], F32, s2); S.dma(RmT[:], A["RmatT"])
                KT = C.sb("KT", [128, NTOK], BF16, s2)
                V = C.sb("V", [128, 34, 128], BF16, s2)
                sq = C.sb("sq", [128, 512], BF16, s2)
                rinv = C.sb("rinv", [128, 512], F32, s2)
                kn = C.sb("kn", [128, 512], F32, s2)
                t1 = C.sb("t1", [128, 512], F32, s2)
                t2 = C.sb("t2", [128, 512], F32, s2)

                def norm_rope(pt, blk, g, out_ap):
                    t0, n = blk
                    S.act(sq[:, 0:n], pt[:, 0:n], AF.Square)
                    S.mm(PS[2][:, 0:n], onesB[:], sq[:, 0:n])
                    S.act(rinv[:, 0:n], PS[2][:, 0:n], AF.Sqrt, bias=EPS, scale=1.0 / 128)
                    S.recip(rinv[:, 0:n], rinv[:, 0:n])
                    if t0 >= NLAT:
                        S.stt(out_ap, pt[:, 0:n], g[:, 0:1], rinv[:, 0:n], ALU.mult, ALU.mult)
                        return
                    S.stt(kn[:, 0:n], pt[:, 0:n], g[:, 0:1], rinv[:, 0:n], ALU.mult, ALU.mult)
                    S.mm(PS[2][:, 0:n], RmT[:], kn[:, 0:n])
                    r0 = t0 // 64
                    for (p0, p1) in ((0, 64), (64, 128)):
                        if p0 == 0:
                            cv = ropec[p0:p1, r0:r0 + 8].unsqueeze(2).broadcast_to([64, 8, 64])
                            sv = ropes[p0:p1, r0:r0 + 8].unsqueeze(2).broadcast_to([64, 8, 64])
                        else:
                            cv = ropec[p0:p1, :].unsqueeze(1).broadcast_to([64, 8, 64])
                            sv = ropes[p0:p1, :].unsqueeze(1).broadcast_to([64, 8, 64])
                        v3 = lambda a: a[p0:p1, 0:512].rearrange("p (r c) -> p r c", c=64)
                        S.tt(v3(t1), v3(kn), cv, ALU.mult, eng="pool")
                        S.tt(v3(t2), v3(PS[2]), sv, ALU.mult)
                    S.tt(out_ap, t1[:, 0:n], t2[:, 0:n], ALU.add)

                wk = load_w(0)
                for bi, blk in enumerate(blocks):
                    proj_fm(wk, blk, PS[bi % 2])
                    norm_rope(PS[bi % 2], blk, gk, KT[:, blk[0]:blk[0] + blk[1]])
                wv = load_w(1)
                for i4 in range(0, 34, 4):
                    nt = min(4, 34 - i4)
                    pt = PS[(i4 // 4) % 2]
                    for j in range(nt):
                        i = i4 + j
                        for kt in range(8):
                            S.mm(pt[:, j * 128:(j + 1) * 128], hT[:, kt, i * 128:(i + 1) * 128], wv[:, kt, :],
                                 start=(kt == 0), stop=(kt == 7))
                    S.act(V[:, i4:i4 + nt, :].rearrange("p a d -> p (a d)"), pt[:, 0:nt * 128], AF.Copy)
                qTb = [C.sb(f"qTb{i}", [128, 512], BF16, s2) for i in range(2)]
                gab = [C.sb(f"gab{i}", [128, 512], F32, s2) for i in range(2)]
                Pb = [C.sb(f"Pb{i}", [128, 512], BF16, s2) for i in range(3)]
                rden = C.sb("rden", [1, 512], F32, s2)
                o1 = C.sb("o1", [128, 512], F32, s2)
                mo = [C.sb(f"mo{i}", [128, 512], BF16, s2) for i in range(2)]
                work = [(h, bi) for h in range(4) for bi in range(len(blocks))]
                wq = {}

                def prep(idx):
                    h, bi = work[idx]
                    blk = blocks[bi]
                    if bi == 0:
                        wq["q"] = load_w(2 + 2 * h)
                        wq["g"] = load_w(3 + 2 * h)
                    proj_fm(wq["q"], blk, PS[0])
                    norm_rope(PS[0], blk, gq, qTb[idx % 2][:, 0:blk[1]])
                    proj_fm(wq["g"], blk, PS[1])
                    S.act(gab[idx % 2][:, 0:blk[1]], PS[1][:, 0:blk[1]], AF.Silu)

                def attend(idx):
                    h, bi = work[idx]
                    t0, n = blocks[bi]
                    ktiles = list(range(34)) if t0 < NLAT else [32, 33]
                    pO, pD = PS[5], PS[6]
                    nk = len(ktiles)

                    def smm(ki):
                        kt_i = ktiles[ki]
                        S.mm(PS[3 + ki % 2][:, 0:n], KT[:, kt_i * 128:(kt_i + 1) * 128], qTb[idx % 2][:, 0:n])
                    smm(0)
                    if nk > 1:
                        smm(1)
                    for ki, kt_i in enumerate(ktiles):
                        pS = PS[3 + ki % 2]
                        P = Pb[ki % 3]
                        S.act(P[:, 0:n], pS[:, 0:n], AF.Exp)
                        S.mm(pO[:, 0:n], V[:, kt_i, :], P[:, 0:n], start=(ki == 0), stop=(ki == nk - 1))
                        S.mm(pD[0:1, 0:n], onesB[:, 0:1], P[:, 0:n], start=(ki == 0), stop=(ki == nk - 1))
                        if ki + 2 < nk:
                            smm(ki + 2)
                    S.recip(rden[0:1, 0:n], pD[0:1, 0:n])
                    S.mm(PS[2][:, 0:n], onesF[0:1, :], rden[0:1, 0:n])
                    S.tt(o1[:, 0:n], pO[:, 0:n], gab[idx % 2][:, 0:n], ALU.mult)
                    m = mo[idx % 2]
                    S.tt(m[:, 0:n], o1[:, 0:n], PS[2][:, 0:n], ALU.mult)
                    outs.append(S.dma(mix.cols(512 + h * 128, 512 + (h + 1) * 128, t0, n), m[:, 0:n], q="sp"))

                prep(0)
                for idx in range(len(work)):
                    if idx + 1 < len(work):
                        prep(idx + 1)
                    attend(idx)
            S.fence()

        if do_hy:
            with ExitStack() as s3:
                convw = C.sb("convw", [128, 4, 3, 3], F32, s3); S.dma(convw[:], A["convw"])
                convb = C.sb("convb", [128, 4, 3], F32, s3); S.dma(convb[:], A["convb"])
                SEG = NLAT + 2 + NCTX + 2
                ust = C.sb("ust", [128, SEG], BF16, s3)
                S.memset(ust[:], 0.0)
                ctmp = C.sb("ctmp", [128, 1024], F32, s3)
                x1c = C.sb("x1c", [128, NTOK], BF16, s3)
                gT = C.sb("gT", [128, NTOK], BF16, s3)
                xg = C.sb("xg", [128, NTOK], BF16, s3)
                Xl = C.sb("Xl", [64, 64, 128], BF16, s3)
                Xfl = C.sb("Xfl", [128, 64, 128], BF16, s3)
                Bre = C.sb("Bre", [128, 64, 128], BF16, s3)
                Bim = C.sb("Bim", [128, 64, 128], BF16, s3)
                segs = [(k * 1024, k * 1024 + 1, 1024) for k in range(4)] + [(4096, NLAT + 3, 256)]

                def conv_stream(ct, s, out_fn):
                    for (t0, u0, n) in segs:
                        S.act(ctmp[:, 0:n], ust[:, u0:u0 + n], AF.Identity, bias=convb[:, ct, s:s + 1],
                              scale=convw[:, ct, s, 1:2])
                        S.stt(ctmp[:, 0:n], ust[:, u0 - 1:u0 - 1 + n], convw[:, ct, s, 0:1], ctmp[:, 0:n],
                              ALU.mult, ALU.add)
                        out_fn(t0, n, ust[:, u0 + 1:u0 + 1 + n], convw[:, ct, s, 2:3], ctmp[:, 0:n])

                def ust_cols(blk):
                    t0, n = blk
                    return slice(t0 + 1, t0 + 1 + n) if t0 < NLAT else slice(NLAT + 3, NLAT + 3 + n)

                for ct in range(n_ct):
                    wbase = 10 + 4 * ct
                    S.dma(Xfl[:, :, :].rearrange("p a n -> p (a n)"), xf_l[ct, :, :], q="act")
                    w = load_w(wbase + 0)
                    for bi, blk in enumerate(blocks):
                        proj_fm(w, blk, PS[bi % 2])
                        S.act(ust[:, ust_cols(blk)], PS[bi % 2][:, 0:blk[1]], AF.Copy)
                    conv_stream(ct, 1, lambda t0, n, a, sc, b: S.stt(x1c[:, t0:t0 + n], a, sc, b, ALU.mult, ALU.add))
                    w = load_w(wbase + 1)
                    for bi, blk in enumerate(blocks):
                        proj_fm(w, blk, PS[bi % 2])
                        S.act(ust[:, ust_cols(blk)], PS[bi % 2][:, 0:blk[1]], AF.Copy)

                    def gfn(t0, n, a, sc, b):
                        S.stt(b, a, sc, b, ALU.mult, ALU.add)
                        S.tt(gT[:, t0:t0 + n], b, x1c[:, t0:t0 + n], ALU.mult, eng="pool")
                    conv_stream(ct, 2, gfn)
                    w = load_w(wbase + 3)
                    for bi, blk in enumerate(blocks):
                        proj_fm(w, blk, PS[bi % 2])
                        S.act(xg[:, blk[0]:blk[0] + blk[1]], PS[bi % 2][:, 0:blk[1]], AF.Silu)
                    w = load_w(wbase + 2)
                    for bi, blk in enumerate(blocks):
                        proj_fm(w, blk, PS[bi % 2])
                        S.act(ust[:, ust_cols(blk)], PS[bi % 2][:, 0:blk[1]], AF.Copy)

                    def xfn(t0, n, a, sc, b):
                        S.stt(b, a, sc, b, ALU.mult, ALU.add)
                        S.tt(xg[:, t0:t0 + n], b, xg[:, t0:t0 + n], ALU.mult, eng="pool")
                    conv_stream(ct, 0, xfn)
                    for n2_0 in range(0, 64, 8):
                        for j in range(8):
                            S.transpose(pTb[0:64, j * 128:(j + 1) * 128], gT[:, n2_0 + j:NLAT:64], identB[:])
                        ov = Xl[0:64, :, :].rearrange("p a (n c) -> p n a c", c=2)[:, n2_0:n2_0 + 8, :, :]
                        iv = pTb[0:64, :].rearrange("p (n a c) -> p n a c", n=8, c=2)
                        S.copy(ov, iv, eng="act")

                    def mk_evac(tok0, H, li):
                        def evac(po, n2_0, ng):
                            ov = gT[:, tok0:tok0 + 64 * H].rearrange("p (a n) -> p n a", n=64)[:, n2_0:n2_0 + ng, :]
                            xv = xg[:, tok0:tok0 + 64 * H].rearrange("p (a n) -> p n a", n=64)[:, n2_0:n2_0 + ng, :]
                            iv = po[:, 0:ng * H].rearrange("p (n a) -> p n a", a=H)
                            S.stt(ov, iv, rnT[:, ct, li:li + 1], xv, ALU.mult, ALU.mult)
                        return evac
                    if "L" not in SKIP:
                        fft_conv_core(S, C, PS, TL, 128, Xl, Xfl, Bre, Bim, ftmp, mk_evac(0, 64, 0))
                    S.dma(Xfl[0:8, :, :].rearrange("p a n -> p (a n)"), xf_c[ct, :, :], q="act")
                    for n2_0 in range(0, 64, 8):
                        for j in range(8):
                            S.transpose(pTb[0:4, j * 128:(j + 1) * 128], gT[:, NLAT + n2_0 + j:NTOK:64], identB[:])
                        ov = Xl[0:4, :, :].rearrange("p a (n c) -> p n a c", c=2)[:, n2_0:n2_0 + 8, :, :]
                        iv = pTb[0:4, :].rearrange("p (n a c) -> p n a c", n=8, c=2)
                        S.copy(ov, iv, eng="act")
                    if "C" not in SKIP:
                        fft_conv_core(S, C, PS, TC, 8, Xl, Xfl, Bre, Bim, ftmp, mk_evac(NLAT, 4, 1))
                    for (c0, cn, cap) in mix.pieces(ct * 128, (ct + 1) * 128):
                        outs.append(S.dma(cap, gT[:, c0:c0 + cn], q="sp"))
        if shared is None:
            S.emit(st, final_wait=outs)
    return nc


def gate_rows(S, C, PS, A, pre, scond, onesF, stk):
    rows = C.sb("grow", [2, 1024], F32, stk)
    bro = C.sb("gbro", [2, 1024], F32, stk); S.dma(bro[:], A[pre + "bgate2"])
    gpo = C.sb("ggpo", [2, 1024], F32, stk); S.dma(gpo[:], A[pre + "gpost2"])
    sel2 = C.sb("sel2", [2, 2, 128], F32, stk); S.dma(sel2[:], A["sel2"])
    wg = [C.sb(f"wgate{i}", [128, 1024], F32, stk) for i in range(2)]
    for kt in range(8):
        w = wg[kt % 2]
        S.dma(w[:], A[pre + "wgate"][kt], q="sp" if kt % 2 == 0 else "act")
        for nb in range(2):
            S.mm(PS[nb][0:2, :], scond[:, kt, 0:2], w[:, nb * 512:(nb + 1) * 512], start=(kt == 0), stop=(kt == 7))
    for nb in range(2):
        S.tt(rows[:, nb * 512:(nb + 1) * 512], PS[nb][0:2, :], bro[:, nb * 512:(nb + 1) * 512], ALU.add)
    S.tt(rows[:], rows[:], gpo[:], ALU.mult)
    gg = []
    for j in range(2):
        g = C.sb(f"gg{j}", [128, 1024], F32, stk)
        for nb in range(2):
            S.mm(PS[2 + nb][:, :], sel2[0:2, j, :], rows[0:2, nb * 512:(nb + 1) * 512])
            S.copy(g[:, nb * 512:(nb + 1) * 512], PS[2 + nb][:, :], eng="act")
        gg.append(g)
    return gg


def outproj_residual(S, C, PS, A, stk, mixg, wout_ap, x_in, tiles, gg, x_out, identF, post=None):
    wout = C.sb("wout", [128, 16, 1024], BF16, stk)
    for kt in range(16):
        S.dma(wout[:, kt, :], wout_ap[kt], q="pool")
    mts = [C.sb(f"mt{i}", [128, 16, 512], BF16, stk) for i in range(2)]
    xts = [C.sb(f"xo{i}", [128, 1024], F32, stk) for i in range(3)]
    junk = C.sb("junk2", [128, 512], BF16, stk)
    ss2_s = [C.sb("ss2", [128, 2], F32, stk) for _ in range(2)]
    rs_s = [C.sb("rs", [128, 1], F32, stk) for _ in range(2)]
    tmp_s = [C.sb("tmpo", [128, 1024], F32, stk) for _ in range(2)]
    outs = []
    if not isinstance(mixg, Chunked):
        mixg = Chunked.wrap(mixg)
    cur = {"t0": None, "mt": None, "n": 0}
    for idx, (i, tok0, jj) in enumerate(tiles):
        g4 = tok0 // 512
        if cur["t0"] != g4:
            mt = mts[cur["n"] % 2]
            cur["n"] += 1
            ncol = min(512, mixg.ntok - g4 * 512)
            S.dma(mt[:, :, 0:ncol], mixg.cols(0, mixg.rows, g4 * 512, ncol).rearrange("(kt p) t -> p kt t", p=128), q="sp")
            cur["t0"], cur["mt"] = g4, mt
        mt = cur["mt"]
        c0 = tok0 - g4 * 512
        xt = xts[idx % 3]
        ss2, rs, tmp = ss2_s[idx % 2], rs_s[idx % 2], tmp_s[idx % 2]
        PY = (PS[0], PS[1]) if idx % 2 == 0 else (PS[4], PS[5])
        S.dma(xt[:], x_in[tok0:tok0 + 128, :], q="act")
        for nb in range(2):
            for kt in range(16):
                S.mm(PY[nb][:, :], mt[:, kt, c0:c0 + 128], wout[:, kt, nb * 512:(nb + 1) * 512],
                     start=(kt == 0), stop=(kt == 15))
            S.act(junk[:], PY[nb][:, :], AF.Square, accum_out=ss2[:, nb:nb + 1])
        S.tt(rs[:], ss2[:, 0:1], ss2[:, 1:2], ALU.add)
        S.act(rs[:], rs[:], AF.Sqrt, bias=EPS, scale=1.0 / D)
        S.recip(rs[:], rs[:])
        for nb in range(2):
            sl = slice(nb * 512, (nb + 1) * 512)
            S.stt(tmp[:, sl], PY[nb][:, :], rs[:, 0:1], gg[jj][:, sl], ALU.mult, ALU.mult)
        S.tt(xt[:], xt[:], tmp[:], ALU.add, eng="pool")
        if x_out is not None:
            outs.append(S.dma(x_out[i * 128:(i + 1) * 128, :], xt[:], q="sp"))
        if post is not None:
            post(i, xt, jj)
    return outs


def adaln_shift_scale(S, C, PS, A, pre, scond, stk):
    bmodT = C.sb("bmodT", [128, 16], F32, stk); S.dma(bmodT[:], A[pre + "bmodT"])
    gpreT = C.sb("gpreT", [128, 8], F32, stk); S.dma(gpreT[:], A[pre + "gpreT"])
    modT = C.sb("modT", [128, 16, 2], F32, stk)
    wm = [C.sb(f"wm{i}", [128, 8, 128], F32, stk) for i in range(2)]
    for fb in range(16):
        w = wm[fb % 2]
        S.dma(w[:], A[pre + "wmod"][fb], q="sp" if fb % 2 == 0 else "act")
        for kt in range(8):
            S.mm(PS[4 + fb % 2][:, 0:2], w[:, kt, :], scond[:, kt, :], start=(kt == 0), stop=(kt == 7))
        S.ts(modT[:, fb, :], PS[4 + fb % 2][:, 0:2], bmodT[:, fb:fb + 1], None, ALU.add)
    Amod = C.sb("Amod", [128, 8, 2], F32, stk)
    S.ts(Amod[:], modT[:, 8:16, :], 1.0, None, ALU.add)
    S.tt(Amod[:], Amod[:], gpreT[:, :].unsqueeze(2).broadcast_to([128, 8, 2]), ALU.mult)
    return Amod, modT


def p2_host_inputs(inp, b, hf, mix0g):
    f = lambda a: np.ascontiguousarray(a, dtype=np.float32)
    o = {}
    o["mix0g"] = np.ascontiguousarray(mix0g)
    o["x_all"] = f(np.concatenate([inp["x"][b], inp["ctx"][b]], 0))
    cond = np.stack([inp["c"][b], inp["c_ctx"]], -1)
    o["condT"] = f(cond.reshape(8, 128, 2).transpose(1, 0, 2))
    o["l0_wgate"] = f(inp["w_mod"][0][:, 2048:3072].reshape(8, 128, 1024))
    o["l0_bgate2"] = f(np.broadcast_to(inp["b_mod"][0][2048:3072][None], (2, 1024)))
    o["l0_gpost2"] = f(np.broadcast_to(inp["g_post"][0][None], (2, 1024)))
    sel2 = np.zeros((2, 2, 128), np.float32); sel2[0, 0] = 1; sel2[1, 1] = 1
    o["sel2"] = sel2
    wo = inp["e_w_out"][0]
    order = np.concatenate([np.arange(r * 512, r * 512 + 512) if part == 0 else np.arange(1024 + r * 512, 1024 + r * 512 + 512)
                            for r in range(2) for part in range(2)])
    o["wout0"] = f(wo[order].reshape(16, 128, 1024))
    wm = inp["w_mod"][1][:, :2048]
    o["l1_wmod"] = f(wm.reshape(8, 128, 16, 128).transpose(2, 1, 0, 3))
    o["l1_bmodT"] = f(inp["b_mod"][1][:2048].reshape(16, 128).T)
    o["l1_gpreT"] = f(inp["g_pre"][1].reshape(8, 128).T)
    W = inp["o_w_in"][0]
    heads = [4 * hf + h for h in range(4)]
    def tl(cols):
        return W[:, cols].reshape(8, 128, len(cols)).transpose(1, 0, 2)
    o["wq"] = f(np.stack([tl(np.arange(hd * 128, hd * 128 + 128)) for hd in heads]))
    o["wk"] = f(np.stack([tl(np.arange(1024 + hd * 128, 1024 + hd * 128 + 128)) for hd in heads]))
    o["wv"] = f(np.stack([tl(np.arange(2048 + hd * 256, 2048 + hd * 256 + 256)) for hd in heads]))
    o["woz"] = f(np.stack([tl(np.concatenate([np.arange(4096 + hd * 256, 4096 + hd * 256 + 256),
                                              np.arange(6144 + hd * 256, 6144 + hd * 256 + 256)])) for hd in heads]))
    gcols = np.array([8192 + g * 8 + hd for g in range(4) for hd in heads])
    o["wg"] = f(tl(gcols))
    cw = inp["o_conv_w"][0]; cb = inp["o_conv_b"][0]
    convw = np.zeros((128, 4, 2, 3), np.float32); convb = np.zeros((128, 4, 2), np.float32)
    for h, hd in enumerate(heads):
        for s in range(2):
            c0 = s * 1024 + hd * 128
            convw[:, h, s, :] = cw[:, c0:c0 + 128].T
            convb[:, h, s] = cb[c0:c0 + 128]
    o["convw1"] = convw; o["convb1"] = convb
    gb = inp["o_gate_b"][0].reshape(4, 8)[:, heads]
    o["gateb"] = f(gb.T)
    hn = inp["o_head_norm"][0].reshape(8, 256)[heads]
    o["hn"] = f(np.broadcast_to(hn[None], (128, 4, 256)))
    t = np.arange(128)
    same = (t[:, None] // 64) == (t[None, :] // 64)
    o["maskF"] = f((same & (t[:, None] <= t[None, :])))
    o["maskB"] = f((same & (t[:, None] >= t[None, :])))
    sel4 = np.zeros((4, 4, 128), np.float32)
    for h in range(4):
        sel4[h, h] = 1
    o["sel4"] = sel4
    o["identF"] = np.eye(128, dtype=np.float32)
    return o


def build_p2(nc, sample, n_heads=4, stop=None, shared=None):
    if shared is None:
        A = _declare_inputs(nc, sample)
        x1o = nc.dram_tensor("x1o", [NTOK, D], F32, kind="ExternalOutput").ap()
        mix1 = Chunked.wrap(nc.dram_tensor("mix1", [1024, NLAT], BF16, kind="ExternalOutput").ap())
    else:
        A = shared["A"]
        x1o = shared["x1o"]
        mix1 = shared["mix1"]
    outs = []
    NCH = NTOK // 64
    with ExitStack() as st:
        if shared is None:
            S = Sched(nc)
            C = Ctx(nc, st)
            PS = [C.ps(f"pb{i}", [128, 512], F32) for i in range(7)]
            pTb = C.ps("pTb", [128, 1024], BF16)
        else:
            S, C, PS, pTb = shared["S"], shared["C"], shared["PS"], shared["pTb"]
            C.stack = st
        identF = C.sb("identF", [128, 128], F32); S.dma(identF[:], A["identF"])
        identB = C.sb("identB", [128, 128], BF16); S.dma(identB[:], A["identF"], q="pool")
        onesF = C.sb("onesF", [128, 128], F32); S.memset(onesF[:], 1.0)
        condT = C.sb("condT", [128, 8, 2], F32); S.dma(condT[:], A["condT"])
        scond = C.sb("scond", [128, 8, 2], F32)
        S.act(scond[:], condT[:], AF.Silu)
        h1T = C.sb("h1T", [128, 8, NTOK], BF16)
        tokq = C.sb("tokq", [128, 34, 24], F32)
        ebB = C.sb("ebB", [128, 2, 4, NCH], F32)
        with ExitStack() as sa:
            Amod, modT = adaln_shift_scale(S, C, PS, A, "l1_", scond, sa)
            gg = gate_rows(S, C, PS, A, "l0_", scond, onesF, sa)
            ssq_s = [C.sb("ssq", [128, 1], F32, sa) for _ in range(2)]
            rstd_s = [C.sb("rstd", [128, 1], F32, sa) for _ in range(2)]
            junk = C.sb("junk", [128, 1024], BF16, sa)
            xh_s = [C.sb("xh", [128, 1024], F32, sa) for _ in range(2)]

            def post(i, xt, jj):
                ssq, rstd, xh = ssq_s[i % 2], rstd_s[i % 2], xh_s[i % 2]
                S.act(junk[:], xt[:], AF.Square, accum_out=ssq[:, 0:1])
                S.act(rstd[:], ssq[:], AF.Sqrt, bias=EPS, scale=1.0 / D)
                S.recip(rstd[:], rstd[:])
                S.ts(xh[:], xt[:], rstd[:, 0:1], None, ALU.mult)
                for half in range(2):
                    pt = PS[2 + half]
                    for k4 in range(4):
                        kt = half * 4 + k4
                        S.transpose(pt[:, k4 * 128:(k4 + 1) * 128], xh[:, kt * 128:(kt + 1) * 128], identF[:])
                    for k4 in range(4):
                        kt = half * 4 + k4
                        S.ts(h1T[:, kt, i * 128:(i + 1) * 128], pt[:, k4 * 128:(k4 + 1) * 128],
                             Amod[:, kt, jj:jj + 1], modT[:, kt, jj:jj + 1], ALU.mult, ALU.add)
            tiles = [(i, i * 128, 0 if i < 32 else 1) for i in range(34)]
            outs += outproj_residual(S, C, PS, A, sa, A["mix0g"], A["wout0"], A["x_all"], tiles, gg, x1o, identF, post)
        S.fence()
        if stop == "A":
            dbg = nc.dram_tensor("h1To", [128, 8, NTOK], BF16, kind="ExternalOutput").ap()
            outs.append(S.dma(dbg, h1T[:]))
            S.emit(st, final_wait=outs)
            return nc
        blocks = [(tb * 512, 512) for tb in range(8)] + [(4096, 256)]
        with ExitStack() as sb_:
            wg = C.sb("wg", [128, 8, 16], BF16, sb_); S.dma(wg[:], A["wg"], q="pool")
            gateb = C.sb("gateb", [4, 4], F32, sb_); S.dma(gateb[:], A["gateb"])
            sel4 = C.sb("sel4", [4, 4, 128], F32, sb_); S.dma(sel4[:], A["sel4"])
            gi = C.sb("gi", [4, NTOK], F32, sb_)
            gf = C.sb("gf", [4, NTOK], F32, sb_)
            cumB = C.sb("cumB", [4, NTOK], F32, sb_)
            e1p = C.sb("e1p", [4, NTOK], F32, sb_)
            ebr = C.sb("ebr", [4, NCH], F32, sb_)
            v3 = lambda t_: t_[:, :].rearrange("p (c j) -> p c j", j=64)
            for d in range(2):
                for gq_, dstg in ((2 * d, gi), (2 * d + 1, gf)):
                    for bi, (t0, n) in enumerate(blocks):
                        pt = PS[bi % 2]
                        for kt in range(8):
                            S.mm(pt[0:4, 0:n], wg[:, kt, gq_ * 4:(gq_ + 1) * 4], h1T[:, kt, t0:t0 + n], start=(kt == 0), stop=(kt == 7))
                        S.ts(dstg[:, t0:t0 + n], pt[0:4, 0:n], gateb[:, gq_:gq_ + 1], None, ALU.add)
                S.act(gf[:], gf[:], AF.Exp, scale=-1.0)
                S.act(gf[:], gf[:], AF.Ln, bias=1.0)
                src, dst = gf, cumB
                for k in range(6):
                    sh = 1 << k
                    s3, d3 = v3(src), v3(dst)
                    if d == 0:
                        S.tt(d3[:, :, sh:64], s3[:, :, sh:64], s3[:, :, 0:64 - sh], ALU.add)
                        S.copy(d3[:, :, 0:sh], s3[:, :, 0:sh], eng="pool")
                    else:
                        S.tt(d3[:, :, 0:64 - sh], s3[:, :, 0:64 - sh], s3[:, :, sh:64], ALU.add)
                        S.copy(d3[:, :, 64 - sh:64], s3[:, :, 64 - sh:64], eng="pool")
                    src, dst = dst, src
                cum, thr = src, dst
                e1 = gi
                endpos = 63 if d == 0 else 0
                S.act(ebr[:], v3(cum)[:, :, endpos], AF.Exp, scale=-1.0)
                S.tt(e1[:], gi[:], cum[:], ALU.add)
                S.act(e1[:], e1[:], AF.Exp)
                S.ts(e1[:], e1[:], 128.0 ** -0.5, None, ALU.mult)
                S.act(thr[:], cum[:], AF.Exp)
                S.tt(v3(e1p), v3(e1), ebr[:, :].unsqueeze(2).broadcast_to([4, NCH, 64]), ALU.mult)
                for h in range(4):
                    S.mm(PS[2][:, 0:NCH], sel4[0:4, h, :], ebr[0:4, :])
                    S.copy(ebB[:, d, h, :], PS[2][:, 0:NCH], eng="act")
                for i in range(34):
                    pt = PS[3 + i % 2]
                    for qi, src_t in enumerate((e1, e1p, thr)):
                        S.transpose(pt[:, qi * 4:(qi + 1) * 4], src_t[0:4, i * 128:(i + 1) * 128], identF[0:4, 0:4])
                    S.copy(tokq[:, i, d * 12:(d + 1) * 12], pt[:, 0:12], eng="act")
        S.fence()
        if stop == "B":
            dbg = nc.dram_tensor("tokqo", [128, 34, 24], F32, kind="ExternalOutput").ap()
            outs.append(S.dma(dbg, tokq[:]))
            dbg2 = nc.dram_tensor("ebBo", [128, 2, 4, NCH], F32, kind="ExternalOutput").ap()
            outs.append(S.dma(dbg2, ebB[:]))
            S.emit(st, final_wait=outs)
            return nc
        with ExitStack() as sc:
            convw = C.sb("convw1", [128, 4, 2, 3], F32, sc); S.dma(convw[:], A["convw1"])
            convb = C.sb("convb1", [128, 4, 2], F32, sc); S.dma(convb[:], A["convb1"])
            hn = C.sb("hn", [128, 256], F32, sc)
            maskF = C.sb("maskF", [128, 128], F32, sc); S.dma(maskF[:], A["maskF"])
            maskB = C.sb("maskB", [128, 128], F32, sc); S.dma(maskB[:], A["maskB"])
            masks = (maskF, maskB)
            wbuf = C.sb("wbuf", [128, 8, 512], BF16, sc)
            wq = wbuf[:, :, 0:128]
            wk = wbuf[:, :, 128:256]
            wv = wbuf[:, :, 256:512]
            woz = wbuf
            SEG = NLAT + 2 + NCTX + 2
            ust = C.sb("ust", [128, SEG], BF16, sc); S.memset(ust[:], 0.0)
            ctmp = C.sb("ctmp", [128, 1024], F32, sc)
            qT = C.sb("qT", [128, NTOK], BF16, sc)
            kT = C.sb("kT", [128, NTOK], BF16, sc)
            ktD = [C.sb(f"ktD{d}", [128, 34, 128], BF16, sc) for d in range(2)]
            vaug = C.sb("vaug", [128, 34, 257], BF16, sc)
            S.memset(vaug[:, :, 256:257], 1.0)
            hsum = C.sb("hsum", [128, 32, 256], F32, sc)
            Cst = [C.sb(f"Cst{d}", [128, 257], F32, sc) for d in range(2)]
            Cbf = [C.sb(f"Cbf{d}", [128, 257], BF16, sc) for d in range(2)]
            Sm = [C.sb(f"Sm{d}", [128, 128], BF16, sc) for d in range(2)]
            dm = [C.sb(f"dm{d}", [128, 1], F32, sc) for d in range(2)]
            fin = [(C.sb("so", [128, 256], F32, sc), C.sb("sz", [128, 256], F32, sc), C.sb("hh", [128, 256], F32, sc),
                    C.sb("junk3", [128, 256], BF16, sc), C.sb("ssq3", [128, 1], F32, sc), C.sb("mixv", [128, 256], BF16, sc))
                   for _ in range(2)]
            mob = [C.sb(f"mob{i}", [128, 2, 512], BF16, sc) for i in range(2)]
            segs = [(k * 1024, k * 1024 + 1, 1024) for k in range(4)] + [(4096, NLAT + 3, 256)]

            def ust_cols(blk):
                t0, n = blk
                return slice(t0 + 1, t0 + 1 + n) if t0 < NLAT else slice(NLAT + 3, NLAT + 3 + n)

            for h in range(n_heads):
                S.dma(wq, A["wq"][h], q="pool")
                S.dma(wk, A["wk"][h], q="pool")
                S.dma(wv, A["wv"][h], q="pool")
                S.dma(hn[:], A["hn"][:, h, :])
                S.memset(hsum[:], 0.0, eng="pool")
                for (w, s, dstT) in ((wq, 0, qT), (wk, 1, kT)):
                    for bi, blk in enumerate(blocks):
                        t0, n = blk
                        pt = PS[bi % 2]
                        for kt in range(8):
                            S.mm(pt[:, 0:n], w[:, kt, :], h1T[:, kt, t0:t0 + n], start=(kt == 0), stop=(kt == 7))
                        S.act(ust[:, ust_cols(blk)], pt[:, 0:n], AF.Copy)
                    for (t0, u0, n) in segs:
                        S.act(ctmp[:, 0:n], ust[:, u0:u0 + n], AF.Identity, bias=convb[:, h, s:s + 1], scale=convw[:, h, s, 1:2])
                        S.stt(ctmp[:, 0:n], ust[:, u0 - 1:u0 - 1 + n], convw[:, h, s, 0:1], ctmp[:, 0:n], ALU.mult, ALU.add)
                        S.stt(ctmp[:, 0:n], ust[:, u0 + 1:u0 + 1 + n], convw[:, h, s, 2:3], ctmp[:, 0:n], ALU.mult, ALU.add)
                        S.act(dstT[:, t0:t0 + n], ctmp[:, 0:n], AF.Silu)
                for i4 in range(0, 34, 8):
                    nt = min(8, 34 - i4)
                    for j in range(nt):
                        i = i4 + j
                        S.transpose(pTb[:, j * 128:(j + 1) * 128], kT[:, i * 128:(i + 1) * 128], identB[:])
                    for j in range(nt):
                        i = i4 + j
                        for d in range(2):
                            S.act(ktD[d][:, i, :], pTb[:, j * 128:(j + 1) * 128], AF.Identity, scale=tokq[:, i, d * 12 + 4 + h:d * 12 + 5 + h])
                for i in range(34):
                    pt = PS[i % 2]
                    for kt in range(8):
                        S.mm(pt[:, 0:256], h1T[:, kt, i * 128:(i + 1) * 128], wbuf[:, kt, 256:512], start=(kt == 0), stop=(kt == 7))
                    S.act(vaug[:, i, 0:256], pt[:, 0:256], AF.Copy)
                S.dma(woz[:], A["woz"][h], q="pool")

                def finish_gen(i, fi):
                    pt = PS[6]
                    so, sz, hh, junk3, ssq3, mixv = fin[fi % 2]
                    for kt in range(8):
                        S.mm(pt[:, :], h1T[:, kt, i * 128:(i + 1) * 128], woz[:, kt, :], start=(kt == 0), stop=(kt == 7))
                        if kt % 2 == 1:
                            yield
                    S.act(so[:], pt[:, 0:256], AF.Sigmoid)
                    S.act(sz[:], pt[:, 256:512], AF.Silu)
                    S.tt(hh[:], hsum[:, i, :], so[:], ALU.mult)
                    S.act(junk3[:], hh[:], AF.Square, accum_out=ssq3[:, 0:1])
                    S.act(ssq3[:], ssq3[:], AF.Sqrt, bias=EPS, scale=1.0 / 256)
                    S.recip(ssq3[:], ssq3[:])
                    S.tt(sz[:], sz[:], hn[:], ALU.mult, eng="pool")
                    S.stt(mixv[:], hh[:], ssq3[:, 0:1], sz[:], ALU.mult, ALU.mult)
                    yield
                    slot = fin_slot[i]
                    mo_ = mob[(slot // 4) % 2]
                    for j in range(2):
                        S.transpose(pTb[:, j * 128:(j + 1) * 128], mixv[:, j * 128:(j + 1) * 128], identB[:])
                    S.copy(mo_[:, :, (slot % 4) * 128:(slot % 4 + 1) * 128],
                           pTb[:, 0:256].rearrange("p (j t) -> p j t", j=2), eng="act")
                    outs.append(S.dma(mix1.cols(h * 256, (h + 1) * 256, i * 128, 128).rearrange("(j p) t -> p j t", p=128),
                                      mo_[:, :, (slot % 4) * 128:(slot % 4 + 1) * 128], q="sp"))

                fin_slot = {}
                pending_fin = []
                done_cnt = [0] * 32
                n_fin = [0]
                for d in range(2):
                    S.memset(Cst[d][:], 0.0)
                    S.memset(Cbf[d][:], 0.0, eng="pool")
                orderF = [32, 33] + list(range(32))
                orderB = [33, 32] + list(range(31, -1, -1))
                for step in range(34):
                    for d, tile_i in ((0, orderF[step]), (1, orderB[step])):
                        i = tile_i
                        is_lat = i < 32
                        pS, pN, pU = PS[3 * d], PS[3 * d + 1], PS[3 * d + 2]
                        tk = slice(i * 128, (i + 1) * 128)
                        chunks = (0, 1) if d == 0 else (1, 0)
                        if is_lat:
                            S.mm(pS[:, 0:128], kT[:, tk], qT[:, tk])
                            S.stt(Sm[d][:], pS[:, 0:128], tokq[:, i, d * 12 + h:d * 12 + h + 1], masks[d][:], ALU.mult, ALU.mult)
                            S.mm(pN[:, 0:257], Sm[d][:], vaug[:, i, :], start=True, stop=False)
                        for ci, c in enumerate(chunks):
                            rows = slice(c * 64, (c + 1) * 64)
                            ch = i * 2 + c
                            if is_lat:
                                S.mm(pN[rows, 0:257], qT[:, i * 128 + c * 64:i * 128 + (c + 1) * 64], Cbf[d][:, :],
                                     start=False, stop=True)
                            S.mm(pU[:, 0:257], ktD[d][rows, i, :], vaug[rows, i, :])
                            S.stt(Cst[d][:], Cst[d][:], ebB[:, d, h, ch:ch + 1], pU[:, 0:257], ALU.mult, ALU.add)
                            S.copy(Cbf[d][:], Cst[d][:], eng="pool")
                        if is_lat:
                            S.act(dm[d][:], pN[:, 256:257], AF.Abs)
                            S.ts(dm[d][:], dm[d][:], tokq[:, i, d * 12 + 8 + h:d * 12 + 9 + h], None, ALU.max)
                            S.recip(dm[d][:], dm[d][:])
                            S.stt(hsum[:, i, :], pN[:, 0:256], dm[d][:, 0:1], hsum[:, i, :], ALU.mult, ALU.add)
                            done_cnt[i] += 1
                            if done_cnt[i] == 2:
                                fin_slot[i] = n_fin[0]
                                pending_fin.append(finish_gen(i, n_fin[0]))
                                n_fin[0] += 1
                        if pending_fin:
                            if next(pending_fin[0], "end") == "end":
                                pending_fin.pop(0)
                for g_ in pending_fin:
                    for _ in g_:
                        pass
        if shared is None:
            S.emit(st, final_wait=outs)
    return nc


def p3_host_inputs(inp, b, hf, mix1g, x1):
    f = lambda a: np.ascontiguousarray(a, dtype=np.float32)
    o = {}
    o["mix1g"] = np.ascontiguousarray(mix1g[:, hf * 2048:(hf + 1) * 2048])
    o["x1loc"] = f(x1[hf * 2048:(hf + 1) * 2048])
    cond = np.stack([inp["c"][b], inp["c_ctx"]], -1)
    o["condT"] = f(cond.reshape(8, 128, 2).transpose(1, 0, 2))
    o["l1_wgate"] = f(inp["w_mod"][1][:, 2048:3072].reshape(8, 128, 1024))
    o["l1_bgate2"] = f(np.broadcast_to(inp["b_mod"][1][2048:3072][None], (2, 1024)))
    o["l1_gpost2"] = f(np.broadcast_to(inp["g_post"][1][None], (2, 1024)))
    sel2 = np.zeros((2, 2, 128), np.float32); sel2[0, 0] = 1; sel2[1, 1] = 1
    o["sel2"] = sel2
    o["wout1"] = f(inp["o_w_out"][0].reshape(16, 128, 1024))
    o["identF"] = np.eye(128, dtype=np.float32)
    return o


def build_p3(nc, sample, shared=None):
    if shared is None:
        A = _declare_inputs(nc, sample)
        yo = nc.dram_tensor("yo", [2048, D], F32, kind="ExternalOutput").ap()
        ntile = 16
    else:
        A = shared["A"]
        yo = shared["yo"]
        ntile = 32
    with ExitStack() as st:
        if shared is None:
            S = Sched(nc)
            C = Ctx(nc, st)
            PS = [C.ps(f"pb{i}", [128, 512], F32) for i in range(7)]
        else:
            S, C, PS = shared["S"], shared["C"], shared["PS"]
            C.stack = st
        identF = C.sb("identF", [128, 128], F32); S.dma(identF[:], A["identF"])
        onesF = C.sb("onesF", [128, 128], F32); S.memset(onesF[:], 1.0)
        condT = C.sb("condT", [128, 8, 2], F32); S.dma(condT[:], A["condT"])
        scond = C.sb("scond", [128, 8, 2], F32)
        S.act(scond[:], condT[:], AF.Silu)
        gg = gate_rows(S, C, PS, A, "l1_", scond, onesF, st)
        tiles = [(i, i * 128, 0) for i in range(ntile)]
        outs = outproj_residual(S, C, PS, A, st, A["mix1g"], A["wout1"], A["x1loc"], tiles, gg, yo, identF, None)
        if shared is None:
            S.emit(st, final_wait=outs)
        else:
            shared["outs"] += outs
    return nc


CORES = [(b, hf) for b in range(4) for hf in range(2)]


def _launch(build, maps, **kw):
    nc = bass.Bass("TRN2", target_bir_lowering=False)
    build(nc, maps[0], **kw)
    res = run_bass_kernel_spmd(nc, maps, core_ids=list(range(len(maps))))
    return res.results


GROUPS = [[0, 1], [2, 3], [4, 5], [6, 7]]


def fused_host_inputs(inp, b, hf):
    o = {}
    dummy_mix0 = np.zeros((2048, NTOK), NPBF)
    p2 = p2_host_inputs(inp, b, hf, dummy_mix0)
    p3 = p3_host_inputs(inp, b, hf, np.zeros((2048, NLAT), NPBF), np.zeros((NLAT, D), np.float32))
    for d in (p3, p2, p1_host_inputs(inp, b, hf)):
        o.update(d)
    for k in ("mix0g", "mix1g", "x1loc"):
        o.pop(k)
    return o


def build_fused(nc, sample):
    A = _declare_inputs(nc, sample)
    yo = nc.dram_tensor("yo", [NLAT, D], F32, kind="ExternalOutput").ap()
    CS = 1024
    mix0 = Chunked.make(nc, "mix0", 1024, NTOK, BF16, CS, kind="Internal")
    mix0g = Chunked.make(nc, "mix0g", 2048, NTOK, BF16, CS, kind="Internal", addr_space="Local")
    x1o = nc.dram_tensor("x1o", [NTOK, D], F32, kind="Internal").ap()
    mix1 = Chunked.make(nc, "mix1", 1024, NLAT, BF16, CS, kind="Internal")
    mix1g = Chunked.make(nc, "mix1g", 2048, NLAT, BF16, CS, kind="Internal", addr_space="Local")
    A["mix0g"] = mix0g
    A["mix1g"] = mix1g
    A["x1loc"] = x1o[0:NLAT, :]
    with ExitStack() as st:
        S = Sched(nc)
        C = Ctx(nc, st)
        PS = [C.ps(f"pb{i}", [128, 512], F32) for i in range(7)]
        pTb = C.ps("pTb", [128, 1024], BF16)
        sh = dict(S=S, C=C, PS=PS, pTb=pTb, A=A, mix0=mix0, x1o=x1o, mix1=mix1, yo=yo, outs=[])
        import os
        nocc = os.environ.get("K_FUSE_NOCC") == "1"
        light = os.environ.get("K_P1_LIGHT") == "1"

        def exchange(gch, lch):
            for (_, _, go), (_, _, gi_) in zip(gch.chunks, lch.chunks):
                if nocc:
                    S.dma(go[0:1024, :], gi_, q="sp")
                    S.dma(go[1024:2048, :], gi_, q="act")
                else:
                    S.allgather(go, gi_, GROUPS)
        if light:
            build_p1(nc, sample, shared=sh, n_ct=1, do_att=False)
        else:
            build_p1(nc, sample, shared=sh)
        S.fence()
        exchange(mix0g, mix0)
        build_p2(nc, sample, shared=sh, n_heads=(1 if light else 4))
        S.fence()
        exchange(mix1g, mix1)
        build_p3(nc, sample, shared=sh)
        C.stack = st
        S.emit(st, final_wait=sh["outs"])
        print("fused stats", S.stats, flush=True)
    return nc


def kernel(**inputs):
    inp = {k: np.asarray(v) for k, v in inputs.items()}
    res = _launch(build_fused, [fused_host_inputs(inp, b, hf) for (b, hf) in CORES])
    out = np.zeros((4, NLAT, D), np.float32)
    for ci, (b, hf) in enumerate(CORES):
        out[b, hf * 2048:(hf + 1) * 2048] = np.asarray(res[ci]["yo"])[hf * 2048:(hf + 1) * 2048]
    return out
```
